# Optimizing a Trainium2 kernel written in Bass

```python
import jax
import jax.numpy as jnp
from jax import lax
import numpy as np

D_MODEL = 1024
BATCH = 8
SEQ = 2048
DEPTH = 4

CHUNK = 64
N_EVEN = (DEPTH + 1) // 2
N_ODD = DEPTH // 2
D_FF = 2816
MIX_WIDTH = D_MODEL
RMS_EPS = 1e-6
LN_EPS = 1e-5

GSU_BLOCK = 128
A_GROUPS = 4
D_A = MIX_WIDTH // 2
A_GROUP_DIM = D_A // A_GROUPS
D_B = MIX_WIDTH - D_A
B_HEAD = 64
B_HEADS = D_B // B_HEAD
LORA_W = 64
LORA_A = 64
LORA_G = 128
LNX_EPS = 64e-5
P_B = 3 * D_B + LORA_W + LORA_A + LORA_G
P_EVEN = 2 * D_A + P_B
C_HEADS = 8
Q_LORA = 256
KV_LORA = 128
QK_NOPE = 64
QK_ROPE = 32
V_HEAD = 64
D_C = C_HEADS * V_HEAD
ROPE_THETA = 10000.0
Q_BLOCK = 128
ATTN_SCALE = (QK_NOPE + QK_ROPE) ** -0.5
D_D = MIX_WIDTH - D_C
CONV_K = 31
P_ODD = Q_LORA + KV_LORA + QK_ROPE + 2 * D_D

kernel_name = 'hybrid_chunk_causal_encoder'


def rmsnorm(x, g):
    xf = x.astype(jnp.float32)
    y = xf * lax.rsqrt(jnp.mean(xf * xf, axis=-1, keepdims=True) + RMS_EPS)
    return (y * g.astype(jnp.float32)).astype(x.dtype)


def layernorm(x, g, b, eps):
    xf = x.astype(jnp.float32)
    mu = jnp.mean(xf, axis=-1, keepdims=True)
    var = jnp.mean(jnp.square(xf - mu), axis=-1, keepdims=True)
    y = (xf - mu) * lax.rsqrt(var + eps) * g.astype(jnp.float32) + b.astype(jnp.float32)
    return y.astype(x.dtype)


def swiglu_ffn(x, w_in, w_out):
    gate, up = jnp.split(x @ w_in, 2, axis=-1)
    return (jax.nn.silu(gate) * up) @ w_out


def token_shift(z):
    return jnp.pad(z[:, :-1], ((0, 0), (1, 0), (0, 0)))


def rope_tables(s):
    inv_freq = ROPE_THETA ** (-jnp.arange(0, QK_ROPE, 2, dtype=jnp.float32) / QK_ROPE)
    ang = jnp.arange(s, dtype=jnp.float32)[:, None] * inv_freq[None, :]
    return jnp.cos(ang), jnp.sin(ang)


def apply_rope(x, cos, sin):
    x1, x2 = jnp.split(x, 2, axis=-1)
    c = cos[None, :, None, :]
    s = sin[None, :, None, :]
    return jnp.concatenate([x1 * c - x2 * s, x1 * s + x2 * c], axis=-1).astype(x.dtype)


def gsu_mixer(za, ws, bs, ln_g, ln_b):
    u, v = jnp.split(jax.nn.gelu(za), 2, axis=-1)
    v = layernorm(v, ln_g, ln_b, LN_EPS)
    b, s, _ = v.shape
    vb = v.reshape(b, s // GSU_BLOCK, GSU_BLOCK, A_GROUPS, A_GROUP_DIM)
    pos_chunk = jnp.arange(GSU_BLOCK) // CHUNK
    mask = pos_chunk[:, None] >= pos_chunk[None, :]
    w = jnp.where(mask[None], ws, jnp.zeros_like(ws))
    mixed = jnp.einsum('gij,bnjgc->bnigc', w, vb) + bs.T[None, None, :, :, None]
    return u * mixed.reshape(b, s, D_A)


def rwkv7_mixer(zb, shift_mu, decay_w0, decay_up, iclr_a0, iclr_up, gate_up,
                k_k, k_a, r_k, lnx_g, lnx_b):
    b, s, _ = zb.shape
    z = zb + shift_mu * (token_shift(zb) - zb)
    r, k, v, xw, xa, xg = jnp.split(
        z, [D_B, 2 * D_B, 3 * D_B, 3 * D_B + LORA_W, 3 * D_B + LORA_W + LORA_A], axis=-1)
    logw = -jax.nn.softplus(-(decay_w0 + jnp.tanh(xw) @ decay_up)) - 0.5
    decay = jnp.exp(-jnp.exp(logw.astype(jnp.float32)))
    a = jax.nn.sigmoid(iclr_a0 + xa @ iclr_up)
    g = jax.nn.sigmoid(xg) @ gate_up
    heads = lambda t: t.reshape(b, s, B_HEADS, B_HEAD)
    kk = heads(k * k_k).astype(jnp.float32)
    kk = kk / jnp.maximum(jnp.linalg.norm(kk, axis=-1, keepdims=True), 1e-12)
    k = k * (1.0 + (a - 1.0) * k_a)
    rh, kh, vh, ah = heads(r), heads(k), heads(v), heads(a)
    tm = lambda t: jnp.moveaxis(t, 1, 0).astype(jnp.float32)
    xs = (tm(rh), tm(heads(decay)), tm(kh), tm(vh), tm(-kk), tm(kk * ah))

    def step(state, inp):
        r_t, w_t, k_t, v_t, a_t, b_t = inp
        sa = jnp.einsum('bhvk,bhk->bhv', state, a_t)
        state = (state * w_t[:, :, None, :] + sa[..., None] * b_t[:, :, None, :]
                 + v_t[..., None] * k_t[:, :, None, :])
        return state, jnp.einsum('bhvk,bhk->bhv', state, r_t)

    state0 = jnp.zeros((b, B_HEADS, B_HEAD, B_HEAD), jnp.float32)
    _, y = lax.scan(step, state0, xs)
    y = jnp.moveaxis(y, 0, 1)
    y = layernorm(y, lnx_g.reshape(B_HEADS, B_HEAD), lnx_b.reshape(B_HEADS, B_HEAD), LNX_EPS)
    bonus = jnp.sum(rh * kh * r_k, axis=-1, keepdims=True) * vh
    return ((y + bonus).reshape(b, s, D_B) * g).astype(zb.dtype)


def mla_mixer(cq, ckv, kr, q_norm, wq_up, kv_norm, wkv_up, cos, sin):
    b, s, _ = cq.shape
    q = (rmsnorm(cq, q_norm) @ wq_up).reshape(b, s, C_HEADS, QK_NOPE + QK_ROPE)
    q_nope = q[..., :QK_NOPE]
    q_rope = apply_rope(q[..., QK_NOPE:], cos, sin)
    kv = (rmsnorm(ckv, kv_norm) @ wkv_up).reshape(b, s, C_HEADS, QK_NOPE + V_HEAD)
    k_nope, v = kv[..., :QK_NOPE], kv[..., QK_NOPE:]
    k_rope = apply_rope(kr[:, :, None, :], cos, sin)[:, :, 0]
    n_blk = s // Q_BLOCK

    def to_blocks(t):
        return jnp.moveaxis(t.reshape(b, n_blk, Q_BLOCK, C_HEADS, t.shape[-1]), 1, 0)

    key_chunk = jnp.arange(s) // CHUNK

    def attend(args):
        qn, qr, blk = args
        sc = (jnp.einsum('bqhd,bkhd->bhqk', qn, k_nope)
              + jnp.einsum('bqhd,bkd->bhqk', qr, k_rope))
        sc = sc.astype(jnp.float32) * ATTN_SCALE
        q_chunk = (blk * Q_BLOCK + jnp.arange(Q_BLOCK)) // CHUNK
        sc = jnp.where(key_chunk[None, :] <= q_chunk[:, None], sc, -jnp.inf)
        p = jax.nn.softmax(sc, axis=-1).astype(v.dtype)
        return jnp.einsum('bhqk,bkhd->bqhd', p, v)

    o = lax.map(attend, (to_blocks(q_nope), to_blocks(q_rope), jnp.arange(n_blk)))
    return jnp.moveaxis(o, 0, 1).reshape(b, s, D_C)


def conv_mixer(zd, conv_w, conv_b, ln_g, ln_b):
    h = jax.nn.glu(zd, axis=-1)
    h = lax.conv_general_dilated(h, conv_w[:, None, :], window_strides=(1,),
                                 padding=[(CONV_K - 1, 0)],
                                 dimension_numbers=('NWC', 'WIO', 'NWC'),
                                 feature_group_count=D_D) + conv_b
    return jax.nn.silu(layernorm(h, ln_g, ln_b, LN_EPS))


def setup_inputs(seed: int = 0) -> dict:
    key = jax.random.key(seed)
    ks = iter(jax.random.split(key, 48))
    f32 = jnp.float32

    def nrm(shape, scale):
        return jax.random.normal(next(ks), shape, f32) * scale

    def gain(shape):
        return 1.0 + nrm(shape, 0.01)

    return {
        'x': nrm((BATCH, SEQ, D_MODEL), 1.0),
        'norm_ffn1': gain((DEPTH, D_MODEL)),
        'ffn1_in': nrm((DEPTH, D_MODEL, 2 * D_FF), D_MODEL ** -0.5),
        'ffn1_out': nrm((DEPTH, D_FF, D_MODEL), D_FF ** -0.5),
        'norm_mix': gain((DEPTH, D_MODEL)),
        'norm_ffn2': gain((DEPTH, D_MODEL)),
        'ffn2_in': nrm((DEPTH, D_MODEL, 2 * D_FF), D_MODEL ** -0.5),
        'ffn2_out': nrm((DEPTH, D_FF, D_MODEL), D_FF ** -0.5),
        'even_w_in': nrm((N_EVEN, D_MODEL, P_EVEN), D_MODEL ** -0.5),
        'even_w_out': nrm((N_EVEN, MIX_WIDTH, D_MODEL), MIX_WIDTH ** -0.5),
        'gsu_ws': nrm((N_EVEN, A_GROUPS, GSU_BLOCK, GSU_BLOCK), 0.5 * GSU_BLOCK ** -0.5),
        'gsu_bs': 1.0 + nrm((N_EVEN, A_GROUPS, GSU_BLOCK), 0.1),
        'gsu_ln_g': gain((N_EVEN, D_A)),
        'gsu_ln_b': nrm((N_EVEN, D_A), 0.01),
        'shift_mu': jax.random.uniform(next(ks), (N_EVEN, P_B), f32),
        'decay_w0': jnp.linspace(-6.0, -1.0, D_B, dtype=f32)[None, :] + nrm((N_EVEN, D_B), 0.1),
        'decay_up': nrm((N_EVEN, LORA_W, D_B), 0.1),
        'iclr_a0': nrm((N_EVEN, D_B), 0.1),
        'iclr_up': nrm((N_EVEN, LORA_A, D_B), 0.5 * LORA_A ** -0.5),
        'gate_up': nrm((N_EVEN, LORA_G, D_B), LORA_G ** -0.5),
        'k_k': 0.85 + nrm((N_EVEN, D_B), 0.05),
        'k_a': 1.0 + nrm((N_EVEN, D_B), 0.05),
        'r_k': nrm((N_EVEN, B_HEADS, B_HEAD), 0.1),
        'lnx_g': gain((N_EVEN, D_B)),
        'lnx_b': nrm((N_EVEN, D_B), 0.01),
        'odd_w_in': nrm((N_ODD, D_MODEL, P_ODD), D_MODEL ** -0.5),
        'odd_w_out': nrm((N_ODD, MIX_WIDTH, D_MODEL), MIX_WIDTH ** -0.5),
        'q_norm': gain((N_ODD, Q_LORA)),
        'wq_up': nrm((N_ODD, Q_LORA, C_HEADS * (QK_NOPE + QK_ROPE)), Q_LORA ** -0.5),
        'kv_norm': gain((N_ODD, KV_LORA)),
        'wkv_up': nrm((N_ODD, KV_LORA, C_HEADS * (QK_NOPE + V_HEAD)), KV_LORA ** -0.5),
        'conv_w': nrm((N_ODD, CONV_K, D_D), CONV_K ** -0.5),
        'conv_b': nrm((N_ODD, D_D), 0.01),
        'conv_ln_g': gain((N_ODD, D_D)),
        'conv_ln_b': nrm((N_ODD, D_D), 0.01),
        'final_norm': gain((D_MODEL,)),
    }


def reference(x, norm_ffn1, ffn1_in, ffn1_out, norm_mix, norm_ffn2, ffn2_in, ffn2_out,
              even_w_in, even_w_out, gsu_ws, gsu_bs, gsu_ln_g, gsu_ln_b,
              shift_mu, decay_w0, decay_up, iclr_a0, iclr_up, gate_up,
              k_k, k_a, r_k, lnx_g, lnx_b,
              odd_w_in, odd_w_out, q_norm, wq_up, kv_norm, wkv_up,
              conv_w, conv_b, conv_ln_g, conv_ln_b, final_norm):
    cos, sin = rope_tables(x.shape[1])
    for layer in range(DEPTH):
        x = x + 0.5 * swiglu_ffn(rmsnorm(x, norm_ffn1[layer]), ffn1_in[layer], ffn1_out[layer])
        h = rmsnorm(x, norm_mix[layer])
        if layer % 2 == 0:
            e = layer // 2
            z = h @ even_w_in[e]
            ya = gsu_mixer(z[..., :2 * D_A], gsu_ws[e], gsu_bs[e], gsu_ln_g[e], gsu_ln_b[e])
            yb = rwkv7_mixer(z[..., 2 * D_A:], shift_mu[e], decay_w0[e], decay_up[e],
                             iclr_a0[e], iclr_up[e], gate_up[e], k_k[e], k_a[e], r_k[e],
                             lnx_g[e], lnx_b[e])
            x = x + jnp.concatenate([ya, yb], axis=-1) @ even_w_out[e]
        else:
            o = layer // 2
            z = h @ odd_w_in[o]
            cq = z[..., :Q_LORA]
            ckv = z[..., Q_LORA:Q_LORA + KV_LORA]
            kr = z[..., Q_LORA + KV_LORA:Q_LORA + KV_LORA + QK_ROPE]
            zd = z[..., Q_LORA + KV_LORA + QK_ROPE:]
            yc = mla_mixer(cq, ckv, kr, q_norm[o], wq_up[o], kv_norm[o], wkv_up[o], cos, sin)
            yd = conv_mixer(zd, conv_w[o], conv_b[o], conv_ln_g[o], conv_ln_b[o])
            x = x + jnp.concatenate([yc, yd], axis=-1) @ odd_w_out[o]
        x = x + 0.5 * swiglu_ffn(rmsnorm(x, norm_ffn2[layer]), ffn2_in[layer], ffn2_out[layer])
    return rmsnorm(x, final_norm)
```

```python
import numpy as np
from contextlib import ExitStack
import concourse.bass as bass
import concourse.mybir as mybir
from concourse.bass_utils import run_bass_kernel_spmd

F32 = mybir.dt.float32
BF16 = mybir.dt.bfloat16
AF = mybir.ActivationFunctionType
ALU = mybir.AluOpType
AX = mybir.AxisListType

S = 2048
D = 1024
NC_ = 8
NTB = 4
DFF = 2816
NHC = 22
DEPTH = 4
GROUPS = [(0, 6), (6, 12), (12, 17), (17, 22)]
RMS_EPS = 1e-6
LN_EPS = 1e-5
ODD_NS = 2 + 1 + 4 * 31 + 4 + 4 + 4
ATTN_SCALE = 96.0 ** -0.5
LNX_EPS = 64e-5
EV_NR = 1792 + 9 * 512
OFF_MU, OFF_KK, OFF_KA, OFF_RK, OFF_LG, OFF_LB, OFF_GLG, OFF_GLB, OFF_W0, OFF_A0 = (
    0, 1792, 2304, 2816, 3328, 3840, 4352, 4864, 5376, 5888)
C0 = float(np.exp(-0.5))
NEUMANN_L = 4
NHS = 4


class Eng:
    def __init__(self, name, eng, sem, skip_self):
        self.name = name
        self.eng = eng
        self.sem = sem
        self.count = 0
        self.seen = {}
        self.skip_self = skip_self


class Buf:
    __slots__ = ("ap", "lw", "rd", "sem", "dcnt", "name", "persist", "psum")

    def __init__(self, ap, name="", persist=False, psum=False):
        self.persist = persist
        self.psum = psum
        self.ap = ap
        self.lw = None
        self.rd = []
        self.sem = None
        self.dcnt = 0
        self.name = name


class KB:
    def __init__(self, nc, es):
        self.nc = nc
        self.es = es
        self.nsem = 0
        self.PE = Eng("pe", nc.tensor, self.mksem("s_pe"), True)
        self.ACT = Eng("act", nc.scalar, self.mksem("s_act"), False)
        self.DVE = Eng("dve", nc.vector, self.mksem("s_dve"), False)
        self.POOL = Eng("pool", nc.gpsimd, self.mksem("s_pool"), False)
        self.SP = Eng("sp", nc.sync, self.mksem("s_sp"), False)
        self.compute = [self.PE, self.ACT, self.DVE, self.POOL]
        self.all = self.compute + [self.SP]
        self.nuniq = 0
        self.sem_pool = []
        self.phase_stack = []

    def phase(self):
        kb = self

        class _Ph:
            def __enter__(self_):
                kb.phase_stack.append([])

            def __exit__(self_, *a):
                for b in kb.phase_stack.pop():
                    kb.sem_pool.append((b.sem, b.dcnt))
                    b.sem = None
                return False
        return _Ph()

    def rotate(self, E):
        E.sem = self.mksem("s_%s_r%d" % (E.name, self.nsem))
        E.count = 0

    def mksem(self, name):
        self.nsem += 1
        return self.es.enter_context(self.nc.semaphore(name))

    def sb(self, es, shape, dt, name=None):
        self.nuniq += 1
        return es.enter_context(self.nc.sbuf_tensor("sb_%s_%d" % (name or "t", self.nuniq), shape, dt))

    def _wait(self, E, tok):
        sem, val, src = tok
        if src is E and E.skip_self:
            return
        k = id(sem)
        if E.seen.get(k, 0) >= val:
            return
        E.eng.wait_ge(sem, val)
        E.seen[k] = val

    def op(self, E, fn, w=(), r=()):
        for b in r:
            if b.lw is not None:
                self._wait(E, b.lw)
            if b.psum:
                for t in b.rd:
                    if t[2] is not E:
                        self._wait(E, t)
        for b in w:
            if b.lw is not None:
                self._wait(E, b.lw)
            for t in b.rd:
                if t[2] is not E:
                    self._wait(E, t)
        inst = fn()
        E.count += 1
        inst.then_inc(E.sem, 1)
        tok = (E.sem, E.count, E)
        for b in r:
            b.rd.append(tok)
        for b in w:
            b.lw = tok
            b.rd = []
        return inst

    def dma(self, Q, out_ap, in_ap, w=(), r=()):
        for b in r:
            if b.lw is not None:
                self._wait(Q, b.lw)
        for b in w:
            if b.lw is not None:
                self._wait(Q, b.lw)
            for t in b.rd:
                self._wait(Q, t)
        owner = w[0] if len(w) else r[0]
        if owner.sem is None:
            if self.sem_pool and self.phase_stack and not owner.persist:
                owner.sem, owner.dcnt = self.sem_pool.pop()
            else:
                owner.sem = self.mksem("s_b%d" % self.nsem)
            if self.phase_stack and not owner.persist:
                self.phase_stack[-1].append(owner)
        inst = Q.eng.dma_start(out=out_ap, in_=in_ap)
        owner.dcnt += 1
        inst.then_inc(owner.sem, 16)
        tok = (owner.sem, 16 * owner.dcnt, None)
        for b in r:
            b.rd.append(tok)
        for b in w:
            b.lw = tok
            b.rd = []
        return tok

    def barrier(self):
        for E in self.all:
            for P in self.compute:
                if P is E:
                    continue
                if P.count > 0:
                    self._wait(E, (P.sem, P.count, None))

    def wait_tok(self, E, tok):
        self._wait(E, tok)


class WStream:
    def __init__(self, K, Q, slots):
        self.K = K
        self.Q = Q
        self.slots = slots
        self.jobs = []
        self.issued = 0
        self.released = 0

    def add(self, dram_ap, view):
        self.jobs.append((dram_ap, view))
        return len(self.jobs) - 1

    def ensure(self, j):
        j = min(j, len(self.jobs) - 1)
        n = len(self.slots)
        while self.issued <= j:
            i = self.issued
            assert i - n < self.released, "weight ring slot still in use"
            slot = self.slots[i % n]
            dram_ap, view = self.jobs[i]
            self.K.dma(self.Q, view(slot.ap), dram_ap, w=[slot])
            self.issued += 1

    def get(self, j):
        self.ensure(j)
        return self.slots[j % len(self.slots)]

    def release(self, j):
        self.released = max(self.released, j + 1)
        self.ensure(self.released + len(self.slots) - 1)


class Prog:
    def __init__(self, cfg):
        self.cfg = cfg

    def build(self):
        cfg = self.cfg
        nc = bass.Bass("TRN2", target_bir_lowering=False)
        self.nc = nc
        dram = {}

        def din(name, shape, dt=F32):
            dram[name] = nc.dram_tensor(name, list(shape), dt, kind="ExternalInput").ap()
            return dram[name]

        self.d_x = din("xT", [128, NC_, S])
        self.d_gains = din("gains", [128, 13 * NC_])
        self.d_ffn_in = din("ffn_in", [2 * DEPTH, NHC, 128, NC_, 256])
        self.d_ffn_out = din("ffn_out", [2 * DEPTH, NHC, 128, D])
        self.d_odd_in = din("odd_in", [2, 7, 128, NC_, 256])
        self.d_odd_out = din("odd_out", [2, 8, 128, D])
        self.d_wq = din("wq", [2, 128, 2, 768])
        self.d_wqr = din("wqr", [2, 128, 2, 768])
        self.d_wkv = din("wkv", [2, 128, 1024])
        self.d_odd_small = din("odd_small", [2, 128, ODD_NS])
        self.d_rope = din("rope", [128, 2, S])
        self.d_even_in = din("even_in", [2, 11, 128, NC_, 256])
        self.d_even_out = din("even_out", [2, 8, 128, D])
        self.d_gsu_ws = din("gsu_wsT", [2, 128, 4, 128])
        self.d_gsu_bs = din("gsu_bs4", [2, 1, 4, 512])
        self.d_even_rows = din("even_rows", [2, 1, EV_NR])
        self.d_lora_d = din("lora_d", [2, 64, 512])
        self.d_lora_i = din("lora_i", [2, 128, 512])
        self.d_lora_g = din("lora_g", [2, 128, 512])
        self.d_zB = nc.dram_tensor("zB_scratch", [S + 1, 1792], F32).ap()
        self.zB = Buf(self.d_zB, "zB", persist=True)
        self.d_out = nc.dram_tensor("outT", [128, NC_, S], F32, kind="ExternalOutput").ap()

        self.dbg_toks = {}
        with ExitStack() as es:
            K = KB(nc, es)
            self.K = K
            xT = K.sb(es, [128, NC_, S], F32, "xT")
            self.xT = xT
            self.x = [[Buf(xT[:, c, tb * 512:(tb + 1) * 512], "x%d_%d" % (c, tb)) for tb in range(NTB)]
                      for c in range(NC_)]
            gains = K.sb(es, [128, 13 * NC_], F32, "gains")
            self.gains = Buf(gains[:], "gains")
            ones_bf = K.sb(es, [128, 128], BF16, "ones_bf")
            self.ones_bf = Buf(ones_bf[:], "ones")
            ones_f = K.sb(es, [128, 128], F32, "ones_f")
            self.ones_f = Buf(ones_f[:], "ones_f")
            K.op(K.DVE, lambda: nc.vector.memset(ones_f[:], 1.0), w=[self.ones_f])
            ident_f = K.sb(es, [128, 128], F32, "ident_f")
            self.ident_f = Buf(ident_f[:], "ident_f")
            K.op(K.POOL, lambda: nc.gpsimd.memset(ident_f[:], 1.0), w=[self.ident_f])
            K.op(K.POOL, lambda: nc.gpsimd.affine_select(out=ident_f[:], in_=ident_f[:], pattern=[[-1, 128]],
                                                         compare_op=ALU.is_equal, fill=0.0, base=0,
                                                         channel_multiplier=1), w=[self.ident_f], r=[self.ident_f])
            ident_b = K.sb(es, [128, 128], BF16, "ident_b")
            self.ident_b = Buf(ident_b[:], "ident_b")
            K.op(K.DVE, lambda: nc.vector.tensor_copy(ident_b[:], ident_f[:]), w=[self.ident_b], r=[self.ident_f])
            epst = K.sb(es, [128, 4], F32, "epst")
            self.epsb = Buf(epst[:], "eps")
            self.eps_cols = {}
            for i, e in enumerate([1e-6, 1e-5, 64e-5, 0.0]):
                K.op(K.DVE, lambda i=i, e=e: nc.vector.memset(epst[:, i:i + 1], e), w=[self.epsb])
                self.eps_cols[e] = epst[:, i:i + 1]
            self.nrm_sq = [Buf(K.sb(es, [128, 512], BF16)[:], "sq") for _ in range(2)]
            self.nrm_rs = [Buf(K.sb(es, [128, 512], F32)[:], "rstd") for _ in range(2)]
            wi_t = [K.sb(es, [128, NC_, 256], BF16, "wi%d" % i) for i in range(3)]
            wo_t = [K.sb(es, [128, D], BF16, "wo%d" % i) for i in range(12)]
            self.wi = WStream(K, K.POOL, [Buf(t[:], "wi", persist=True) for t in wi_t])
            self.wo = WStream(K, K.POOL, [Buf(t[:], "wo", persist=True) for t in wo_t])
            self.ps = [Buf(es.enter_context(nc.psum_tensor("ps%d" % i, [128, 512], F32))[:], "ps%d" % i, psum=True)
                       for i in range(8)]
            self.ps_i = 0

            self.plan_jobs()

            K.op(K.DVE, lambda: nc.vector.memset(ones_bf[:], 1.0), w=[self.ones_bf])
            K.dma(K.SP, gains[:], self.d_gains[:, :], w=[self.gains])
            for c in range(NC_):
                for tb in range(NTB):
                    K.dma(K.SP, self.x[c][tb].ap, self.d_x[:, c, tb * 512:(tb + 1) * 512], w=[self.x[c][tb]])
            self.wi.ensure(2)
            self.wo.ensure(5)

            for l in range(cfg["layers"]):
                if l in cfg.get("skip_layers", []):
                    continue
                if l > 0:
                    K.barrier()
                    K.rotate(K.PE)
                if cfg.get("ffn1", True):
                    with K.phase():
                        self.ffn(l, 0)
                if cfg.get("mixer", True):
                    with K.phase():
                        self.mixer(l)
                if cfg.get("ffn2", True):
                    with K.phase():
                        self.ffn(l, 1)
            with K.phase():
                self.final(cfg.get("final_norm", True))
        return nc

    def dbg(self, name, ap, buf):
        if not self.cfg.get("dbg"):
            return
        if name in self.dbg_toks:
            return
        if self.cfg.get("dbg_names") is not None and name not in self.cfg["dbg_names"]:
            return
        d = self.nc.dram_tensor("dbg_" + name, list(ap.shape), ap.dtype, kind="ExternalOutput").ap()
        self.dbg_toks[name] = self.K.dma(self.K.SP, d, ap, r=[buf])

    def psum(self):
        b = self.ps[self.ps_i % 8]
        self.ps_i += 1
        return b

    def plan_jobs(self):
        cfg = self.cfg
        self.jobs_wi = {}
        self.jobs_wo = {}
        full = lambda ap: ap
        for l in range(cfg["layers"]):
            if l in cfg.get("skip_layers", []):
                continue
            for which in range(2):
                if which == 1 and cfg.get("mixer", True):
                    if l % 2 == 0:
                        e = l // 2
                        for sl in range(11):
                            self.jobs_wi[("even", l, sl)] = self.wi.add(self.d_even_in[e, sl], full)
                        for ch in range(8):
                            self.jobs_wo[("even", l, ch)] = self.wo.add(self.d_even_out[e, ch], full)
                    if l % 2 == 1:
                        o = l // 2
                        for sl in range(7):
                            self.jobs_wi[("odd", l, sl)] = self.wi.add(self.d_odd_in[o, sl], full)
                        for ch in [4, 5, 6, 7, 0, 1, 2, 3]:
                            self.jobs_wo[("odd", l, ch)] = self.wo.add(self.d_odd_out[o, ch], full)
                if not cfg.get("ffn%d" % (which + 1), True):
                    continue
                f = l * 2 + which
                for (a, b) in GROUPS:
                    for hc in range(a, b):
                        self.jobs_wi[(f, hc)] = self.wi.add(self.d_ffn_in[f, hc], full)
                    for hc in range(a, b):
                        self.jobs_wo[(f, hc)] = self.wo.add(self.d_ffn_out[f, hc], full)

    def rmsnorm_fm(self, es, src, gain_col, dst, nchunks, nfeat, eps, tbs=range(NTB), extra_r=()):
        K, nc = self.K, self.nc
        sq, rs = self.nrm_sq, self.nrm_rs
        for tb in tbs:
            pb = self.psum()
            for c in range(nchunks):
                s = sq[c % 2]
                K.op(K.ACT, lambda s=s, c=c: nc.scalar.activation(out=s.ap, in_=src[c][tb].ap, func=AF.Square),
                     w=[s], r=[src[c][tb]])
                K.op(K.PE, lambda s=s, c=c: nc.tensor.matmul(pb.ap, self.ones_bf.ap, s.ap, start=(c == 0),
                                                             stop=(c == nchunks - 1)),
                     w=[pb], r=[s, self.ones_bf])
            r = rs[tb % 2]
            K.op(K.ACT, lambda: nc.scalar.activation(out=r.ap, in_=pb.ap, func=AF.Sqrt, scale=1.0 / nfeat,
                                                     bias=self.eps_ap(eps)),
                 w=[r], r=[pb, self.epsb])
            K.op(K.DVE, lambda: nc.vector.reciprocal(out=r.ap, in_=r.ap), w=[r], r=[r])
            for c in range(nchunks):
                K.op(K.DVE, lambda c=c: nc.vector.scalar_tensor_tensor(
                    out=dst[c][tb].ap, in0=src[c][tb].ap, scalar=gain_col(c), in1=r.ap,
                    op0=ALU.mult, op1=ALU.mult), w=[dst[c][tb]], r=[src[c][tb], r, self.gains] + list(extra_r))

    def eps_ap(self, eps):
        return self.eps_cols[eps]

    def gain_col(self, idx):
        return lambda c: self.gains.ap[:, idx * NC_ + c: idx * NC_ + c + 1]

    def ffn(self, l, which):
        K, nc = self.K, self.nc
        f = l * 2 + which
        gidx = (0 if which == 0 else 2) * DEPTH + l
        with ExitStack() as es:
            hT_t = K.sb(es, [128, NC_, S], BF16, "hT")
            hT = [[Buf(hT_t[:, c, tb * 512:(tb + 1) * 512]) for tb in range(NTB)] for c in range(NC_)]
            hid_t = K.sb(es, [128, 6, S], BF16, "hid")
            hid = [[Buf(hid_t[:, j, tb * 512:(tb + 1) * 512]) for tb in range(NTB)] for j in range(6)]
            sg = [Buf(K.sb(es, [128, 512], F32)[:], "sg") for _ in range(2)]
            self.rmsnorm_fm(es, self.x, self.gain_col(gidx), hT, NC_, D, RMS_EPS)
            n = 0
            for (a, b) in GROUPS:
                for j, hc in enumerate(range(a, b)):
                    ji = self.jobs_wi[(f, hc)]
                    wi = self.wi.get(ji)
                    for tb in range(NTB):
                        pg = self.psum()
                        pu = self.psum()
                        for c in range(NC_):
                            K.op(K.PE, lambda c=c: nc.tensor.matmul(pg.ap, wi.ap[:, c, 0:128], hT[c][tb].ap,
                                                                    start=(c == 0), stop=(c == NC_ - 1)),
                                 w=[pg], r=[wi, hT[c][tb]])
                        for c in range(NC_):
                            K.op(K.PE, lambda c=c: nc.tensor.matmul(pu.ap, wi.ap[:, c, 128:256], hT[c][tb].ap,
                                                                    start=(c == 0), stop=(c == NC_ - 1)),
                                 w=[pu], r=[wi, hT[c][tb]])
                        s = sg[n % 2]
                        n += 1
                        K.op(K.ACT, lambda: nc.scalar.activation(out=s.ap, in_=pg.ap, func=AF.Silu), w=[s], r=[pg])
                        K.op(K.DVE, lambda: nc.vector.tensor_tensor(out=hid[j][tb].ap, in0=s.ap, in1=pu.ap,
                                                                    op=ALU.mult), w=[hid[j][tb]], r=[s, pu])
                    self.wi.release(ji)
                wos = [self.wo.get(self.jobs_wo[(f, hc)]) for hc in range(a, b)]
                ng = b - a
                for d in range(NC_):
                    for tb in range(NTB):
                        po = self.psum()
                        for j in range(ng):
                            K.op(K.PE, lambda j=j: nc.tensor.matmul(po.ap, wos[j].ap[:, d * 128:(d + 1) * 128],
                                                                    hid[j][tb].ap, start=(j == 0), stop=(j == ng - 1)),
                                 w=[po], r=[wos[j], hid[j][tb]])
                        xb = self.x[d][tb]
                        K.op(K.DVE, lambda: nc.vector.scalar_tensor_tensor(
                            out=xb.ap, in0=po.ap, scalar=0.5, in1=xb.ap, op0=ALU.mult, op1=ALU.add),
                            w=[xb], r=[po, xb])
                for hc in range(a, b):
                    self.wo.release(self.jobs_wo[(f, hc)])
            K.barrier()

    def mixer(self, l):
        if l % 2 == 1:
            self.mixer_odd(l)
        else:
            self.mixer_even(l)

    def mixer_even(self, l):
        K, nc = self.K, self.nc
        e = l // 2
        PE, ACT, DVE, POOL, SP = K.PE, K.ACT, K.DVE, K.POOL, K.SP
        ps = self.ps
        cfg = self.cfg

        def outproj(mixb, chs, tbs=range(NTB)):
            wos = [self.wo.get(self.jobs_wo[("even", l, ch)]) for ch in chs]
            for d in range(NC_):
                for tb in tbs:
                    po = self.psum4()
                    for i, ch in enumerate(chs):
                        K.op(PE, lambda i=i: nc.tensor.matmul(po.ap, wos[i].ap[:, d * 128:(d + 1) * 128], mixb[i][tb].ap,
                                                              start=(i == 0), stop=(i == len(chs) - 1)),
                             w=[po], r=[wos[i], mixb[i][tb]])
                    xb = self.x[d][tb]
                    K.op(DVE, lambda: nc.vector.tensor_tensor(out=xb.ap, in0=po.ap, in1=xb.ap, op=ALU.add),
                         w=[xb], r=[po, xb])

        with ExitStack() as es:
            rows = self.d_even_rows[e]

            def bc_tile(esx, off, n, name):
                t = K.sb(esx, [128, n], F32, name)
                bf = Buf(t[:], name)
                K.dma(SP, t[:], rows[0:1, off:off + n].partition_broadcast(128), w=[bf])
                return t, bf

            with ExitStack() as esg:
                uT_t = K.sb(esg, [128, 4, S], BF16, "uT")
                uT = [[Buf(uT_t[:, c, tb * 512:(tb + 1) * 512]) for tb in range(NTB)] for c in range(4)]
                vtm_t = K.sb(esg, [128, 16, 512], BF16, "vtm")
                vtm = [Buf(vtm_t[:, i, :]) for i in range(16)]
                wsT_t = K.sb(esg, [128, 4, 128], BF16, "wsT")
                wsT = Buf(wsT_t[:], "wsT")
                K.dma(POOL, wsT_t[:], self.d_gsu_ws[e], w=[wsT])
                K.op(POOL, lambda: nc.gpsimd.memset(wsT_t[64:128, :, 0:64], 0.0), w=[wsT], r=[wsT])
                bs4_t = K.sb(esg, [1, 4, 512], F32, "bs4")
                bs4 = Buf(bs4_t[:], "bs4")
                K.dma(SP, bs4_t[:], self.d_gsu_bs[e], w=[bs4])
                with ExitStack() as esa:
                    hT_t = K.sb(esa, [128, NC_, S], BF16, "hT")
                    hT = [[Buf(hT_t[:, c, tb * 512:(tb + 1) * 512]) for tb in range(NTB)] for c in range(NC_)]
                    self.rmsnorm_fm(esa, self.x, self.gain_col(DEPTH + l), hT, NC_, D, RMS_EPS)
                    glg_t, glg = bc_tile(esa, OFF_GLG, 512, "glg")
                    glb_t, glb = bc_tile(esa, OFF_GLB, 512, "glb")
                    g32 = [Buf(K.sb(esa, [128, 512], F32, "g32_%d" % i)[:]) for i in range(2)]
                    gsq = Buf(K.sb(esa, [128, 512], F32, "gsq")[:])
                    st_t = K.sb(esa, [128, 8], F32, "vstat")
                    st = Buf(st_t[:], "vstat")
                    zst_t = [K.sb(esa, [128, 4, 256], F32, "zstage%d" % i) for i in range(2)]
                    zst = [Buf(t[:]) for t in zst_t]
                    zrow_t = K.sb(esa, [1, 1792], F32, "zrow")
                    zrow = Buf(zrow_t[:], "zrow")
                    K.op(DVE, lambda: nc.vector.memset(zrow_t[:], 0.0), w=[zrow])
                    K.dma(SP, self.d_zB[0:1, :], zrow_t[:], w=[self.zB], r=[zrow])
                    for sl in range(2):
                        jj = self.jobs_wi[("even", l, sl)]
                        wb = self.wi.get(jj)
                        for tb in range(NTB):
                            for half in range(2):
                                pb = self.psum()
                                for c in range(NC_):
                                    K.op(PE, lambda c=c, pb=pb: nc.tensor.matmul(
                                        pb.ap, wb.ap[:, c, half * 128:(half + 1) * 128], hT[c][tb].ap,
                                        start=(c == 0), stop=(c == NC_ - 1)), w=[pb], r=[wb, hT[c][tb]])
                                ub = uT[sl * 2 + half][tb]
                                K.op(ACT, lambda pb=pb, ub=ub: nc.scalar.activation(out=ub.ap, in_=pb.ap,
                                                                                    func=AF.Gelu_apprx_tanh),
                                     w=[ub], r=[pb])
                        self.wi.release(jj)
                    j2 = self.jobs_wi[("even", l, 2)]
                    w2 = self.wi.get(j2)
                    j3 = self.jobs_wi[("even", l, 3)]
                    w3 = self.wi.get(j3)
                    for i in range(16):
                        tb, off = i // 4, (i % 4) * 128
                        pv = self.psum()
                        for hi, wb in enumerate((w2, w3)):
                            for c in range(NC_):
                                K.op(PE, lambda c=c, wb=wb, hi=hi: nc.tensor.matmul(
                                    pv.ap[:, hi * 256:(hi + 1) * 256], hT_t[:, c, i * 128:(i + 1) * 128], wb.ap[:, c, :],
                                    start=(c == 0), stop=(c == NC_ - 1)), w=[pv], r=[wb, hT[c][tb]])
                        gb = g32[i % 2]
                        K.op(ACT, lambda gb=gb, pv=pv: nc.scalar.activation(out=gb.ap, in_=pv.ap, func=AF.Gelu_apprx_tanh),
                             w=[gb], r=[pv])
                        K.op(DVE, lambda gb=gb: nc.vector.tensor_reduce(out=st_t[:, 0:1], in_=gb.ap, axis=AX.X, op=ALU.add),
                             w=[st], r=[gb])
                        K.op(ACT, lambda gb=gb: nc.scalar.activation(out=gsq.ap, in_=gb.ap, func=AF.Square),
                             w=[gsq], r=[gb])
                        K.op(DVE, lambda: nc.vector.tensor_reduce(out=st_t[:, 1:2], in_=gsq.ap, axis=AX.X, op=ALU.add),
                             w=[st], r=[gsq])
                        K.op(DVE, lambda: nc.vector.tensor_scalar(out=st_t[:, 2:3], in0=st_t[:, 0:1], scalar1=1.0 / 512,
                                                                  scalar2=None, op0=ALU.mult), w=[st], r=[st])
                        K.op(DVE, lambda: nc.vector.tensor_tensor(out=st_t[:, 3:4], in0=st_t[:, 2:3], in1=st_t[:, 2:3],
                                                                  op=ALU.mult), w=[st], r=[st])
                        K.op(DVE, lambda: nc.vector.scalar_tensor_tensor(out=st_t[:, 4:5], in0=st_t[:, 1:2],
                                                                         scalar=1.0 / 512, in1=st_t[:, 3:4],
                                                                         op0=ALU.mult, op1=ALU.subtract), w=[st], r=[st])
                        K.op(ACT, lambda: nc.scalar.activation(out=st_t[:, 5:6], in_=st_t[:, 4:5], func=AF.Sqrt, scale=1.0,
                                                               bias=self.eps_cols[LN_EPS]), w=[st], r=[st, self.epsb])
                        K.op(DVE, lambda: nc.vector.reciprocal(out=st_t[:, 6:7], in_=st_t[:, 5:6]), w=[st], r=[st])
                        K.op(DVE, lambda gb=gb: nc.vector.tensor_scalar(out=gb.ap, in0=gb.ap, scalar1=st_t[:, 2:3],
                                                                        scalar2=st_t[:, 6:7], op0=ALU.subtract,
                                                                        op1=ALU.mult), w=[gb], r=[gb, st])
                        K.op(DVE, lambda gb=gb: nc.vector.tensor_tensor(out=gb.ap, in0=gb.ap, in1=glg_t[:], op=ALU.mult),
                             w=[gb], r=[gb, glg])
                        K.op(DVE, lambda gb=gb, i=i: nc.vector.tensor_tensor(out=vtm[i].ap, in0=gb.ap, in1=glb_t[:],
                                                                             op=ALU.add), w=[vtm[i]], r=[gb, glb])
                    self.wi.release(j2)
                    self.wi.release(j3)
                    nz = 0
                    for sl in range(4, 11):
                        if cfg.get("only") == "a":
                            jj = self.jobs_wi[("even", l, sl)]
                            self.wi.get(jj)
                            self.wi.release(jj)
                            continue
                        jj = self.jobs_wi[("even", l, sl)]
                        wb = self.wi.get(jj)
                        for tb in range(NTB):
                            zb_, zb_t = zst[nz % 2], zst_t[nz % 2]
                            nz += 1
                            for ti in range(4):
                                i = tb * 4 + ti
                                pz = self.psum()
                                for c in range(NC_):
                                    K.op(PE, lambda c=c, pz=pz, i=i: nc.tensor.matmul(
                                        pz.ap[:, 0:256], hT_t[:, c, i * 128:(i + 1) * 128], wb.ap[:, c, :],
                                        start=(c == 0), stop=(c == NC_ - 1)), w=[pz], r=[wb, hT[c][tb]])
                                K.op(ACT, lambda pz=pz, ti=ti, zb_t=zb_t: nc.scalar.copy(out=zb_t[:, ti, :],
                                                                                         in_=pz.ap[:, 0:256]),
                                     w=[zb_], r=[pz])
                            col0 = (sl - 4) * 256
                            dst = self.d_zB[1 + tb * 512: 1 + (tb + 1) * 512, col0:col0 + 256].rearrange(
                                "(ti p) n -> p ti n", p=128)
                            K.dma(SP, dst, zb_t[:], w=[self.zB], r=[zb_])
                        self.wi.release(jj)
                    K.barrier()
                with ExitStack() as esm:
                    ya_t = K.sb(esm, [128, 4, S], BF16, "yaT")
                    ya = [[Buf(ya_t[:, g, tb * 512:(tb + 1) * 512]) for tb in range(NTB)] for g in range(4)]
                    for tb in range(NTB):
                        for g in range(4):
                            pm = self.psum()
                            K.op(PE, lambda g=g, pm=pm: nc.tensor.matmul(pm.ap, self.ones_f.ap[0:1, :], bs4_t[0:1, g, :],
                                                                         start=True, stop=False),
                                 w=[pm], r=[self.ones_f, bs4])
                            for nb in range(4):
                                i = tb * 4 + nb
                                K.op(PE, lambda i=i, g=g, nb=nb, pm=pm: nc.tensor.matmul(
                                    pm.ap[:, nb * 128:(nb + 1) * 128], vtm_t[:, i, g * 128:(g + 1) * 128], wsT_t[:, g, :],
                                    start=False, stop=(nb == 3)), w=[pm], r=[vtm[i], wsT])
                            K.op(DVE, lambda g=g, tb=tb, pm=pm: nc.vector.tensor_tensor(
                                out=ya[g][tb].ap, in0=pm.ap, in1=uT[g][tb].ap, op=ALU.mult),
                                w=[ya[g][tb]], r=[pm, uT[g][tb]])
                    self.psum4 = self.psum
                    outproj(ya, [0, 1, 2, 3])
                    for ch in range(4):
                        self.wo.release(self.jobs_wo[("even", l, ch)])
                    K.barrier()

            if cfg.get("only") == "a":
                for ch in range(4, 8):
                    self.wo.get(self.jobs_wo[("even", l, ch)])
                    self.wo.release(self.jobs_wo[("even", l, ch)])
            else:
                self.rwkv(l, es, bc_tile, outproj)
            K.barrier()

    def rwkv(self, l, es, bc_tile, outproj):
        K, nc = self.K, self.nc
        e = l // 2
        PE, ACT, DVE, POOL, SP = K.PE, K.ACT, K.DVE, K.POOL, K.SP
        ps = self.ps
        rr = [0]

        def psum4():
            for k in range(4):
                bb = ps[(rr[0] + k) % 4]
                if bb.lw is None or len(bb.rd) > 0:
                    rr[0] += k + 1
                    return bb
            raise AssertionError("no free rotating PSUM bank")
        self.psum4 = psum4
        pCH, pF, pG, pY = ps[4], ps[5], ps[6], ps[7]
        rows = self.d_even_rows[e]
        L = NEUMANN_L
        with ExitStack() as esr:
            def T(shape, dt, name):
                t = K.sb(esr, shape, dt, name)
                return t, Buf(t[:], name)
            mu_t, mu = bc_tile(esr, OFF_MU, 1792, "mu")
            kkb_t, kkb = bc_tile(esr, OFF_KK, 512, "kkb")
            kab_t, kab = bc_tile(esr, OFF_KA, 512, "kab")
            rkb_t, rkb = bc_tile(esr, OFF_RK, 512, "rkb")
            lgb_t, lgb = bc_tile(esr, OFF_LG, 512, "lgb")
            lbb_t, lbb = bc_tile(esr, OFF_LB, 512, "lbb")
            w0a0_t, w0a0 = T([1, 512], F32, "a0row")
            K.dma(SP, w0a0_t[:], rows[0:1, OFF_A0:OFF_A0 + 512], w=[w0a0])
            dup_t, dup = T([65, 512], F32, "dup")
            K.dma(SP, dup_t[0:64, :], self.d_lora_d[e], w=[dup])
            K.dma(SP, dup_t[64:65, :], rows[0:1, OFF_W0:OFF_W0 + 512], w=[dup])
            iup_t, iup = T([128, 512], BF16, "iup")
            K.dma(POOL, iup_t[:], self.d_lora_i[e], w=[iup])
            gup_t, gup = T([128, 512], BF16, "gup")
            K.dma(POOL, gup_t[:], self.d_lora_g[e], w=[gup])
            Ui_t, Ui = T([128, 128], F32, "Uincl")
            Us_t, Us = T([128, 128], F32, "Ustrict")
            Ls_t, Ls = T([128, 128], F32, "Lstrict")
            mk2_t, mk2 = T([128, 256], F32, "mask2")
            for (t_, b_, cmp_, st_, cm_) in ((Ui_t, Ui, ALU.is_ge, 1, -1), (Us_t, Us, ALU.is_gt, 1, -1),
                                             (Ls_t, Ls, ALU.is_gt, -1, 1)):
                K.op(POOL, lambda t_=t_: nc.gpsimd.memset(t_[:], 1.0), w=[b_])
                K.op(POOL, lambda t_=t_, cmp_=cmp_, st_=st_, cm_=cm_: nc.gpsimd.affine_select(
                    out=t_[:], in_=t_[:], pattern=[[st_, 128]], compare_op=cmp_, fill=0.0, base=0,
                    channel_multiplier=cm_), w=[b_], r=[b_])
            K.op(POOL, lambda: nc.gpsimd.tensor_copy(mk2_t[:, 0:128], Us_t[:]), w=[mk2], r=[Us])
            K.op(POOL, lambda: nc.gpsimd.tensor_copy(mk2_t[:, 128:256], Ui_t[:]), w=[mk2], r=[Ui])
            Hf_t = [K.sb(esr, [64, 512], F32, "Hf%d" % i) for i in range(2)]
            Hf = [Buf(t[:]) for t in Hf_t]
            Hb_t = [K.sb(esr, [64, 512], BF16, "Hb%d" % i) for i in range(2)]
            Hb = [Buf(t[:]) for t in Hb_t]
            K.op(DVE, lambda: nc.vector.memset(Hf_t[0][:], 0.0), w=[Hf[0]])
            K.op(DVE, lambda: nc.vector.memset(Hb_t[0][:], 0.0), w=[Hb[0]])
            GTa_t, GTa = T([64, 512], F32, "GTa")
            Fa_t, Fa = T([64, 512], F32, "Fa")
            pC_t, pCs = T([64, 8], F32, "pC")
            ybT_t = [K.sb(esr, [128, 4, 512], BF16, "ybT%d" % i) for i in range(1)]
            ybT = [[Buf(ybT_t[i][:, q, :]) for q in range(4)] for i in range(1)]
            zt_t, zt = T([128, 1792], F32, "zt")
            scr_t = K.sb(esr, [128, 2048], F32, "scr")
            zs_t, zs = scr_t[:, 0:1792], Buf(scr_t[:, 0:1792], "zs")
            lwT_t, lwT = T([65, 128], F32, "lwT")
            K.op(DVE, lambda: nc.vector.memset(lwT_t[64:65, :], 1.0), w=[lwT])
            laT_t, laT = T([128, 128], BF16, "laT")
            lgT_t, lgT = T([128, 128], BF16, "lgT")
            sg_t, sg = T([128, 512], F32, "sg")
            as_t, asg = T([128, 512], F32, "asig")
            gg_t, gg = T([128, 512], F32, "gg")
            Ea_t, Ea = scr_t[:, 1024:1536], Buf(scr_t[:, 1024:1536], "Ea")
            Eb_t, Eb = scr_t[:, 1536:2048], Buf(scr_t[:, 1536:2048], "Eb")
            kk_t, kk = T([128, 512], F32, "kk")
            km_t, km = T([128, 512], F32, "kmod")
            bq_t, bq = T([128, 512], F32, "bq")
            tA_t, tA = scr_t[:, 0:512], Buf(scr_t[:, 0:512], "tA")
            tB_t, tB = scr_t[:, 512:1024], Buf(scr_t[:, 512:1024], "tB")
            ZS = [zs, tA, tB, Ea, Eb]
            st_t, st = T([128, 64], F32, "rst")
            at_t, at_b = T([128, 512], BF16, "at_b")
            rt_t, rt_b = T([128, 512], BF16, "rt_b")
            bt_t, bt_b = T([128, 512], BF16, "bt_b")
            kt_t, kt_b = T([128, 512], BF16, "kt_b")
            bh_t, bh_b = T([128, 512], BF16, "bh_b")
            kh_t, kh_b = T([128, 512], BF16, "kh_b")
            v_t, v_b = T([128, 512], BF16, "v_b")
            yb_t, yb_b = T([128, 512], BF16, "yb_b")
            GT_t = [K.sb(esr, [128, 4, 128], BF16, "GT%d" % i) for i in range(4)]
            GT = [Buf(t[:]) for t in GT_t]
            HS = []
            for i in range(NHS):
                d = {}
                for nm, shp in (("M1", [128, 256]), ("M2", [128, 256]), ("XT0", [128, 128]), ("XXa", [128, 256]),
                                ("XXb", [128, 256]), ("Pa", [128, 128]), ("Pb", [128, 128]), ("W1", [128, 64]),
                                ("AU", [128, 128]), ("RbT", [64, 128])):
                    t_ = K.sb(esr, shp, BF16, "%s_%d" % (nm, i))
                    d[nm] = (t_, Buf(t_[:], nm))
                HS.append(d)

            v3 = lambda ap: ap.rearrange("p (h f) -> p h f", f=64)
            bc3 = lambda ap: ap.unsqueeze(2).broadcast_to([128, 8, 64])

            for i in range(16):
                tb, ti = i // 4, i % 4
                cur, nxt = i % 2, (i + 1) % 2
                K.dma(SP, zt_t[:], self.d_zB[1 + i * 128: 1 + (i + 1) * 128, :], w=[zt], r=[self.zB])
                K.dma(SP, zs_t, self.d_zB[i * 128:(i + 1) * 128, :], w=ZS, r=[self.zB])
                K.op(DVE, lambda: nc.vector.tensor_tensor(out=zs_t, in0=zs_t, in1=zt_t[:], op=ALU.subtract),
                     w=ZS, r=[zs, zt])
                K.op(POOL, lambda: nc.gpsimd.tensor_tensor(out=zs_t, in0=zs_t, in1=mu_t[:], op=ALU.mult),
                     w=ZS, r=[zs, mu])
                K.op(DVE, lambda: nc.vector.tensor_tensor(out=zt_t[:], in0=zt_t[:], in1=zs_t, op=ALU.add),
                     w=[zt] + ZS, r=[zs, zt])
                r_ap, k_ap, vv_ap = zt_t[:, 0:512], zt_t[:, 512:1024], zt_t[:, 1024:1536]
                DT = self.cfg.get("dbg_tile", 0)
                if i == DT:
                    self.dbg("zp", zt_t[:], zt)
                pl = psum4()
                K.op(PE, lambda: nc.tensor.transpose(pl.ap[:, 0:128], zt_t[:, 1536:1664], self.ident_f.ap),
                     w=[pl], r=[zt, self.ident_f])
                K.op(PE, lambda: nc.tensor.transpose(pl.ap[:, 128:256], zt_t[:, 1664:1792], self.ident_f.ap),
                     w=[pl], r=[zt, self.ident_f])
                K.op(ACT, lambda: nc.scalar.activation(out=lwT_t[0:64, :], in_=pl.ap[0:64, 0:128], func=AF.Tanh),
                     w=[lwT], r=[pl])
                K.op(ACT, lambda: nc.scalar.copy(out=laT_t[64:128, :], in_=pl.ap[64:128, 0:128]), w=[laT], r=[pl])
                K.op(ACT, lambda: nc.scalar.activation(out=lgT_t[:], in_=pl.ap[:, 128:256], func=AF.Sigmoid),
                     w=[lgT], r=[pl])
                pw = psum4()
                K.op(PE, lambda: nc.tensor.matmul(pw.ap, lwT_t[0:65, :], dup_t[0:65, :], start=True, stop=True),
                     w=[pw], r=[lwT, dup])
                K.op(ACT, lambda: nc.scalar.activation(out=sg_t[:], in_=pw.ap, func=AF.Sigmoid), w=[sg], r=[pw])
                pa = psum4()
                K.op(PE, lambda: nc.tensor.matmul(pa.ap, self.ones_f.ap[0:1, :], w0a0_t[0:1, 0:512], start=True,
                                                  stop=False), w=[pa], r=[self.ones_f, w0a0])
                K.op(PE, lambda: nc.tensor.matmul(pa.ap, laT_t[64:128, :], iup_t[64:128, :], start=False, stop=True),
                     w=[pa], r=[laT, iup])
                K.op(ACT, lambda: nc.scalar.activation(out=as_t[:], in_=pa.ap, func=AF.Sigmoid), w=[asg], r=[pa])
                pg = psum4()
                K.op(PE, lambda: nc.tensor.matmul(pg.ap, lgT_t[:], gup_t[:], start=True, stop=True),
                     w=[pg], r=[lgT, gup])
                K.op(ACT, lambda: nc.scalar.copy(out=gg_t[:], in_=pg.ap), w=[gg], r=[pg])
                pcs = psum4()
                K.op(PE, lambda: nc.tensor.matmul(pcs.ap, Ui_t[:], sg_t[:], start=True, stop=True), w=[pcs], r=[Ui, sg])
                pcx = psum4()
                K.op(PE, lambda: nc.tensor.matmul(pcx.ap, Us_t[:], sg_t[:], start=True, stop=True), w=[pcx], r=[Us, sg])
                prq = psum4()
                K.op(PE, lambda: nc.tensor.matmul(prq.ap, Ls_t[:], sg_t[:], start=True, stop=True), w=[prq], r=[Ls, sg])
                for h in range(8):
                    K.op(PE, lambda h=h: nc.tensor.matmul(pCH.ap[0:64, h:h + 1], sg_t[:, h * 64:(h + 1) * 64],
                                                          self.ones_f.ap[:, 0:1], start=True, stop=True),
                         w=[pCH], r=[sg, self.ones_f])
                K.op(ACT, lambda: nc.scalar.activation(out=pC_t[:], in_=pCH.ap[0:64, 0:8], func=AF.Exp, scale=-C0),
                     w=[pCs], r=[pCH])
                K.op(DVE, lambda: nc.vector.tensor_tensor(out=tA_t, in0=k_ap, in1=kkb_t[:], op=ALU.mult),
                     w=[tA], r=[zt, kkb])
                K.op(POOL, lambda: nc.gpsimd.tensor_tensor(out=tB_t, in0=tA_t, in1=tA_t, op=ALU.mult),
                     w=[tB], r=[tA])
                K.op(DVE, lambda: nc.vector.tensor_reduce(out=st_t[:, 0:8], in_=v3(tB_t), axis=AX.X, op=ALU.add),
                     w=[st], r=[tB])
                K.op(ACT, lambda: nc.scalar.activation(out=st_t[:, 8:16], in_=st_t[:, 0:8], func=AF.Sqrt),
                     w=[st], r=[st])
                K.op(DVE, lambda: nc.vector.tensor_scalar(out=st_t[:, 8:16], in0=st_t[:, 8:16], scalar1=1e-12,
                                                          scalar2=None, op0=ALU.max), w=[st], r=[st])
                K.op(DVE, lambda: nc.vector.reciprocal(out=st_t[:, 16:24], in_=st_t[:, 8:16]), w=[st], r=[st])
                K.op(DVE, lambda: nc.vector.tensor_tensor(out=v3(kk_t[:]), in0=v3(tA_t), in1=bc3(st_t[:, 16:24]),
                                                          op=ALU.mult), w=[kk], r=[tA, st])
                K.op(DVE, lambda: nc.vector.scalar_tensor_tensor(out=km_t[:], in0=as_t[:], scalar=-1.0, in1=kab_t[:],
                                                                 op0=ALU.add, op1=ALU.mult), w=[km], r=[asg, kab])
                K.op(DVE, lambda: nc.vector.scalar_tensor_tensor(out=km_t[:], in0=km_t[:], scalar=1.0, in1=k_ap,
                                                                 op0=ALU.add, op1=ALU.mult), w=[km], r=[km, zt])
                K.op(POOL, lambda: nc.gpsimd.tensor_tensor(out=bq_t[:], in0=kk_t[:], in1=as_t[:], op=ALU.mult),
                     w=[bq], r=[kk, asg])
                K.op(POOL, lambda: nc.gpsimd.tensor_tensor(out=tB_t, in0=r_ap, in1=km_t[:], op=ALU.mult),
                     w=[tB], r=[zt, km])
                K.op(POOL, lambda: nc.gpsimd.tensor_tensor(out=tB_t, in0=tB_t, in1=rkb_t[:], op=ALU.mult),
                     w=[tB], r=[tB, rkb])
                K.op(DVE, lambda: nc.vector.tensor_reduce(out=st_t[:, 24:32], in_=v3(tB_t), axis=AX.X, op=ALU.add),
                     w=[st], r=[tB])
                K.op(ACT, lambda: nc.scalar.activation(out=Ea_t, in_=pcs.ap, func=AF.Exp, scale=-C0), w=[Ea], r=[pcs])
                K.op(DVE, lambda: nc.vector.tensor_tensor(out=rt_t[:], in0=r_ap, in1=Ea_t, op=ALU.mult),
                     w=[rt_b], r=[zt, Ea])
                K.op(ACT, lambda: nc.scalar.activation(out=Eb_t, in_=pcx.ap, func=AF.Exp, scale=-C0), w=[Eb], r=[pcx])
                K.op(DVE, lambda: nc.vector.scalar_tensor_tensor(out=at_t[:], in0=kk_t[:], scalar=-1.0, in1=Eb_t,
                                                                 op0=ALU.mult, op1=ALU.mult), w=[at_b], r=[kk, Eb])
                K.op(ACT, lambda: nc.scalar.activation(out=Ea_t, in_=pcs.ap, func=AF.Exp, scale=C0), w=[Ea], r=[pcs])
                K.op(POOL, lambda: nc.gpsimd.tensor_tensor(out=bt_t[:], in0=bq_t[:], in1=Ea_t, op=ALU.mult),
                     w=[bt_b], r=[bq, Ea])
                K.op(DVE, lambda: nc.vector.tensor_tensor(out=kt_t[:], in0=km_t[:], in1=Ea_t, op=ALU.mult),
                     w=[kt_b], r=[km, Ea])
                K.op(ACT, lambda: nc.scalar.activation(out=Eb_t, in_=prq.ap, func=AF.Exp, scale=-C0), w=[Eb], r=[prq])
                K.op(POOL, lambda: nc.gpsimd.tensor_tensor(out=bh_t[:], in0=bq_t[:], in1=Eb_t, op=ALU.mult),
                     w=[bh_b], r=[bq, Eb])
                K.op(DVE, lambda: nc.vector.tensor_tensor(out=kh_t[:], in0=km_t[:], in1=Eb_t, op=ALU.mult),
                     w=[kh_b], r=[km, Eb])
                K.op(POOL, lambda: nc.gpsimd.tensor_copy(v_t[:], vv_ap), w=[v_b], r=[zt])
                if i == DT:
                    for nm_, t_, b_ in (("sg", sg_t, sg), ("asig", as_t, asg), ("gg", gg_t, gg), ("kk", kk_t, kk),
                                        ("km", km_t, km), ("at", at_t, at_b), ("rt", rt_t, rt_b), ("bt", bt_t, bt_b),
                                        ("kt", kt_t, kt_b), ("bh", bh_t, bh_b), ("kh", kh_t, kh_b), ("vb", v_t, v_b),
                                        ("pC", pC_t, pCs), ("st", st_t, st)):
                        self.dbg(nm_, t_[:], b_)
                for pr in range(4):
                    ptb = psum4()
                    pt16 = ptb.ap.bitcast(BF16)
                    for kind, (xt_, xb_) in enumerate(((at_t, at_b), (rt_t, rt_b), (bt_t, bt_b), (kt_t, kt_b))):
                        K.op(PE, lambda kind=kind, xt_=xt_, pr=pr, pt16=pt16: nc.tensor.transpose(
                            pt16[:, kind * 128:(kind + 1) * 128], xt_[:, pr * 128:(pr + 1) * 128], self.ident_b.ap),
                            w=[ptb], r=[xb_, self.ident_b])
                    K.op(ACT, lambda pr=pr, pt16=pt16: nc.scalar.copy(
                        out=GT_t[pr][:].rearrange("p k t -> p (k t)"), in_=pt16[:, 0:512]), w=[GT[pr]], r=[ptb])
                if i == DT:
                    self.dbg("GT0", GT_t[0][:], GT[0])
                def head_gen(h, i=i, cur=cur):
                    pr, hb = h // 2, (h % 2) * 64
                    hs = HS[h % NHS]
                    hc = slice(h * 64, (h + 1) * 64)
                    gt = GT_t[pr]
                    gtb = GT[pr]
                    ar = gt[hb:hb + 64, 0:2, :].rearrange("p k t -> p (k t)")
                    M1_t, M1 = hs["M1"]
                    M2_t, M2 = hs["M2"]
                    XT0_t, XT0 = hs["XT0"]
                    p12 = psum4()
                    K.op(PE, lambda: nc.tensor.matmul(p12.ap[:, 0:256], gt[hb:hb + 64, 2, :], ar, start=True, stop=True),
                         w=[p12], r=[gtb])
                    K.op(PE, lambda: nc.tensor.matmul(p12.ap[:, 256:512], gt[hb:hb + 64, 3, :], ar, start=True, stop=True),
                         w=[p12], r=[gtb])
                    yield
                    K.op(DVE, lambda: nc.vector.tensor_tensor(out=M1_t[:], in0=p12.ap[:, 0:256], in1=mk2_t[:],
                                                              op=ALU.mult), w=[M1], r=[p12, mk2])
                    K.op(DVE, lambda: nc.vector.tensor_tensor(out=M2_t[:], in0=p12.ap[:, 256:512], in1=mk2_t[:],
                                                              op=ALU.mult), w=[M2], r=[p12, mk2])
                    p3 = psum4()
                    K.op(PE, lambda: nc.tensor.matmul(p3.ap[:, 0:128], gt[hb:hb + 64, 0, :], gt[hb:hb + 64, 2, :],
                                                      start=True, stop=True), w=[p3], r=[gtb])
                    yield
                    K.op(DVE, lambda: nc.vector.tensor_tensor(out=XT0_t[:], in0=p3.ap[:, 0:128], in1=Ls_t[:],
                                                              op=ALU.mult), w=[XT0], r=[p3, Ls])
                    Pc_t, Pc = hs["Pa"]
                    Pn_t, Pn = hs["Pb"]
                    K.op(POOL, lambda: nc.gpsimd.tensor_tensor(out=Pc_t[:], in0=M1_t[:, 0:128], in1=self.ident_b.ap,
                                                               op=ALU.add), w=[Pc], r=[M1, self.ident_b])
                    X_ap, XT_ap, Xb, XTb = M1_t[:, 0:128], XT0_t[:], M1, XT0
                    XXc = hs["XXa"]
                    XXn = hs["XXb"]
                    for j in range(L):
                        yield
                        px = psum4()
                        if j < L - 1:
                            K.op(PE, lambda px=px, X_ap=X_ap, XT_ap=XT_ap: nc.tensor.matmul(
                                px.ap[:, 0:128], XT_ap, X_ap, start=True, stop=True), w=[px], r=[Xb, XTb])
                        K.op(PE, lambda px=px, X_ap=X_ap, XT_ap=XT_ap: nc.tensor.matmul(
                            px.ap[:, 128:256], X_ap, XT_ap, start=True, stop=True), w=[px], r=[Xb, XTb])
                        XX_t, XX = XXc
                        yield
                        if j < L - 1:
                            K.op(ACT, lambda px=px, XX_t=XX_t: nc.scalar.copy(out=XX_t[:], in_=px.ap[:, 0:256]),
                                 w=[XX], r=[px])
                        else:
                            K.op(ACT, lambda px=px, XX_t=XX_t: nc.scalar.copy(out=XX_t[:, 128:256], in_=px.ap[:, 128:256]),
                                 w=[XX], r=[px])
                        yield
                        pp = psum4()
                        K.op(PE, lambda pp=pp, XX_t=XX_t, Pc_t=Pc_t: nc.tensor.matmul(
                            pp.ap[:, 0:128], XX_t[:, 128:256], Pc_t[:], start=True, stop=True), w=[pp], r=[XX, Pc])
                        yield
                        K.op(DVE, lambda pp=pp, Pc_t=Pc_t, Pn_t=Pn_t: nc.vector.tensor_tensor(
                            out=Pn_t[:], in0=pp.ap[:, 0:128], in1=Pc_t[:], op=ALU.add), w=[Pn], r=[pp, Pc])
                        X_ap, XT_ap, Xb, XTb = XX_t[:, 0:128], XX_t[:, 128:256], XX, XX
                        XXc, XXn = XXn, XXc
                        Pc_t, Pc, Pn_t, Pn = Pn_t, Pn, Pc_t, Pc
                    W1_t, W1 = hs["W1"]
                    AU_t, AU = hs["AU"]
                    RbT_t, RbT = hs["RbT"]
                    yield
                    pw1 = psum4()
                    K.op(PE, lambda: nc.tensor.matmul(pw1.ap[:, 0:64], M2_t[:, 0:128], v_t[:, hc], start=True, stop=True),
                         w=[pw1], r=[M2, v_b])
                    yield
                    K.op(ACT, lambda: nc.scalar.copy(out=W1_t[:], in_=pw1.ap[:, 0:64]), w=[W1], r=[pw1])
                    yield
                    pau = psum4()
                    K.op(PE, lambda: nc.tensor.matmul(pau.ap[:, 0:64], Pc_t[:], at_t[:, hc], start=True, stop=True),
                         w=[pau], r=[Pc, at_b])
                    K.op(PE, lambda: nc.tensor.matmul(pau.ap[:, 64:128], Pc_t[:], W1_t[:], start=True, stop=True),
                         w=[pau], r=[Pc, W1])
                    yield
                    K.op(ACT, lambda: nc.scalar.copy(out=AU_t[:], in_=pau.ap[:, 0:128]), w=[AU], r=[pau])
                    yield
                    if i == DT and h == self.cfg.get("dbg_head", 0):
                        self.dbg("M1", M1_t[:], M1)
                        self.dbg("M2", M2_t[:], M2)
                        self.dbg("XT0", XT0_t[:], XT0)
                        self.dbg("P", Pc_t[:], Pc)
                        self.dbg("AU", AU_t[:], AU)
                    K.op(PE, lambda: nc.tensor.matmul(pG.ap[0:64, hc], AU_t[:, 0:64], bh_t[:, hc], start=True, stop=True),
                         w=[pG], r=[AU, bh_b])
                    K.op(PE, lambda: nc.tensor.matmul(pF.ap[0:64, hc], bh_t[:, hc], AU_t[:, 64:128], start=True,
                                                      stop=False), w=[pF], r=[AU, bh_b])
                    K.op(PE, lambda: nc.tensor.matmul(pF.ap[0:64, hc], kh_t[:, hc], v_t[:, hc], start=False, stop=True),
                         w=[pF], r=[kh_b, v_b])
                    prb = psum4()
                    K.op(PE, lambda: nc.tensor.matmul(prb.ap[0:64, 0:128], AU_t[:, 0:64], M1_t[:, 128:256], start=True,
                                                      stop=False), w=[prb], r=[AU, M1])
                    K.op(PE, lambda: nc.tensor.matmul(prb.ap[0:64, 0:128], rt_t[:, hc], self.ident_b.ap, start=False,
                                                      stop=True), w=[prb], r=[rt_b, self.ident_b])
                    yield
                    K.op(ACT, lambda: nc.scalar.copy(out=RbT_t[:], in_=prb.ap[0:64, 0:128]), w=[RbT], r=[prb])
                    yield
                    K.op(PE, lambda: nc.tensor.matmul(pY.ap[:, hc], M1_t[:, 128:256], AU_t[:, 64:128], start=True,
                                                      stop=False), w=[pY], r=[M1, AU])
                    K.op(PE, lambda: nc.tensor.matmul(pY.ap[:, hc], M2_t[:, 128:256], v_t[:, hc], start=False,
                                                      stop=False), w=[pY], r=[M2, v_b])
                    K.op(PE, lambda: nc.tensor.matmul(pY.ap[:, hc], RbT_t[0:64, :], Hb_t[cur][0:64, hc], start=False,
                                                      stop=True), w=[pY], r=[RbT, Hb[cur]])
                for g0 in range(0, 8, NHS):
                    gens = [head_gen(h) for h in range(g0, g0 + NHS)]
                    while gens:
                        for g_ in list(gens):
                            try:
                                next(g_)
                            except StopIteration:
                                gens.remove(g_)
                for h in range(8):
                    hc = slice(h * 64, (h + 1) * 64)
                    K.op(DVE, lambda h=h, hc=hc: nc.vector.scalar_tensor_tensor(
                        out=GTa_t[:, hc], in0=self.ident_f.ap[0:64, 0:64], scalar=pC_t[:, h:h + 1], in1=pG.ap[0:64, hc],
                        op0=ALU.mult, op1=ALU.add), w=[GTa], r=[self.ident_f, pCs, pG])
                K.op(ACT, lambda: nc.scalar.copy(out=Fa_t[:], in_=pF.ap[0:64, :]), w=[Fa], r=[pF])
                for h in range(8):
                    hc = slice(h * 64, (h + 1) * 64)
                    K.op(PE, lambda hc=hc: nc.tensor.matmul(pCH.ap[0:64, hc], GTa_t[:, hc], Hf_t[cur][:, hc], start=True,
                                                            stop=True), w=[pCH], r=[GTa, Hf[cur]])
                K.op(DVE, lambda: nc.vector.tensor_tensor(out=Hf_t[nxt][:], in0=pCH.ap[0:64, :], in1=Fa_t[:], op=ALU.add),
                     w=[Hf[nxt]], r=[pCH, Fa])
                K.op(ACT, lambda: nc.scalar.copy(out=Hb_t[nxt][:], in_=Hf_t[nxt][:]), w=[Hb[nxt]], r=[Hf[nxt]])
                if i == DT:
                    self.dbg("GTa", GTa_t[:], GTa)
                    self.dbg("Fa", Fa_t[:], Fa)
                    self.dbg("Hn", Hf_t[nxt][:], Hf[nxt])
                K.op(ACT, lambda: nc.scalar.activation(out=tB_t, in_=pY.ap, func=AF.Square), w=[tB], r=[pY])
                K.op(DVE, lambda: nc.vector.tensor_reduce(out=st_t[:, 32:40], in_=v3(pY.ap), axis=AX.X, op=ALU.add),
                     w=[st], r=[pY])
                K.op(DVE, lambda: nc.vector.tensor_reduce(out=st_t[:, 40:48], in_=v3(tB_t), axis=AX.X, op=ALU.add),
                     w=[st], r=[tB])
                K.op(DVE, lambda: nc.vector.tensor_scalar(out=st_t[:, 32:40], in0=st_t[:, 32:40], scalar1=1.0 / 64,
                                                          scalar2=None, op0=ALU.mult), w=[st], r=[st])
                K.op(DVE, lambda: nc.vector.tensor_tensor(out=st_t[:, 48:56], in0=st_t[:, 32:40], in1=st_t[:, 32:40],
                                                          op=ALU.mult), w=[st], r=[st])
                K.op(DVE, lambda: nc.vector.scalar_tensor_tensor(out=st_t[:, 40:48], in0=st_t[:, 40:48], scalar=1.0 / 64,
                                                                 in1=st_t[:, 48:56], op0=ALU.mult, op1=ALU.subtract),
                     w=[st], r=[st])
                K.op(ACT, lambda: nc.scalar.activation(out=st_t[:, 40:48], in_=st_t[:, 40:48], func=AF.Sqrt, scale=1.0,
                                                       bias=self.eps_cols[LNX_EPS]), w=[st], r=[st, self.epsb])
                K.op(DVE, lambda: nc.vector.reciprocal(out=st_t[:, 56:64], in_=st_t[:, 40:48]), w=[st], r=[st])
                K.op(DVE, lambda: nc.vector.tensor_tensor(out=v3(tA_t), in0=v3(pY.ap), in1=bc3(st_t[:, 32:40]),
                                                          op=ALU.subtract), w=[tA], r=[pY, st])
                K.op(DVE, lambda: nc.vector.tensor_tensor(out=v3(tA_t), in0=v3(tA_t), in1=bc3(st_t[:, 56:64]),
                                                          op=ALU.mult), w=[tA], r=[tA, st])
                K.op(POOL, lambda: nc.gpsimd.tensor_tensor(out=tA_t, in0=tA_t, in1=lgb_t[:], op=ALU.mult),
                     w=[tA], r=[tA, lgb])
                K.op(POOL, lambda: nc.gpsimd.tensor_tensor(out=tA_t, in0=tA_t, in1=lbb_t[:], op=ALU.add),
                     w=[tA], r=[tA, lbb])
                K.op(DVE, lambda: nc.vector.tensor_tensor(out=v3(tB_t), in0=v3(vv_ap), in1=bc3(st_t[:, 24:32]),
                                                          op=ALU.mult), w=[tB], r=[zt, st])
                K.op(POOL, lambda: nc.gpsimd.tensor_tensor(out=tA_t, in0=tA_t, in1=tB_t, op=ALU.add),
                     w=[tA], r=[tA, tB])
                K.op(DVE, lambda: nc.vector.tensor_tensor(out=yb_t[:], in0=tA_t, in1=gg_t[:], op=ALU.mult),
                     w=[yb_b], r=[tA, gg])
                if i == DT:
                    self.dbg("yb", yb_t[:], yb_b)
                pyt = psum4()
                py16 = pyt.ap.bitcast(BF16)
                for q in range(4):
                    K.op(PE, lambda q=q: nc.tensor.transpose(py16[:, q * 128:(q + 1) * 128], yb_t[:, q * 128:(q + 1) * 128],
                                                             self.ident_b.ap), w=[pyt], r=[yb_b, self.ident_b])
                ybt = ybT_t[0]
                K.op(ACT, lambda: nc.scalar.copy(out=ybt[:, :, ti * 128:(ti + 1) * 128],
                                                 in_=py16[:, 0:512].rearrange("p (q t) -> p q t", q=4)),
                     w=ybT[0], r=[pyt])
                if ti == 3:
                    mixb = [{tb: ybT[0][q]} for q in range(4)]
                    outproj(mixb, [4, 5, 6, 7], tbs=[tb])
            for ch in range(4, 8):
                self.wo.release(self.jobs_wo[("even", l, ch)])

    def mixer_odd(self, l):
        K, nc = self.K, self.nc
        o = l // 2
        PE, ACT, DVE, POOL, SP = K.PE, K.ACT, K.DVE, K.POOL, K.SP
        ps = self.ps

        def outproj(mixb, chs):
            wos = [self.wo.get(self.jobs_wo[("odd", l, ch)]) for ch in chs]
            for d in range(NC_):
                for tb in range(NTB):
                    po = self.psum()
                    for i, ch in enumerate(chs):
                        K.op(PE, lambda i=i: nc.tensor.matmul(po.ap, wos[i].ap[:, d * 128:(d + 1) * 128], mixb[i][tb].ap,
                                                              start=(i == 0), stop=(i == len(chs) - 1)),
                             w=[po], r=[wos[i], mixb[i][tb]])
                    xb = self.x[d][tb]
                    K.op(DVE, lambda: nc.vector.tensor_tensor(out=xb.ap, in0=po.ap, in1=xb.ap, op=ALU.add),
                         w=[xb], r=[po, xb])
            for ch in chs:
                self.wo.release(self.jobs_wo[("odd", l, ch)])

        with ExitStack() as es:
            small_t = K.sb(es, [128, ODD_NS], F32, "osmall")
            small = Buf(small_t[:], "osmall")
            K.dma(SP, small_t[:], self.d_odd_small[o], w=[small])
            qn_col = lambda c: small_t[:, c:c + 1]
            kvn_col = lambda c: small_t[:, 2:3]
            cw = lambda q, j: small_t[:, 3 + q * 31 + j: 3 + q * 31 + j + 1]
            cb = lambda q: small_t[:, 127 + q:128 + q]
            lng = lambda q: small_t[:, 131 + q:132 + q]
            lnb = lambda q: small_t[:, 135 + q:136 + q]
            cqn_t = K.sb(es, [128, 2, S], BF16, "cqn")
            cqn = [[Buf(cqn_t[:, c, tb * 512:(tb + 1) * 512]) for tb in range(NTB)] for c in range(2)]
            ckvn_t = K.sb(es, [128, S], BF16, "ckvn")
            ckvn = [[Buf(ckvn_t[:, tb * 512:(tb + 1) * 512]) for tb in range(NTB)]]
            kr_t = K.sb(es, [128, S], BF16, "krope")
            kr = [Buf(kr_t[:, tb * 512:(tb + 1) * 512]) for tb in range(NTB)]
            rope_t = K.sb(es, [128, 2, S], BF16, "rope")
            rope = Buf(rope_t[:], "rope")
            K.dma(POOL, rope_t[:], self.d_rope[:, :, :], w=[rope])
            wq_t = K.sb(es, [128, 2, 768], BF16, "wq")
            wq = Buf(wq_t[:], "wq")
            wqr_t = K.sb(es, [128, 2, 768], BF16, "wqr")
            wqr = Buf(wqr_t[:], "wqr")
            wkv_t = K.sb(es, [128, 1024], BF16, "wkv")
            wkv = Buf(wkv_t[:], "wkv")
            K.dma(POOL, wq_t[:], self.d_wq[o], w=[wq])
            K.dma(POOL, wqr_t[:], self.d_wqr[o], w=[wqr])
            K.dma(POOL, wkv_t[:], self.d_wkv[o], w=[wkv])

            with ExitStack() as esx:
                hglu_t = K.sb(esx, [128, 4, 30 + S], BF16, "hglu")
                hglu = [Buf(hglu_t[:, q, :]) for q in range(4)]
                for q in range(4):
                    K.op(DVE, lambda q=q: nc.vector.memset(hglu_t[:, q, 0:30], 0.0), w=[hglu[q]])
                with ExitStack() as esa:
                    hT_t = K.sb(esa, [128, NC_, S], BF16, "hT")
                    hT = [[Buf(hT_t[:, c, tb * 512:(tb + 1) * 512]) for tb in range(NTB)] for c in range(NC_)]
                    self.rmsnorm_fm(esa, self.x, self.gain_col(DEPTH + l), hT, NC_, D, RMS_EPS)
                    cq32_t = K.sb(esa, [128, 2, 512], F32, "cq32")
                    cq32 = [[Buf(cq32_t[:, c, :])] for c in range(2)]
                    ckv32_t = K.sb(esa, [128, 512], F32, "ckv32")
                    ckv32 = [[Buf(ckv32_t[:])]]
                    tmpa = Buf(K.sb(esa, [128, 512], F32, "tmpa")[:])
                    tmpb = Buf(K.sb(esa, [128, 512], F32, "tmpb")[:])

                    def proj(wb, lo, M, tb, pb):
                        for c in range(NC_):
                            K.op(PE, lambda c=c: nc.tensor.matmul(pb.ap[0:M, :], wb.ap[:, c, lo:lo + M], hT[c][tb].ap,
                                                                  start=(c == 0), stop=(c == NC_ - 1)),
                                 w=[pb], r=[wb, hT[c][tb]])

                    j0 = self.jobs_wi[("odd", l, 0)]
                    w0 = self.wi.get(j0)
                    for tb in range(NTB):
                        for c in range(2):
                            pb = self.psum()
                            proj(w0, c * 128, 128, tb, pb)
                            K.op(ACT, lambda c=c, pb=pb: nc.scalar.copy(out=cq32[c][0].ap, in_=pb.ap),
                                 w=[cq32[c][0]], r=[pb])
                        self.rmsnorm_fm(esa, [[cq32[0][0]], [cq32[1][0]]], qn_col, [[cqn[0][tb]], [cqn[1][tb]]], 2, 256,
                                        RMS_EPS, tbs=[0], extra_r=[small])
                    self.wi.release(j0)
                    j1 = self.jobs_wi[("odd", l, 1)]
                    w1 = self.wi.get(j1)
                    j2 = self.jobs_wi[("odd", l, 2)]
                    w2 = self.wi.get(j2)
                    for tb in range(NTB):
                        pb = self.psum()
                        proj(w1, 0, 128, tb, pb)
                        K.op(ACT, lambda pb=pb: nc.scalar.copy(out=ckv32[0][0].ap, in_=pb.ap), w=[ckv32[0][0]], r=[pb])
                        self.rmsnorm_fm(esa, [[ckv32[0][0]]], kvn_col, [[ckvn[0][tb]]], 1, 128, RMS_EPS, tbs=[0],
                                        extra_r=[small])
                        p1 = self.psum()
                        proj(w1, 128, 96, tb, p1)
                        p2 = self.psum()
                        proj(w2, 0, 96, tb, p2)
                        sl = slice(tb * 512, (tb + 1) * 512)
                        K.op(DVE, lambda p1=p1, sl=sl: nc.vector.tensor_tensor(out=tmpa.ap[64:96, :], in0=p1.ap[64:96, :],
                                                                               in1=rope_t[64:96, 0, sl], op=ALU.mult),
                             w=[tmpa], r=[p1, rope])
                        K.op(DVE, lambda p2=p2, sl=sl: nc.vector.tensor_tensor(out=tmpb.ap[64:96, :], in0=p2.ap[64:96, :],
                                                                               in1=rope_t[64:96, 1, sl], op=ALU.mult),
                             w=[tmpb], r=[p2, rope])
                        K.op(DVE, lambda tb=tb: nc.vector.tensor_tensor(out=kr[tb].ap[64:96, :], in0=tmpa.ap[64:96, :],
                                                                        in1=tmpb.ap[64:96, :], op=ALU.add),
                             w=[kr[tb]], r=[tmpa, tmpb])
                    self.wi.release(j1)
                    self.wi.release(j2)
                    for q in range(4):
                        jq = self.jobs_wi[("odd", l, 3 + q)]
                        wq_ = self.wi.get(jq)
                        for tb in range(NTB):
                            pa = self.psum()
                            proj(wq_, 0, 128, tb, pa)
                            pbb = self.psum()
                            proj(wq_, 128, 128, tb, pbb)
                            K.op(ACT, lambda pbb=pbb: nc.scalar.activation(out=tmpa.ap, in_=pbb.ap, func=AF.Sigmoid),
                                 w=[tmpa], r=[pbb])
                            K.op(DVE, lambda pa=pa, q=q, tb=tb: nc.vector.tensor_tensor(
                                out=hglu_t[:, q, 30 + tb * 512: 30 + (tb + 1) * 512], in0=pa.ap, in1=tmpa.ap,
                                op=ALU.mult), w=[hglu[q]], r=[pa, tmpa])
                        self.wi.release(jq)
                    K.barrier()

                with ExitStack() as esb:
                    yd_t = K.sb(esb, [128, 4, S], BF16, "ydT")
                    yd = [[Buf(yd_t[:, q, tb * 512:(tb + 1) * 512]) for tb in range(NTB)] for q in range(4)]
                    diag_t = [K.sb(esb, [128, 31, 128], BF16, "diag%d" % i) for i in range(2)]
                    diag = [Buf(t[:]) for t in diag_t]
                    c32_t = K.sb(esb, [128, 4, 512], F32, "c32")
                    c32 = [Buf(c32_t[:, q, :]) for q in range(4)]
                    sq32 = [Buf(K.sb(esb, [128, 512], F32, "sq32_%d" % i)[:]) for i in range(2)]
                    mean = Buf(K.sb(esb, [128, 512], F32, "mean")[:])
                    rstd = Buf(K.sb(esb, [128, 512], F32, "rstd")[:])
                    msq = Buf(K.sb(esb, [128, 512], F32, "msq")[:])
                    nd = 0
                    for tb in range(NTB):
                        p_s1 = ps[4]
                        p_s2 = ps[5]
                        for q in range(4):
                            dg, dg_t = diag[nd % 2], diag_t[nd % 2]
                            nd += 1
                            for j in range(31):
                                K.op(DVE, lambda j=j, q=q, dg_t=dg_t: nc.vector.tensor_scalar(
                                    out=dg_t[:, j, :], in0=self.ident_f.ap, scalar1=cw(q, j), scalar2=None,
                                    op0=ALU.mult), w=[dg], r=[self.ident_f, small])
                            pc = ps[q]
                            for j in range(31):
                                K.op(PE, lambda q=q, j=j, pc=pc, dg_t=dg_t: nc.tensor.matmul(
                                    pc.ap, dg_t[:, j, :], hglu_t[:, q, tb * 512 + j: tb * 512 + j + 512],
                                    start=(j == 0), stop=(j == 30)), w=[pc], r=[dg, hglu[q]])
                            K.op(ACT, lambda q=q, pc=pc: nc.scalar.activation(out=c32[q].ap, in_=pc.ap, func=AF.Identity,
                                                                              bias=cb(q), scale=1.0),
                                 w=[c32[q]], r=[pc, small])
                            sqb = sq32[q % 2]
                            K.op(ACT, lambda q=q, sqb=sqb: nc.scalar.activation(out=sqb.ap, in_=c32[q].ap, func=AF.Square),
                                 w=[sqb], r=[c32[q]])
                            K.op(PE, lambda q=q: nc.tensor.matmul(p_s1.ap, self.ones_f.ap, c32[q].ap, start=(q == 0),
                                                                  stop=(q == 3)), w=[p_s1], r=[self.ones_f, c32[q]])
                            K.op(PE, lambda q=q, sqb=sqb: nc.tensor.matmul(p_s2.ap, self.ones_f.ap, sqb.ap, start=(q == 0),
                                                                           stop=(q == 3)), w=[p_s2], r=[self.ones_f, sqb])
                        K.op(ACT, lambda: nc.scalar.mul(out=mean.ap, in_=p_s1.ap, mul=1.0 / 512), w=[mean], r=[p_s1])
                        K.op(DVE, lambda: nc.vector.tensor_tensor(out=msq.ap, in0=mean.ap, in1=mean.ap, op=ALU.mult),
                             w=[msq], r=[mean])
                        K.op(DVE, lambda: nc.vector.scalar_tensor_tensor(out=rstd.ap, in0=p_s2.ap, scalar=1.0 / 512,
                                                                         in1=msq.ap, op0=ALU.mult, op1=ALU.subtract),
                             w=[rstd], r=[p_s2, msq])
                        K.op(ACT, lambda: nc.scalar.activation(out=rstd.ap, in_=rstd.ap, func=AF.Sqrt, scale=1.0,
                                                               bias=self.eps_cols[LN_EPS]), w=[rstd], r=[rstd, self.epsb])
                        K.op(DVE, lambda: nc.vector.reciprocal(out=rstd.ap, in_=rstd.ap), w=[rstd], r=[rstd])
                        for q in range(4):
                            K.op(DVE, lambda q=q: nc.vector.tensor_tensor(out=c32[q].ap, in0=c32[q].ap, in1=mean.ap,
                                                                          op=ALU.subtract), w=[c32[q]], r=[c32[q], mean])
                            K.op(DVE, lambda q=q: nc.vector.tensor_tensor(out=c32[q].ap, in0=c32[q].ap, in1=rstd.ap,
                                                                          op=ALU.mult), w=[c32[q]], r=[c32[q], rstd])
                            K.op(ACT, lambda q=q, tb=tb: nc.scalar.activation(out=yd[q][tb].ap, in_=c32[q].ap,
                                                                              func=AF.Silu, bias=lnb(q), scale=lng(q)),
                                 w=[yd[q][tb]], r=[c32[q], small])
                    outproj(yd, [4, 5, 6, 7])
                    K.barrier()

            with ExitStack() as esc:
                yc_t = K.sb(esc, [128, 4, S], BF16, "ycT")
                yc = [[Buf(yc_t[:, c, tb * 512:(tb + 1) * 512]) for tb in range(NTB)] for c in range(4)]
                vall_t = K.sb(esc, [128, 16, 512], BF16, "vall")
                vall = [Buf(vall_t[:, i, :]) for i in range(16)]
                for i in range(16):
                    pv = self.psum()
                    tbi = i // 4
                    K.op(PE, lambda i=i, pv=pv: nc.tensor.matmul(
                        pv.ap, ckvn_t[:, i * 128:(i + 1) * 128], wkv_t[:, 512:1024],
                        start=True, stop=True), w=[pv], r=[ckvn[0][tbi], wkv])
                    K.op(ACT, lambda i=i, pv=pv: nc.scalar.copy(out=vall[i].ap, in_=pv.ap), w=[vall[i]], r=[pv])
                qT_t = [K.sb(esc, [128, S], BF16, "qT%d" % i) for i in range(2)]
                kT_t = [K.sb(esc, [128, S], BF16, "kT%d" % i) for i in range(2)]
                qT = [Buf(t[:]) for t in qT_t]
                kT = [Buf(t[:]) for t in kT_t]
                oc_t = [K.sb(esc, [128, S], BF16, "oc%d" % i) for i in range(2)]
                oc = [Buf(t[:]) for t in oc_t]
                pT_t = [K.sb(esc, [128, 512], BF16, "pT%d" % i) for i in range(3)]
                pT = [Buf(t[:]) for t in pT_t]
                rden = Buf(K.sb(esc, [128, 512], F32, "rden")[:])
                tq1 = Buf(K.sb(esc, [128, 512], F32, "tq1")[:])
                tq2 = Buf(K.sb(esc, [128, 512], F32, "tq2")[:])
                npt = 0
                for h in range(8):
                    qh, kh, och = qT[h % 2], kT[h % 2], oc[h % 2]
                    qh_t, kh_t, och_t = qT_t[h % 2], kT_t[h % 2], oc_t[h % 2]
                    for tb in range(NTB):
                        sl = slice(tb * 512, (tb + 1) * 512)
                        pq = ps[0 + (tb % 2)]
                        pr = ps[2 + (tb % 2)]
                        pk = ps[4 + (tb % 2)]
                        for c in range(2):
                            K.op(PE, lambda c=c, pq=pq: nc.tensor.matmul(pq.ap[0:96, :], wq_t[:, c, h * 96:(h + 1) * 96],
                                                                         cqn[c][tb].ap, start=(c == 0), stop=(c == 1)),
                                 w=[pq], r=[wq, cqn[c][tb]])
                        for c in range(2):
                            K.op(PE, lambda c=c, pr=pr: nc.tensor.matmul(pr.ap[0:96, :], wqr_t[:, c, h * 96:(h + 1) * 96],
                                                                         cqn[c][tb].ap, start=(c == 0), stop=(c == 1)),
                                 w=[pr], r=[wqr, cqn[c][tb]])
                        K.op(PE, lambda pk=pk: nc.tensor.matmul(pk.ap[0:64, :], wkv_t[:, h * 64:h * 64 + 64],
                                                                ckvn[0][tb].ap, start=True, stop=True),
                             w=[pk], r=[wkv, ckvn[0][tb]])
                        K.op(ACT, lambda pq=pq, sl=sl: nc.scalar.copy(out=qh_t[0:64, sl], in_=pq.ap[0:64, :]),
                             w=[qh], r=[pq])
                        K.op(DVE, lambda pq=pq, sl=sl: nc.vector.tensor_tensor(out=tq1.ap[64:96, :], in0=pq.ap[64:96, :],
                                                                               in1=rope_t[64:96, 0, sl], op=ALU.mult),
                             w=[tq1], r=[pq, rope])
                        K.op(DVE, lambda pr=pr, sl=sl: nc.vector.tensor_tensor(out=tq2.ap[64:96, :], in0=pr.ap[64:96, :],
                                                                               in1=rope_t[64:96, 1, sl], op=ALU.mult),
                             w=[tq2], r=[pr, rope])
                        K.op(DVE, lambda sl=sl: nc.vector.tensor_tensor(out=qh_t[64:96, sl], in0=tq1.ap[64:96, :],
                                                                        in1=tq2.ap[64:96, :], op=ALU.add),
                             w=[qh], r=[tq1, tq2])
                        K.op(ACT, lambda pk=pk, sl=sl: nc.scalar.copy(out=kh_t[0:64, sl], in_=pk.ap[0:64, :]),
                             w=[kh], r=[pk])
                        K.op(DVE, lambda sl=sl, tb=tb: nc.vector.tensor_copy(kh_t[64:96, sl], kr_t[64:96, sl]),
                             w=[kh], r=[kr[tb]])
                    for qb in range(NTB):
                        pO = ps[6]
                        pD = ps[7]
                        nkt = 4 * qb + 4
                        def emit_S(kt, qb=qb, nkt=nkt):
                            nonlocal npt
                            m = kt - 4 * qb
                            q0 = max(m, 0) * 128
                            pS = ps[kt % 2]
                            pt = pT[npt % 3]
                            pt_t = pT_t[npt % 3]
                            npt += 1
                            K.op(PE, lambda: nc.tensor.matmul(
                                pS.ap[:, q0:512], kh_t[0:96, kt * 128:(kt + 1) * 128],
                                qh_t[0:96, qb * 512 + q0:(qb + 1) * 512], start=True, stop=True),
                                w=[pS], r=[kh, qh])
                            K.op(ACT, lambda: nc.scalar.activation(
                                out=pt_t[:, q0:512], in_=pS.ap[:, q0:512], func=AF.Exp, scale=ATTN_SCALE),
                                w=[pt], r=[pS])
                            if m >= 0:
                                K.op(POOL, lambda: nc.gpsimd.memset(pt_t[64:128, q0:q0 + 64], 0.0),
                                     w=[pt], r=[pt])
                            return (pt, pt_t, q0)

                        def emit_PV(kt, st_, nkt=nkt):
                            pt, pt_t, q0 = st_
                            K.op(PE, lambda: nc.tensor.matmul(
                                pO.ap[0:64, q0:512], vall_t[:, kt, h * 64:(h + 1) * 64], pt_t[:, q0:512],
                                start=(kt == 0), stop=(kt == nkt - 1)), w=[pO], r=[vall[kt], pt])
                            K.op(PE, lambda: nc.tensor.matmul(
                                pD.ap[0:64, q0:512], self.ones_bf.ap[:, 0:64], pt_t[:, q0:512],
                                start=(kt == 0), stop=(kt == nkt - 1)), w=[pD], r=[self.ones_bf, pt])

                        nxt_st = emit_S(0)
                        for kt in range(nkt):
                            cur_st = nxt_st
                            if kt + 1 < nkt:
                                nxt_st = emit_S(kt + 1)
                            emit_PV(kt, cur_st)
                        K.op(DVE, lambda: nc.vector.reciprocal(out=rden.ap[0:64, :], in_=pD.ap[0:64, :]),
                             w=[rden], r=[pD])
                        K.op(DVE, lambda qb=qb: nc.vector.tensor_tensor(out=och_t[0:64, qb * 512:(qb + 1) * 512],
                                                                        in0=pO.ap[0:64, :], in1=rden.ap[0:64, :],
                                                                        op=ALU.mult), w=[och], r=[pO, rden])
                    pb0 = (h % 2) * 64
                    K.dma(SP, yc_t[pb0:pb0 + 64, h // 2, :], och_t[0:64, :], w=[yc[h // 2][tb] for tb in range(NTB)],
                          r=[och])
                outproj(yc, [0, 1, 2, 3])
                K.barrier()

    def final(self, do_norm):
        K, nc = self.K, self.nc
        with ExitStack() as es:
            if do_norm:
                o_t = K.sb(es, [128, NC_, S], F32, "oT")
                o = [[Buf(o_t[:, c, tb * 512:(tb + 1) * 512]) for tb in range(NTB)] for c in range(NC_)]
                self.rmsnorm_fm(es, self.x, self.gain_col(12), o, NC_, D, RMS_EPS)
            else:
                o = self.x
            toks = []
            for c in range(NC_):
                for tb in range(NTB):
                    toks.append(K.dma(K.SP, self.d_out[:, c, tb * 512:(tb + 1) * 512], o[c][tb].ap, r=[o[c][tb]]))
            for t in toks + list(self.dbg_toks.values()):
                K.wait_tok(K.SP, t)
            K.barrier()


def build_program(cfg):
    p = Prog(cfg)
    return p.build()


def prep_shared(inp):
    f32 = np.float32
    sh = {}
    gains = np.concatenate([inp["norm_ffn1"], inp["norm_mix"], inp["norm_ffn2"], inp["final_norm"][None]], axis=0)
    sh["gains"] = np.ascontiguousarray(gains.reshape(13, NC_, 128).transpose(2, 0, 1).reshape(128, 13 * NC_)).astype(f32)
    fin = np.stack([inp["ffn1_in"], inp["ffn2_in"]], axis=1).reshape(2 * DEPTH, D, 2 * DFF)
    fin = fin.reshape(2 * DEPTH, NC_, 128, 2, NHC, 128).transpose(0, 4, 2, 1, 3, 5)
    sh["ffn_in"] = np.ascontiguousarray(fin).reshape(2 * DEPTH, NHC, 128, NC_, 256)
    fout = np.stack([inp["ffn1_out"], inp["ffn2_out"]], axis=1).reshape(2 * DEPTH, NHC, 128, D)
    sh["ffn_out"] = np.ascontiguousarray(fout)
    wi = inp["odd_w_in"]
    Z = lambda n: np.zeros((2, D, n), f32)
    cq, ckv, krc = wi[:, :, 0:256], wi[:, :, 256:384], wi[:, :, 384:416]
    za, zb = wi[:, :, 416:928], wi[:, :, 928:1440]
    kr_rot = np.concatenate([krc[:, :, 16:32], krc[:, :, 0:16]], axis=2)
    slabs = [cq,
             np.concatenate([ckv, Z(64), krc, Z(32)], axis=2),
             np.concatenate([Z(64), kr_rot, Z(160)], axis=2)]
    for q in range(4):
        slabs.append(np.concatenate([za[:, :, q * 128:(q + 1) * 128], zb[:, :, q * 128:(q + 1) * 128]], axis=2))
    oin = np.stack(slabs, axis=1)
    oin = oin.reshape(2, 7, NC_, 128, 256).transpose(0, 1, 3, 2, 4)
    sh["odd_in"] = np.ascontiguousarray(oin).astype(f32)
    sh["odd_out"] = np.ascontiguousarray(inp["odd_w_out"].reshape(2, 8, 128, D)).astype(f32)
    wq = inp["wq_up"]
    sh["wq"] = np.ascontiguousarray(wq.reshape(2, 2, 128, 768).transpose(0, 2, 1, 3)).astype(f32)
    wq4 = wq.reshape(2, 256, 8, 96)
    wqr = np.concatenate([np.zeros((2, 256, 8, 64), f32), wq4[..., 80:96], wq4[..., 64:80]], axis=-1).reshape(2, 256, 768)
    sh["wqr"] = np.ascontiguousarray(wqr.reshape(2, 2, 128, 768).transpose(0, 2, 1, 3)).astype(f32)
    wkv4 = inp["wkv_up"].reshape(2, 128, 8, 128)
    sh["wkv"] = np.ascontiguousarray(np.concatenate([wkv4[..., 0:64].reshape(2, 128, 512),
                                                     wkv4[..., 64:128].reshape(2, 128, 512)], axis=-1)).astype(f32)
    col = lambda v, n: v.reshape(2, n, 128).transpose(0, 2, 1)
    cwp = inp["conv_w"].reshape(2, 31, 4, 128).transpose(0, 3, 2, 1).reshape(2, 128, 124)
    sh["odd_small"] = np.ascontiguousarray(np.concatenate(
        [col(inp["q_norm"], 2), col(inp["kv_norm"], 1), cwp, col(inp["conv_b"], 4), col(inp["conv_ln_g"], 4),
         col(inp["conv_ln_b"], 4)], axis=2)).astype(f32)
    sh["rope"] = rope_table()
    ew = inp["even_w_in"]
    ein = ew.reshape(2, NC_, 128, 11, 256).transpose(0, 3, 2, 1, 4)
    sh["even_in"] = np.ascontiguousarray(ein).astype(f32)
    sh["even_out"] = np.ascontiguousarray(inp["even_w_out"].reshape(2, 8, 128, D)).astype(f32)
    sh["gsu_wsT"] = np.ascontiguousarray(inp["gsu_ws"].transpose(0, 3, 1, 2)).astype(f32)
    sh["gsu_bs4"] = np.ascontiguousarray(np.tile(inp["gsu_bs"], (1, 1, 4)).reshape(2, 1, 4, 512)).astype(f32)
    rows = np.concatenate([inp["shift_mu"], inp["k_k"], inp["k_a"], inp["r_k"].reshape(2, 512), inp["lnx_g"],
                           inp["lnx_b"], inp["gsu_ln_g"], inp["gsu_ln_b"], inp["decay_w0"], inp["iclr_a0"]], axis=1)
    sh["even_rows"] = np.ascontiguousarray(rows.reshape(2, 1, EV_NR)).astype(f32)
    sh["lora_d"] = np.ascontiguousarray(inp["decay_up"]).astype(f32)
    sh["lora_i"] = np.ascontiguousarray(np.concatenate([np.zeros((2, 64, 512), f32), inp["iclr_up"]], axis=1)).astype(f32)
    sh["lora_g"] = np.ascontiguousarray(inp["gate_up"]).astype(f32)
    return sh


def rope_table():
    f32 = np.float32
    inv_freq = (f32(10000.0) ** (-(np.arange(0, 32, 2, dtype=f32) / f32(32)))).astype(f32)
    ang = (np.arange(S, dtype=f32)[:, None] * inv_freq[None, :]).astype(f32)
    cos = np.cos(ang.astype(np.float64)).astype(f32).T
    sin = np.sin(ang.astype(np.float64)).astype(f32).T
    t = np.zeros((128, 2, S), f32)
    t[64:80, 0] = cos
    t[80:96, 0] = cos
    t[64:80, 1] = -sin
    t[80:96, 1] = sin
    return t


def prep_x(x):
    return [np.ascontiguousarray(x[b].T.reshape(NC_, 128, S).transpose(1, 0, 2)) for b in range(x.shape[0])]


def unprep_out(o):
    return np.ascontiguousarray(o.transpose(2, 1, 0).reshape(S, D))


FULL_CFG = {"layers": DEPTH}


def kernel(**inputs):
    inp = {k: np.asarray(v) for k, v in inputs.items()}
    sh = prep_shared(inp)
    xs = prep_x(inp["x"].astype(np.float32))
    nc = build_program(FULL_CFG)
    in_maps = [dict(sh, xT=xs[b]) for b in range(len(xs))]
    res = run_bass_kernel_spmd(nc, in_maps, core_ids=list(range(len(xs))))
    out = np.stack([unprep_out(np.asarray(r["outT"])) for r in res.results], axis=0)
    return out.astype(np.float32)
```

```python
import numpy as np
from contextlib import ExitStack
import concourse.bass as bass
import concourse.mybir as mybir
from concourse.bass_utils import run_bass_kernel_spmd

F32 = mybir.dt.float32
BF16 = mybir.dt.bfloat16
AF = mybir.ActivationFunctionType
ALU = mybir.AluOpType
AX = mybir.AxisListType

S = 2048
D = 1024
NC_ = 8
NTB = 4
DFF = 2816
NHC = 22
DEPTH = 4
GROUPS = [(0, 6), (6, 12), (12, 17), (17, 22)]
RMS_EPS = 1e-6
LN_EPS = 1e-5
ODD_NS = 2 + 1 + 4 * 31 + 4 + 4 + 4
ATTN_SCALE = 96.0 ** -0.5
LNX_EPS = 64e-5
EV_NR = 1792 + 9 * 512
OFF_MU, OFF_KK, OFF_KA, OFF_RK, OFF_LG, OFF_LB, OFF_GLG, OFF_GLB, OFF_W0, OFF_A0 = (
    0, 1792, 2304, 2816, 3328, 3840, 4352, 4864, 5376, 5888)
C0 = float(np.exp(-0.5))
NEUMANN_L = 4
NHS = 4


class Eng:
    def __init__(self, name, eng, sem, skip_self):
        self.name = name
        self.eng = eng
        self.sem = sem
        self.count = 0
        self.seen = {}
        self.skip_self = skip_self


class Buf:
    __slots__ = ("ap", "lw", "rd", "sem", "dcnt", "name", "persist", "psum")

    def __init__(self, ap, name="", persist=False, psum=False):
        self.persist = persist
        self.psum = psum
        self.ap = ap
        self.lw = None
        self.rd = []
        self.sem = None
        self.dcnt = 0
        self.name = name


class KB:
    def __init__(self, nc, es):
        self.nc = nc
        self.es = es
        self.nsem = 0
        self.PE = Eng("pe", nc.tensor, self.mksem("s_pe"), True)
        self.ACT = Eng("act", nc.scalar, self.mksem("s_act"), False)
        self.DVE = Eng("dve", nc.vector, self.mksem("s_dve"), False)
        self.POOL = Eng("pool", nc.gpsimd, self.mksem("s_pool"), False)
        self.SP = Eng("sp", nc.sync, self.mksem("s_sp"), False)
        self.compute = [self.PE, self.ACT, self.DVE, self.POOL]
        self.all = self.compute + [self.SP]
        self.nuniq = 0
        self.sem_pool = []
        self.nwait = 0
        self.phase_stack = []

    def phase(self):
        kb = self

        class _Ph:
            def __enter__(self_):
                kb.phase_stack.append([])

            def __exit__(self_, *a):
                for b in kb.phase_stack.pop():
                    kb.sem_pool.append((b.sem, b.dcnt))
                    b.sem = None
                return False
        return _Ph()

    def rotate(self, E):
        E.sem = self.mksem("s_%s_r%d" % (E.name, self.nsem))
        E.count = 0

    def mksem(self, name):
        self.nsem += 1
        return self.es.enter_context(self.nc.semaphore(name))

    def sb(self, es, shape, dt, name=None):
        self.nuniq += 1
        return es.enter_context(self.nc.sbuf_tensor("sb_%s_%d" % (name or "t", self.nuniq), shape, dt))

    def _need(self, E, tok, lst):
        sem, val, src = tok
        if src is E and E.skip_self:
            return
        k = id(sem)
        if E.seen.get(k, 0) >= val:
            return
        for i, (s2, v2) in enumerate(lst):
            if s2 is sem:
                if val > v2:
                    lst[i] = (sem, val)
                return
        lst.append((sem, val))

    def _emit_waits(self, E, lst, inst_fn):
        for sem, val in lst[:-1]:
            E.eng.wait_ge(sem, val)
            E.seen[id(sem)] = val
            self.nwait += 1
        inst = inst_fn()
        if lst:
            sem, val = lst[-1]
            inst._wait_ge(sem, val)
            E.seen[id(sem)] = val
        return inst

    def _wait(self, E, tok):
        lst = []
        self._need(E, tok, lst)
        for sem, val in lst:
            E.eng.wait_ge(sem, val)
            E.seen[id(sem)] = val
            self.nwait += 1

    def op(self, E, fn, w=(), r=()):
        lst = []
        for b in r:
            if b.lw is not None:
                self._need(E, b.lw, lst)
            if b.psum:
                for t in b.rd:
                    if t[2] is not E:
                        self._need(E, t, lst)
        for b in w:
            if b.lw is not None:
                self._need(E, b.lw, lst)
            for t in b.rd:
                if t[2] is not E:
                    self._need(E, t, lst)
        inst = self._emit_waits(E, lst, fn)
        E.count += 1
        inst.then_inc(E.sem, 1)
        tok = (E.sem, E.count, E)
        for b in r:
            b.rd.append(tok)
        for b in w:
            b.lw = tok
            b.rd = []
        return inst

    def dma(self, Q, out_ap, in_ap, w=(), r=()):
        lst = []
        for b in r:
            if b.lw is not None:
                self._need(Q, b.lw, lst)
        for b in w:
            if b.lw is not None:
                self._need(Q, b.lw, lst)
            for t in b.rd:
                self._need(Q, t, lst)
        owner = w[0] if len(w) else r[0]
        if owner.sem is None:
            if self.sem_pool and self.phase_stack and not owner.persist:
                owner.sem, owner.dcnt = self.sem_pool.pop()
            else:
                owner.sem = self.mksem("s_b%d" % self.nsem)
            if self.phase_stack and not owner.persist:
                self.phase_stack[-1].append(owner)
        inst = self._emit_waits(Q, lst, lambda: Q.eng.dma_start(out=out_ap, in_=in_ap))
        owner.dcnt += 1
        inst.then_inc(owner.sem, 16)
        tok = (owner.sem, 16 * owner.dcnt, None)
        for b in r:
            b.rd.append(tok)
        for b in w:
            b.lw = tok
            b.rd = []
        return tok

    def barrier(self):
        for E in self.all:
            for P in self.compute:
                if P is E:
                    continue
                if P.count > 0:
                    self._wait(E, (P.sem, P.count, None))

    def wait_tok(self, E, tok):
        self._wait(E, tok)


class WStream:
    def __init__(self, K, Q, slots):
        self.K = K
        self.Q = Q
        self.slots = slots
        self.jobs = []
        self.issued = 0
        self.released = 0

    def add(self, dram_ap, view):
        self.jobs.append((dram_ap, view))
        return len(self.jobs) - 1

    def ensure(self, j):
        j = min(j, len(self.jobs) - 1)
        n = len(self.slots)
        while self.issued <= j:
            i = self.issued
            assert i - n < self.released, "weight ring slot still in use"
            slot = self.slots[i % n]
            dram_ap, view = self.jobs[i]
            self.K.dma(self.Q, view(slot.ap), dram_ap, w=[slot])
            self.issued += 1

    def get(self, j):
        self.ensure(j)
        return self.slots[j % len(self.slots)]

    def release(self, j):
        self.released = max(self.released, j + 1)
        self.ensure(self.released + len(self.slots) - 1)


class Prog:
    def __init__(self, cfg):
        self.cfg = cfg

    def build(self):
        cfg = self.cfg
        nc = bass.Bass("TRN2", target_bir_lowering=False)
        self.nc = nc
        dram = {}

        def din(name, shape, dt=F32):
            dram[name] = nc.dram_tensor(name, list(shape), dt, kind="ExternalInput").ap()
            return dram[name]

        self.d_x = din("xT", [128, NC_, S])
        self.d_gains = din("gains", [128, 13 * NC_])
        self.d_ffn_in = din("ffn_in", [2 * DEPTH, NHC, 128, NC_, 256])
        self.d_ffn_out = din("ffn_out", [2 * DEPTH, NHC, 128, D])
        self.d_odd_in = din("odd_in", [2, 7, 128, NC_, 256])
        self.d_odd_out = din("odd_out", [2, 8, 128, D])
        self.d_wq = din("wq", [2, 128, 2, 768])
        self.d_wqr = din("wqr", [2, 128, 2, 768])
        self.d_wkv = din("wkv", [2, 128, 1024])
        self.d_odd_small = din("odd_small", [2, 128, ODD_NS])
        self.d_rope = din("rope", [128, 2, S])
        self.d_even_in = din("even_in", [2, 11, 128, NC_, 256])
        self.d_even_out = din("even_out", [2, 8, 128, D])
        self.d_gsu_ws = din("gsu_wsT", [2, 128, 4, 128])
        self.d_gsu_bs = din("gsu_bs4", [2, 1, 4, 512])
        self.d_even_rows = din("even_rows", [2, 1, EV_NR])
        self.d_lora_d = din("lora_d", [2, 64, 512])
        self.d_lora_i = din("lora_i", [2, 128, 512])
        self.d_lora_g = din("lora_g", [2, 128, 512])
        self.d_zB = nc.dram_tensor("zB_scratch", [S + 1, 1792], F32).ap()
        self.zB = Buf(self.d_zB, "zB", persist=True)
        self.d_out = nc.dram_tensor("outT", [128, NC_, S], F32, kind="ExternalOutput").ap()

        self.dbg_toks = {}
        with ExitStack() as es:
            K = KB(nc, es)
            self.K = K
            xT = K.sb(es, [128, NC_, S], F32, "xT")
            self.xT = xT
            self.x = [[Buf(xT[:, c, tb * 512:(tb + 1) * 512], "x%d_%d" % (c, tb)) for tb in range(NTB)]
                      for c in range(NC_)]
            gains = K.sb(es, [128, 13 * NC_], F32, "gains")
            self.gains = Buf(gains[:], "gains")
            ones_bf = K.sb(es, [128, 128], BF16, "ones_bf")
            self.ones_bf = Buf(ones_bf[:], "ones")
            ones_f = K.sb(es, [128, 128], F32, "ones_f")
            self.ones_f = Buf(ones_f[:], "ones_f")
            K.op(K.DVE, lambda: nc.vector.memset(ones_f[:], 1.0), w=[self.ones_f])
            ident_f = K.sb(es, [128, 128], F32, "ident_f")
            self.ident_f = Buf(ident_f[:], "ident_f")
            K.op(K.POOL, lambda: nc.gpsimd.memset(ident_f[:], 1.0), w=[self.ident_f])
            K.op(K.POOL, lambda: nc.gpsimd.affine_select(out=ident_f[:], in_=ident_f[:], pattern=[[-1, 128]],
                                                         compare_op=ALU.is_equal, fill=0.0, base=0,
                                                         channel_multiplier=1), w=[self.ident_f], r=[self.ident_f])
            ident_b = K.sb(es, [128, 128], BF16, "ident_b")
            self.ident_b = Buf(ident_b[:], "ident_b")
            K.op(K.DVE, lambda: nc.vector.tensor_copy(ident_b[:], ident_f[:]), w=[self.ident_b], r=[self.ident_f])
            epst = K.sb(es, [128, 4], F32, "epst")
            self.epsb = Buf(epst[:], "eps")
            self.eps_cols = {}
            for i, e in enumerate([1e-6, 1e-5, 64e-5, 0.0]):
                K.op(K.DVE, lambda i=i, e=e: nc.vector.memset(epst[:, i:i + 1], e), w=[self.epsb])
                self.eps_cols[e] = epst[:, i:i + 1]
            self.nrm_sq = [Buf(K.sb(es, [128, 512], BF16)[:], "sq") for _ in range(2)]
            self.nrm_rs = [Buf(K.sb(es, [128, 512], F32)[:], "rstd") for _ in range(2)]
            wi_t = [K.sb(es, [128, NC_, 256], BF16, "wi%d" % i) for i in range(3)]
            wo_t = [K.sb(es, [128, D], BF16, "wo%d" % i) for i in range(12)]
            self.wi = WStream(K, K.POOL, [Buf(t[:], "wi", persist=True) for t in wi_t])
            self.wo = WStream(K, K.POOL, [Buf(t[:], "wo", persist=True) for t in wo_t])
            self.ps = [Buf(es.enter_context(nc.psum_tensor("ps%d" % i, [128, 512], F32))[:], "ps%d" % i, psum=True)
                       for i in range(8)]
            self.ps_i = 0

            self.plan_jobs()

            K.op(K.DVE, lambda: nc.vector.memset(ones_bf[:], 1.0), w=[self.ones_bf])
            K.dma(K.SP, gains[:], self.d_gains[:, :], w=[self.gains])
            for c in range(NC_):
                for tb in range(NTB):
                    K.dma(K.SP, self.x[c][tb].ap, self.d_x[:, c, tb * 512:(tb + 1) * 512], w=[self.x[c][tb]])
            self.wi.ensure(2)
            self.wo.ensure(5)

            for l in range(cfg["layers"]):
                if l in cfg.get("skip_layers", []):
                    continue
                if l > 0:
                    K.barrier()
                    K.rotate(K.PE)
                if cfg.get("ffn1", True):
                    with K.phase():
                        self.ffn(l, 0)
                if cfg.get("mixer", True):
                    with K.phase():
                        self.mixer(l)
                if cfg.get("ffn2", True):
                    with K.phase():
                        self.ffn(l, 1)
            with K.phase():
                self.final(cfg.get("final_norm", True))
        return nc

    def dbg(self, name, ap, buf):
        if not self.cfg.get("dbg"):
            return
        if name in self.dbg_toks:
            return
        if self.cfg.get("dbg_names") is not None and name not in self.cfg["dbg_names"]:
            return
        d = self.nc.dram_tensor("dbg_" + name, list(ap.shape), ap.dtype, kind="ExternalOutput").ap()
        self.dbg_toks[name] = self.K.dma(self.K.SP, d, ap, r=[buf])

    def psum(self):
        b = self.ps[self.ps_i % 8]
        self.ps_i += 1
        return b

    def plan_jobs(self):
        cfg = self.cfg
        self.jobs_wi = {}
        self.jobs_wo = {}
        full = lambda ap: ap
        for l in range(cfg["layers"]):
            if l in cfg.get("skip_layers", []):
                continue
            for which in range(2):
                if which == 1 and cfg.get("mixer", True):
                    if l % 2 == 0:
                        e = l // 2
                        for sl in range(11):
                            self.jobs_wi[("even", l, sl)] = self.wi.add(self.d_even_in[e, sl], full)
                        for ch in range(8):
                            self.jobs_wo[("even", l, ch)] = self.wo.add(self.d_even_out[e, ch], full)
                    if l % 2 == 1:
                        o = l // 2
                        for sl in range(7):
                            self.jobs_wi[("odd", l, sl)] = self.wi.add(self.d_odd_in[o, sl], full)
                        for ch in [4, 5, 6, 7, 0, 1, 2, 3]:
                            self.jobs_wo[("odd", l, ch)] = self.wo.add(self.d_odd_out[o, ch], full)
                if not cfg.get("ffn%d" % (which + 1), True):
                    continue
                f = l * 2 + which
                for (a, b) in GROUPS:
                    for hc in range(a, b):
                        self.jobs_wi[(f, hc)] = self.wi.add(self.d_ffn_in[f, hc], full)
                    for hc in range(a, b):
                        self.jobs_wo[(f, hc)] = self.wo.add(self.d_ffn_out[f, hc], full)

    def rmsnorm_fm(self, es, src, gain_col, dst, nchunks, nfeat, eps, tbs=range(NTB), extra_r=()):
        K, nc = self.K, self.nc
        sq, rs = self.nrm_sq, self.nrm_rs
        for tb in tbs:
            pb = self.psum()
            for c in range(nchunks):
                s = sq[c % 2]
                K.op(K.ACT, lambda s=s, c=c: nc.scalar.activation(out=s.ap, in_=src[c][tb].ap, func=AF.Square),
                     w=[s], r=[src[c][tb]])
                K.op(K.PE, lambda s=s, c=c: nc.tensor.matmul(pb.ap, self.ones_bf.ap, s.ap, start=(c == 0),
                                                             stop=(c == nchunks - 1)),
                     w=[pb], r=[s, self.ones_bf])
            r = rs[tb % 2]
            K.op(K.ACT, lambda: nc.scalar.activation(out=r.ap, in_=pb.ap, func=AF.Sqrt, scale=1.0 / nfeat,
                                                     bias=self.eps_ap(eps)),
                 w=[r], r=[pb, self.epsb])
            K.op(K.DVE, lambda: nc.vector.reciprocal(out=r.ap, in_=r.ap), w=[r], r=[r])
            for c in range(nchunks):
                K.op(K.DVE, lambda c=c: nc.vector.scalar_tensor_tensor(
                    out=dst[c][tb].ap, in0=src[c][tb].ap, scalar=gain_col(c), in1=r.ap,
                    op0=ALU.mult, op1=ALU.mult), w=[dst[c][tb]], r=[src[c][tb], r, self.gains] + list(extra_r))

    def eps_ap(self, eps):
        return self.eps_cols[eps]

    def gain_col(self, idx):
        return lambda c: self.gains.ap[:, idx * NC_ + c: idx * NC_ + c + 1]

    def ffn(self, l, which):
        K, nc = self.K, self.nc
        f = l * 2 + which
        gidx = (0 if which == 0 else 2) * DEPTH + l
        with ExitStack() as es:
            hT_t = K.sb(es, [128, NC_, S], BF16, "hT")
            hT = [[Buf(hT_t[:, c, tb * 512:(tb + 1) * 512]) for tb in range(NTB)] for c in range(NC_)]
            hid_t = K.sb(es, [128, 6, S], BF16, "hid")
            hid = [[Buf(hid_t[:, j, tb * 512:(tb + 1) * 512]) for tb in range(NTB)] for j in range(6)]
            sg = [Buf(K.sb(es, [128, 512], F32)[:], "sg") for _ in range(2)]
            self.rmsnorm_fm(es, self.x, self.gain_col(gidx), hT, NC_, D, RMS_EPS)
            n = 0
            for (a, b) in GROUPS:
                for j, hc in enumerate(range(a, b)):
                    ji = self.jobs_wi[(f, hc)]
                    wi = self.wi.get(ji)
                    for tb in range(NTB):
                        pg = self.psum()
                        pu = self.psum()
                        for c in range(NC_):
                            K.op(K.PE, lambda c=c: nc.tensor.matmul(pg.ap, wi.ap[:, c, 0:128], hT[c][tb].ap,
                                                                    start=(c == 0), stop=(c == NC_ - 1)),
                                 w=[pg], r=[wi, hT[c][tb]])
                        for c in range(NC_):
                            K.op(K.PE, lambda c=c: nc.tensor.matmul(pu.ap, wi.ap[:, c, 128:256], hT[c][tb].ap,
                                                                    start=(c == 0), stop=(c == NC_ - 1)),
                                 w=[pu], r=[wi, hT[c][tb]])
                        s = sg[n % 2]
                        n += 1
                        K.op(K.ACT, lambda: nc.scalar.activation(out=s.ap, in_=pg.ap, func=AF.Silu), w=[s], r=[pg])
                        K.op(K.DVE, lambda: nc.vector.tensor_tensor(out=hid[j][tb].ap, in0=s.ap, in1=pu.ap,
                                                                    op=ALU.mult), w=[hid[j][tb]], r=[s, pu])
                    self.wi.release(ji)
                wos = [self.wo.get(self.jobs_wo[(f, hc)]) for hc in range(a, b)]
                ng = b - a
                for d in range(NC_):
                    for tb in range(NTB):
                        po = self.psum()
                        for j in range(ng):
                            K.op(K.PE, lambda j=j: nc.tensor.matmul(po.ap, wos[j].ap[:, d * 128:(d + 1) * 128],
                                                                    hid[j][tb].ap, start=(j == 0), stop=(j == ng - 1)),
                                 w=[po], r=[wos[j], hid[j][tb]])
                        xb = self.x[d][tb]
                        K.op(K.DVE, lambda: nc.vector.scalar_tensor_tensor(
                            out=xb.ap, in0=po.ap, scalar=0.5, in1=xb.ap, op0=ALU.mult, op1=ALU.add),
                            w=[xb], r=[po, xb])
                for hc in range(a, b):
                    self.wo.release(self.jobs_wo[(f, hc)])
            K.barrier()

    def mixer(self, l):
        if l % 2 == 1:
            self.mixer_odd(l)
        else:
            self.mixer_even(l)

    def mixer_even(self, l):
        K, nc = self.K, self.nc
        e = l // 2
        PE, ACT, DVE, POOL, SP = K.PE, K.ACT, K.DVE, K.POOL, K.SP
        ps = self.ps
        cfg = self.cfg

        def outproj(mixb, chs, tbs=range(NTB)):
            wos = [self.wo.get(self.jobs_wo[("even", l, ch)]) for ch in chs]
            for d in range(NC_):
                for tb in tbs:
                    po = self.psum4()
                    for i, ch in enumerate(chs):
                        K.op(PE, lambda i=i: nc.tensor.matmul(po.ap, wos[i].ap[:, d * 128:(d + 1) * 128], mixb[i][tb].ap,
                                                              start=(i == 0), stop=(i == len(chs) - 1)),
                             w=[po], r=[wos[i], mixb[i][tb]])
                    xb = self.x[d][tb]
                    K.op(DVE, lambda: nc.vector.tensor_tensor(out=xb.ap, in0=po.ap, in1=xb.ap, op=ALU.add),
                         w=[xb], r=[po, xb])

        with ExitStack() as es:
            rows = self.d_even_rows[e]

            def bc_tile(esx, off, n, name):
                t = K.sb(esx, [128, n], F32, name)
                bf = Buf(t[:], name)
                K.dma(SP, t[:], rows[0:1, off:off + n].partition_broadcast(128), w=[bf])
                return t, bf

            with ExitStack() as esg:
                uT_t = K.sb(esg, [128, 4, S], BF16, "uT")
                uT = [[Buf(uT_t[:, c, tb * 512:(tb + 1) * 512]) for tb in range(NTB)] for c in range(4)]
                vtm_t = K.sb(esg, [128, 16, 512], BF16, "vtm")
                vtm = [Buf(vtm_t[:, i, :]) for i in range(16)]
                wsT_t = K.sb(esg, [128, 4, 128], BF16, "wsT")
                wsT = Buf(wsT_t[:], "wsT")
                K.dma(POOL, wsT_t[:], self.d_gsu_ws[e], w=[wsT])
                K.op(POOL, lambda: nc.gpsimd.memset(wsT_t[64:128, :, 0:64], 0.0), w=[wsT], r=[wsT])
                bs4_t = K.sb(esg, [1, 4, 512], F32, "bs4")
                bs4 = Buf(bs4_t[:], "bs4")
                K.dma(SP, bs4_t[:], self.d_gsu_bs[e], w=[bs4])
                with ExitStack() as esa:
                    hT_t = K.sb(esa, [128, NC_, S], BF16, "hT")
                    hT = [[Buf(hT_t[:, c, tb * 512:(tb + 1) * 512]) for tb in range(NTB)] for c in range(NC_)]
                    self.rmsnorm_fm(esa, self.x, self.gain_col(DEPTH + l), hT, NC_, D, RMS_EPS)
                    glg_t, glg = bc_tile(esa, OFF_GLG, 512, "glg")
                    glb_t, glb = bc_tile(esa, OFF_GLB, 512, "glb")
                    g32 = [Buf(K.sb(esa, [128, 512], F32, "g32_%d" % i)[:]) for i in range(2)]
                    gsq = Buf(K.sb(esa, [128, 512], F32, "gsq")[:])
                    st_t = K.sb(esa, [128, 8], F32, "vstat")
                    st = Buf(st_t[:], "vstat")
                    zst_t = [K.sb(esa, [128, 4, 256], F32, "zstage%d" % i) for i in range(2)]
                    zst = [Buf(t[:]) for t in zst_t]
                    zrow_t = K.sb(esa, [1, 1792], F32, "zrow")
                    zrow = Buf(zrow_t[:], "zrow")
                    K.op(DVE, lambda: nc.vector.memset(zrow_t[:], 0.0), w=[zrow])
                    K.dma(SP, self.d_zB[0:1, :], zrow_t[:], w=[self.zB], r=[zrow])
                    for sl in range(2):
                        jj = self.jobs_wi[("even", l, sl)]
                        wb = self.wi.get(jj)
                        for tb in range(NTB):
                            for half in range(2):
                                pb = self.psum()
                                for c in range(NC_):
                                    K.op(PE, lambda c=c, pb=pb: nc.tensor.matmul(
                                        pb.ap, wb.ap[:, c, half * 128:(half + 1) * 128], hT[c][tb].ap,
                                        start=(c == 0), stop=(c == NC_ - 1)), w=[pb], r=[wb, hT[c][tb]])
                                ub = uT[sl * 2 + half][tb]
                                K.op(ACT, lambda pb=pb, ub=ub: nc.scalar.activation(out=ub.ap, in_=pb.ap,
                                                                                    func=AF.Gelu_apprx_tanh),
                                     w=[ub], r=[pb])
                        self.wi.release(jj)
                    j2 = self.jobs_wi[("even", l, 2)]
                    w2 = self.wi.get(j2)
                    j3 = self.jobs_wi[("even", l, 3)]
                    w3 = self.wi.get(j3)
                    for i in range(16):
                        tb, off = i // 4, (i % 4) * 128
                        pv = self.psum()
                        for hi, wb in enumerate((w2, w3)):
                            for c in range(NC_):
                                K.op(PE, lambda c=c, wb=wb, hi=hi: nc.tensor.matmul(
                                    pv.ap[:, hi * 256:(hi + 1) * 256], hT_t[:, c, i * 128:(i + 1) * 128], wb.ap[:, c, :],
                                    start=(c == 0), stop=(c == NC_ - 1)), w=[pv], r=[wb, hT[c][tb]])
                        gb = g32[i % 2]
                        K.op(ACT, lambda gb=gb, pv=pv: nc.scalar.activation(out=gb.ap, in_=pv.ap, func=AF.Gelu_apprx_tanh),
                             w=[gb], r=[pv])
                        K.op(DVE, lambda gb=gb: nc.vector.tensor_reduce(out=st_t[:, 0:1], in_=gb.ap, axis=AX.X, op=ALU.add),
                             w=[st], r=[gb])
                        K.op(ACT, lambda gb=gb: nc.scalar.activation(out=gsq.ap, in_=gb.ap, func=AF.Square),
                             w=[gsq], r=[gb])
                        K.op(DVE, lambda: nc.vector.tensor_reduce(out=st_t[:, 1:2], in_=gsq.ap, axis=AX.X, op=ALU.add),
                             w=[st], r=[gsq])
                        K.op(DVE, lambda: nc.vector.tensor_scalar(out=st_t[:, 2:3], in0=st_t[:, 0:1], scalar1=1.0 / 512,
                                                                  scalar2=None, op0=ALU.mult), w=[st], r=[st])
                        K.op(DVE, lambda: nc.vector.tensor_tensor(out=st_t[:, 3:4], in0=st_t[:, 2:3], in1=st_t[:, 2:3],
                                                                  op=ALU.mult), w=[st], r=[st])
                        K.op(DVE, lambda: nc.vector.scalar_tensor_tensor(out=st_t[:, 4:5], in0=st_t[:, 1:2],
                                                                         scalar=1.0 / 512, in1=st_t[:, 3:4],
                                                                         op0=ALU.mult, op1=ALU.subtract), w=[st], r=[st])
                        K.op(ACT, lambda: nc.scalar.activation(out=st_t[:, 5:6], in_=st_t[:, 4:5], func=AF.Sqrt, scale=1.0,
                                                               bias=self.eps_cols[LN_EPS]), w=[st], r=[st, self.epsb])
                        K.op(DVE, lambda: nc.vector.reciprocal(out=st_t[:, 6:7], in_=st_t[:, 5:6]), w=[st], r=[st])
                        K.op(DVE, lambda gb=gb: nc.vector.tensor_scalar(out=gb.ap, in0=gb.ap, scalar1=st_t[:, 2:3],
                                                                        scalar2=st_t[:, 6:7], op0=ALU.subtract,
                                                                        op1=ALU.mult), w=[gb], r=[gb, st])
                        K.op(DVE, lambda gb=gb: nc.vector.tensor_tensor(out=gb.ap, in0=gb.ap, in1=glg_t[:], op=ALU.mult),
                             w=[gb], r=[gb, glg])
                        K.op(DVE, lambda gb=gb, i=i: nc.vector.tensor_tensor(out=vtm[i].ap, in0=gb.ap, in1=glb_t[:],
                                                                             op=ALU.add), w=[vtm[i]], r=[gb, glb])
                    self.wi.release(j2)
                    self.wi.release(j3)
                    nz = 0
                    for sl in range(4, 11):
                        if cfg.get("only") == "a":
                            jj = self.jobs_wi[("even", l, sl)]
                            self.wi.get(jj)
                            self.wi.release(jj)
                            continue
                        jj = self.jobs_wi[("even", l, sl)]
                        wb = self.wi.get(jj)
                        for tb in range(NTB):
                            zb_, zb_t = zst[nz % 2], zst_t[nz % 2]
                            nz += 1
                            for ti in range(4):
                                i = tb * 4 + ti
                                pz = self.psum()
                                for c in range(NC_):
                                    K.op(PE, lambda c=c, pz=pz, i=i: nc.tensor.matmul(
                                        pz.ap[:, 0:256], hT_t[:, c, i * 128:(i + 1) * 128], wb.ap[:, c, :],
                                        start=(c == 0), stop=(c == NC_ - 1)), w=[pz], r=[wb, hT[c][tb]])
                                K.op(ACT, lambda pz=pz, ti=ti, zb_t=zb_t: nc.scalar.copy(out=zb_t[:, ti, :],
                                                                                         in_=pz.ap[:, 0:256]),
                                     w=[zb_], r=[pz])
                            col0 = (sl - 4) * 256
                            dst = self.d_zB[1 + tb * 512: 1 + (tb + 1) * 512, col0:col0 + 256].rearrange(
                                "(ti p) n -> p ti n", p=128)
                            K.dma(SP, dst, zb_t[:], w=[self.zB], r=[zb_])
                        self.wi.release(jj)
                    K.barrier()
                with ExitStack() as esm:
                    ya_t = K.sb(esm, [128, 4, S], BF16, "yaT")
                    ya = [[Buf(ya_t[:, g, tb * 512:(tb + 1) * 512]) for tb in range(NTB)] for g in range(4)]
                    for tb in range(NTB):
                        for g in range(4):
                            pm = self.psum()
                            K.op(PE, lambda g=g, pm=pm: nc.tensor.matmul(pm.ap, self.ones_f.ap[0:1, :], bs4_t[0:1, g, :],
                                                                         start=True, stop=False),
                                 w=[pm], r=[self.ones_f, bs4])
                            for nb in range(4):
                                i = tb * 4 + nb
                                K.op(PE, lambda i=i, g=g, nb=nb, pm=pm: nc.tensor.matmul(
                                    pm.ap[:, nb * 128:(nb + 1) * 128], vtm_t[:, i, g * 128:(g + 1) * 128], wsT_t[:, g, :],
                                    start=False, stop=(nb == 3)), w=[pm], r=[vtm[i], wsT])
                            K.op(DVE, lambda g=g, tb=tb, pm=pm: nc.vector.tensor_tensor(
                                out=ya[g][tb].ap, in0=pm.ap, in1=uT[g][tb].ap, op=ALU.mult),
                                w=[ya[g][tb]], r=[pm, uT[g][tb]])
                    self.psum4 = self.psum
                    outproj(ya, [0, 1, 2, 3])
                    for ch in range(4):
                        self.wo.release(self.jobs_wo[("even", l, ch)])
                    K.barrier()

            if cfg.get("only") == "a":
                for ch in range(4, 8):
                    self.wo.get(self.jobs_wo[("even", l, ch)])
                    self.wo.release(self.jobs_wo[("even", l, ch)])
            else:
                self.rwkv(l, es, bc_tile, outproj)
            K.barrier()

    def rwkv(self, l, es, bc_tile, outproj):
        K, nc = self.K, self.nc
        e = l // 2
        PE, ACT, DVE, POOL, SP = K.PE, K.ACT, K.DVE, K.POOL, K.SP
        ps = self.ps
        rr = [0]

        def psum4():
            for k in range(4):
                bb = ps[(rr[0] + k) % 4]
                if bb.lw is None or len(bb.rd) > 0:
                    rr[0] += k + 1
                    return bb
            raise AssertionError("no free rotating PSUM bank")
        self.psum4 = psum4
        pCH, pF, pG, pY = ps[4], ps[5], ps[6], ps[7]
        rows = self.d_even_rows[e]
        L = NEUMANN_L
        with ExitStack() as esr:
            def T(shape, dt, name):
                t = K.sb(esr, shape, dt, name)
                return t, Buf(t[:], name)
            mu_t, mu = bc_tile(esr, OFF_MU, 1792, "mu")
            kkb_t, kkb = bc_tile(esr, OFF_KK, 512, "kkb")
            kab_t, kab = bc_tile(esr, OFF_KA, 512, "kab")
            rkb_t, rkb = bc_tile(esr, OFF_RK, 512, "rkb")
            lgb_t, lgb = bc_tile(esr, OFF_LG, 512, "lgb")
            lbb_t, lbb = bc_tile(esr, OFF_LB, 512, "lbb")
            w0a0_t, w0a0 = T([1, 512], F32, "a0row")
            K.dma(SP, w0a0_t[:], rows[0:1, OFF_A0:OFF_A0 + 512], w=[w0a0])
            dup_t, dup = T([65, 512], F32, "dup")
            K.dma(SP, dup_t[0:64, :], self.d_lora_d[e], w=[dup])
            K.dma(SP, dup_t[64:65, :], rows[0:1, OFF_W0:OFF_W0 + 512], w=[dup])
            iup_t, iup = T([128, 512], BF16, "iup")
            K.dma(POOL, iup_t[:], self.d_lora_i[e], w=[iup])
            gup_t, gup = T([128, 512], BF16, "gup")
            K.dma(POOL, gup_t[:], self.d_lora_g[e], w=[gup])
            Ui_t, Ui = T([128, 128], F32, "Uincl")
            Us_t, Us = T([128, 128], F32, "Ustrict")
            Ls_t, Ls = T([128, 128], F32, "Lstrict")
            mk2_t, mk2 = T([128, 256], F32, "mask2")
            for (t_, b_, cmp_, st_, cm_) in ((Ui_t, Ui, ALU.is_ge, 1, -1), (Us_t, Us, ALU.is_gt, 1, -1),
                                             (Ls_t, Ls, ALU.is_gt, -1, 1)):
                K.op(POOL, lambda t_=t_: nc.gpsimd.memset(t_[:], 1.0), w=[b_])
                K.op(POOL, lambda t_=t_, cmp_=cmp_, st_=st_, cm_=cm_: nc.gpsimd.affine_select(
                    out=t_[:], in_=t_[:], pattern=[[st_, 128]], compare_op=cmp_, fill=0.0, base=0,
                    channel_multiplier=cm_), w=[b_], r=[b_])
            K.op(POOL, lambda: nc.gpsimd.tensor_copy(mk2_t[:, 0:128], Us_t[:]), w=[mk2], r=[Us])
            K.op(POOL, lambda: nc.gpsimd.tensor_copy(mk2_t[:, 128:256], Ui_t[:]), w=[mk2], r=[Ui])
            Hf_t = [K.sb(esr, [64, 512], F32, "Hf%d" % i) for i in range(2)]
            Hf = [Buf(t[:]) for t in Hf_t]
            Hb_t = [K.sb(esr, [64, 512], BF16, "Hb%d" % i) for i in range(2)]
            Hb = [Buf(t[:]) for t in Hb_t]
            K.op(DVE, lambda: nc.vector.memset(Hf_t[0][:], 0.0), w=[Hf[0]])
            K.op(DVE, lambda: nc.vector.memset(Hb_t[0][:], 0.0), w=[Hb[0]])
            GTa_t, GTa = T([64, 512], F32, "GTa")
            Fa_t, Fa = T([64, 512], F32, "Fa")
            pC_t, pCs = T([64, 8], F32, "pC")
            ybT_t = [K.sb(esr, [128, 4, 512], BF16, "ybT%d" % i) for i in range(1)]
            ybT = [[Buf(ybT_t[i][:, q, :]) for q in range(4)] for i in range(1)]
            zt_t, zt = T([128, 1792], F32, "zt")
            scr_t = K.sb(esr, [128, 2048], F32, "scr")
            zs_t, zs = scr_t[:, 0:1792], Buf(scr_t[:, 0:1792], "zs")
            lwT_t, lwT = T([65, 128], F32, "lwT")
            K.op(DVE, lambda: nc.vector.memset(lwT_t[64:65, :], 1.0), w=[lwT])
            laT_t, laT = T([128, 128], BF16, "laT")
            lgT_t, lgT = T([128, 128], BF16, "lgT")
            sg_t, sg = T([128, 512], F32, "sg")
            as_t, asg = T([128, 512], F32, "asig")
            gg_t, gg = T([128, 512], F32, "gg")
            Ea_t, Ea = scr_t[:, 1024:1536], Buf(scr_t[:, 1024:1536], "Ea")
            Eb_t, Eb = scr_t[:, 1536:2048], Buf(scr_t[:, 1536:2048], "Eb")
            kk_t, kk = T([128, 512], F32, "kk")
            km_t, km = T([128, 512], F32, "kmod")
            bq_t, bq = T([128, 512], F32, "bq")
            tA_t, tA = scr_t[:, 0:512], Buf(scr_t[:, 0:512], "tA")
            tB_t, tB = scr_t[:, 512:1024], Buf(scr_t[:, 512:1024], "tB")
            ZS = [zs, tA, tB, Ea, Eb]
            st_t, st = T([128, 64], F32, "rst")
            at_t, at_b = T([128, 512], BF16, "at_b")
            rt_t, rt_b = T([128, 512], BF16, "rt_b")
            bt_t, bt_b = T([128, 512], BF16, "bt_b")
            kt_t, kt_b = T([128, 512], BF16, "kt_b")
            bh_t, bh_b = T([128, 512], BF16, "bh_b")
            kh_t, kh_b = T([128, 512], BF16, "kh_b")
            v_t, v_b = T([128, 512], BF16, "v_b")
            yb_t, yb_b = T([128, 512], BF16, "yb_b")
            GT_t = [K.sb(esr, [128, 4, 128], BF16, "GT%d" % i) for i in range(4)]
            GT = [Buf(t[:]) for t in GT_t]
            HS = []
            for i in range(NHS):
                d = {}
                for nm, shp in (("M1", [128, 256]), ("M2", [128, 256]), ("XT0", [128, 128]), ("XXa", [128, 256]),
                                ("XXb", [128, 256]), ("Pa", [128, 128]), ("Pb", [128, 128]), ("W1", [128, 64]),
                                ("AU", [128, 128]), ("RbT", [64, 128])):
                    t_ = K.sb(esr, shp, BF16, "%s_%d" % (nm, i))
                    d[nm] = (t_, Buf(t_[:], nm))
                HS.append(d)

            v3 = lambda ap: ap.rearrange("p (h f) -> p h f", f=64)
            bc3 = lambda ap: ap.unsqueeze(2).broadcast_to([128, 8, 64])

            for i in range(16):
                tb, ti = i // 4, i % 4
                cur, nxt = i % 2, (i + 1) % 2
                K.dma(SP, zt_t[:], self.d_zB[1 + i * 128: 1 + (i + 1) * 128, :], w=[zt], r=[self.zB])
                K.dma(SP, zs_t, self.d_zB[i * 128:(i + 1) * 128, :], w=ZS, r=[self.zB])
                K.op(DVE, lambda: nc.vector.tensor_tensor(out=zs_t, in0=zs_t, in1=zt_t[:], op=ALU.subtract),
                     w=ZS, r=[zs, zt])
                K.op(POOL, lambda: nc.gpsimd.tensor_tensor(out=zs_t, in0=zs_t, in1=mu_t[:], op=ALU.mult),
                     w=ZS, r=[zs, mu])
                K.op(DVE, lambda: nc.vector.tensor_tensor(out=zt_t[:], in0=zt_t[:], in1=zs_t, op=ALU.add),
                     w=[zt] + ZS, r=[zs, zt])
                r_ap, k_ap, vv_ap = zt_t[:, 0:512], zt_t[:, 512:1024], zt_t[:, 1024:1536]
                DT = self.cfg.get("dbg_tile", 0)
                if i == DT:
                    self.dbg("zp", zt_t[:], zt)
                pl = psum4()
                K.op(PE, lambda: nc.tensor.transpose(pl.ap[:, 0:128], zt_t[:, 1536:1664], self.ident_f.ap),
                     w=[pl], r=[zt, self.ident_f])
                K.op(PE, lambda: nc.tensor.transpose(pl.ap[:, 128:256], zt_t[:, 1664:1792], self.ident_f.ap),
                     w=[pl], r=[zt, self.ident_f])
                K.op(ACT, lambda: nc.scalar.activation(out=lwT_t[0:64, :], in_=pl.ap[0:64, 0:128], func=AF.Tanh),
                     w=[lwT], r=[pl])
                K.op(ACT, lambda: nc.scalar.copy(out=laT_t[64:128, :], in_=pl.ap[64:128, 0:128]), w=[laT], r=[pl])
                K.op(ACT, lambda: nc.scalar.activation(out=lgT_t[:], in_=pl.ap[:, 128:256], func=AF.Sigmoid),
                     w=[lgT], r=[pl])
                pw = psum4()
                K.op(PE, lambda: nc.tensor.matmul(pw.ap, lwT_t[0:65, :], dup_t[0:65, :], start=True, stop=True),
                     w=[pw], r=[lwT, dup])
                K.op(ACT, lambda: nc.scalar.activation(out=sg_t[:], in_=pw.ap, func=AF.Sigmoid), w=[sg], r=[pw])
                pa = psum4()
                K.op(PE, lambda: nc.tensor.matmul(pa.ap, self.ones_f.ap[0:1, :], w0a0_t[0:1, 0:512], start=True,
                                                  stop=False), w=[pa], r=[self.ones_f, w0a0])
                K.op(PE, lambda: nc.tensor.matmul(pa.ap, laT_t[64:128, :], iup_t[64:128, :], start=False, stop=True),
                     w=[pa], r=[laT, iup])
                K.op(ACT, lambda: nc.scalar.activation(out=as_t[:], in_=pa.ap, func=AF.Sigmoid), w=[asg], r=[pa])
                pg = psum4()
                K.op(PE, lambda: nc.tensor.matmul(pg.ap, lgT_t[:], gup_t[:], start=True, stop=True),
                     w=[pg], r=[lgT, gup])
                K.op(ACT, lambda: nc.scalar.copy(out=gg_t[:], in_=pg.ap), w=[gg], r=[pg])
                pcs = psum4()
                K.op(PE, lambda: nc.tensor.matmul(pcs.ap, Ui_t[:], sg_t[:], start=True, stop=True), w=[pcs], r=[Ui, sg])
                pcx = psum4()
                K.op(PE, lambda: nc.tensor.matmul(pcx.ap, Us_t[:], sg_t[:], start=True, stop=True), w=[pcx], r=[Us, sg])
                prq = psum4()
                K.op(PE, lambda: nc.tensor.matmul(prq.ap, Ls_t[:], sg_t[:], start=True, stop=True), w=[prq], r=[Ls, sg])
                for h in range(8):
                    K.op(PE, lambda h=h: nc.tensor.matmul(pCH.ap[0:64, h:h + 1], sg_t[:, h * 64:(h + 1) * 64],
                                                          self.ones_f.ap[:, 0:1], start=True, stop=True),
                         w=[pCH], r=[sg, self.ones_f])
                K.op(ACT, lambda: nc.scalar.activation(out=pC_t[:], in_=pCH.ap[0:64, 0:8], func=AF.Exp, scale=-C0),
                     w=[pCs], r=[pCH])
                K.op(DVE, lambda: nc.vector.tensor_tensor(out=tA_t, in0=k_ap, in1=kkb_t[:], op=ALU.mult),
                     w=[tA], r=[zt, kkb])
                K.op(POOL, lambda: nc.gpsimd.tensor_tensor(out=tB_t, in0=tA_t, in1=tA_t, op=ALU.mult),
                     w=[tB], r=[tA])
                K.op(DVE, lambda: nc.vector.tensor_reduce(out=st_t[:, 0:8], in_=v3(tB_t), axis=AX.X, op=ALU.add),
                     w=[st], r=[tB])
                K.op(ACT, lambda: nc.scalar.activation(out=st_t[:, 8:16], in_=st_t[:, 0:8], func=AF.Sqrt),
                     w=[st], r=[st])
                K.op(DVE, lambda: nc.vector.tensor_scalar(out=st_t[:, 8:16], in0=st_t[:, 8:16], scalar1=1e-12,
                                                          scalar2=None, op0=ALU.max), w=[st], r=[st])
                K.op(DVE, lambda: nc.vector.reciprocal(out=st_t[:, 16:24], in_=st_t[:, 8:16]), w=[st], r=[st])
                K.op(DVE, lambda: nc.vector.tensor_tensor(out=v3(kk_t[:]), in0=v3(tA_t), in1=bc3(st_t[:, 16:24]),
                                                          op=ALU.mult), w=[kk], r=[tA, st])
                K.op(DVE, lambda: nc.vector.scalar_tensor_tensor(out=km_t[:], in0=as_t[:], scalar=-1.0, in1=kab_t[:],
                                                                 op0=ALU.add, op1=ALU.mult), w=[km], r=[asg, kab])
                K.op(DVE, lambda: nc.vector.scalar_tensor_tensor(out=km_t[:], in0=km_t[:], scalar=1.0, in1=k_ap,
                                                                 op0=ALU.add, op1=ALU.mult), w=[km], r=[km, zt])
                K.op(POOL, lambda: nc.gpsimd.tensor_tensor(out=bq_t[:], in0=kk_t[:], in1=as_t[:], op=ALU.mult),
                     w=[bq], r=[kk, asg])
                K.op(POOL, lambda: nc.gpsimd.tensor_tensor(out=tB_t, in0=r_ap, in1=km_t[:], op=ALU.mult),
                     w=[tB], r=[zt, km])
                K.op(POOL, lambda: nc.gpsimd.tensor_tensor(out=tB_t, in0=tB_t, in1=rkb_t[:], op=ALU.mult),
                     w=[tB], r=[tB, rkb])
                K.op(DVE, lambda: nc.vector.tensor_reduce(out=st_t[:, 24:32], in_=v3(tB_t), axis=AX.X, op=ALU.add),
                     w=[st], r=[tB])
                K.op(ACT, lambda: nc.scalar.activation(out=Ea_t, in_=pcs.ap, func=AF.Exp, scale=-C0), w=[Ea], r=[pcs])
                K.op(DVE, lambda: nc.vector.tensor_tensor(out=rt_t[:], in0=r_ap, in1=Ea_t, op=ALU.mult),
                     w=[rt_b], r=[zt, Ea])
                K.op(ACT, lambda: nc.scalar.activation(out=Eb_t, in_=pcx.ap, func=AF.Exp, scale=-C0), w=[Eb], r=[pcx])
                K.op(DVE, lambda: nc.vector.scalar_tensor_tensor(out=at_t[:], in0=kk_t[:], scalar=-1.0, in1=Eb_t,
                                                                 op0=ALU.mult, op1=ALU.mult), w=[at_b], r=[kk, Eb])
                K.op(ACT, lambda: nc.scalar.activation(out=Ea_t, in_=pcs.ap, func=AF.Exp, scale=C0), w=[Ea], r=[pcs])
                K.op(POOL, lambda: nc.gpsimd.tensor_tensor(out=bt_t[:], in0=bq_t[:], in1=Ea_t, op=ALU.mult),
                     w=[bt_b], r=[bq, Ea])
                K.op(DVE, lambda: nc.vector.tensor_tensor(out=kt_t[:], in0=km_t[:], in1=Ea_t, op=ALU.mult),
                     w=[kt_b], r=[km, Ea])
                K.op(ACT, lambda: nc.scalar.activation(out=Eb_t, in_=prq.ap, func=AF.Exp, scale=-C0), w=[Eb], r=[prq])
                K.op(POOL, lambda: nc.gpsimd.tensor_tensor(out=bh_t[:], in0=bq_t[:], in1=Eb_t, op=ALU.mult),
                     w=[bh_b], r=[bq, Eb])
                K.op(DVE, lambda: nc.vector.tensor_tensor(out=kh_t[:], in0=km_t[:], in1=Eb_t, op=ALU.mult),
                     w=[kh_b], r=[km, Eb])
                K.op(POOL, lambda: nc.gpsimd.tensor_copy(v_t[:], vv_ap), w=[v_b], r=[zt])
                if i == DT:
                    for nm_, t_, b_ in (("sg", sg_t, sg), ("asig", as_t, asg), ("gg", gg_t, gg), ("kk", kk_t, kk),
                                        ("km", km_t, km), ("at", at_t, at_b), ("rt", rt_t, rt_b), ("bt", bt_t, bt_b),
                                        ("kt", kt_t, kt_b), ("bh", bh_t, bh_b), ("kh", kh_t, kh_b), ("vb", v_t, v_b),
                                        ("pC", pC_t, pCs), ("st", st_t, st)):
                        self.dbg(nm_, t_[:], b_)
                for pr in range(4):
                    ptb = psum4()
                    pt16 = ptb.ap.bitcast(BF16)
                    for kind, (xt_, xb_) in enumerate(((at_t, at_b), (rt_t, rt_b), (bt_t, bt_b), (kt_t, kt_b))):
                        K.op(PE, lambda kind=kind, xt_=xt_, pr=pr, pt16=pt16: nc.tensor.transpose(
                            pt16[:, kind * 128:(kind + 1) * 128], xt_[:, pr * 128:(pr + 1) * 128], self.ident_b.ap),
                            w=[ptb], r=[xb_, self.ident_b])
                    K.op(ACT, lambda pr=pr, pt16=pt16: nc.scalar.copy(
                        out=GT_t[pr][:].rearrange("p k t -> p (k t)"), in_=pt16[:, 0:512]), w=[GT[pr]], r=[ptb])
                if i == DT:
                    self.dbg("GT0", GT_t[0][:], GT[0])
                def head_gen(h, i=i, cur=cur):
                    pr, hb = h // 2, (h % 2) * 64
                    hs = HS[h % NHS]
                    hc = slice(h * 64, (h + 1) * 64)
                    gt = GT_t[pr]
                    gtb = GT[pr]
                    ar = gt[hb:hb + 64, 0:2, :].rearrange("p k t -> p (k t)")
                    M1_t, M1 = hs["M1"]
                    M2_t, M2 = hs["M2"]
                    XT0_t, XT0 = hs["XT0"]
                    p12 = psum4()
                    K.op(PE, lambda: nc.tensor.matmul(p12.ap[:, 0:256], gt[hb:hb + 64, 2, :], ar, start=True, stop=True),
                         w=[p12], r=[gtb])
                    K.op(PE, lambda: nc.tensor.matmul(p12.ap[:, 256:512], gt[hb:hb + 64, 3, :], ar, start=True, stop=True),
                         w=[p12], r=[gtb])
                    yield
                    K.op(DVE, lambda: nc.vector.tensor_tensor(out=M1_t[:], in0=p12.ap[:, 0:256], in1=mk2_t[:],
                                                              op=ALU.mult), w=[M1], r=[p12, mk2])
                    K.op(DVE, lambda: nc.vector.tensor_tensor(out=M2_t[:], in0=p12.ap[:, 256:512], in1=mk2_t[:],
                                                              op=ALU.mult), w=[M2], r=[p12, mk2])
                    p3 = psum4()
                    K.op(PE, lambda: nc.tensor.matmul(p3.ap[:, 0:128], gt[hb:hb + 64, 0, :], gt[hb:hb + 64, 2, :],
                                                      start=True, stop=True), w=[p3], r=[gtb])
                    yield
                    K.op(DVE, lambda: nc.vector.tensor_tensor(out=XT0_t[:], in0=p3.ap[:, 0:128], in1=Ls_t[:],
                                                              op=ALU.mult), w=[XT0], r=[p3, Ls])
                    Pc_t, Pc = hs["Pa"]
                    Pn_t, Pn = hs["Pb"]
                    K.op(POOL, lambda: nc.gpsimd.tensor_tensor(out=Pc_t[:], in0=M1_t[:, 0:128], in1=self.ident_b.ap,
                                                               op=ALU.add), w=[Pc], r=[M1, self.ident_b])
                    X_ap, XT_ap, Xb, XTb = M1_t[:, 0:128], XT0_t[:], M1, XT0
                    XXc = hs["XXa"]
                    XXn = hs["XXb"]
                    for j in range(L):
                        yield
                        px = psum4()
                        if j < L - 1:
                            K.op(PE, lambda px=px, X_ap=X_ap, XT_ap=XT_ap: nc.tensor.matmul(
                                px.ap[:, 0:128], XT_ap, X_ap, start=True, stop=True), w=[px], r=[Xb, XTb])
                        K.op(PE, lambda px=px, X_ap=X_ap, XT_ap=XT_ap: nc.tensor.matmul(
                            px.ap[:, 128:256], X_ap, XT_ap, start=True, stop=True), w=[px], r=[Xb, XTb])
                        XX_t, XX = XXc
                        yield
                        if j < L - 1:
                            K.op(ACT, lambda px=px, XX_t=XX_t: nc.scalar.copy(out=XX_t[:], in_=px.ap[:, 0:256]),
                                 w=[XX], r=[px])
                        else:
                            K.op(ACT, lambda px=px, XX_t=XX_t: nc.scalar.copy(out=XX_t[:, 128:256], in_=px.ap[:, 128:256]),
                                 w=[XX], r=[px])
                        yield
                        pp = psum4()
                        K.op(PE, lambda pp=pp, XX_t=XX_t, Pc_t=Pc_t: nc.tensor.matmul(
                            pp.ap[:, 0:128], XX_t[:, 128:256], Pc_t[:], start=True, stop=True), w=[pp], r=[XX, Pc])
                        yield
                        K.op(DVE, lambda pp=pp, Pc_t=Pc_t, Pn_t=Pn_t: nc.vector.tensor_tensor(
                            out=Pn_t[:], in0=pp.ap[:, 0:128], in1=Pc_t[:], op=ALU.add), w=[Pn], r=[pp, Pc])
                        X_ap, XT_ap, Xb, XTb = XX_t[:, 0:128], XX_t[:, 128:256], XX, XX
                        XXc, XXn = XXn, XXc
                        Pc_t, Pc, Pn_t, Pn = Pn_t, Pn, Pc_t, Pc
                    W1_t, W1 = hs["W1"]
                    AU_t, AU = hs["AU"]
                    RbT_t, RbT = hs["RbT"]
                    yield
                    pw1 = psum4()
                    K.op(PE, lambda: nc.tensor.matmul(pw1.ap[:, 0:64], M2_t[:, 0:128], v_t[:, hc], start=True, stop=True),
                         w=[pw1], r=[M2, v_b])
                    yield
                    K.op(ACT, lambda: nc.scalar.copy(out=W1_t[:], in_=pw1.ap[:, 0:64]), w=[W1], r=[pw1])
                    yield
                    pau = psum4()
                    K.op(PE, lambda: nc.tensor.matmul(pau.ap[:, 0:64], Pc_t[:], at_t[:, hc], start=True, stop=True),
                         w=[pau], r=[Pc, at_b])
                    K.op(PE, lambda: nc.tensor.matmul(pau.ap[:, 64:128], Pc_t[:], W1_t[:], start=True, stop=True),
                         w=[pau], r=[Pc, W1])
                    yield
                    K.op(ACT, lambda: nc.scalar.copy(out=AU_t[:], in_=pau.ap[:, 0:128]), w=[AU], r=[pau])
                    yield
                    if i == DT and h == self.cfg.get("dbg_head", 0):
                        self.dbg("M1", M1_t[:], M1)
                        self.dbg("M2", M2_t[:], M2)
                        self.dbg("XT0", XT0_t[:], XT0)
                        self.dbg("P", Pc_t[:], Pc)
                        self.dbg("AU", AU_t[:], AU)
                    K.op(PE, lambda: nc.tensor.matmul(pG.ap[0:64, hc], AU_t[:, 0:64], bh_t[:, hc], start=True, stop=True),
                         w=[pG], r=[AU, bh_b])
                    K.op(PE, lambda: nc.tensor.matmul(pF.ap[0:64, hc], bh_t[:, hc], AU_t[:, 64:128], start=True,
                                                      stop=False), w=[pF], r=[AU, bh_b])
                    K.op(PE, lambda: nc.tensor.matmul(pF.ap[0:64, hc], kh_t[:, hc], v_t[:, hc], start=False, stop=True),
                         w=[pF], r=[kh_b, v_b])
                    prb = psum4()
                    K.op(PE, lambda: nc.tensor.matmul(prb.ap[0:64, 0:128], AU_t[:, 0:64], M1_t[:, 128:256], start=True,
                                                      stop=False), w=[prb], r=[AU, M1])
                    K.op(PE, lambda: nc.tensor.matmul(prb.ap[0:64, 0:128], rt_t[:, hc], self.ident_b.ap, start=False,
                                                      stop=True), w=[prb], r=[rt_b, self.ident_b])
                    yield
                    K.op(ACT, lambda: nc.scalar.copy(out=RbT_t[:], in_=prb.ap[0:64, 0:128]), w=[RbT], r=[prb])
                    yield
                    K.op(PE, lambda: nc.tensor.matmul(pY.ap[:, hc], M1_t[:, 128:256], AU_t[:, 64:128], start=True,
                                                      stop=False), w=[pY], r=[M1, AU])
                    K.op(PE, lambda: nc.tensor.matmul(pY.ap[:, hc], M2_t[:, 128:256], v_t[:, hc], start=False,
                                                      stop=False), w=[pY], r=[M2, v_b])
                    K.op(PE, lambda: nc.tensor.matmul(pY.ap[:, hc], RbT_t[0:64, :], Hb_t[cur][0:64, hc], start=False,
                                                      stop=True), w=[pY], r=[RbT, Hb[cur]])
                for g0 in range(0, 8, NHS):
                    gens = [head_gen(h) for h in range(g0, g0 + NHS)]
                    while gens:
                        for g_ in list(gens):
                            try:
                                next(g_)
                            except StopIteration:
                                gens.remove(g_)
                for h in range(8):
                    hc = slice(h * 64, (h + 1) * 64)
                    K.op(DVE, lambda h=h, hc=hc: nc.vector.scalar_tensor_tensor(
                        out=GTa_t[:, hc], in0=self.ident_f.ap[0:64, 0:64], scalar=pC_t[:, h:h + 1], in1=pG.ap[0:64, hc],
                        op0=ALU.mult, op1=ALU.add), w=[GTa], r=[self.ident_f, pCs, pG])
                K.op(ACT, lambda: nc.scalar.copy(out=Fa_t[:], in_=pF.ap[0:64, :]), w=[Fa], r=[pF])
                for h in range(8):
                    hc = slice(h * 64, (h + 1) * 64)
                    K.op(PE, lambda hc=hc: nc.tensor.matmul(pCH.ap[0:64, hc], GTa_t[:, hc], Hf_t[cur][:, hc], start=True,
                                                            stop=True), w=[pCH], r=[GTa, Hf[cur]])
                K.op(DVE, lambda: nc.vector.tensor_tensor(out=Hf_t[nxt][:], in0=pCH.ap[0:64, :], in1=Fa_t[:], op=ALU.add),
                     w=[Hf[nxt]], r=[pCH, Fa])
                K.op(ACT, lambda: nc.scalar.copy(out=Hb_t[nxt][:], in_=Hf_t[nxt][:]), w=[Hb[nxt]], r=[Hf[nxt]])
                if i == DT:
                    self.dbg("GTa", GTa_t[:], GTa)
                    self.dbg("Fa", Fa_t[:], Fa)
                    self.dbg("Hn", Hf_t[nxt][:], Hf[nxt])
                K.op(ACT, lambda: nc.scalar.activation(out=tB_t, in_=pY.ap, func=AF.Square), w=[tB], r=[pY])
                K.op(DVE, lambda: nc.vector.tensor_reduce(out=st_t[:, 32:40], in_=v3(pY.ap), axis=AX.X, op=ALU.add),
                     w=[st], r=[pY])
                K.op(DVE, lambda: nc.vector.tensor_reduce(out=st_t[:, 40:48], in_=v3(tB_t), axis=AX.X, op=ALU.add),
                     w=[st], r=[tB])
                K.op(DVE, lambda: nc.vector.tensor_scalar(out=st_t[:, 32:40], in0=st_t[:, 32:40], scalar1=1.0 / 64,
                                                          scalar2=None, op0=ALU.mult), w=[st], r=[st])
                K.op(DVE, lambda: nc.vector.tensor_tensor(out=st_t[:, 48:56], in0=st_t[:, 32:40], in1=st_t[:, 32:40],
                                                          op=ALU.mult), w=[st], r=[st])
                K.op(DVE, lambda: nc.vector.scalar_tensor_tensor(out=st_t[:, 40:48], in0=st_t[:, 40:48], scalar=1.0 / 64,
                                                                 in1=st_t[:, 48:56], op0=ALU.mult, op1=ALU.subtract),
                     w=[st], r=[st])
                K.op(ACT, lambda: nc.scalar.activation(out=st_t[:, 40:48], in_=st_t[:, 40:48], func=AF.Sqrt, scale=1.0,
                                                       bias=self.eps_cols[LNX_EPS]), w=[st], r=[st, self.epsb])
                K.op(DVE, lambda: nc.vector.reciprocal(out=st_t[:, 56:64], in_=st_t[:, 40:48]), w=[st], r=[st])
                K.op(DVE, lambda: nc.vector.tensor_tensor(out=v3(tA_t), in0=v3(pY.ap), in1=bc3(st_t[:, 32:40]),
                                                          op=ALU.subtract), w=[tA], r=[pY, st])
                K.op(DVE, lambda: nc.vector.tensor_tensor(out=v3(tA_t), in0=v3(tA_t), in1=bc3(st_t[:, 56:64]),
                                                          op=ALU.mult), w=[tA], r=[tA, st])
                K.op(POOL, lambda: nc.gpsimd.tensor_tensor(out=tA_t, in0=tA_t, in1=lgb_t[:], op=ALU.mult),
                     w=[tA], r=[tA, lgb])
                K.op(POOL, lambda: nc.gpsimd.tensor_tensor(out=tA_t, in0=tA_t, in1=lbb_t[:], op=ALU.add),
                     w=[tA], r=[tA, lbb])
                K.op(DVE, lambda: nc.vector.tensor_tensor(out=v3(tB_t), in0=v3(vv_ap), in1=bc3(st_t[:, 24:32]),
                                                          op=ALU.mult), w=[tB], r=[zt, st])
                K.op(POOL, lambda: nc.gpsimd.tensor_tensor(out=tA_t, in0=tA_t, in1=tB_t, op=ALU.add),
                     w=[tA], r=[tA, tB])
                K.op(DVE, lambda: nc.vector.tensor_tensor(out=yb_t[:], in0=tA_t, in1=gg_t[:], op=ALU.mult),
                     w=[yb_b], r=[tA, gg])
                if i == DT:
                    self.dbg("yb", yb_t[:], yb_b)
                pyt = psum4()
                py16 = pyt.ap.bitcast(BF16)
                for q in range(4):
                    K.op(PE, lambda q=q: nc.tensor.transpose(py16[:, q * 128:(q + 1) * 128], yb_t[:, q * 128:(q + 1) * 128],
                                                             self.ident_b.ap), w=[pyt], r=[yb_b, self.ident_b])
                ybt = ybT_t[0]
                K.op(ACT, lambda: nc.scalar.copy(out=ybt[:, :, ti * 128:(ti + 1) * 128],
                                                 in_=py16[:, 0:512].rearrange("p (q t) -> p q t", q=4)),
                     w=ybT[0], r=[pyt])
                if ti == 3:
                    mixb = [{tb: ybT[0][q]} for q in range(4)]
                    outproj(mixb, [4, 5, 6, 7], tbs=[tb])
            for ch in range(4, 8):
                self.wo.release(self.jobs_wo[("even", l, ch)])

    def mixer_odd(self, l):
        K, nc = self.K, self.nc
        o = l // 2
        PE, ACT, DVE, POOL, SP = K.PE, K.ACT, K.DVE, K.POOL, K.SP
        ps = self.ps

        def outproj(mixb, chs):
            wos = [self.wo.get(self.jobs_wo[("odd", l, ch)]) for ch in chs]
            for d in range(NC_):
                for tb in range(NTB):
                    po = self.psum()
                    for i, ch in enumerate(chs):
                        K.op(PE, lambda i=i: nc.tensor.matmul(po.ap, wos[i].ap[:, d * 128:(d + 1) * 128], mixb[i][tb].ap,
                                                              start=(i == 0), stop=(i == len(chs) - 1)),
                             w=[po], r=[wos[i], mixb[i][tb]])
                    xb = self.x[d][tb]
                    K.op(DVE, lambda: nc.vector.tensor_tensor(out=xb.ap, in0=po.ap, in1=xb.ap, op=ALU.add),
                         w=[xb], r=[po, xb])
            for ch in chs:
                self.wo.release(self.jobs_wo[("odd", l, ch)])

        with ExitStack() as es:
            small_t = K.sb(es, [128, ODD_NS], F32, "osmall")
            small = Buf(small_t[:], "osmall")
            K.dma(SP, small_t[:], self.d_odd_small[o], w=[small])
            qn_col = lambda c: small_t[:, c:c + 1]
            kvn_col = lambda c: small_t[:, 2:3]
            cw = lambda q, j: small_t[:, 3 + q * 31 + j: 3 + q * 31 + j + 1]
            cb = lambda q: small_t[:, 127 + q:128 + q]
            lng = lambda q: small_t[:, 131 + q:132 + q]
            lnb = lambda q: small_t[:, 135 + q:136 + q]
            cqn_t = K.sb(es, [128, 2, S], BF16, "cqn")
            cqn = [[Buf(cqn_t[:, c, tb * 512:(tb + 1) * 512]) for tb in range(NTB)] for c in range(2)]
            ckvn_t = K.sb(es, [128, S], BF16, "ckvn")
            ckvn = [[Buf(ckvn_t[:, tb * 512:(tb + 1) * 512]) for tb in range(NTB)]]
            kr_t = K.sb(es, [128, S], BF16, "krope")
            kr = [Buf(kr_t[:, tb * 512:(tb + 1) * 512]) for tb in range(NTB)]
            rope_t = K.sb(es, [128, 2, S], BF16, "rope")
            rope = Buf(rope_t[:], "rope")
            K.dma(POOL, rope_t[:], self.d_rope[:, :, :], w=[rope])
            wq_t = K.sb(es, [128, 2, 768], BF16, "wq")
            wq = Buf(wq_t[:], "wq")
            wqr_t = K.sb(es, [128, 2, 768], BF16, "wqr")
            wqr = Buf(wqr_t[:], "wqr")
            wkv_t = K.sb(es, [128, 1024], BF16, "wkv")
            wkv = Buf(wkv_t[:], "wkv")
            K.dma(POOL, wq_t[:], self.d_wq[o], w=[wq])
            K.dma(POOL, wqr_t[:], self.d_wqr[o], w=[wqr])
            K.dma(POOL, wkv_t[:], self.d_wkv[o], w=[wkv])

            with ExitStack() as esx:
                hglu_t = K.sb(esx, [128, 4, 30 + S], BF16, "hglu")
                hglu = [Buf(hglu_t[:, q, :]) for q in range(4)]
                for q in range(4):
                    K.op(DVE, lambda q=q: nc.vector.memset(hglu_t[:, q, 0:30], 0.0), w=[hglu[q]])
                with ExitStack() as esa:
                    hT_t = K.sb(esa, [128, NC_, S], BF16, "hT")
                    hT = [[Buf(hT_t[:, c, tb * 512:(tb + 1) * 512]) for tb in range(NTB)] for c in range(NC_)]
                    self.rmsnorm_fm(esa, self.x, self.gain_col(DEPTH + l), hT, NC_, D, RMS_EPS)
                    cq32_t = K.sb(esa, [128, 2, 512], F32, "cq32")
                    cq32 = [[Buf(cq32_t[:, c, :])] for c in range(2)]
                    ckv32_t = K.sb(esa, [128, 512], F32, "ckv32")
                    ckv32 = [[Buf(ckv32_t[:])]]
                    tmpa = Buf(K.sb(esa, [128, 512], F32, "tmpa")[:])
                    tmpb = Buf(K.sb(esa, [128, 512], F32, "tmpb")[:])

                    def proj(wb, lo, M, tb, pb):
                        for c in range(NC_):
                            K.op(PE, lambda c=c: nc.tensor.matmul(pb.ap[0:M, :], wb.ap[:, c, lo:lo + M], hT[c][tb].ap,
                                                                  start=(c == 0), stop=(c == NC_ - 1)),
                                 w=[pb], r=[wb, hT[c][tb]])

                    j0 = self.jobs_wi[("odd", l, 0)]
                    w0 = self.wi.get(j0)
                    for tb in range(NTB):
                        for c in range(2):
                            pb = self.psum()
                            proj(w0, c * 128, 128, tb, pb)
                            K.op(ACT, lambda c=c, pb=pb: nc.scalar.copy(out=cq32[c][0].ap, in_=pb.ap),
                                 w=[cq32[c][0]], r=[pb])
                        self.rmsnorm_fm(esa, [[cq32[0][0]], [cq32[1][0]]], qn_col, [[cqn[0][tb]], [cqn[1][tb]]], 2, 256,
                                        RMS_EPS, tbs=[0], extra_r=[small])
                    self.wi.release(j0)
                    j1 = self.jobs_wi[("odd", l, 1)]
                    w1 = self.wi.get(j1)
                    j2 = self.jobs_wi[("odd", l, 2)]
                    w2 = self.wi.get(j2)
                    for tb in range(NTB):
                        pb = self.psum()
                        proj(w1, 0, 128, tb, pb)
                        K.op(ACT, lambda pb=pb: nc.scalar.copy(out=ckv32[0][0].ap, in_=pb.ap), w=[ckv32[0][0]], r=[pb])
                        self.rmsnorm_fm(esa, [[ckv32[0][0]]], kvn_col, [[ckvn[0][tb]]], 1, 128, RMS_EPS, tbs=[0],
                                        extra_r=[small])
                        p1 = self.psum()
                        proj(w1, 128, 96, tb, p1)
                        p2 = self.psum()
                        proj(w2, 0, 96, tb, p2)
                        sl = slice(tb * 512, (tb + 1) * 512)
                        K.op(DVE, lambda p1=p1, sl=sl: nc.vector.tensor_tensor(out=tmpa.ap[64:96, :], in0=p1.ap[64:96, :],
                                                                               in1=rope_t[64:96, 0, sl], op=ALU.mult),
                             w=[tmpa], r=[p1, rope])
                        K.op(DVE, lambda p2=p2, sl=sl: nc.vector.tensor_tensor(out=tmpb.ap[64:96, :], in0=p2.ap[64:96, :],
                                                                               in1=rope_t[64:96, 1, sl], op=ALU.mult),
                             w=[tmpb], r=[p2, rope])
                        K.op(DVE, lambda tb=tb: nc.vector.tensor_tensor(out=kr[tb].ap[64:96, :], in0=tmpa.ap[64:96, :],
                                                                        in1=tmpb.ap[64:96, :], op=ALU.add),
                             w=[kr[tb]], r=[tmpa, tmpb])
                    self.wi.release(j1)
                    self.wi.release(j2)
                    for q in range(4):
                        jq = self.jobs_wi[("odd", l, 3 + q)]
                        wq_ = self.wi.get(jq)
                        for tb in range(NTB):
                            pa = self.psum()
                            proj(wq_, 0, 128, tb, pa)
                            pbb = self.psum()
                            proj(wq_, 128, 128, tb, pbb)
                            K.op(ACT, lambda pbb=pbb: nc.scalar.activation(out=tmpa.ap, in_=pbb.ap, func=AF.Sigmoid),
                                 w=[tmpa], r=[pbb])
                            K.op(DVE, lambda pa=pa, q=q, tb=tb: nc.vector.tensor_tensor(
                                out=hglu_t[:, q, 30 + tb * 512: 30 + (tb + 1) * 512], in0=pa.ap, in1=tmpa.ap,
                                op=ALU.mult), w=[hglu[q]], r=[pa, tmpa])
                        self.wi.release(jq)
                    K.barrier()

                with ExitStack() as esb:
                    yd_t = K.sb(esb, [128, 4, S], BF16, "ydT")
                    yd = [[Buf(yd_t[:, q, tb * 512:(tb + 1) * 512]) for tb in range(NTB)] for q in range(4)]
                    diag_t = [K.sb(esb, [128, 31, 128], BF16, "diag%d" % i) for i in range(2)]
                    diag = [Buf(t[:]) for t in diag_t]
                    c32_t = K.sb(esb, [128, 4, 512], F32, "c32")
                    c32 = [Buf(c32_t[:, q, :]) for q in range(4)]
                    sq32 = [Buf(K.sb(esb, [128, 512], F32, "sq32_%d" % i)[:]) for i in range(2)]
                    mean = Buf(K.sb(esb, [128, 512], F32, "mean")[:])
                    rstd = Buf(K.sb(esb, [128, 512], F32, "rstd")[:])
                    msq = Buf(K.sb(esb, [128, 512], F32, "msq")[:])
                    nd = 0
                    for tb in range(NTB):
                        p_s1 = ps[4]
                        p_s2 = ps[5]
                        for q in range(4):
                            dg, dg_t = diag[nd % 2], diag_t[nd % 2]
                            nd += 1
                            for j in range(31):
                                K.op(DVE, lambda j=j, q=q, dg_t=dg_t: nc.vector.tensor_scalar(
                                    out=dg_t[:, j, :], in0=self.ident_f.ap, scalar1=cw(q, j), scalar2=None,
                                    op0=ALU.mult), w=[dg], r=[self.ident_f, small])
                            pc = ps[q]
                            for j in range(31):
                                K.op(PE, lambda q=q, j=j, pc=pc, dg_t=dg_t: nc.tensor.matmul(
                                    pc.ap, dg_t[:, j, :], hglu_t[:, q, tb * 512 + j: tb * 512 + j + 512],
                                    start=(j == 0), stop=(j == 30)), w=[pc], r=[dg, hglu[q]])
                            K.op(ACT, lambda q=q, pc=pc: nc.scalar.activation(out=c32[q].ap, in_=pc.ap, func=AF.Identity,
                                                                              bias=cb(q), scale=1.0),
                                 w=[c32[q]], r=[pc, small])
                            sqb = sq32[q % 2]
                            K.op(ACT, lambda q=q, sqb=sqb: nc.scalar.activation(out=sqb.ap, in_=c32[q].ap, func=AF.Square),
                                 w=[sqb], r=[c32[q]])
                            K.op(PE, lambda q=q: nc.tensor.matmul(p_s1.ap, self.ones_f.ap, c32[q].ap, start=(q == 0),
                                                                  stop=(q == 3)), w=[p_s1], r=[self.ones_f, c32[q]])
                            K.op(PE, lambda q=q, sqb=sqb: nc.tensor.matmul(p_s2.ap, self.ones_f.ap, sqb.ap, start=(q == 0),
                                                                           stop=(q == 3)), w=[p_s2], r=[self.ones_f, sqb])
                        K.op(ACT, lambda: nc.scalar.mul(out=mean.ap, in_=p_s1.ap, mul=1.0 / 512), w=[mean], r=[p_s1])
                        K.op(DVE, lambda: nc.vector.tensor_tensor(out=msq.ap, in0=mean.ap, in1=mean.ap, op=ALU.mult),
                             w=[msq], r=[mean])
                        K.op(DVE, lambda: nc.vector.scalar_tensor_tensor(out=rstd.ap, in0=p_s2.ap, scalar=1.0 / 512,
                                                                         in1=msq.ap, op0=ALU.mult, op1=ALU.subtract),
                             w=[rstd], r=[p_s2, msq])
                        K.op(ACT, lambda: nc.scalar.activation(out=rstd.ap, in_=rstd.ap, func=AF.Sqrt, scale=1.0,
                                                               bias=self.eps_cols[LN_EPS]), w=[rstd], r=[rstd, self.epsb])
                        K.op(DVE, lambda: nc.vector.reciprocal(out=rstd.ap, in_=rstd.ap), w=[rstd], r=[rstd])
                        for q in range(4):
                            K.op(DVE, lambda q=q: nc.vector.tensor_tensor(out=c32[q].ap, in0=c32[q].ap, in1=mean.ap,
                                                                          op=ALU.subtract), w=[c32[q]], r=[c32[q], mean])
                            K.op(DVE, lambda q=q: nc.vector.tensor_tensor(out=c32[q].ap, in0=c32[q].ap, in1=rstd.ap,
                                                                          op=ALU.mult), w=[c32[q]], r=[c32[q], rstd])
                            K.op(ACT, lambda q=q, tb=tb: nc.scalar.activation(out=yd[q][tb].ap, in_=c32[q].ap,
                                                                              func=AF.Silu, bias=lnb(q), scale=lng(q)),
                                 w=[yd[q][tb]], r=[c32[q], small])
                    outproj(yd, [4, 5, 6, 7])
                    K.barrier()

            with ExitStack() as esc:
                yc_t = K.sb(esc, [128, 4, S], BF16, "ycT")
                yc = [[Buf(yc_t[:, c, tb * 512:(tb + 1) * 512]) for tb in range(NTB)] for c in range(4)]
                vall_t = K.sb(esc, [128, 16, 512], BF16, "vall")
                vall = [Buf(vall_t[:, i, :]) for i in range(16)]
                for i in range(16):
                    pv = self.psum()
                    tbi = i // 4
                    K.op(PE, lambda i=i, pv=pv: nc.tensor.matmul(
                        pv.ap, ckvn_t[:, i * 128:(i + 1) * 128], wkv_t[:, 512:1024],
                        start=True, stop=True), w=[pv], r=[ckvn[0][tbi], wkv])
                    K.op(ACT, lambda i=i, pv=pv: nc.scalar.copy(out=vall[i].ap, in_=pv.ap), w=[vall[i]], r=[pv])
                qT_t = [K.sb(esc, [128, S], BF16, "qT%d" % i) for i in range(2)]
                kT_t = [K.sb(esc, [128, S], BF16, "kT%d" % i) for i in range(2)]
                qT = [Buf(t[:]) for t in qT_t]
                kT = [Buf(t[:]) for t in kT_t]
                oc_t = [K.sb(esc, [128, S], BF16, "oc%d" % i) for i in range(2)]
                oc = [Buf(t[:]) for t in oc_t]
                pT_t = [K.sb(esc, [128, 512], BF16, "pT%d" % i) for i in range(3)]
                pT = [Buf(t[:]) for t in pT_t]
                rden = Buf(K.sb(esc, [128, 512], F32, "rden")[:])
                tq1 = Buf(K.sb(esc, [128, 512], F32, "tq1")[:])
                tq2 = Buf(K.sb(esc, [128, 512], F32, "tq2")[:])
                npt = 0
                for h in range(8):
                    qh, kh, och = qT[h % 2], kT[h % 2], oc[h % 2]
                    qh_t, kh_t, och_t = qT_t[h % 2], kT_t[h % 2], oc_t[h % 2]
                    for tb in range(NTB):
                        sl = slice(tb * 512, (tb + 1) * 512)
                        pq = ps[0 + (tb % 2)]
                        pr = ps[2 + (tb % 2)]
                        pk = ps[4 + (tb % 2)]
                        for c in range(2):
                            K.op(PE, lambda c=c, pq=pq: nc.tensor.matmul(pq.ap[0:96, :], wq_t[:, c, h * 96:(h + 1) * 96],
                                                                         cqn[c][tb].ap, start=(c == 0), stop=(c == 1)),
                                 w=[pq], r=[wq, cqn[c][tb]])
                        for c in range(2):
                            K.op(PE, lambda c=c, pr=pr: nc.tensor.matmul(pr.ap[0:96, :], wqr_t[:, c, h * 96:(h + 1) * 96],
                                                                         cqn[c][tb].ap, start=(c == 0), stop=(c == 1)),
                                 w=[pr], r=[wqr, cqn[c][tb]])
                        K.op(PE, lambda pk=pk: nc.tensor.matmul(pk.ap[0:64, :], wkv_t[:, h * 64:h * 64 + 64],
                                                                ckvn[0][tb].ap, start=True, stop=True),
                             w=[pk], r=[wkv, ckvn[0][tb]])
                        K.op(ACT, lambda pq=pq, sl=sl: nc.scalar.copy(out=qh_t[0:64, sl], in_=pq.ap[0:64, :]),
                             w=[qh], r=[pq])
                        K.op(DVE, lambda pq=pq, sl=sl: nc.vector.tensor_tensor(out=tq1.ap[64:96, :], in0=pq.ap[64:96, :],
                                                                               in1=rope_t[64:96, 0, sl], op=ALU.mult),
                             w=[tq1], r=[pq, rope])
                        K.op(DVE, lambda pr=pr, sl=sl: nc.vector.tensor_tensor(out=tq2.ap[64:96, :], in0=pr.ap[64:96, :],
                                                                               in1=rope_t[64:96, 1, sl], op=ALU.mult),
                             w=[tq2], r=[pr, rope])
                        K.op(DVE, lambda sl=sl: nc.vector.tensor_tensor(out=qh_t[64:96, sl], in0=tq1.ap[64:96, :],
                                                                        in1=tq2.ap[64:96, :], op=ALU.add),
                             w=[qh], r=[tq1, tq2])
                        K.op(ACT, lambda pk=pk, sl=sl: nc.scalar.copy(out=kh_t[0:64, sl], in_=pk.ap[0:64, :]),
                             w=[kh], r=[pk])
                        K.op(DVE, lambda sl=sl, tb=tb: nc.vector.tensor_copy(kh_t[64:96, sl], kr_t[64:96, sl]),
                             w=[kh], r=[kr[tb]])
                    for qb in range(NTB):
                        pO = ps[6]
                        pD = ps[7]
                        nkt = 4 * qb + 4
                        def emit_S(kt, qb=qb, nkt=nkt):
                            nonlocal npt
                            m = kt - 4 * qb
                            q0 = max(m, 0) * 128
                            pS = ps[kt % 2]
                            pt = pT[npt % 3]
                            pt_t = pT_t[npt % 3]
                            npt += 1
                            K.op(PE, lambda: nc.tensor.matmul(
                                pS.ap[:, q0:512], kh_t[0:96, kt * 128:(kt + 1) * 128],
                                qh_t[0:96, qb * 512 + q0:(qb + 1) * 512], start=True, stop=True),
                                w=[pS], r=[kh, qh])
                            K.op(ACT, lambda: nc.scalar.activation(
                                out=pt_t[:, q0:512], in_=pS.ap[:, q0:512], func=AF.Exp, scale=ATTN_SCALE),
                                w=[pt], r=[pS])
                            if m >= 0:
                                K.op(POOL, lambda: nc.gpsimd.memset(pt_t[64:128, q0:q0 + 64], 0.0),
                                     w=[pt], r=[pt])
                            return (pt, pt_t, q0)

                        def emit_PV(kt, st_, nkt=nkt):
                            pt, pt_t, q0 = st_
                            K.op(PE, lambda: nc.tensor.matmul(
                                pO.ap[0:64, q0:512], vall_t[:, kt, h * 64:(h + 1) * 64], pt_t[:, q0:512],
                                start=(kt == 0), stop=(kt == nkt - 1)), w=[pO], r=[vall[kt], pt])
                            K.op(PE, lambda: nc.tensor.matmul(
                                pD.ap[0:64, q0:512], self.ones_bf.ap[:, 0:64], pt_t[:, q0:512],
                                start=(kt == 0), stop=(kt == nkt - 1)), w=[pD], r=[self.ones_bf, pt])

                        nxt_st = emit_S(0)
                        for kt in range(nkt):
                            cur_st = nxt_st
                            if kt + 1 < nkt:
                                nxt_st = emit_S(kt + 1)
                            emit_PV(kt, cur_st)
                        K.op(DVE, lambda: nc.vector.reciprocal(out=rden.ap[0:64, :], in_=pD.ap[0:64, :]),
                             w=[rden], r=[pD])
                        K.op(DVE, lambda qb=qb: nc.vector.tensor_tensor(out=och_t[0:64, qb * 512:(qb + 1) * 512],
                                                                        in0=pO.ap[0:64, :], in1=rden.ap[0:64, :],
                                                                        op=ALU.mult), w=[och], r=[pO, rden])
                    pb0 = (h % 2) * 64
                    K.dma(SP, yc_t[pb0:pb0 + 64, h // 2, :], och_t[0:64, :], w=[yc[h // 2][tb] for tb in range(NTB)],
                          r=[och])
                outproj(yc, [0, 1, 2, 3])
                K.barrier()

    def final(self, do_norm):
        K, nc = self.K, self.nc
        with ExitStack() as es:
            if do_norm:
                o_t = K.sb(es, [128, NC_, S], F32, "oT")
                o = [[Buf(o_t[:, c, tb * 512:(tb + 1) * 512]) for tb in range(NTB)] for c in range(NC_)]
                self.rmsnorm_fm(es, self.x, self.gain_col(12), o, NC_, D, RMS_EPS)
            else:
                o = self.x
            toks = []
            for c in range(NC_):
                for tb in range(NTB):
                    toks.append(K.dma(K.SP, self.d_out[:, c, tb * 512:(tb + 1) * 512], o[c][tb].ap, r=[o[c][tb]]))
            for t in toks + list(self.dbg_toks.values()):
                K.wait_tok(K.SP, t)
            K.barrier()


def build_program(cfg):
    p = Prog(cfg)
    return p.build()


def prep_shared(inp):
    f32 = np.float32
    sh = {}
    gains = np.concatenate([inp["norm_ffn1"], inp["norm_mix"], inp["norm_ffn2"], inp["final_norm"][None]], axis=0)
    sh["gains"] = np.ascontiguousarray(gains.reshape(13, NC_, 128).transpose(2, 0, 1).reshape(128, 13 * NC_)).astype(f32)
    fin = np.stack([inp["ffn1_in"], inp["ffn2_in"]], axis=1).reshape(2 * DEPTH, D, 2 * DFF)
    fin = fin.reshape(2 * DEPTH, NC_, 128, 2, NHC, 128).transpose(0, 4, 2, 1, 3, 5)
    sh["ffn_in"] = np.ascontiguousarray(fin).reshape(2 * DEPTH, NHC, 128, NC_, 256)
    fout = np.stack([inp["ffn1_out"], inp["ffn2_out"]], axis=1).reshape(2 * DEPTH, NHC, 128, D)
    sh["ffn_out"] = np.ascontiguousarray(fout)
    wi = inp["odd_w_in"]
    Z = lambda n: np.zeros((2, D, n), f32)
    cq, ckv, krc = wi[:, :, 0:256], wi[:, :, 256:384], wi[:, :, 384:416]
    za, zb = wi[:, :, 416:928], wi[:, :, 928:1440]
    kr_rot = np.concatenate([krc[:, :, 16:32], krc[:, :, 0:16]], axis=2)
    slabs = [cq,
             np.concatenate([ckv, Z(64), krc, Z(32)], axis=2),
             np.concatenate([Z(64), kr_rot, Z(160)], axis=2)]
    for q in range(4):
        slabs.append(np.concatenate([za[:, :, q * 128:(q + 1) * 128], zb[:, :, q * 128:(q + 1) * 128]], axis=2))
    oin = np.stack(slabs, axis=1)
    oin = oin.reshape(2, 7, NC_, 128, 256).transpose(0, 1, 3, 2, 4)
    sh["odd_in"] = np.ascontiguousarray(oin).astype(f32)
    sh["odd_out"] = np.ascontiguousarray(inp["odd_w_out"].reshape(2, 8, 128, D)).astype(f32)
    wq = inp["wq_up"]
    sh["wq"] = np.ascontiguousarray(wq.reshape(2, 2, 128, 768).transpose(0, 2, 1, 3)).astype(f32)
    wq4 = wq.reshape(2, 256, 8, 96)
    wqr = np.concatenate([np.zeros((2, 256, 8, 64), f32), wq4[..., 80:96], wq4[..., 64:80]], axis=-1).reshape(2, 256, 768)
    sh["wqr"] = np.ascontiguousarray(wqr.reshape(2, 2, 128, 768).transpose(0, 2, 1, 3)).astype(f32)
    wkv4 = inp["wkv_up"].reshape(2, 128, 8, 128)
    sh["wkv"] = np.ascontiguousarray(np.concatenate([wkv4[..., 0:64].reshape(2, 128, 512),
                                                     wkv4[..., 64:128].reshape(2, 128, 512)], axis=-1)).astype(f32)
    col = lambda v, n: v.reshape(2, n, 128).transpose(0, 2, 1)
    cwp = inp["conv_w"].reshape(2, 31, 4, 128).transpose(0, 3, 2, 1).reshape(2, 128, 124)
    sh["odd_small"] = np.ascontiguousarray(np.concatenate(
        [col(inp["q_norm"], 2), col(inp["kv_norm"], 1), cwp, col(inp["conv_b"], 4), col(inp["conv_ln_g"], 4),
         col(inp["conv_ln_b"], 4)], axis=2)).astype(f32)
    sh["rope"] = rope_table()
    ew = inp["even_w_in"]
    ein = ew.reshape(2, NC_, 128, 11, 256).transpose(0, 3, 2, 1, 4)
    sh["even_in"] = np.ascontiguousarray(ein).astype(f32)
    sh["even_out"] = np.ascontiguousarray(inp["even_w_out"].reshape(2, 8, 128, D)).astype(f32)
    sh["gsu_wsT"] = np.ascontiguousarray(inp["gsu_ws"].transpose(0, 3, 1, 2)).astype(f32)
    sh["gsu_bs4"] = np.ascontiguousarray(np.tile(inp["gsu_bs"], (1, 1, 4)).reshape(2, 1, 4, 512)).astype(f32)
    rows = np.concatenate([inp["shift_mu"], inp["k_k"], inp["k_a"], inp["r_k"].reshape(2, 512), inp["lnx_g"],
                           inp["lnx_b"], inp["gsu_ln_g"], inp["gsu_ln_b"], inp["decay_w0"], inp["iclr_a0"]], axis=1)
    sh["even_rows"] = np.ascontiguousarray(rows.reshape(2, 1, EV_NR)).astype(f32)
    sh["lora_d"] = np.ascontiguousarray(inp["decay_up"]).astype(f32)
    sh["lora_i"] = np.ascontiguousarray(np.concatenate([np.zeros((2, 64, 512), f32), inp["iclr_up"]], axis=1)).astype(f32)
    sh["lora_g"] = np.ascontiguousarray(inp["gate_up"]).astype(f32)
    return sh


def rope_table():
    f32 = np.float32
    inv_freq = (f32(10000.0) ** (-(np.arange(0, 32, 2, dtype=f32) / f32(32)))).astype(f32)
    ang = (np.arange(S, dtype=f32)[:, None] * inv_freq[None, :]).astype(f32)
    cos = np.cos(ang.astype(np.float64)).astype(f32).T
    sin = np.sin(ang.astype(np.float64)).astype(f32).T
    t = np.zeros((128, 2, S), f32)
    t[64:80, 0] = cos
    t[80:96, 0] = cos
    t[64:80, 1] = -sin
    t[80:96, 1] = sin
    return t


def prep_x(x):
    return [np.ascontiguousarray(x[b].T.reshape(NC_, 128, S).transpose(1, 0, 2)) for b in range(x.shape[0])]


def unprep_out(o):
    return np.ascontiguousarray(o.transpose(2, 1, 0).reshape(S, D))


FULL_CFG = {"layers": DEPTH}


def kernel(**inputs):
    inp = {k: np.asarray(v) for k, v in inputs.items()}
    sh = prep_shared(inp)
    xs = prep_x(inp["x"].astype(np.float32))
    nc = build_program(FULL_CFG)
    in_maps = [dict(sh, xT=xs[b]) for b in range(len(xs))]
    res = run_bass_kernel_spmd(nc, in_maps, core_ids=list(range(len(xs))))
    out = np.stack([unprep_out(np.asarray(r["outT"])) for r in res.results], axis=0)
    return out.astype(np.float32)
```

```python
import numpy as np
from contextlib import ExitStack
import concourse.bass as bass
import concourse.mybir as mybir
from concourse.bass_utils import run_bass_kernel_spmd

F32 = mybir.dt.float32
BF16 = mybir.dt.bfloat16
AF = mybir.ActivationFunctionType
ALU = mybir.AluOpType
AX = mybir.AxisListType

S = 2048
D = 1024
NC_ = 8
NTB = 4
DFF = 2816
NHC = 22
DEPTH = 4
GROUPS = [(0, 4), (4, 8), (8, 12), (12, 16), (16, 19), (19, 22)]
RMS_EPS = 1e-6
LN_EPS = 1e-5
ODD_NS = 2 + 1 + 4 * 31 + 4 + 4 + 4
ATTN_SCALE = 96.0 ** -0.5
LNX_EPS = 64e-5
EV_NR = 1792 + 9 * 512
OFF_MU, OFF_KK, OFF_KA, OFF_RK, OFF_LG, OFF_LB, OFF_GLG, OFF_GLB, OFF_W0, OFF_A0 = (
    0, 1792, 2304, 2816, 3328, 3840, 4352, 4864, 5376, 5888)
C0 = float(np.exp(-0.5))
NEUMANN_L = 4
NHS = 4


class Eng:
    def __init__(self, name, eng, sem, skip_self):
        self.name = name
        self.eng = eng
        self.sem = sem
        self.count = 0
        self.seen = {}
        self.skip_self = skip_self


class Buf:
    __slots__ = ("ap", "lw", "rd", "sem", "dcnt", "name", "persist", "psum")

    def __init__(self, ap, name="", persist=False, psum=False):
        self.persist = persist
        self.psum = psum
        self.ap = ap
        self.lw = None
        self.rd = []
        self.sem = None
        self.dcnt = 0
        self.name = name


class KB:
    def __init__(self, nc, es):
        self.nc = nc
        self.es = es
        self.nsem = 0
        self.PE = Eng("pe", nc.tensor, self.mksem("s_pe"), True)
        self.ACT = Eng("act", nc.scalar, self.mksem("s_act"), False)
        self.DVE = Eng("dve", nc.vector, self.mksem("s_dve"), False)
        self.POOL = Eng("pool", nc.gpsimd, self.mksem("s_pool"), False)
        self.SP = Eng("sp", nc.sync, self.mksem("s_sp"), False)
        self.compute = [self.PE, self.ACT, self.DVE, self.POOL]
        self.all = self.compute + [self.SP]
        self.nuniq = 0
        self.sem_pool = []
        self.nwait = 0
        self.phase_stack = []

    def phase(self):
        kb = self

        class _Ph:
            def __enter__(self_):
                kb.phase_stack.append([])

            def __exit__(self_, *a):
                for b in kb.phase_stack.pop():
                    kb.sem_pool.append((b.sem, b.dcnt))
                    b.sem = None
                return False
        return _Ph()

    def rotate(self, E):
        E.sem = self.mksem("s_%s_r%d" % (E.name, self.nsem))
        E.count = 0

    def mksem(self, name):
        self.nsem += 1
        return self.es.enter_context(self.nc.semaphore(name))

    def sb(self, es, shape, dt, name=None):
        self.nuniq += 1
        return es.enter_context(self.nc.sbuf_tensor("sb_%s_%d" % (name or "t", self.nuniq), shape, dt))

    def _need(self, E, tok, lst):
        sem, val, src = tok
        if src is E and E.skip_self:
            return
        k = id(sem)
        if E.seen.get(k, 0) >= val:
            return
        for i, (s2, v2) in enumerate(lst):
            if s2 is sem:
                if val > v2:
                    lst[i] = (sem, val)
                return
        lst.append((sem, val))

    def _emit_waits(self, E, lst, inst_fn):
        for sem, val in lst[:-1]:
            E.eng.wait_ge(sem, val)
            E.seen[id(sem)] = val
            self.nwait += 1
        inst = inst_fn()
        if lst:
            sem, val = lst[-1]
            inst._wait_ge(sem, val)
            E.seen[id(sem)] = val
        return inst

    def _wait(self, E, tok):
        lst = []
        self._need(E, tok, lst)
        for sem, val in lst:
            E.eng.wait_ge(sem, val)
            E.seen[id(sem)] = val
            self.nwait += 1

    def op(self, E, fn, w=(), r=()):
        lst = []
        for b in r:
            if b.lw is not None:
                self._need(E, b.lw, lst)
            if b.psum:
                for t in b.rd:
                    if t[2] is not E:
                        self._need(E, t, lst)
        for b in w:
            if b.lw is not None:
                self._need(E, b.lw, lst)
            for t in b.rd:
                if t[2] is not E:
                    self._need(E, t, lst)
        inst = self._emit_waits(E, lst, fn)
        E.count += 1
        inst.then_inc(E.sem, 1)
        tok = (E.sem, E.count, E)
        for b in r:
            b.rd.append(tok)
        for b in w:
            b.lw = tok
            b.rd = []
        return inst

    def dma(self, Q, out_ap, in_ap, w=(), r=()):
        lst = []
        for b in r:
            if b.lw is not None:
                self._need(Q, b.lw, lst)
        for b in w:
            if b.lw is not None:
                self._need(Q, b.lw, lst)
            for t in b.rd:
                self._need(Q, t, lst)
        owner = w[0] if len(w) else r[0]
        if owner.sem is None:
            if self.sem_pool and self.phase_stack and not owner.persist:
                owner.sem, owner.dcnt = self.sem_pool.pop()
            else:
                owner.sem = self.mksem("s_b%d" % self.nsem)
            if self.phase_stack and not owner.persist:
                self.phase_stack[-1].append(owner)
        inst = self._emit_waits(Q, lst, lambda: Q.eng.dma_start(out=out_ap, in_=in_ap))
        owner.dcnt += 1
        inst.then_inc(owner.sem, 16)
        tok = (owner.sem, 16 * owner.dcnt, None)
        for b in r:
            b.rd.append(tok)
        for b in w:
            b.lw = tok
            b.rd = []
        return tok

    def barrier(self):
        for E in self.all:
            for P in self.compute:
                if P is E:
                    continue
                if P.count > 0:
                    self._wait(E, (P.sem, P.count, None))

    def wait_tok(self, E, tok):
        self._wait(E, tok)


class WStream:
    def __init__(self, K, Q, slots):
        self.K = K
        self.Q = Q
        self.slots = slots
        self.jobs = []
        self.issued = 0
        self.released = 0

    def add(self, dram_ap, view):
        self.jobs.append((dram_ap, view))
        return len(self.jobs) - 1

    def ensure(self, j):
        j = min(j, len(self.jobs) - 1)
        n = len(self.slots)
        while self.issued <= j:
            i = self.issued
            assert i - n < self.released, "weight ring slot still in use"
            slot = self.slots[i % n]
            dram_ap, view = self.jobs[i]
            self.K.dma(self.Q, view(slot.ap), dram_ap, w=[slot])
            self.issued += 1

    def get(self, j):
        self.ensure(j)
        return self.slots[j % len(self.slots)]

    def release(self, j):
        self.released = max(self.released, j + 1)
        self.ensure(self.released + len(self.slots) - 1)


class Prog:
    def __init__(self, cfg):
        self.cfg = cfg

    def build(self):
        cfg = self.cfg
        nc = bass.Bass("TRN2", target_bir_lowering=False)
        self.nc = nc
        dram = {}

        def din(name, shape, dt=F32):
            dram[name] = nc.dram_tensor(name, list(shape), dt, kind="ExternalInput").ap()
            return dram[name]

        self.d_x = din("xT", [128, NC_, S])
        self.d_gains = din("gains", [128, 13 * NC_])
        self.d_ffn_in = din("ffn_in", [2 * DEPTH, NHC, 128, NC_, 256])
        self.d_ffn_out = din("ffn_out", [2 * DEPTH, NHC, 128, D])
        self.d_odd_in = din("odd_in", [2, 7, 128, NC_, 256])
        self.d_odd_out = din("odd_out", [2, 8, 128, D])
        self.d_wq = din("wq", [2, 128, 2, 768])
        self.d_wqr = din("wqr", [2, 128, 2, 768])
        self.d_wkv = din("wkv", [2, 128, 1024])
        self.d_odd_small = din("odd_small", [2, 128, ODD_NS])
        self.d_rope = din("rope", [128, 2, S])
        self.d_even_in = din("even_in", [2, 11, 128, NC_, 256])
        self.d_even_out = din("even_out", [2, 8, 128, D])
        self.d_gsu_ws = din("gsu_wsT", [2, 128, 4, 128])
        self.d_gsu_bs = din("gsu_bs4", [2, 1, 4, 512])
        self.d_even_rows = din("even_rows", [2, 1, EV_NR])
        self.d_lora_d = din("lora_d", [2, 64, 512])
        self.d_lora_i = din("lora_i", [2, 128, 512])
        self.d_lora_g = din("lora_g", [2, 128, 512])
        self.d_zB = nc.dram_tensor("zB_scratch", [S + 1, 1792], F32).ap()
        self.zB = Buf(self.d_zB, "zB", persist=True)
        self.d_out = nc.dram_tensor("outT", [128, NC_, S], F32, kind="ExternalOutput").ap()

        self.dbg_toks = {}
        with ExitStack() as es:
            K = KB(nc, es)
            self.K = K
            xT = K.sb(es, [128, NC_, S], F32, "xT")
            self.xT = xT
            self.x = [[Buf(xT[:, c, tb * 512:(tb + 1) * 512], "x%d_%d" % (c, tb)) for tb in range(NTB)]
                      for c in range(NC_)]
            gains = K.sb(es, [128, 13 * NC_], F32, "gains")
            self.gains = Buf(gains[:], "gains")
            ones_bf = K.sb(es, [128, 128], BF16, "ones_bf")
            self.ones_bf = Buf(ones_bf[:], "ones")
            ones_f = K.sb(es, [128, 128], F32, "ones_f")
            self.ones_f = Buf(ones_f[:], "ones_f")
            K.op(K.DVE, lambda: nc.vector.memset(ones_f[:], 1.0), w=[self.ones_f])
            ident_f = K.sb(es, [128, 128], F32, "ident_f")
            self.ident_f = Buf(ident_f[:], "ident_f")
            K.op(K.POOL, lambda: nc.gpsimd.memset(ident_f[:], 1.0), w=[self.ident_f])
            K.op(K.POOL, lambda: nc.gpsimd.affine_select(out=ident_f[:], in_=ident_f[:], pattern=[[-1, 128]],
                                                         compare_op=ALU.is_equal, fill=0.0, base=0,
                                                         channel_multiplier=1), w=[self.ident_f], r=[self.ident_f])
            ident_b = K.sb(es, [128, 128], BF16, "ident_b")
            self.ident_b = Buf(ident_b[:], "ident_b")
            K.op(K.DVE, lambda: nc.vector.tensor_copy(ident_b[:], ident_f[:]), w=[self.ident_b], r=[self.ident_f])
            epst = K.sb(es, [128, 4], F32, "epst")
            self.epsb = Buf(epst[:], "eps")
            self.eps_cols = {}
            for i, e in enumerate([1e-6, 1e-5, 64e-5, 0.0]):
                K.op(K.DVE, lambda i=i, e=e: nc.vector.memset(epst[:, i:i + 1], e), w=[self.epsb])
                self.eps_cols[e] = epst[:, i:i + 1]
            self.nrm_sq = [Buf(K.sb(es, [128, 512], BF16)[:], "sq") for _ in range(2)]
            self.nrm_rs = [Buf(K.sb(es, [128, 512], F32)[:], "rstd") for _ in range(2)]
            wi_t = [K.sb(es, [128, NC_, 256], BF16, "wi%d" % i) for i in range(3)]
            wo_t = [K.sb(es, [128, D], BF16, "wo%d" % i) for i in range(8)]
            self.wi = WStream(K, K.POOL, [Buf(t[:], "wi", persist=True) for t in wi_t])
            self.wo = WStream(K, K.POOL, [Buf(t[:], "wo", persist=True) for t in wo_t])
            self.ps = [Buf(es.enter_context(nc.psum_tensor("ps%d" % i, [128, 512], F32))[:], "ps%d" % i, psum=True)
                       for i in range(8)]
            self.ps_i = 0

            self.plan_jobs()

            K.op(K.DVE, lambda: nc.vector.memset(ones_bf[:], 1.0), w=[self.ones_bf])
            K.dma(K.SP, gains[:], self.d_gains[:, :], w=[self.gains])
            for c in range(NC_):
                for tb in range(NTB):
                    K.dma(K.SP, self.x[c][tb].ap, self.d_x[:, c, tb * 512:(tb + 1) * 512], w=[self.x[c][tb]])
            self.wi.ensure(2)
            self.wo.ensure(3)

            for l in range(cfg["layers"]):
                if l in cfg.get("skip_layers", []):
                    continue
                if l > 0:
                    K.barrier()
                    K.rotate(K.PE)
                if cfg.get("ffn1", True):
                    with K.phase():
                        self.ffn(l, 0)
                if cfg.get("mixer", True):
                    with K.phase():
                        self.mixer(l)
                if cfg.get("ffn2", True):
                    with K.phase():
                        self.ffn(l, 1)
            with K.phase():
                self.final(cfg.get("final_norm", True))
        return nc

    def dbg(self, name, ap, buf):
        if not self.cfg.get("dbg"):
            return
        if name in self.dbg_toks:
            return
        if self.cfg.get("dbg_names") is not None and name not in self.cfg["dbg_names"]:
            return
        d = self.nc.dram_tensor("dbg_" + name, list(ap.shape), ap.dtype, kind="ExternalOutput").ap()
        self.dbg_toks[name] = self.K.dma(self.K.SP, d, ap, r=[buf])

    def psum(self):
        b = self.ps[self.ps_i % 8]
        self.ps_i += 1
        return b

    def plan_jobs(self):
        cfg = self.cfg
        self.jobs_wi = {}
        self.jobs_wo = {}
        full = lambda ap: ap
        for l in range(cfg["layers"]):
            if l in cfg.get("skip_layers", []):
                continue
            for which in range(2):
                if which == 1 and cfg.get("mixer", True):
                    if l % 2 == 0:
                        e = l // 2
                        for sl in range(11):
                            self.jobs_wi[("even", l, sl)] = self.wi.add(self.d_even_in[e, sl], full)
                        for ch in range(8):
                            self.jobs_wo[("even", l, ch)] = self.wo.add(self.d_even_out[e, ch], full)
                    if l % 2 == 1:
                        o = l // 2
                        for sl in range(7):
                            self.jobs_wi[("odd", l, sl)] = self.wi.add(self.d_odd_in[o, sl], full)
                        for ch in [4, 5, 6, 7, 0, 1, 2, 3]:
                            self.jobs_wo[("odd", l, ch)] = self.wo.add(self.d_odd_out[o, ch], full)
                if not cfg.get("ffn%d" % (which + 1), True):
                    continue
                f = l * 2 + which
                for (a, b) in GROUPS:
                    for hc in range(a, b):
                        self.jobs_wi[(f, hc)] = self.wi.add(self.d_ffn_in[f, hc], full)
                    for hc in range(a, b):
                        self.jobs_wo[(f, hc)] = self.wo.add(self.d_ffn_out[f, hc], full)

    def rmsnorm_fm(self, es, src, gain_col, dst, nchunks, nfeat, eps, tbs=range(NTB), extra_r=()):
        K, nc = self.K, self.nc
        sq, rs = self.nrm_sq, self.nrm_rs
        for tb in tbs:
            pb = self.psum()
            for c in range(nchunks):
                s = sq[c % 2]
                K.op(K.ACT, lambda s=s, c=c: nc.scalar.activation(out=s.ap, in_=src[c][tb].ap, func=AF.Square),
                     w=[s], r=[src[c][tb]])
                K.op(K.PE, lambda s=s, c=c: nc.tensor.matmul(pb.ap, self.ones_bf.ap, s.ap, start=(c == 0),
                                                             stop=(c == nchunks - 1)),
                     w=[pb], r=[s, self.ones_bf])
            r = rs[tb % 2]
            K.op(K.ACT, lambda: nc.scalar.activation(out=r.ap, in_=pb.ap, func=AF.Sqrt, scale=1.0 / nfeat,
                                                     bias=self.eps_ap(eps)),
                 w=[r], r=[pb, self.epsb])
            K.op(K.DVE, lambda: nc.vector.reciprocal(out=r.ap, in_=r.ap), w=[r], r=[r])
            for c in range(nchunks):
                K.op(K.DVE, lambda c=c: nc.vector.scalar_tensor_tensor(
                    out=dst[c][tb].ap, in0=src[c][tb].ap, scalar=gain_col(c), in1=r.ap,
                    op0=ALU.mult, op1=ALU.mult), w=[dst[c][tb]], r=[src[c][tb], r, self.gains] + list(extra_r))

    def eps_ap(self, eps):
        return self.eps_cols[eps]

    def gain_col(self, idx):
        return lambda c: self.gains.ap[:, idx * NC_ + c: idx * NC_ + c + 1]

    def ffn(self, l, which):
        K, nc = self.K, self.nc
        f = l * 2 + which
        gidx = (0 if which == 0 else 2) * DEPTH + l
        with ExitStack() as es:
            hT_t = K.sb(es, [128, NC_, S], BF16, "hT")
            hT = [[Buf(hT_t[:, c, tb * 512:(tb + 1) * 512]) for tb in range(NTB)] for c in range(NC_)]
            hid_t = K.sb(es, [128, 4, S], BF16, "hid")
            hid = [[Buf(hid_t[:, j, tb * 512:(tb + 1) * 512]) for tb in range(NTB)] for j in range(4)]
            sg = [Buf(K.sb(es, [128, 512], F32)[:], "sg") for _ in range(2)]
            self.rmsnorm_fm(es, self.x, self.gain_col(gidx), hT, NC_, D, RMS_EPS)
            n = 0
            for (a, b) in GROUPS:
                for j, hc in enumerate(range(a, b)):
                    ji = self.jobs_wi[(f, hc)]
                    wi = self.wi.get(ji)
                    for tb in range(NTB):
                        pg = self.psum()
                        pu = self.psum()
                        for c in range(NC_):
                            K.op(K.PE, lambda c=c: nc.tensor.matmul(pg.ap, wi.ap[:, c, 0:128], hT[c][tb].ap,
                                                                    start=(c == 0), stop=(c == NC_ - 1)),
                                 w=[pg], r=[wi, hT[c][tb]])
                        for c in range(NC_):
                            K.op(K.PE, lambda c=c: nc.tensor.matmul(pu.ap, wi.ap[:, c, 128:256], hT[c][tb].ap,
                                                                    start=(c == 0), stop=(c == NC_ - 1)),
                                 w=[pu], r=[wi, hT[c][tb]])
                        s = sg[n % 2]
                        n += 1
                        K.op(K.ACT, lambda: nc.scalar.activation(out=s.ap, in_=pg.ap, func=AF.Silu), w=[s], r=[pg])
                        K.op(K.DVE, lambda: nc.vector.tensor_tensor(out=hid[j][tb].ap, in0=s.ap, in1=pu.ap,
                                                                    op=ALU.mult), w=[hid[j][tb]], r=[s, pu])
                    self.wi.release(ji)
                wos = [self.wo.get(self.jobs_wo[(f, hc)]) for hc in range(a, b)]
                ng = b - a
                for d in range(NC_):
                    for tb in range(NTB):
                        po = self.psum()
                        for j in range(ng):
                            K.op(K.PE, lambda j=j: nc.tensor.matmul(po.ap, wos[j].ap[:, d * 128:(d + 1) * 128],
                                                                    hid[j][tb].ap, start=(j == 0), stop=(j == ng - 1)),
                                 w=[po], r=[wos[j], hid[j][tb]])
                        xb = self.x[d][tb]
                        K.op(K.DVE, lambda: nc.vector.scalar_tensor_tensor(
                            out=xb.ap, in0=po.ap, scalar=0.5, in1=xb.ap, op0=ALU.mult, op1=ALU.add),
                            w=[xb], r=[po, xb])
                for hc in range(a, b):
                    self.wo.release(self.jobs_wo[(f, hc)])
            K.barrier()

    def mixer(self, l):
        if l % 2 == 1:
            self.mixer_odd(l)
        else:
            self.mixer_even(l)

    def mixer_even(self, l):
        K, nc = self.K, self.nc
        e = l // 2
        PE, ACT, DVE, POOL, SP = K.PE, K.ACT, K.DVE, K.POOL, K.SP
        ps = self.ps
        cfg = self.cfg

        def outproj(mixb, chs, tbs=range(NTB)):
            wos = [self.wo.get(self.jobs_wo[("even", l, ch)]) for ch in chs]
            for d in range(NC_):
                for tb in tbs:
                    po = self.psum4()
                    for i, ch in enumerate(chs):
                        K.op(PE, lambda i=i: nc.tensor.matmul(po.ap, wos[i].ap[:, d * 128:(d + 1) * 128], mixb[i][tb].ap,
                                                              start=(i == 0), stop=(i == len(chs) - 1)),
                             w=[po], r=[wos[i], mixb[i][tb]])
                    xb = self.x[d][tb]
                    K.op(DVE, lambda: nc.vector.tensor_tensor(out=xb.ap, in0=po.ap, in1=xb.ap, op=ALU.add),
                         w=[xb], r=[po, xb])

        with ExitStack() as es:
            rows = self.d_even_rows[e]

            def bc_tile(esx, off, n, name):
                t = K.sb(esx, [128, n], F32, name)
                bf = Buf(t[:], name)
                K.dma(SP, t[:], rows[0:1, off:off + n].partition_broadcast(128), w=[bf])
                return t, bf

            with ExitStack() as esg:
                uT_t = K.sb(esg, [128, 4, S], BF16, "uT")
                uT = [[Buf(uT_t[:, c, tb * 512:(tb + 1) * 512]) for tb in range(NTB)] for c in range(4)]
                vtm_t = K.sb(esg, [128, 16, 512], BF16, "vtm")
                vtm = [Buf(vtm_t[:, i, :]) for i in range(16)]
                wsT_t = K.sb(esg, [128, 4, 128], BF16, "wsT")
                wsT = Buf(wsT_t[:], "wsT")
                K.dma(POOL, wsT_t[:], self.d_gsu_ws[e], w=[wsT])
                K.op(POOL, lambda: nc.gpsimd.memset(wsT_t[64:128, :, 0:64], 0.0), w=[wsT], r=[wsT])
                bs4_t = K.sb(esg, [1, 4, 512], F32, "bs4")
                bs4 = Buf(bs4_t[:], "bs4")
                K.dma(SP, bs4_t[:], self.d_gsu_bs[e], w=[bs4])
                with ExitStack() as esa:
                    hT_t = K.sb(esa, [128, NC_, S], BF16, "hT")
                    hT = [[Buf(hT_t[:, c, tb * 512:(tb + 1) * 512]) for tb in range(NTB)] for c in range(NC_)]
                    self.rmsnorm_fm(esa, self.x, self.gain_col(DEPTH + l), hT, NC_, D, RMS_EPS)
                    glg_t, glg = bc_tile(esa, OFF_GLG, 512, "glg")
                    glb_t, glb = bc_tile(esa, OFF_GLB, 512, "glb")
                    g32 = [Buf(K.sb(esa, [128, 512], F32, "g32_%d" % i)[:]) for i in range(2)]
                    gsq = Buf(K.sb(esa, [128, 512], F32, "gsq")[:])
                    st_t = K.sb(esa, [128, 8], F32, "vstat")
                    st = Buf(st_t[:], "vstat")
                    zst_t = [K.sb(esa, [128, 4, 256], F32, "zstage%d" % i) for i in range(2)]
                    zst = [Buf(t[:]) for t in zst_t]
                    zrow_t = K.sb(esa, [1, 1792], F32, "zrow")
                    zrow = Buf(zrow_t[:], "zrow")
                    K.op(DVE, lambda: nc.vector.memset(zrow_t[:], 0.0), w=[zrow])
                    K.dma(SP, self.d_zB[0:1, :], zrow_t[:], w=[self.zB], r=[zrow])
                    for sl in range(2):
                        jj = self.jobs_wi[("even", l, sl)]
                        wb = self.wi.get(jj)
                        for tb in range(NTB):
                            for half in range(2):
                                pb = self.psum()
                                for c in range(NC_):
                                    K.op(PE, lambda c=c, pb=pb: nc.tensor.matmul(
                                        pb.ap, wb.ap[:, c, half * 128:(half + 1) * 128], hT[c][tb].ap,
                                        start=(c == 0), stop=(c == NC_ - 1)), w=[pb], r=[wb, hT[c][tb]])
                                ub = uT[sl * 2 + half][tb]
                                K.op(ACT, lambda pb=pb, ub=ub: nc.scalar.activation(out=ub.ap, in_=pb.ap,
                                                                                    func=AF.Gelu_apprx_tanh),
                                     w=[ub], r=[pb])
                        self.wi.release(jj)
                    j2 = self.jobs_wi[("even", l, 2)]
                    w2 = self.wi.get(j2)
                    j3 = self.jobs_wi[("even", l, 3)]
                    w3 = self.wi.get(j3)
                    for i in range(16):
                        tb, off = i // 4, (i % 4) * 128
                        pv = self.psum()
                        for hi, wb in enumerate((w2, w3)):
                            for c in range(NC_):
                                K.op(PE, lambda c=c, wb=wb, hi=hi: nc.tensor.matmul(
                                    pv.ap[:, hi * 256:(hi + 1) * 256], hT_t[:, c, i * 128:(i + 1) * 128], wb.ap[:, c, :],
                                    start=(c == 0), stop=(c == NC_ - 1)), w=[pv], r=[wb, hT[c][tb]])
                        gb = g32[i % 2]
                        K.op(ACT, lambda gb=gb, pv=pv: nc.scalar.activation(out=gb.ap, in_=pv.ap, func=AF.Gelu_apprx_tanh),
                             w=[gb], r=[pv])
                        K.op(DVE, lambda gb=gb: nc.vector.tensor_reduce(out=st_t[:, 0:1], in_=gb.ap, axis=AX.X, op=ALU.add),
                             w=[st], r=[gb])
                        K.op(ACT, lambda gb=gb: nc.scalar.activation(out=gsq.ap, in_=gb.ap, func=AF.Square),
                             w=[gsq], r=[gb])
                        K.op(DVE, lambda: nc.vector.tensor_reduce(out=st_t[:, 1:2], in_=gsq.ap, axis=AX.X, op=ALU.add),
                             w=[st], r=[gsq])
                        K.op(DVE, lambda: nc.vector.tensor_scalar(out=st_t[:, 2:3], in0=st_t[:, 0:1], scalar1=1.0 / 512,
                                                                  scalar2=None, op0=ALU.mult), w=[st], r=[st])
                        K.op(DVE, lambda: nc.vector.tensor_tensor(out=st_t[:, 3:4], in0=st_t[:, 2:3], in1=st_t[:, 2:3],
                                                                  op=ALU.mult), w=[st], r=[st])
                        K.op(DVE, lambda: nc.vector.scalar_tensor_tensor(out=st_t[:, 4:5], in0=st_t[:, 1:2],
                                                                         scalar=1.0 / 512, in1=st_t[:, 3:4],
                                                                         op0=ALU.mult, op1=ALU.subtract), w=[st], r=[st])
                        K.op(ACT, lambda: nc.scalar.activation(out=st_t[:, 5:6], in_=st_t[:, 4:5], func=AF.Sqrt, scale=1.0,
                                                               bias=self.eps_cols[LN_EPS]), w=[st], r=[st, self.epsb])
                        K.op(DVE, lambda: nc.vector.reciprocal(out=st_t[:, 6:7], in_=st_t[:, 5:6]), w=[st], r=[st])
                        K.op(DVE, lambda gb=gb: nc.vector.tensor_scalar(out=gb.ap, in0=gb.ap, scalar1=st_t[:, 2:3],
                                                                        scalar2=st_t[:, 6:7], op0=ALU.subtract,
                                                                        op1=ALU.mult), w=[gb], r=[gb, st])
                        K.op(DVE, lambda gb=gb: nc.vector.tensor_tensor(out=gb.ap, in0=gb.ap, in1=glg_t[:], op=ALU.mult),
                             w=[gb], r=[gb, glg])
                        K.op(DVE, lambda gb=gb, i=i: nc.vector.tensor_tensor(out=vtm[i].ap, in0=gb.ap, in1=glb_t[:],
                                                                             op=ALU.add), w=[vtm[i]], r=[gb, glb])
                    self.wi.release(j2)
                    self.wi.release(j3)
                    nz = 0
                    for sl in range(4, 11):
                        if cfg.get("only") == "a":
                            jj = self.jobs_wi[("even", l, sl)]
                            self.wi.get(jj)
                            self.wi.release(jj)
                            continue
                        jj = self.jobs_wi[("even", l, sl)]
                        wb = self.wi.get(jj)
                        for tb in range(NTB):
                            zb_, zb_t = zst[nz % 2], zst_t[nz % 2]
                            nz += 1
                            for ti in range(4):
                                i = tb * 4 + ti
                                pz = self.psum()
                                for c in range(NC_):
                                    K.op(PE, lambda c=c, pz=pz, i=i: nc.tensor.matmul(
                                        pz.ap[:, 0:256], hT_t[:, c, i * 128:(i + 1) * 128], wb.ap[:, c, :],
                                        start=(c == 0), stop=(c == NC_ - 1)), w=[pz], r=[wb, hT[c][tb]])
                                K.op(ACT, lambda pz=pz, ti=ti, zb_t=zb_t: nc.scalar.copy(out=zb_t[:, ti, :],
                                                                                         in_=pz.ap[:, 0:256]),
                                     w=[zb_], r=[pz])
                            col0 = (sl - 4) * 256
                            dst = self.d_zB[1 + tb * 512: 1 + (tb + 1) * 512, col0:col0 + 256].rearrange(
                                "(ti p) n -> p ti n", p=128)
                            K.dma(SP, dst, zb_t[:], w=[self.zB], r=[zb_])
                        self.wi.release(jj)
                    K.barrier()
                with ExitStack() as esm:
                    ya_t = K.sb(esm, [128, 4, S], BF16, "yaT")
                    ya = [[Buf(ya_t[:, g, tb * 512:(tb + 1) * 512]) for tb in range(NTB)] for g in range(4)]
                    for tb in range(NTB):
                        for g in range(4):
                            pm = self.psum()
                            K.op(PE, lambda g=g, pm=pm: nc.tensor.matmul(pm.ap, self.ones_f.ap[0:1, :], bs4_t[0:1, g, :],
                                                                         start=True, stop=False),
                                 w=[pm], r=[self.ones_f, bs4])
                            for nb in range(4):
                                i = tb * 4 + nb
                                K.op(PE, lambda i=i, g=g, nb=nb, pm=pm: nc.tensor.matmul(
                                    pm.ap[:, nb * 128:(nb + 1) * 128], vtm_t[:, i, g * 128:(g + 1) * 128], wsT_t[:, g, :],
                                    start=False, stop=(nb == 3)), w=[pm], r=[vtm[i], wsT])
                            K.op(DVE, lambda g=g, tb=tb, pm=pm: nc.vector.tensor_tensor(
                                out=ya[g][tb].ap, in0=pm.ap, in1=uT[g][tb].ap, op=ALU.mult),
                                w=[ya[g][tb]], r=[pm, uT[g][tb]])
                    self.psum4 = self.psum
                    outproj(ya, [0, 1, 2, 3])
                    for ch in range(4):
                        self.wo.release(self.jobs_wo[("even", l, ch)])
                    K.barrier()

            if cfg.get("only") == "a":
                for ch in range(4, 8):
                    self.wo.get(self.jobs_wo[("even", l, ch)])
                    self.wo.release(self.jobs_wo[("even", l, ch)])
            else:
                self.rwkv(l, es, bc_tile, outproj)
            K.barrier()

    def rwkv(self, l, es, bc_tile, outproj):
        K, nc = self.K, self.nc
        e = l // 2
        PE, ACT, DVE, POOL, SP = K.PE, K.ACT, K.DVE, K.POOL, K.SP
        ps = self.ps
        rr = [0]
        NROT = 6

        pinned = []

        def psum4():
            for k in range(NROT):
                bb = ps[(rr[0] + k) % NROT]
                if any(bb is p_ for p_ in pinned):
                    continue
                if bb.lw is None or len(bb.rd) > 0:
                    rr[0] += k + 1
                    return bb
            raise AssertionError("no free rotating PSUM bank")
        self.psum4 = psum4
        pCH, pY = ps[6], ps[7]
        rows = self.d_even_rows[e]
        with ExitStack() as esr:
            def T(shape, dt, name):
                t = K.sb(esr, shape, dt, name)
                return t, Buf(t[:], name)

            def T2(shape, dt, name):
                return [T(shape, dt, name + "_0"), T(shape, dt, name + "_1")]
            mu_t, mu = bc_tile(esr, OFF_MU, 1792, "mu")
            kkb_t, kkb = bc_tile(esr, OFF_KK, 512, "kkb")
            kab_t, kab = bc_tile(esr, OFF_KA, 512, "kab")
            rkb_t, rkb = bc_tile(esr, OFF_RK, 512, "rkb")
            lgb_t, lgb = bc_tile(esr, OFF_LG, 512, "lgb")
            lbb_t, lbb = bc_tile(esr, OFF_LB, 512, "lbb")
            dup_t, dup = T([65, 512], F32, "dup")
            K.dma(SP, dup_t[0:64, :], self.d_lora_d[e], w=[dup])
            K.dma(SP, dup_t[64:65, :], rows[0:1, OFF_W0:OFF_W0 + 512], w=[dup])
            iup_t, iup = T([65, 512], F32, "iup")
            K.dma(SP, iup_t[0:64, :], self.d_lora_i[e][64:128, :], w=[iup])
            K.dma(SP, iup_t[64:65, :], rows[0:1, OFF_A0:OFF_A0 + 512], w=[iup])
            gup_t, gup = T([128, 512], BF16, "gup")
            K.dma(POOL, gup_t[:], self.d_lora_g[e], w=[gup])
            Ui_t, Ui = T([128, 128], F32, "Uincl")
            Us_t, Us = T([128, 128], F32, "Ustrict")
            Ls_t, Ls = T([128, 128], F32, "Lstrict")
            mk2_t, mk2 = T([128, 256], F32, "mask2")
            for (t_, b_, cmp_, st_, cm_) in ((Ui_t, Ui, ALU.is_ge, 1, -1), (Us_t, Us, ALU.is_gt, 1, -1),
                                             (Ls_t, Ls, ALU.is_gt, -1, 1)):
                K.op(POOL, lambda t_=t_: nc.gpsimd.memset(t_[:], 1.0), w=[b_])
                K.op(POOL, lambda t_=t_, cmp_=cmp_, st_=st_, cm_=cm_: nc.gpsimd.affine_select(
                    out=t_[:], in_=t_[:], pattern=[[st_, 128]], compare_op=cmp_, fill=0.0, base=0,
                    channel_multiplier=cm_), w=[b_], r=[b_])
            K.op(POOL, lambda: nc.gpsimd.tensor_copy(mk2_t[:, 0:128], Us_t[:]), w=[mk2], r=[Us])
            K.op(POOL, lambda: nc.gpsimd.tensor_copy(mk2_t[:, 128:256], Ui_t[:]), w=[mk2], r=[Ui])
            Hf_t, Hf = T([64, 512], F32, "Hf")
            Hb_t, Hb = T([64, 512], BF16, "Hb")
            K.op(DVE, lambda: nc.vector.memset(Hf_t[:], 0.0), w=[Hf])
            K.op(DVE, lambda: nc.vector.memset(Hb_t[:], 0.0), w=[Hb])
            GTa_t, GTa = T([64, 512], F32, "GTa")
            Fa_t, Fa = T([64, 512], F32, "Fa")
            ybT_t = K.sb(esr, [128, 4, 512], BF16, "ybT")
            ybT = [Buf(ybT_t[:, q, :]) for q in range(4)]
            scr_t = K.sb(esr, [128, 2048], F32, "scr")
            zs_t, zs = scr_t[:, 0:1792], Buf(scr_t[:, 0:1792], "zs")
            tA_t, tA = scr_t[:, 0:512], Buf(scr_t[:, 0:512], "tA")
            tB_t, tB = scr_t[:, 512:1024], Buf(scr_t[:, 512:1024], "tB")
            Ea_t, Ea = scr_t[:, 1024:1536], Buf(scr_t[:, 1024:1536], "Ea")
            Eb_t, Eb = scr_t[:, 1536:2048], Buf(scr_t[:, 1536:2048], "Eb")
            ZS = [zs, tA, tB, Ea, Eb]
            lwT_t, lwT = T([65, 128], F32, "lwT")
            K.op(DVE, lambda: nc.vector.memset(lwT_t[64:65, :], 1.0), w=[lwT])
            laT_t, laT = T([65, 128], F32, "laT")
            K.op(DVE, lambda: nc.vector.memset(laT_t[64:65, :], 1.0), w=[laT])
            lgT_t, lgT = T([128, 128], BF16, "lgT")
            sg_t, sg = T([128, 512], F32, "sg")
            as_t, asg = T([128, 512], F32, "asig")
            kk_t, kk = T([128, 512], F32, "kk")
            km_t, km = T([128, 512], F32, "kmod")
            bq_t, bq = T([128, 512], F32, "bq")
            bt_t, bt_b = T([128, 512], BF16, "bt_b")
            kt_t, kt_b = T([128, 512], BF16, "kt_b")
            yb_t, yb_b = T([128, 512], BF16, "yb_b")
            zt2 = T2([128, 1792], F32, "zt")
            gg2 = T2([128, 512], F32, "gg")
            st2 = T2([128, 64], F32, "rst")
            at2 = T2([128, 512], BF16, "at_b")
            rt2 = T2([128, 512], BF16, "rt_b")
            bh2 = T2([128, 512], BF16, "bh_b")
            kh2 = T2([128, 512], BF16, "kh_b")
            v2 = T2([128, 512], BF16, "v_b")
            pC2 = T2([64, 8], F32, "pC")
            GT2 = [[T([128, 4, 128], BF16, "GT%d_%d" % (p_, i_)) for i_ in range(4)] for p_ in range(2)]
            HS = []
            for i_ in range(NHS):
                d = {}
                for nm, shp in (("M1", [128, 256]), ("M2", [128, 256]), ("XT0", [128, 128]), ("XXa", [128, 256]),
                                ("XXb", [128, 256]), ("Pa", [128, 128]), ("Pb", [128, 128]), ("W1", [128, 64]),
                                ("AU", [128, 128]), ("RbT", [64, 128])):
                    t_ = K.sb(esr, shp, BF16, "%s_%d" % (nm, i_))
                    d[nm] = (t_, Buf(t_[:], nm))
                HS.append(d)

            v3 = lambda ap: ap.rearrange("p (h f) -> p h f", f=64)
            bc3 = lambda ap: ap.unsqueeze(2).broadcast_to([128, 8, 64])
            L = NEUMANN_L

            def pre_gen(i):
                par = i % 2
                zt_t, zt = zt2[par]
                gg_t, gg = gg2[par]
                st_t, st = st2[par]
                at_t, at_b = at2[par]
                rt_t, rt_b = rt2[par]
                bh_t, bh_b = bh2[par]
                kh_t, kh_b = kh2[par]
                v_t, v_b = v2[par]
                pC_t, pCs = pC2[par]
                K.dma(SP, zt_t[:], self.d_zB[1 + i * 128: 1 + (i + 1) * 128, :], w=[zt], r=[self.zB])
                K.dma(SP, zs_t, self.d_zB[i * 128:(i + 1) * 128, :], w=ZS, r=[self.zB])
                yield
                K.op(DVE, lambda: nc.vector.tensor_tensor(out=zs_t, in0=zs_t, in1=zt_t[:], op=ALU.subtract),
                     w=ZS, r=[zs, zt])
                yield
                K.op(POOL, lambda: nc.gpsimd.tensor_tensor(out=zs_t, in0=zs_t, in1=mu_t[:], op=ALU.mult),
                     w=ZS, r=[zs, mu])
                yield
                K.op(DVE, lambda: nc.vector.tensor_tensor(out=zt_t[:], in0=zt_t[:], in1=zs_t, op=ALU.add),
                     w=[zt] + ZS, r=[zs, zt])
                r_ap, k_ap, vv_ap = zt_t[:, 0:512], zt_t[:, 512:1024], zt_t[:, 1024:1536]
                yield
                pl = psum4()
                K.op(PE, lambda: nc.tensor.transpose(pl.ap[0:64, 0:128], zt_t[:, 1536:1600], self.ident_f.ap),
                     w=[pl], r=[zt, self.ident_f])
                K.op(PE, lambda: nc.tensor.transpose(pl.ap[0:64, 128:256], zt_t[:, 1600:1664], self.ident_f.ap),
                     w=[pl], r=[zt, self.ident_f])
                K.op(PE, lambda: nc.tensor.transpose(pl.ap[:, 256:384], zt_t[:, 1664:1792], self.ident_f.ap),
                     w=[pl], r=[zt, self.ident_f])
                yield
                K.op(ACT, lambda: nc.scalar.activation(out=lwT_t[0:64, :], in_=pl.ap[0:64, 0:128], func=AF.Tanh),
                     w=[lwT], r=[pl])
                K.op(ACT, lambda: nc.scalar.activation(out=lgT_t[:], in_=pl.ap[:, 256:384], func=AF.Sigmoid),
                     w=[lgT], r=[pl])
                K.op(ACT, lambda: nc.scalar.copy(out=laT_t[0:64, :], in_=pl.ap[0:64, 128:256]), w=[laT], r=[pl])
                yield
                pw = psum4()
                K.op(PE, lambda: nc.tensor.matmul(pw.ap, lwT_t[0:65, :], dup_t[0:65, :], start=True, stop=True),
                     w=[pw], r=[lwT, dup])
                pa = psum4()
                K.op(PE, lambda: nc.tensor.matmul(pa.ap, laT_t[0:65, :], iup_t[0:65, :], start=True, stop=True),
                     w=[pa], r=[laT, iup])
                yield
                K.op(ACT, lambda: nc.scalar.activation(out=sg_t[:], in_=pw.ap, func=AF.Sigmoid), w=[sg], r=[pw])
                K.op(ACT, lambda: nc.scalar.activation(out=as_t[:], in_=pa.ap, func=AF.Sigmoid), w=[asg], r=[pa])
                K.op(DVE, lambda: nc.vector.tensor_tensor(out=tA_t, in0=k_ap, in1=kkb_t[:], op=ALU.mult),
                     w=[tA], r=[zt, kkb])
                yield
                pg = psum4()
                K.op(PE, lambda: nc.tensor.matmul(pg.ap, lgT_t[:], gup_t[:], start=True, stop=True),
                     w=[pg], r=[lgT, gup])
                pcs = psum4()
                pinned.append(pcs)
                K.op(PE, lambda: nc.tensor.matmul(pcs.ap, Ui_t[:], sg_t[:], start=True, stop=True), w=[pcs], r=[Ui, sg])
                for h in range(8):
                    K.op(PE, lambda h=h: nc.tensor.matmul(pCH.ap[0:64, h:h + 1], sg_t[:, h * 64:(h + 1) * 64],
                                                          self.ones_f.ap[:, 0:1], start=True, stop=True),
                         w=[pCH], r=[sg, self.ones_f])
                K.op(DVE, lambda: nc.vector.scalar_tensor_tensor(out=km_t[:], in0=as_t[:], scalar=-1.0, in1=kab_t[:],
                                                                 op0=ALU.add, op1=ALU.mult), w=[km], r=[asg, kab])
                K.op(POOL, lambda: nc.gpsimd.tensor_tensor(out=tB_t, in0=tA_t, in1=tA_t, op=ALU.mult),
                     w=[tB], r=[tA])
                yield
                K.op(ACT, lambda: nc.scalar.copy(out=gg_t[:], in_=pg.ap), w=[gg], r=[pg])
                K.op(ACT, lambda: nc.scalar.activation(out=pC_t[:], in_=pCH.ap[0:64, 0:8], func=AF.Exp, scale=-C0),
                     w=[pCs], r=[pCH])
                K.op(ACT, lambda: nc.scalar.activation(out=Ea_t, in_=pcs.ap, func=AF.Exp, scale=-C0), w=[Ea], r=[pcs])
                K.op(DVE, lambda: nc.vector.scalar_tensor_tensor(out=km_t[:], in0=km_t[:], scalar=1.0, in1=k_ap,
                                                                 op0=ALU.add, op1=ALU.mult), w=[km], r=[km, zt])
                K.op(DVE, lambda: nc.vector.tensor_reduce(out=st_t[:, 0:8], in_=v3(tB_t), axis=AX.X, op=ALU.add),
                     w=[st], r=[tB])
                yield
                pcx = psum4()
                K.op(PE, lambda: nc.tensor.matmul(pcx.ap, Us_t[:], sg_t[:], start=True, stop=True), w=[pcx], r=[Us, sg])
                K.op(DVE, lambda: nc.vector.tensor_tensor(out=rt_t[:], in0=r_ap, in1=Ea_t, op=ALU.mult),
                     w=[rt_b], r=[zt, Ea])
                K.op(ACT, lambda: nc.scalar.activation(out=st_t[:, 8:16], in_=st_t[:, 0:8], func=AF.Sqrt),
                     w=[st], r=[st])
                K.op(POOL, lambda: nc.gpsimd.tensor_tensor(out=tB_t, in0=r_ap, in1=km_t[:], op=ALU.mult),
                     w=[tB], r=[zt, km])
                yield
                K.op(ACT, lambda: nc.scalar.activation(out=Ea_t, in_=pcs.ap, func=AF.Exp, scale=C0), w=[Ea], r=[pcs])
                pinned.remove(pcs)
                K.op(ACT, lambda: nc.scalar.activation(out=Eb_t, in_=pcx.ap, func=AF.Exp, scale=-C0), w=[Eb], r=[pcx])
                K.op(DVE, lambda: nc.vector.tensor_scalar(out=st_t[:, 8:16], in0=st_t[:, 8:16], scalar1=1e-12,
                                                          scalar2=None, op0=ALU.max), w=[st], r=[st])
                K.op(DVE, lambda: nc.vector.reciprocal(out=st_t[:, 16:24], in_=st_t[:, 8:16]), w=[st], r=[st])
                K.op(DVE, lambda: nc.vector.tensor_tensor(out=v3(kk_t[:]), in0=v3(tA_t), in1=bc3(st_t[:, 16:24]),
                                                          op=ALU.mult), w=[kk], r=[tA, st])
                K.op(POOL, lambda: nc.gpsimd.tensor_tensor(out=tB_t, in0=tB_t, in1=rkb_t[:], op=ALU.mult),
                     w=[tB], r=[tB, rkb])
                yield
                prq = psum4()
                K.op(PE, lambda: nc.tensor.matmul(prq.ap, Ls_t[:], sg_t[:], start=True, stop=True), w=[prq], r=[Ls, sg])
                K.op(DVE, lambda: nc.vector.tensor_tensor(out=kt_t[:], in0=km_t[:], in1=Ea_t, op=ALU.mult),
                     w=[kt_b], r=[km, Ea])
                K.op(DVE, lambda: nc.vector.scalar_tensor_tensor(out=at_t[:], in0=kk_t[:], scalar=-1.0, in1=Eb_t,
                                                                 op0=ALU.mult, op1=ALU.mult), w=[at_b], r=[kk, Eb])
                K.op(POOL, lambda: nc.gpsimd.tensor_tensor(out=bq_t[:], in0=kk_t[:], in1=as_t[:], op=ALU.mult),
                     w=[bq], r=[kk, asg])
                yield
                K.op(DVE, lambda: nc.vector.tensor_reduce(out=st_t[:, 24:32], in_=v3(tB_t), axis=AX.X, op=ALU.add),
                     w=[st], r=[tB])
                K.op(DVE, lambda: nc.vector.tensor_tensor(out=bt_t[:], in0=bq_t[:], in1=Ea_t, op=ALU.mult),
                     w=[bt_b], r=[bq, Ea])
                K.op(ACT, lambda: nc.scalar.activation(out=Eb_t, in_=prq.ap, func=AF.Exp, scale=-C0), w=[Eb], r=[prq])
                K.op(POOL, lambda: nc.gpsimd.tensor_copy(v_t[:], vv_ap), w=[v_b], r=[zt])
                yield
                K.op(DVE, lambda: nc.vector.tensor_tensor(out=bh_t[:], in0=bq_t[:], in1=Eb_t, op=ALU.mult),
                     w=[bh_b], r=[bq, Eb])
                K.op(DVE, lambda: nc.vector.tensor_tensor(out=kh_t[:], in0=km_t[:], in1=Eb_t, op=ALU.mult),
                     w=[kh_b], r=[km, Eb])
                for pr in range(4):
                    yield
                    ptb = psum4()
                    pt16 = ptb.ap.bitcast(BF16)
                    for kind, (xt_, xb_) in enumerate(((at_t, at_b), (rt_t, rt_b), (bt_t, bt_b), (kt_t, kt_b))):
                        K.op(PE, lambda kind=kind, xt_=xt_: nc.tensor.transpose(
                            pt16[:, kind * 128:(kind + 1) * 128], xt_[:, pr * 128:(pr + 1) * 128], self.ident_b.ap),
                            w=[ptb], r=[xb_, self.ident_b])
                    yield
                    gT_t, gT = GT2[par][pr]
                    K.op(ACT, lambda: nc.scalar.copy(out=gT_t[:].rearrange("p k t -> p (k t)"), in_=pt16[:, 0:512]),
                         w=[gT], r=[ptb])

            def head_gen(h, i):
                par = i % 2
                at_t, at_b = at2[par]
                rt_t, rt_b = rt2[par]
                bh_t, bh_b = bh2[par]
                kh_t, kh_b = kh2[par]
                v_t, v_b = v2[par]
                pC_t, pCs = pC2[par]
                pr, hb = h // 2, (h % 2) * 64
                hs = HS[h % NHS]
                hc = slice(h * 64, (h + 1) * 64)
                gt, gtb = GT2[par][pr]
                ar = gt[hb:hb + 64, 0:2, :].rearrange("p k t -> p (k t)")
                M1_t, M1 = hs["M1"]
                M2_t, M2 = hs["M2"]
                XT0_t, XT0 = hs["XT0"]
                p12 = psum4()
                K.op(PE, lambda: nc.tensor.matmul(p12.ap[:, 0:256], gt[hb:hb + 64, 2, :], ar, start=True, stop=True),
                     w=[p12], r=[gtb])
                K.op(PE, lambda: nc.tensor.matmul(p12.ap[:, 256:512], gt[hb:hb + 64, 3, :], ar, start=True, stop=True),
                     w=[p12], r=[gtb])
                yield
                K.op(DVE, lambda: nc.vector.tensor_tensor(out=M1_t[:], in0=p12.ap[:, 0:256], in1=mk2_t[:],
                                                          op=ALU.mult), w=[M1], r=[p12, mk2])
                K.op(DVE, lambda: nc.vector.tensor_tensor(out=M2_t[:], in0=p12.ap[:, 256:512], in1=mk2_t[:],
                                                          op=ALU.mult), w=[M2], r=[p12, mk2])
                p3 = psum4()
                K.op(PE, lambda: nc.tensor.matmul(p3.ap[:, 0:128], gt[hb:hb + 64, 0, :], gt[hb:hb + 64, 2, :],
                                                  start=True, stop=True), w=[p3], r=[gtb])
                yield
                K.op(DVE, lambda: nc.vector.tensor_tensor(out=XT0_t[:], in0=p3.ap[:, 0:128], in1=Ls_t[:],
                                                          op=ALU.mult), w=[XT0], r=[p3, Ls])
                Pc_t, Pc = hs["Pa"]
                Pn_t, Pn = hs["Pb"]
                K.op(POOL, lambda: nc.gpsimd.tensor_tensor(out=Pc_t[:], in0=M1_t[:, 0:128], in1=self.ident_b.ap,
                                                           op=ALU.add), w=[Pc], r=[M1, self.ident_b])
                X_ap, XT_ap, Xb, XTb = M1_t[:, 0:128], XT0_t[:], M1, XT0
                XXc = hs["XXa"]
                XXn = hs["XXb"]
                for j in range(L):
                    yield
                    px = psum4()
                    if j < L - 1:
                        K.op(PE, lambda: nc.tensor.matmul(px.ap[:, 0:128], XT_ap, X_ap, start=True, stop=True),
                             w=[px], r=[Xb, XTb])
                    K.op(PE, lambda: nc.tensor.matmul(px.ap[:, 128:256], X_ap, XT_ap, start=True, stop=True),
                         w=[px], r=[Xb, XTb])
                    XX_t, XX = XXc
                    yield
                    if j < L - 1:
                        K.op(ACT, lambda: nc.scalar.copy(out=XX_t[:], in_=px.ap[:, 0:256]), w=[XX], r=[px])
                    else:
                        K.op(ACT, lambda: nc.scalar.copy(out=XX_t[:, 128:256], in_=px.ap[:, 128:256]), w=[XX], r=[px])
                    yield
                    pp = psum4()
                    K.op(PE, lambda: nc.tensor.matmul(pp.ap[:, 0:128], XX_t[:, 128:256], Pc_t[:], start=True, stop=True),
                         w=[pp], r=[XX, Pc])
                    yield
                    K.op(DVE, lambda: nc.vector.tensor_tensor(out=Pn_t[:], in0=pp.ap[:, 0:128], in1=Pc_t[:], op=ALU.add),
                         w=[Pn], r=[pp, Pc])
                    X_ap, XT_ap, Xb, XTb = XX_t[:, 0:128], XX_t[:, 128:256], XX, XX
                    XXc, XXn = XXn, XXc
                    Pc_t, Pc, Pn_t, Pn = Pn_t, Pn, Pc_t, Pc
                W1_t, W1 = hs["W1"]
                AU_t, AU = hs["AU"]
                RbT_t, RbT = hs["RbT"]
                yield
                pw1 = psum4()
                K.op(PE, lambda: nc.tensor.matmul(pw1.ap[:, 0:64], M2_t[:, 0:128], v_t[:, hc], start=True, stop=True),
                     w=[pw1], r=[M2, v_b])
                yield
                K.op(ACT, lambda: nc.scalar.copy(out=W1_t[:], in_=pw1.ap[:, 0:64]), w=[W1], r=[pw1])
                yield
                pau = psum4()
                K.op(PE, lambda: nc.tensor.matmul(pau.ap[:, 0:64], Pc_t[:], at_t[:, hc], start=True, stop=True),
                     w=[pau], r=[Pc, at_b])
                K.op(PE, lambda: nc.tensor.matmul(pau.ap[:, 64:128], Pc_t[:], W1_t[:], start=True, stop=True),
                     w=[pau], r=[Pc, W1])
                yield
                K.op(ACT, lambda: nc.scalar.copy(out=AU_t[:], in_=pau.ap[:, 0:128]), w=[AU], r=[pau])
                yield
                if i == self.cfg.get("dbg_tile", 0) and h == self.cfg.get("dbg_head", 0):
                    self.dbg("M1", M1_t[:], M1)
                    self.dbg("P", Pc_t[:], Pc)
                    self.dbg("AU", AU_t[:], AU)
                pgf = psum4()
                K.op(PE, lambda: nc.tensor.matmul(pgf.ap[0:64, 0:64], AU_t[:, 0:64], bh_t[:, hc], start=True, stop=True),
                     w=[pgf], r=[AU, bh_b])
                K.op(PE, lambda: nc.tensor.matmul(pgf.ap[0:64, 64:128], bh_t[:, hc], AU_t[:, 64:128], start=True,
                                                  stop=False), w=[pgf], r=[AU, bh_b])
                K.op(PE, lambda: nc.tensor.matmul(pgf.ap[0:64, 64:128], kh_t[:, hc], v_t[:, hc], start=False, stop=True),
                     w=[pgf], r=[kh_b, v_b])
                prb = pgf
                K.op(PE, lambda: nc.tensor.matmul(prb.ap[0:64, 128:256], AU_t[:, 0:64], M1_t[:, 128:256], start=True,
                                                  stop=False), w=[prb], r=[AU, M1])
                K.op(PE, lambda: nc.tensor.matmul(prb.ap[0:64, 128:256], rt_t[:, hc], self.ident_b.ap, start=False,
                                                  stop=True), w=[prb], r=[rt_b, self.ident_b])
                yield
                K.op(DVE, lambda: nc.vector.scalar_tensor_tensor(
                    out=GTa_t[:, hc], in0=self.ident_f.ap[0:64, 0:64], scalar=pC_t[:, h:h + 1], in1=pgf.ap[0:64, 0:64],
                    op0=ALU.mult, op1=ALU.add), w=[GTa], r=[self.ident_f, pCs, pgf])
                K.op(DVE, lambda: nc.vector.tensor_copy(Fa_t[:, hc], pgf.ap[0:64, 64:128]), w=[Fa], r=[pgf])
                K.op(ACT, lambda: nc.scalar.copy(out=RbT_t[:], in_=prb.ap[0:64, 128:256]), w=[RbT], r=[prb])
                yield
                K.op(PE, lambda: nc.tensor.matmul(pY.ap[:, hc], M1_t[:, 128:256], AU_t[:, 64:128], start=True,
                                                  stop=False), w=[pY], r=[M1, AU])
                K.op(PE, lambda: nc.tensor.matmul(pY.ap[:, hc], M2_t[:, 128:256], v_t[:, hc], start=False,
                                                  stop=False), w=[pY], r=[M2, v_b])
                K.op(PE, lambda: nc.tensor.matmul(pY.ap[:, hc], RbT_t[0:64, :], Hb_t[0:64, hc], start=False,
                                                  stop=True), w=[pY], r=[RbT, Hb])

            def post(i):
                par = i % 2
                tb, ti = i // 4, i % 4
                zt_t, zt = zt2[par]
                gg_t, gg = gg2[par]
                st_t, st = st2[par]
                vv_ap = zt_t[:, 1024:1536]
                for h in range(8):
                    hc = slice(h * 64, (h + 1) * 64)
                    K.op(PE, lambda hc=hc: nc.tensor.matmul(pCH.ap[0:64, hc], GTa_t[:, hc], Hf_t[:, hc], start=True,
                                                            stop=True), w=[pCH], r=[GTa, Hf])
                K.op(DVE, lambda: nc.vector.tensor_tensor(out=Hf_t[:], in0=pCH.ap[0:64, :], in1=Fa_t[:], op=ALU.add),
                     w=[Hf], r=[pCH, Fa])
                K.op(ACT, lambda: nc.scalar.copy(out=Hb_t[:], in_=Hf_t[:]), w=[Hb], r=[Hf])
                K.op(ACT, lambda: nc.scalar.activation(out=tB_t, in_=pY.ap, func=AF.Square), w=[tB], r=[pY])
                K.op(DVE, lambda: nc.vector.tensor_reduce(out=st_t[:, 32:40], in_=v3(pY.ap), axis=AX.X, op=ALU.add),
                     w=[st], r=[pY])
                K.op(DVE, lambda: nc.vector.tensor_reduce(out=st_t[:, 40:48], in_=v3(tB_t), axis=AX.X, op=ALU.add),
                     w=[st], r=[tB])
                K.op(DVE, lambda: nc.vector.tensor_scalar(out=st_t[:, 32:40], in0=st_t[:, 32:40], scalar1=1.0 / 64,
                                                          scalar2=None, op0=ALU.mult), w=[st], r=[st])
                K.op(DVE, lambda: nc.vector.tensor_tensor(out=st_t[:, 48:56], in0=st_t[:, 32:40], in1=st_t[:, 32:40],
                                                          op=ALU.mult), w=[st], r=[st])
                K.op(DVE, lambda: nc.vector.scalar_tensor_tensor(out=st_t[:, 40:48], in0=st_t[:, 40:48], scalar=1.0 / 64,
                                                                 in1=st_t[:, 48:56], op0=ALU.mult, op1=ALU.subtract),
                     w=[st], r=[st])
                K.op(ACT, lambda: nc.scalar.activation(out=st_t[:, 40:48], in_=st_t[:, 40:48], func=AF.Sqrt, scale=1.0,
                                                       bias=self.eps_cols[LNX_EPS]), w=[st], r=[st, self.epsb])
                K.op(DVE, lambda: nc.vector.reciprocal(out=st_t[:, 56:64], in_=st_t[:, 40:48]), w=[st], r=[st])
                K.op(DVE, lambda: nc.vector.tensor_tensor(out=v3(tA_t), in0=v3(pY.ap), in1=bc3(st_t[:, 32:40]),
                                                          op=ALU.subtract), w=[tA], r=[pY, st])
                K.op(DVE, lambda: nc.vector.tensor_tensor(out=v3(tA_t), in0=v3(tA_t), in1=bc3(st_t[:, 56:64]),
                                                          op=ALU.mult), w=[tA], r=[tA, st])
                K.op(POOL, lambda: nc.gpsimd.tensor_tensor(out=tA_t, in0=tA_t, in1=lgb_t[:], op=ALU.mult),
                     w=[tA], r=[tA, lgb])
                K.op(POOL, lambda: nc.gpsimd.tensor_tensor(out=tA_t, in0=tA_t, in1=lbb_t[:], op=ALU.add),
                     w=[tA], r=[tA, lbb])
                K.op(DVE, lambda: nc.vector.tensor_tensor(out=v3(tB_t), in0=v3(vv_ap), in1=bc3(st_t[:, 24:32]),
                                                          op=ALU.mult), w=[tB], r=[zt, st])
                K.op(POOL, lambda: nc.gpsimd.tensor_tensor(out=tA_t, in0=tA_t, in1=tB_t, op=ALU.add),
                     w=[tA], r=[tA, tB])
                K.op(DVE, lambda: nc.vector.tensor_tensor(out=yb_t[:], in0=tA_t, in1=gg_t[:], op=ALU.mult),
                     w=[yb_b], r=[tA, gg])
                if i == self.cfg.get("dbg_tile", 0):
                    self.dbg("yb", yb_t[:], yb_b)
                pyt = psum4()
                py16 = pyt.ap.bitcast(BF16)
                for q in range(4):
                    K.op(PE, lambda q=q: nc.tensor.transpose(py16[:, q * 128:(q + 1) * 128], yb_t[:, q * 128:(q + 1) * 128],
                                                             self.ident_b.ap), w=[pyt], r=[yb_b, self.ident_b])
                K.op(ACT, lambda: nc.scalar.copy(out=ybT_t[:, :, ti * 128:(ti + 1) * 128],
                                                 in_=py16[:, 0:512].rearrange("p (q t) -> p q t", q=4)),
                     w=ybT, r=[pyt])
                if ti == 3:
                    mixb = [{tb: ybT[q]} for q in range(4)]
                    outproj(mixb, [4, 5, 6, 7], tbs=[tb])

            def step(g):
                try:
                    next(g)
                    return True
                except StopIteration:
                    return False

            def run(mains, bg):
                mains = list(mains)
                while mains:
                    for g_ in list(mains):
                        if not step(g_):
                            mains.remove(g_)
                    if bg[0] is not None and not step(bg[0]):
                        bg[0] = None

            bg = [pre_gen(0)]
            while bg[0] is not None:
                if not step(bg[0]):
                    bg[0] = None
            for i in range(16):
                bg = [pre_gen(i + 1) if i + 1 < 16 else None]
                for g0 in range(0, 8, NHS):
                    run([head_gen(h, i) for h in range(g0, g0 + NHS)], bg)
                while bg[0] is not None:
                    if not step(bg[0]):
                        bg[0] = None
                post(i)
            for ch in range(4, 8):
                self.wo.release(self.jobs_wo[("even", l, ch)])

    def mixer_odd(self, l):
        K, nc = self.K, self.nc
        o = l // 2
        PE, ACT, DVE, POOL, SP = K.PE, K.ACT, K.DVE, K.POOL, K.SP
        ps = self.ps

        def outproj(mixb, chs):
            wos = [self.wo.get(self.jobs_wo[("odd", l, ch)]) for ch in chs]
            for d in range(NC_):
                for tb in range(NTB):
                    po = self.psum()
                    for i, ch in enumerate(chs):
                        K.op(PE, lambda i=i: nc.tensor.matmul(po.ap, wos[i].ap[:, d * 128:(d + 1) * 128], mixb[i][tb].ap,
                                                              start=(i == 0), stop=(i == len(chs) - 1)),
                             w=[po], r=[wos[i], mixb[i][tb]])
                    xb = self.x[d][tb]
                    K.op(DVE, lambda: nc.vector.tensor_tensor(out=xb.ap, in0=po.ap, in1=xb.ap, op=ALU.add),
                         w=[xb], r=[po, xb])
            for ch in chs:
                self.wo.release(self.jobs_wo[("odd", l, ch)])

        with ExitStack() as es:
            small_t = K.sb(es, [128, ODD_NS], F32, "osmall")
            small = Buf(small_t[:], "osmall")
            K.dma(SP, small_t[:], self.d_odd_small[o], w=[small])
            qn_col = lambda c: small_t[:, c:c + 1]
            kvn_col = lambda c: small_t[:, 2:3]
            cw = lambda q, j: small_t[:, 3 + q * 31 + j: 3 + q * 31 + j + 1]
            cb = lambda q: small_t[:, 127 + q:128 + q]
            lng = lambda q: small_t[:, 131 + q:132 + q]
            lnb = lambda q: small_t[:, 135 + q:136 + q]
            cqn_t = K.sb(es, [128, 2, S], BF16, "cqn")
            cqn = [[Buf(cqn_t[:, c, tb * 512:(tb + 1) * 512]) for tb in range(NTB)] for c in range(2)]
            ckvn_t = K.sb(es, [128, S], BF16, "ckvn")
            ckvn = [[Buf(ckvn_t[:, tb * 512:(tb + 1) * 512]) for tb in range(NTB)]]
            kr_t = K.sb(es, [128, S], BF16, "krope")
            kr = [Buf(kr_t[:, tb * 512:(tb + 1) * 512]) for tb in range(NTB)]
            rope_t = K.sb(es, [128, 2, S], BF16, "rope")
            rope = Buf(rope_t[:], "rope")
            K.dma(POOL, rope_t[:], self.d_rope[:, :, :], w=[rope])
            wq_t = K.sb(es, [128, 2, 768], BF16, "wq")
            wq = Buf(wq_t[:], "wq")
            wqr_t = K.sb(es, [128, 2, 768], BF16, "wqr")
            wqr = Buf(wqr_t[:], "wqr")
            wkv_t = K.sb(es, [128, 1024], BF16, "wkv")
            wkv = Buf(wkv_t[:], "wkv")
            K.dma(POOL, wq_t[:], self.d_wq[o], w=[wq])
            K.dma(POOL, wqr_t[:], self.d_wqr[o], w=[wqr])
            K.dma(POOL, wkv_t[:], self.d_wkv[o], w=[wkv])

            with ExitStack() as esx:
                hglu_t = K.sb(esx, [128, 4, 30 + S], BF16, "hglu")
                hglu = [Buf(hglu_t[:, q, :]) for q in range(4)]
                for q in range(4):
                    K.op(DVE, lambda q=q: nc.vector.memset(hglu_t[:, q, 0:30], 0.0), w=[hglu[q]])
                with ExitStack() as esa:
                    hT_t = K.sb(esa, [128, NC_, S], BF16, "hT")
                    hT = [[Buf(hT_t[:, c, tb * 512:(tb + 1) * 512]) for tb in range(NTB)] for c in range(NC_)]
                    self.rmsnorm_fm(esa, self.x, self.gain_col(DEPTH + l), hT, NC_, D, RMS_EPS)
                    cq32_t = K.sb(esa, [128, 2, 512], F32, "cq32")
                    cq32 = [[Buf(cq32_t[:, c, :])] for c in range(2)]
                    ckv32_t = K.sb(esa, [128, 512], F32, "ckv32")
                    ckv32 = [[Buf(ckv32_t[:])]]
                    tmpa = Buf(K.sb(esa, [128, 512], F32, "tmpa")[:])
                    tmpb = Buf(K.sb(esa, [128, 512], F32, "tmpb")[:])

                    def proj(wb, lo, M, tb, pb):
                        for c in range(NC_):
                            K.op(PE, lambda c=c: nc.tensor.matmul(pb.ap[0:M, :], wb.ap[:, c, lo:lo + M], hT[c][tb].ap,
                                                                  start=(c == 0), stop=(c == NC_ - 1)),
                                 w=[pb], r=[wb, hT[c][tb]])

                    j0 = self.jobs_wi[("odd", l, 0)]
                    w0 = self.wi.get(j0)
                    for tb in range(NTB):
                        for c in range(2):
                            pb = self.psum()
                            proj(w0, c * 128, 128, tb, pb)
                            K.op(ACT, lambda c=c, pb=pb: nc.scalar.copy(out=cq32[c][0].ap, in_=pb.ap),
                                 w=[cq32[c][0]], r=[pb])
                        self.rmsnorm_fm(esa, [[cq32[0][0]], [cq32[1][0]]], qn_col, [[cqn[0][tb]], [cqn[1][tb]]], 2, 256,
                                        RMS_EPS, tbs=[0], extra_r=[small])
                    self.wi.release(j0)
                    j1 = self.jobs_wi[("odd", l, 1)]
                    w1 = self.wi.get(j1)
                    j2 = self.jobs_wi[("odd", l, 2)]
                    w2 = self.wi.get(j2)
                    for tb in range(NTB):
                        pb = self.psum()
                        proj(w1, 0, 128, tb, pb)
                        K.op(ACT, lambda pb=pb: nc.scalar.copy(out=ckv32[0][0].ap, in_=pb.ap), w=[ckv32[0][0]], r=[pb])
                        self.rmsnorm_fm(esa, [[ckv32[0][0]]], kvn_col, [[ckvn[0][tb]]], 1, 128, RMS_EPS, tbs=[0],
                                        extra_r=[small])
                        p1 = self.psum()
                        proj(w1, 128, 96, tb, p1)
                        p2 = self.psum()
                        proj(w2, 0, 96, tb, p2)
                        sl = slice(tb * 512, (tb + 1) * 512)
                        K.op(DVE, lambda p1=p1, sl=sl: nc.vector.tensor_tensor(out=tmpa.ap[64:96, :], in0=p1.ap[64:96, :],
                                                                               in1=rope_t[64:96, 0, sl], op=ALU.mult),
                             w=[tmpa], r=[p1, rope])
                        K.op(DVE, lambda p2=p2, sl=sl: nc.vector.tensor_tensor(out=tmpb.ap[64:96, :], in0=p2.ap[64:96, :],
                                                                               in1=rope_t[64:96, 1, sl], op=ALU.mult),
                             w=[tmpb], r=[p2, rope])
                        K.op(DVE, lambda tb=tb: nc.vector.tensor_tensor(out=kr[tb].ap[64:96, :], in0=tmpa.ap[64:96, :],
                                                                        in1=tmpb.ap[64:96, :], op=ALU.add),
                             w=[kr[tb]], r=[tmpa, tmpb])
                    self.wi.release(j1)
                    self.wi.release(j2)
                    for q in range(4):
                        jq = self.jobs_wi[("odd", l, 3 + q)]
                        wq_ = self.wi.get(jq)
                        for tb in range(NTB):
                            pa = self.psum()
                            proj(wq_, 0, 128, tb, pa)
                            pbb = self.psum()
                            proj(wq_, 128, 128, tb, pbb)
                            K.op(ACT, lambda pbb=pbb: nc.scalar.activation(out=tmpa.ap, in_=pbb.ap, func=AF.Sigmoid),
                                 w=[tmpa], r=[pbb])
                            K.op(DVE, lambda pa=pa, q=q, tb=tb: nc.vector.tensor_tensor(
                                out=hglu_t[:, q, 30 + tb * 512: 30 + (tb + 1) * 512], in0=pa.ap, in1=tmpa.ap,
                                op=ALU.mult), w=[hglu[q]], r=[pa, tmpa])
                        self.wi.release(jq)
                    K.barrier()

                with ExitStack() as esb:
                    yd_t = K.sb(esb, [128, 4, S], BF16, "ydT")
                    yd = [[Buf(yd_t[:, q, tb * 512:(tb + 1) * 512]) for tb in range(NTB)] for q in range(4)]
                    diag_t = [K.sb(esb, [128, 31, 128], BF16, "diag%d" % i) for i in range(2)]
                    diag = [Buf(t[:]) for t in diag_t]
                    c32_t = K.sb(esb, [128, 4, 512], F32, "c32")
                    c32 = [Buf(c32_t[:, q, :]) for q in range(4)]
                    sq32 = [Buf(K.sb(esb, [128, 512], F32, "sq32_%d" % i)[:]) for i in range(2)]
                    mean = Buf(K.sb(esb, [128, 512], F32, "mean")[:])
                    rstd = Buf(K.sb(esb, [128, 512], F32, "rstd")[:])
                    msq = Buf(K.sb(esb, [128, 512], F32, "msq")[:])
                    nd = 0
                    for tb in range(NTB):
                        p_s1 = ps[4]
                        p_s2 = ps[5]
                        for q in range(4):
                            dg, dg_t = diag[nd % 2], diag_t[nd % 2]
                            nd += 1
                            for j in range(31):
                                K.op(DVE, lambda j=j, q=q, dg_t=dg_t: nc.vector.tensor_scalar(
                                    out=dg_t[:, j, :], in0=self.ident_f.ap, scalar1=cw(q, j), scalar2=None,
                                    op0=ALU.mult), w=[dg], r=[self.ident_f, small])
                            pc = ps[q]
                            for j in range(31):
                                K.op(PE, lambda q=q, j=j, pc=pc, dg_t=dg_t: nc.tensor.matmul(
                                    pc.ap, dg_t[:, j, :], hglu_t[:, q, tb * 512 + j: tb * 512 + j + 512],
                                    start=(j == 0), stop=(j == 30)), w=[pc], r=[dg, hglu[q]])
                            K.op(ACT, lambda q=q, pc=pc: nc.scalar.activation(out=c32[q].ap, in_=pc.ap, func=AF.Identity,
                                                                              bias=cb(q), scale=1.0),
                                 w=[c32[q]], r=[pc, small])
                            sqb = sq32[q % 2]
                            K.op(ACT, lambda q=q, sqb=sqb: nc.scalar.activation(out=sqb.ap, in_=c32[q].ap, func=AF.Square),
                                 w=[sqb], r=[c32[q]])
                            K.op(PE, lambda q=q: nc.tensor.matmul(p_s1.ap, self.ones_f.ap, c32[q].ap, start=(q == 0),
                                                                  stop=(q == 3)), w=[p_s1], r=[self.ones_f, c32[q]])
                            K.op(PE, lambda q=q, sqb=sqb: nc.tensor.matmul(p_s2.ap, self.ones_f.ap, sqb.ap, start=(q == 0),
                                                                           stop=(q == 3)), w=[p_s2], r=[self.ones_f, sqb])
                        K.op(ACT, lambda: nc.scalar.mul(out=mean.ap, in_=p_s1.ap, mul=1.0 / 512), w=[mean], r=[p_s1])
                        K.op(DVE, lambda: nc.vector.tensor_tensor(out=msq.ap, in0=mean.ap, in1=mean.ap, op=ALU.mult),
                             w=[msq], r=[mean])
                        K.op(DVE, lambda: nc.vector.scalar_tensor_tensor(out=rstd.ap, in0=p_s2.ap, scalar=1.0 / 512,
                                                                         in1=msq.ap, op0=ALU.mult, op1=ALU.subtract),
                             w=[rstd], r=[p_s2, msq])
                        K.op(ACT, lambda: nc.scalar.activation(out=rstd.ap, in_=rstd.ap, func=AF.Sqrt, scale=1.0,
                                                               bias=self.eps_cols[LN_EPS]), w=[rstd], r=[rstd, self.epsb])
                        K.op(DVE, lambda: nc.vector.reciprocal(out=rstd.ap, in_=rstd.ap), w=[rstd], r=[rstd])
                        for q in range(4):
                            K.op(DVE, lambda q=q: nc.vector.tensor_tensor(out=c32[q].ap, in0=c32[q].ap, in1=mean.ap,
                                                                          op=ALU.subtract), w=[c32[q]], r=[c32[q], mean])
                            K.op(DVE, lambda q=q: nc.vector.tensor_tensor(out=c32[q].ap, in0=c32[q].ap, in1=rstd.ap,
                                                                          op=ALU.mult), w=[c32[q]], r=[c32[q], rstd])
                            K.op(ACT, lambda q=q, tb=tb: nc.scalar.activation(out=yd[q][tb].ap, in_=c32[q].ap,
                                                                              func=AF.Silu, bias=lnb(q), scale=lng(q)),
                                 w=[yd[q][tb]], r=[c32[q], small])
                    outproj(yd, [4, 5, 6, 7])
                    K.barrier()

            with ExitStack() as esc:
                yc_t = K.sb(esc, [128, 4, S], BF16, "ycT")
                yc = [[Buf(yc_t[:, c, tb * 512:(tb + 1) * 512]) for tb in range(NTB)] for c in range(4)]
                vall_t = K.sb(esc, [128, 16, 512], BF16, "vall")
                vall = [Buf(vall_t[:, i, :]) for i in range(16)]
                for i in range(16):
                    pv = self.psum()
                    tbi = i // 4
                    K.op(PE, lambda i=i, pv=pv: nc.tensor.matmul(
                        pv.ap, ckvn_t[:, i * 128:(i + 1) * 128], wkv_t[:, 512:1024],
                        start=True, stop=True), w=[pv], r=[ckvn[0][tbi], wkv])
                    K.op(ACT, lambda i=i, pv=pv: nc.scalar.copy(out=vall[i].ap, in_=pv.ap), w=[vall[i]], r=[pv])
                qT_t = [K.sb(esc, [128, S], BF16, "qT%d" % i) for i in range(2)]
                kT_t = [K.sb(esc, [128, S], BF16, "kT%d" % i) for i in range(2)]
                qT = [Buf(t[:]) for t in qT_t]
                kT = [Buf(t[:]) for t in kT_t]
                oc_t = [K.sb(esc, [128, S], BF16, "oc%d" % i) for i in range(2)]
                oc = [Buf(t[:]) for t in oc_t]
                pT_t = [K.sb(esc, [128, 512], BF16, "pT%d" % i) for i in range(3)]
                pT = [Buf(t[:]) for t in pT_t]
                rden = Buf(K.sb(esc, [128, 512], F32, "rden")[:])
                tq1 = Buf(K.sb(esc, [128, 512], F32, "tq1")[:])
                tq2 = Buf(K.sb(esc, [128, 512], F32, "tq2")[:])
                npt = 0
                for h in range(8):
                    qh, kh, och = qT[h % 2], kT[h % 2], oc[h % 2]
                    qh_t, kh_t, och_t = qT_t[h % 2], kT_t[h % 2], oc_t[h % 2]
                    for tb in range(NTB):
                        sl = slice(tb * 512, (tb + 1) * 512)
                        pq = ps[0 + (tb % 2)]
                        pr = ps[2 + (tb % 2)]
                        pk = ps[4 + (tb % 2)]
                        for c in range(2):
                            K.op(PE, lambda c=c, pq=pq: nc.tensor.matmul(pq.ap[0:96, :], wq_t[:, c, h * 96:(h + 1) * 96],
                                                                         cqn[c][tb].ap, start=(c == 0), stop=(c == 1)),
                                 w=[pq], r=[wq, cqn[c][tb]])
                        for c in range(2):
                            K.op(PE, lambda c=c, pr=pr: nc.tensor.matmul(pr.ap[0:96, :], wqr_t[:, c, h * 96:(h + 1) * 96],
                                                                         cqn[c][tb].ap, start=(c == 0), stop=(c == 1)),
                                 w=[pr], r=[wqr, cqn[c][tb]])
                        K.op(PE, lambda pk=pk: nc.tensor.matmul(pk.ap[0:64, :], wkv_t[:, h * 64:h * 64 + 64],
                                                                ckvn[0][tb].ap, start=True, stop=True),
                             w=[pk], r=[wkv, ckvn[0][tb]])
                        K.op(ACT, lambda pq=pq, sl=sl: nc.scalar.copy(out=qh_t[0:64, sl], in_=pq.ap[0:64, :]),
                             w=[qh], r=[pq])
                        K.op(DVE, lambda pq=pq, sl=sl: nc.vector.tensor_tensor(out=tq1.ap[64:96, :], in0=pq.ap[64:96, :],
                                                                               in1=rope_t[64:96, 0, sl], op=ALU.mult),
                             w=[tq1], r=[pq, rope])
                        K.op(DVE, lambda pr=pr, sl=sl: nc.vector.tensor_tensor(out=tq2.ap[64:96, :], in0=pr.ap[64:96, :],
                                                                               in1=rope_t[64:96, 1, sl], op=ALU.mult),
                             w=[tq2], r=[pr, rope])
                        K.op(DVE, lambda sl=sl: nc.vector.tensor_tensor(out=qh_t[64:96, sl], in0=tq1.ap[64:96, :],
                                                                        in1=tq2.ap[64:96, :], op=ALU.add),
                             w=[qh], r=[tq1, tq2])
                        K.op(ACT, lambda pk=pk, sl=sl: nc.scalar.copy(out=kh_t[0:64, sl], in_=pk.ap[0:64, :]),
                             w=[kh], r=[pk])
                        K.op(DVE, lambda sl=sl, tb=tb: nc.vector.tensor_copy(kh_t[64:96, sl], kr_t[64:96, sl]),
                             w=[kh], r=[kr[tb]])
                    for qb in range(NTB):
                        pO = ps[6]
                        pD = ps[7]
                        nkt = 4 * qb + 4
                        def emit_S(kt, qb=qb, nkt=nkt):
                            nonlocal npt
                            m = kt - 4 * qb
                            q0 = max(m, 0) * 128
                            pS = ps[kt % 2]
                            pt = pT[npt % 3]
                            pt_t = pT_t[npt % 3]
                            npt += 1
                            K.op(PE, lambda: nc.tensor.matmul(
                                pS.ap[:, q0:512], kh_t[0:96, kt * 128:(kt + 1) * 128],
                                qh_t[0:96, qb * 512 + q0:(qb + 1) * 512], start=True, stop=True),
                                w=[pS], r=[kh, qh])
                            K.op(ACT, lambda: nc.scalar.activation(
                                out=pt_t[:, q0:512], in_=pS.ap[:, q0:512], func=AF.Exp, scale=ATTN_SCALE),
                                w=[pt], r=[pS])
                            if m >= 0:
                                K.op(POOL, lambda: nc.gpsimd.memset(pt_t[64:128, q0:q0 + 64], 0.0),
                                     w=[pt], r=[pt])
                            return (pt, pt_t, q0)

                        def emit_PV(kt, st_, nkt=nkt):
                            pt, pt_t, q0 = st_
                            K.op(PE, lambda: nc.tensor.matmul(
                                pO.ap[0:64, q0:512], vall_t[:, kt, h * 64:(h + 1) * 64], pt_t[:, q0:512],
                                start=(kt == 0), stop=(kt == nkt - 1)), w=[pO], r=[vall[kt], pt])
                            K.op(PE, lambda: nc.tensor.matmul(
                                pD.ap[0:64, q0:512], self.ones_bf.ap[:, 0:64], pt_t[:, q0:512],
                                start=(kt == 0), stop=(kt == nkt - 1)), w=[pD], r=[self.ones_bf, pt])

                        nxt_st = emit_S(0)
                        for kt in range(nkt):
                            cur_st = nxt_st
                            if kt + 1 < nkt:
                                nxt_st = emit_S(kt + 1)
                            emit_PV(kt, cur_st)
                        K.op(DVE, lambda: nc.vector.reciprocal(out=rden.ap[0:64, :], in_=pD.ap[0:64, :]),
                             w=[rden], r=[pD])
                        K.op(DVE, lambda qb=qb: nc.vector.tensor_tensor(out=och_t[0:64, qb * 512:(qb + 1) * 512],
                                                                        in0=pO.ap[0:64, :], in1=rden.ap[0:64, :],
                                                                        op=ALU.mult), w=[och], r=[pO, rden])
                    pb0 = (h % 2) * 64
                    K.dma(SP, yc_t[pb0:pb0 + 64, h // 2, :], och_t[0:64, :], w=[yc[h // 2][tb] for tb in range(NTB)],
                          r=[och])
                outproj(yc, [0, 1, 2, 3])
                K.barrier()

    def final(self, do_norm):
        K, nc = self.K, self.nc
        with ExitStack() as es:
            if do_norm:
                o_t = K.sb(es, [128, NC_, S], F32, "oT")
                o = [[Buf(o_t[:, c, tb * 512:(tb + 1) * 512]) for tb in range(NTB)] for c in range(NC_)]
                self.rmsnorm_fm(es, self.x, self.gain_col(12), o, NC_, D, RMS_EPS)
            else:
                o = self.x
            toks = []
            for c in range(NC_):
                for tb in range(NTB):
                    toks.append(K.dma(K.SP, self.d_out[:, c, tb * 512:(tb + 1) * 512], o[c][tb].ap, r=[o[c][tb]]))
            for t in toks + list(self.dbg_toks.values()):
                K.wait_tok(K.SP, t)
            K.barrier()


def build_program(cfg):
    p = Prog(cfg)
    return p.build()


def prep_shared(inp):
    f32 = np.float32
    sh = {}
    gains = np.concatenate([inp["norm_ffn1"], inp["norm_mix"], inp["norm_ffn2"], inp["final_norm"][None]], axis=0)
    sh["gains"] = np.ascontiguousarray(gains.reshape(13, NC_, 128).transpose(2, 0, 1).reshape(128, 13 * NC_)).astype(f32)
    fin = np.stack([inp["ffn1_in"], inp["ffn2_in"]], axis=1).reshape(2 * DEPTH, D, 2 * DFF)
    fin = fin.reshape(2 * DEPTH, NC_, 128, 2, NHC, 128).transpose(0, 4, 2, 1, 3, 5)
    sh["ffn_in"] = np.ascontiguousarray(fin).reshape(2 * DEPTH, NHC, 128, NC_, 256)
    fout = np.stack([inp["ffn1_out"], inp["ffn2_out"]], axis=1).reshape(2 * DEPTH, NHC, 128, D)
    sh["ffn_out"] = np.ascontiguousarray(fout)
    wi = inp["odd_w_in"]
    Z = lambda n: np.zeros((2, D, n), f32)
    cq, ckv, krc = wi[:, :, 0:256], wi[:, :, 256:384], wi[:, :, 384:416]
    za, zb = wi[:, :, 416:928], wi[:, :, 928:1440]
    kr_rot = np.concatenate([krc[:, :, 16:32], krc[:, :, 0:16]], axis=2)
    slabs = [cq,
             np.concatenate([ckv, Z(64), krc, Z(32)], axis=2),
             np.concatenate([Z(64), kr_rot, Z(160)], axis=2)]
    for q in range(4):
        slabs.append(np.concatenate([za[:, :, q * 128:(q + 1) * 128], zb[:, :, q * 128:(q + 1) * 128]], axis=2))
    oin = np.stack(slabs, axis=1)
    oin = oin.reshape(2, 7, NC_, 128, 256).transpose(0, 1, 3, 2, 4)
    sh["odd_in"] = np.ascontiguousarray(oin).astype(f32)
    sh["odd_out"] = np.ascontiguousarray(inp["odd_w_out"].reshape(2, 8, 128, D)).astype(f32)
    wq = inp["wq_up"]
    sh["wq"] = np.ascontiguousarray(wq.reshape(2, 2, 128, 768).transpose(0, 2, 1, 3)).astype(f32)
    wq4 = wq.reshape(2, 256, 8, 96)
    wqr = np.concatenate([np.zeros((2, 256, 8, 64), f32), wq4[..., 80:96], wq4[..., 64:80]], axis=-1).reshape(2, 256, 768)
    sh["wqr"] = np.ascontiguousarray(wqr.reshape(2, 2, 128, 768).transpose(0, 2, 1, 3)).astype(f32)
    wkv4 = inp["wkv_up"].reshape(2, 128, 8, 128)
    sh["wkv"] = np.ascontiguousarray(np.concatenate([wkv4[..., 0:64].reshape(2, 128, 512),
                                                     wkv4[..., 64:128].reshape(2, 128, 512)], axis=-1)).astype(f32)
    col = lambda v, n: v.reshape(2, n, 128).transpose(0, 2, 1)
    cwp = inp["conv_w"].reshape(2, 31, 4, 128).transpose(0, 3, 2, 1).reshape(2, 128, 124)
    sh["odd_small"] = np.ascontiguousarray(np.concatenate(
        [col(inp["q_norm"], 2), col(inp["kv_norm"], 1), cwp, col(inp["conv_b"], 4), col(inp["conv_ln_g"], 4),
         col(inp["conv_ln_b"], 4)], axis=2)).astype(f32)
    sh["rope"] = rope_table()
    ew = inp["even_w_in"]
    ein = ew.reshape(2, NC_, 128, 11, 256).transpose(0, 3, 2, 1, 4)
    sh["even_in"] = np.ascontiguousarray(ein).astype(f32)
    sh["even_out"] = np.ascontiguousarray(inp["even_w_out"].reshape(2, 8, 128, D)).astype(f32)
    sh["gsu_wsT"] = np.ascontiguousarray(inp["gsu_ws"].transpose(0, 3, 1, 2)).astype(f32)
    sh["gsu_bs4"] = np.ascontiguousarray(np.tile(inp["gsu_bs"], (1, 1, 4)).reshape(2, 1, 4, 512)).astype(f32)
    rows = np.concatenate([inp["shift_mu"], inp["k_k"], inp["k_a"], inp["r_k"].reshape(2, 512), inp["lnx_g"],
                           inp["lnx_b"], inp["gsu_ln_g"], inp["gsu_ln_b"], inp["decay_w0"], inp["iclr_a0"]], axis=1)
    sh["even_rows"] = np.ascontiguousarray(rows.reshape(2, 1, EV_NR)).astype(f32)
    sh["lora_d"] = np.ascontiguousarray(inp["decay_up"]).astype(f32)
    sh["lora_i"] = np.ascontiguousarray(np.concatenate([np.zeros((2, 64, 512), f32), inp["iclr_up"]], axis=1)).astype(f32)
    sh["lora_g"] = np.ascontiguousarray(inp["gate_up"]).astype(f32)
    return sh


def rope_table():
    f32 = np.float32
    inv_freq = (f32(10000.0) ** (-(np.arange(0, 32, 2, dtype=f32) / f32(32)))).astype(f32)
    ang = (np.arange(S, dtype=f32)[:, None] * inv_freq[None, :]).astype(f32)
    cos = np.cos(ang.astype(np.float64)).astype(f32).T
    sin = np.sin(ang.astype(np.float64)).astype(f32).T
    t = np.zeros((128, 2, S), f32)
    t[64:80, 0] = cos
    t[80:96, 0] = cos
    t[64:80, 1] = -sin
    t[80:96, 1] = sin
    return t


def prep_x(x):
    return [np.ascontiguousarray(x[b].T.reshape(NC_, 128, S).transpose(1, 0, 2)) for b in range(x.shape[0])]


def unprep_out(o):
    return np.ascontiguousarray(o.transpose(2, 1, 0).reshape(S, D))


FULL_CFG = {"layers": DEPTH}


def kernel(**inputs):
    inp = {k: np.asarray(v) for k, v in inputs.items()}
    sh = prep_shared(inp)
    xs = prep_x(inp["x"].astype(np.float32))
    nc = build_program(FULL_CFG)
    in_maps = [dict(sh, xT=xs[b]) for b in range(len(xs))]
    res = run_bass_kernel_spmd(nc, in_maps, core_ids=list(range(len(xs))))
    out = np.stack([unprep_out(np.asarray(r["outT"])) for r in res.results], axis=0)
    return out.astype(np.float32)
```

```python
import numpy as np
from contextlib import ExitStack
import concourse.bass as bass
import concourse.mybir as mybir
from concourse.bass_utils import run_bass_kernel_spmd

F32 = mybir.dt.float32
BF16 = mybir.dt.bfloat16
AF = mybir.ActivationFunctionType
ALU = mybir.AluOpType
AX = mybir.AxisListType

S = 2048
D = 1024
NC_ = 8
NTB = 4
DFF = 2816
NHC = 22
DEPTH = 4
GROUPS = [(0, 4), (4, 8), (8, 12), (12, 16), (16, 19), (19, 22)]
RMS_EPS = 1e-6
LN_EPS = 1e-5
ODD_NS = 2 + 1 + 4 * 31 + 4 + 4 + 4
ATTN_SCALE = 96.0 ** -0.5
LNX_EPS = 64e-5
EV_NR = 1792 + 9 * 512
OFF_MU, OFF_KK, OFF_KA, OFF_RK, OFF_LG, OFF_LB, OFF_GLG, OFF_GLB, OFF_W0, OFF_A0 = (
    0, 1792, 2304, 2816, 3328, 3840, 4352, 4864, 5376, 5888)
C0 = float(np.exp(-0.5))
NEUMANN_L = 4
NHS = 4


class Eng:
    def __init__(self, name, eng, sem, skip_self):
        self.name = name
        self.eng = eng
        self.sem = sem
        self.count = 0
        self.seen = {}
        self.skip_self = skip_self


class Buf:
    __slots__ = ("ap", "lw", "rd", "sem", "dcnt", "name", "persist", "psum")

    def __init__(self, ap, name="", persist=False, psum=False):
        self.persist = persist
        self.psum = psum
        self.ap = ap
        self.lw = None
        self.rd = []
        self.sem = None
        self.dcnt = 0
        self.name = name


class KB:
    def __init__(self, nc, es):
        self.nc = nc
        self.es = es
        self.nsem = 0
        self.PE = Eng("pe", nc.tensor, self.mksem("s_pe"), True)
        self.ACT = Eng("act", nc.scalar, self.mksem("s_act"), False)
        self.DVE = Eng("dve", nc.vector, self.mksem("s_dve"), False)
        self.POOL = Eng("pool", nc.gpsimd, self.mksem("s_pool"), False)
        self.SP = Eng("sp", nc.sync, self.mksem("s_sp"), False)
        self.compute = [self.PE, self.ACT, self.DVE, self.POOL]
        self.all = self.compute + [self.SP]
        self.nuniq = 0
        self.sem_pool = []
        self.nwait = 0
        self.phase_stack = []

    def phase(self):
        kb = self

        class _Ph:
            def __enter__(self_):
                kb.phase_stack.append([])

            def __exit__(self_, *a):
                for b in kb.phase_stack.pop():
                    kb.sem_pool.append((b.sem, b.dcnt))
                    b.sem = None
                return False
        return _Ph()

    def rotate(self, E):
        E.sem = self.mksem("s_%s_r%d" % (E.name, self.nsem))
        E.count = 0

    def mksem(self, name):
        self.nsem += 1
        return self.es.enter_context(self.nc.semaphore(name))

    def sb(self, es, shape, dt, name=None):
        self.nuniq += 1
        return es.enter_context(self.nc.sbuf_tensor("sb_%s_%d" % (name or "t", self.nuniq), shape, dt))

    def _need(self, E, tok, lst):
        sem, val, src = tok
        if src is E and E.skip_self:
            return
        k = id(sem)
        if E.seen.get(k, 0) >= val:
            return
        for i, (s2, v2) in enumerate(lst):
            if s2 is sem:
                if val > v2:
                    lst[i] = (sem, val)
                return
        lst.append((sem, val))

    def _emit_waits(self, E, lst, inst_fn):
        for sem, val in lst[:-1]:
            E.eng.wait_ge(sem, val)
            E.seen[id(sem)] = val
            self.nwait += 1
        inst = inst_fn()
        if lst:
            sem, val = lst[-1]
            inst._wait_ge(sem, val)
            E.seen[id(sem)] = val
        return inst

    def _wait(self, E, tok):
        lst = []
        self._need(E, tok, lst)
        for sem, val in lst:
            E.eng.wait_ge(sem, val)
            E.seen[id(sem)] = val
            self.nwait += 1

    def op(self, E, fn, w=(), r=()):
        lst = []
        for b in r:
            if b.lw is not None:
                self._need(E, b.lw, lst)
            if b.psum:
                for t in b.rd:
                    if t[2] is not E:
                        self._need(E, t, lst)
        for b in w:
            if b.lw is not None:
                self._need(E, b.lw, lst)
            for t in b.rd:
                if t[2] is not E:
                    self._need(E, t, lst)
        inst = self._emit_waits(E, lst, fn)
        E.count += 1
        inst.then_inc(E.sem, 1)
        tok = (E.sem, E.count, E)
        for b in r:
            b.rd.append(tok)
        for b in w:
            b.lw = tok
            b.rd = []
        return inst

    def dma(self, Q, out_ap, in_ap, w=(), r=()):
        lst = []
        for b in r:
            if b.lw is not None:
                self._need(Q, b.lw, lst)
        for b in w:
            if b.lw is not None:
                self._need(Q, b.lw, lst)
            for t in b.rd:
                self._need(Q, t, lst)
        owner = w[0] if len(w) else r[0]
        if owner.sem is None:
            if self.sem_pool and self.phase_stack and not owner.persist:
                owner.sem, owner.dcnt = self.sem_pool.pop()
            else:
                owner.sem = self.mksem("s_b%d" % self.nsem)
            if self.phase_stack and not owner.persist:
                self.phase_stack[-1].append(owner)
        inst = self._emit_waits(Q, lst, lambda: Q.eng.dma_start(out=out_ap, in_=in_ap))
        owner.dcnt += 1
        inst.then_inc(owner.sem, 16)
        tok = (owner.sem, 16 * owner.dcnt, None)
        for b in r:
            b.rd.append(tok)
        for b in w:
            b.lw = tok
            b.rd = []
        return tok

    def barrier(self):
        for E in self.all:
            for P in self.compute:
                if P is E:
                    continue
                if P.count > 0:
                    self._wait(E, (P.sem, P.count, None))

    def wait_tok(self, E, tok):
        self._wait(E, tok)


class WStream:
    def __init__(self, K, Q, slots):
        self.K = K
        self.Q = Q
        self.slots = slots
        self.jobs = []
        self.issued = 0
        self.released = 0

    def add(self, dram_ap, view):
        self.jobs.append((dram_ap, view))
        return len(self.jobs) - 1

    def ensure(self, j):
        j = min(j, len(self.jobs) - 1)
        n = len(self.slots)
        while self.issued <= j:
            i = self.issued
            assert i - n < self.released, "weight ring slot still in use"
            slot = self.slots[i % n]
            dram_ap, view = self.jobs[i]
            self.K.dma(self.Q, view(slot.ap), dram_ap, w=[slot])
            self.issued += 1

    def get(self, j):
        self.ensure(j)
        return self.slots[j % len(self.slots)]

    def release(self, j):
        self.released = max(self.released, j + 1)
        self.ensure(self.released + len(self.slots) - 1)


class Prog:
    def __init__(self, cfg):
        self.cfg = cfg

    def build(self):
        cfg = self.cfg
        nc = bass.Bass("TRN2", target_bir_lowering=False)
        self.nc = nc
        dram = {}

        def din(name, shape, dt=F32):
            dram[name] = nc.dram_tensor(name, list(shape), dt, kind="ExternalInput").ap()
            return dram[name]

        self.d_x = din("xT", [128, NC_, S])
        self.d_gains = din("gains", [128, 13 * NC_])
        self.d_ffn_in = din("ffn_in", [2 * DEPTH, NHC, 128, NC_, 256])
        self.d_ffn_out = din("ffn_out", [2 * DEPTH, NHC, 128, D])
        self.d_odd_in = din("odd_in", [2, 7, 128, NC_, 256])
        self.d_odd_out = din("odd_out", [2, 8, 128, D])
        self.d_wq = din("wq", [2, 128, 2, 768])
        self.d_wqr = din("wqr", [2, 128, 2, 768])
        self.d_wkv = din("wkv", [2, 128, 1024])
        self.d_odd_small = din("odd_small", [2, 128, ODD_NS])
        self.d_rope = din("rope", [128, 2, S])
        self.d_even_in = din("even_in", [2, 11, 128, NC_, 256])
        self.d_even_out = din("even_out", [2, 8, 128, D])
        self.d_gsu_ws = din("gsu_wsT", [2, 128, 4, 128])
        self.d_gsu_bs = din("gsu_bs4", [2, 1, 4, 512])
        self.d_even_rows = din("even_rows", [2, 1, EV_NR])
        self.d_lora_d = din("lora_d", [2, 64, 512])
        self.d_lora_i = din("lora_i", [2, 128, 512])
        self.d_lora_g = din("lora_g", [2, 128, 512])
        self.d_zB = nc.dram_tensor("zB_scratch", [S + 1, 1792], F32).ap()
        self.zB = Buf(self.d_zB, "zB", persist=True)
        self.d_out = nc.dram_tensor("outT", [128, NC_, S], F32, kind="ExternalOutput").ap()

        self.dbg_toks = {}
        with ExitStack() as es:
            K = KB(nc, es)
            self.K = K
            xT = K.sb(es, [128, NC_, S], F32, "xT")
            self.xT = xT
            self.x = [[Buf(xT[:, c, tb * 512:(tb + 1) * 512], "x%d_%d" % (c, tb)) for tb in range(NTB)]
                      for c in range(NC_)]
            gains = K.sb(es, [128, 13 * NC_], F32, "gains")
            self.gains = Buf(gains[:], "gains")
            ones_bf = K.sb(es, [128, 128], BF16, "ones_bf")
            self.ones_bf = Buf(ones_bf[:], "ones")
            ones_f = K.sb(es, [128, 128], F32, "ones_f")
            self.ones_f = Buf(ones_f[:], "ones_f")
            K.op(K.DVE, lambda: nc.vector.memset(ones_f[:], 1.0), w=[self.ones_f])
            ident_f = K.sb(es, [128, 128], F32, "ident_f")
            self.ident_f = Buf(ident_f[:], "ident_f")
            K.op(K.POOL, lambda: nc.gpsimd.memset(ident_f[:], 1.0), w=[self.ident_f])
            K.op(K.POOL, lambda: nc.gpsimd.affine_select(out=ident_f[:], in_=ident_f[:], pattern=[[-1, 128]],
                                                         compare_op=ALU.is_equal, fill=0.0, base=0,
                                                         channel_multiplier=1), w=[self.ident_f], r=[self.ident_f])
            ident_b = K.sb(es, [128, 128], BF16, "ident_b")
            self.ident_b = Buf(ident_b[:], "ident_b")
            K.op(K.DVE, lambda: nc.vector.tensor_copy(ident_b[:], ident_f[:]), w=[self.ident_b], r=[self.ident_f])
            epst = K.sb(es, [128, 4], F32, "epst")
            self.epsb = Buf(epst[:], "eps")
            self.eps_cols = {}
            for i, e in enumerate([1e-6, 1e-5, 64e-5, 0.0]):
                K.op(K.DVE, lambda i=i, e=e: nc.vector.memset(epst[:, i:i + 1], e), w=[self.epsb])
                self.eps_cols[e] = epst[:, i:i + 1]
            self.nrm_sq = [Buf(K.sb(es, [128, 512], BF16)[:], "sq") for _ in range(2)]
            self.nrm_rs = [Buf(K.sb(es, [128, 512], F32)[:], "rstd") for _ in range(2)]
            wi_t = [K.sb(es, [128, NC_, 256], BF16, "wi%d" % i) for i in range(3)]
            wo_t = [K.sb(es, [128, D], BF16, "wo%d" % i) for i in range(8)]
            self.wi = WStream(K, K.POOL, [Buf(t[:], "wi", persist=True) for t in wi_t])
            self.wo = WStream(K, K.POOL, [Buf(t[:], "wo", persist=True) for t in wo_t])
            self.ps = [Buf(es.enter_context(nc.psum_tensor("ps%d" % i, [128, 512], F32))[:], "ps%d" % i, psum=True)
                       for i in range(8)]
            self.ps_i = 0

            self.plan_jobs()

            K.op(K.DVE, lambda: nc.vector.memset(ones_bf[:], 1.0), w=[self.ones_bf])
            K.dma(K.SP, gains[:], self.d_gains[:, :], w=[self.gains])
            for c in range(NC_):
                for tb in range(NTB):
                    K.dma(K.SP, self.x[c][tb].ap, self.d_x[:, c, tb * 512:(tb + 1) * 512], w=[self.x[c][tb]])
            self.wi.ensure(2)
            self.wo.ensure(3)

            for l in range(cfg["layers"]):
                if l in cfg.get("skip_layers", []):
                    continue
                if l > 0:
                    K.barrier()
                    K.rotate(K.PE)
                if cfg.get("ffn1", True):
                    with K.phase():
                        self.ffn(l, 0)
                if cfg.get("mixer", True):
                    with K.phase():
                        self.mixer(l)
                if cfg.get("ffn2", True):
                    with K.phase():
                        self.ffn(l, 1)
            with K.phase():
                self.final(cfg.get("final_norm", True))
        return nc

    def dbg(self, name, ap, buf):
        if not self.cfg.get("dbg"):
            return
        if name in self.dbg_toks:
            return
        if self.cfg.get("dbg_names") is not None and name not in self.cfg["dbg_names"]:
            return
        d = self.nc.dram_tensor("dbg_" + name, list(ap.shape), ap.dtype, kind="ExternalOutput").ap()
        self.dbg_toks[name] = self.K.dma(self.K.SP, d, ap, r=[buf])

    def psum(self):
        b = self.ps[self.ps_i % 8]
        self.ps_i += 1
        return b

    def plan_jobs(self):
        cfg = self.cfg
        self.jobs_wi = {}
        self.jobs_wo = {}
        full = lambda ap: ap
        for l in range(cfg["layers"]):
            if l in cfg.get("skip_layers", []):
                continue
            for which in range(2):
                if which == 1 and cfg.get("mixer", True):
                    if l % 2 == 0:
                        e = l // 2
                        for sl in range(11):
                            self.jobs_wi[("even", l, sl)] = self.wi.add(self.d_even_in[e, sl], full)
                        for ch in range(8):
                            self.jobs_wo[("even", l, ch)] = self.wo.add(self.d_even_out[e, ch], full)
                    if l % 2 == 1:
                        o = l // 2
                        for sl in range(7):
                            self.jobs_wi[("odd", l, sl)] = self.wi.add(self.d_odd_in[o, sl], full)
                        for ch in [4, 5, 6, 7, 0, 1, 2, 3]:
                            self.jobs_wo[("odd", l, ch)] = self.wo.add(self.d_odd_out[o, ch], full)
                if not cfg.get("ffn%d" % (which + 1), True):
                    continue
                f = l * 2 + which
                for (a, b) in GROUPS:
                    for hc in range(a, b):
                        self.jobs_wi[(f, hc)] = self.wi.add(self.d_ffn_in[f, hc], full)
                    for hc in range(a, b):
                        self.jobs_wo[(f, hc)] = self.wo.add(self.d_ffn_out[f, hc], full)

    def rmsnorm_fm(self, es, src, gain_col, dst, nchunks, nfeat, eps, tbs=range(NTB), extra_r=()):
        K, nc = self.K, self.nc
        sq, rs = self.nrm_sq, self.nrm_rs
        for tb in tbs:
            pb = self.psum()
            for c in range(nchunks):
                s = sq[c % 2]
                K.op(K.ACT, lambda s=s, c=c: nc.scalar.activation(out=s.ap, in_=src[c][tb].ap, func=AF.Square),
                     w=[s], r=[src[c][tb]])
                K.op(K.PE, lambda s=s, c=c: nc.tensor.matmul(pb.ap, self.ones_bf.ap, s.ap, start=(c == 0),
                                                             stop=(c == nchunks - 1)),
                     w=[pb], r=[s, self.ones_bf])
            r = rs[tb % 2]
            K.op(K.ACT, lambda: nc.scalar.activation(out=r.ap, in_=pb.ap, func=AF.Sqrt, scale=1.0 / nfeat,
                                                     bias=self.eps_ap(eps)),
                 w=[r], r=[pb, self.epsb])
            K.op(K.DVE, lambda: nc.vector.reciprocal(out=r.ap, in_=r.ap), w=[r], r=[r])
            for c in range(nchunks):
                K.op(K.DVE, lambda c=c: nc.vector.scalar_tensor_tensor(
                    out=dst[c][tb].ap, in0=src[c][tb].ap, scalar=gain_col(c), in1=r.ap,
                    op0=ALU.mult, op1=ALU.mult), w=[dst[c][tb]], r=[src[c][tb], r, self.gains] + list(extra_r))

    def eps_ap(self, eps):
        return self.eps_cols[eps]

    def gain_col(self, idx):
        return lambda c: self.gains.ap[:, idx * NC_ + c: idx * NC_ + c + 1]

    def ffn(self, l, which):
        K, nc = self.K, self.nc
        f = l * 2 + which
        gidx = (0 if which == 0 else 2) * DEPTH + l
        with ExitStack() as es:
            hT_t = K.sb(es, [128, NC_, S], BF16, "hT")
            hT = [[Buf(hT_t[:, c, tb * 512:(tb + 1) * 512]) for tb in range(NTB)] for c in range(NC_)]
            hid_t = K.sb(es, [128, 4, S], BF16, "hid")
            hid = [[Buf(hid_t[:, j, tb * 512:(tb + 1) * 512]) for tb in range(NTB)] for j in range(4)]
            sg = [Buf(K.sb(es, [128, 512], F32)[:], "sg") for _ in range(2)]
            self.rmsnorm_fm(es, self.x, self.gain_col(gidx), hT, NC_, D, RMS_EPS)
            n = 0
            for (a, b) in GROUPS:
                for j, hc in enumerate(range(a, b)):
                    ji = self.jobs_wi[(f, hc)]
                    wi = self.wi.get(ji)
                    for tb in range(NTB):
                        pg = self.psum()
                        pu = self.psum()
                        for c in range(NC_):
                            K.op(K.PE, lambda c=c: nc.tensor.matmul(pg.ap, wi.ap[:, c, 0:128], hT[c][tb].ap,
                                                                    start=(c == 0), stop=(c == NC_ - 1)),
                                 w=[pg], r=[wi, hT[c][tb]])
                        for c in range(NC_):
                            K.op(K.PE, lambda c=c: nc.tensor.matmul(pu.ap, wi.ap[:, c, 128:256], hT[c][tb].ap,
                                                                    start=(c == 0), stop=(c == NC_ - 1)),
                                 w=[pu], r=[wi, hT[c][tb]])
                        s = sg[n % 2]
                        n += 1
                        K.op(K.ACT, lambda: nc.scalar.activation(out=s.ap, in_=pg.ap, func=AF.Silu), w=[s], r=[pg])
                        K.op(K.DVE, lambda: nc.vector.tensor_tensor(out=hid[j][tb].ap, in0=s.ap, in1=pu.ap,
                                                                    op=ALU.mult), w=[hid[j][tb]], r=[s, pu])
                    self.wi.release(ji)
                wos = [self.wo.get(self.jobs_wo[(f, hc)]) for hc in range(a, b)]
                ng = b - a
                for d in range(NC_):
                    for tb in range(NTB):
                        po = self.psum()
                        for j in range(ng):
                            K.op(K.PE, lambda j=j: nc.tensor.matmul(po.ap, wos[j].ap[:, d * 128:(d + 1) * 128],
                                                                    hid[j][tb].ap, start=(j == 0), stop=(j == ng - 1)),
                                 w=[po], r=[wos[j], hid[j][tb]])
                        xb = self.x[d][tb]
                        K.op(K.DVE, lambda: nc.vector.scalar_tensor_tensor(
                            out=xb.ap, in0=po.ap, scalar=0.5, in1=xb.ap, op0=ALU.mult, op1=ALU.add),
                            w=[xb], r=[po, xb])
                for hc in range(a, b):
                    self.wo.release(self.jobs_wo[(f, hc)])
            K.barrier()

    def mixer(self, l):
        if l % 2 == 1:
            self.mixer_odd(l)
        else:
            self.mixer_even(l)

    def mixer_even(self, l):
        K, nc = self.K, self.nc
        e = l // 2
        PE, ACT, DVE, POOL, SP = K.PE, K.ACT, K.DVE, K.POOL, K.SP
        ps = self.ps
        cfg = self.cfg

        def outproj(mixb, chs, tbs=range(NTB)):
            wos = [self.wo.get(self.jobs_wo[("even", l, ch)]) for ch in chs]
            for d in range(NC_):
                for tb in tbs:
                    po = self.psum4()
                    for i, ch in enumerate(chs):
                        K.op(PE, lambda i=i: nc.tensor.matmul(po.ap, wos[i].ap[:, d * 128:(d + 1) * 128], mixb[i][tb].ap,
                                                              start=(i == 0), stop=(i == len(chs) - 1)),
                             w=[po], r=[wos[i], mixb[i][tb]])
                    xb = self.x[d][tb]
                    K.op(DVE, lambda: nc.vector.tensor_tensor(out=xb.ap, in0=po.ap, in1=xb.ap, op=ALU.add),
                         w=[xb], r=[po, xb])

        with ExitStack() as es:
            rows = self.d_even_rows[e]

            def bc_tile(esx, off, n, name):
                t = K.sb(esx, [128, n], F32, name)
                bf = Buf(t[:], name)
                K.dma(SP, t[:], rows[0:1, off:off + n].partition_broadcast(128), w=[bf])
                return t, bf

            with ExitStack() as esg:
                uT_t = K.sb(esg, [128, 4, S], BF16, "uT")
                uT = [[Buf(uT_t[:, c, tb * 512:(tb + 1) * 512]) for tb in range(NTB)] for c in range(4)]
                vtm_t = K.sb(esg, [128, 16, 512], BF16, "vtm")
                vtm = [Buf(vtm_t[:, i, :]) for i in range(16)]
                wsT_t = K.sb(esg, [128, 4, 128], BF16, "wsT")
                wsT = Buf(wsT_t[:], "wsT")
                K.dma(POOL, wsT_t[:], self.d_gsu_ws[e], w=[wsT])
                K.op(POOL, lambda: nc.gpsimd.memset(wsT_t[64:128, :, 0:64], 0.0), w=[wsT], r=[wsT])
                bs4_t = K.sb(esg, [1, 4, 512], F32, "bs4")
                bs4 = Buf(bs4_t[:], "bs4")
                K.dma(SP, bs4_t[:], self.d_gsu_bs[e], w=[bs4])
                with ExitStack() as esa:
                    hT_t = K.sb(esa, [128, NC_, S], BF16, "hT")
                    hT = [[Buf(hT_t[:, c, tb * 512:(tb + 1) * 512]) for tb in range(NTB)] for c in range(NC_)]
                    self.rmsnorm_fm(esa, self.x, self.gain_col(DEPTH + l), hT, NC_, D, RMS_EPS)
                    glg_t, glg = bc_tile(esa, OFF_GLG, 512, "glg")
                    glb_t, glb = bc_tile(esa, OFF_GLB, 512, "glb")
                    g32 = [Buf(K.sb(esa, [128, 512], F32, "g32_%d" % i)[:]) for i in range(2)]
                    gsq = Buf(K.sb(esa, [128, 512], F32, "gsq")[:])
                    st_t = K.sb(esa, [128, 8], F32, "vstat")
                    st = Buf(st_t[:], "vstat")
                    zst_t = [K.sb(esa, [128, 4, 256], F32, "zstage%d" % i) for i in range(2)]
                    zst = [Buf(t[:]) for t in zst_t]
                    zrow_t = K.sb(esa, [1, 1792], F32, "zrow")
                    zrow = Buf(zrow_t[:], "zrow")
                    K.op(DVE, lambda: nc.vector.memset(zrow_t[:], 0.0), w=[zrow])
                    K.dma(SP, self.d_zB[0:1, :], zrow_t[:], w=[self.zB], r=[zrow])
                    for sl in range(2):
                        jj = self.jobs_wi[("even", l, sl)]
                        wb = self.wi.get(jj)
                        for tb in range(NTB):
                            for half in range(2):
                                pb = self.psum()
                                for c in range(NC_):
                                    K.op(PE, lambda c=c, pb=pb: nc.tensor.matmul(
                                        pb.ap, wb.ap[:, c, half * 128:(half + 1) * 128], hT[c][tb].ap,
                                        start=(c == 0), stop=(c == NC_ - 1)), w=[pb], r=[wb, hT[c][tb]])
                                ub = uT[sl * 2 + half][tb]
                                K.op(ACT, lambda pb=pb, ub=ub: nc.scalar.activation(out=ub.ap, in_=pb.ap,
                                                                                    func=AF.Gelu_apprx_tanh),
                                     w=[ub], r=[pb])
                        self.wi.release(jj)
                    j2 = self.jobs_wi[("even", l, 2)]
                    w2 = self.wi.get(j2)
                    j3 = self.jobs_wi[("even", l, 3)]
                    w3 = self.wi.get(j3)
                    for i in range(16):
                        tb, off = i // 4, (i % 4) * 128
                        pv = self.psum()
                        for hi, wb in enumerate((w2, w3)):
                            for c in range(NC_):
                                K.op(PE, lambda c=c, wb=wb, hi=hi: nc.tensor.matmul(
                                    pv.ap[:, hi * 256:(hi + 1) * 256], hT_t[:, c, i * 128:(i + 1) * 128], wb.ap[:, c, :],
                                    start=(c == 0), stop=(c == NC_ - 1)), w=[pv], r=[wb, hT[c][tb]])
                        gb = g32[i % 2]
                        K.op(ACT, lambda gb=gb, pv=pv: nc.scalar.activation(out=gb.ap, in_=pv.ap, func=AF.Gelu_apprx_tanh),
                             w=[gb], r=[pv])
                        K.op(DVE, lambda gb=gb: nc.vector.tensor_reduce(out=st_t[:, 0:1], in_=gb.ap, axis=AX.X, op=ALU.add),
                             w=[st], r=[gb])
                        K.op(ACT, lambda gb=gb: nc.scalar.activation(out=gsq.ap, in_=gb.ap, func=AF.Square),
                             w=[gsq], r=[gb])
                        K.op(DVE, lambda: nc.vector.tensor_reduce(out=st_t[:, 1:2], in_=gsq.ap, axis=AX.X, op=ALU.add),
                             w=[st], r=[gsq])
                        K.op(DVE, lambda: nc.vector.tensor_scalar(out=st_t[:, 2:3], in0=st_t[:, 0:1], scalar1=1.0 / 512,
                                                                  scalar2=None, op0=ALU.mult), w=[st], r=[st])
                        K.op(DVE, lambda: nc.vector.tensor_tensor(out=st_t[:, 3:4], in0=st_t[:, 2:3], in1=st_t[:, 2:3],
                                                                  op=ALU.mult), w=[st], r=[st])
                        K.op(DVE, lambda: nc.vector.scalar_tensor_tensor(out=st_t[:, 4:5], in0=st_t[:, 1:2],
                                                                         scalar=1.0 / 512, in1=st_t[:, 3:4],
                                                                         op0=ALU.mult, op1=ALU.subtract), w=[st], r=[st])
                        K.op(ACT, lambda: nc.scalar.activation(out=st_t[:, 5:6], in_=st_t[:, 4:5], func=AF.Sqrt, scale=1.0,
                                                               bias=self.eps_cols[LN_EPS]), w=[st], r=[st, self.epsb])
                        K.op(DVE, lambda: nc.vector.reciprocal(out=st_t[:, 6:7], in_=st_t[:, 5:6]), w=[st], r=[st])
                        K.op(DVE, lambda gb=gb: nc.vector.tensor_scalar(out=gb.ap, in0=gb.ap, scalar1=st_t[:, 2:3],
                                                                        scalar2=st_t[:, 6:7], op0=ALU.subtract,
                                                                        op1=ALU.mult), w=[gb], r=[gb, st])
                        K.op(DVE, lambda gb=gb: nc.vector.tensor_tensor(out=gb.ap, in0=gb.ap, in1=glg_t[:], op=ALU.mult),
                             w=[gb], r=[gb, glg])
                        K.op(DVE, lambda gb=gb, i=i: nc.vector.tensor_tensor(out=vtm[i].ap, in0=gb.ap, in1=glb_t[:],
                                                                             op=ALU.add), w=[vtm[i]], r=[gb, glb])
                    self.wi.release(j2)
                    self.wi.release(j3)
                    nz = 0
                    for sl in range(4, 11):
                        if cfg.get("only") == "a":
                            jj = self.jobs_wi[("even", l, sl)]
                            self.wi.get(jj)
                            self.wi.release(jj)
                            continue
                        jj = self.jobs_wi[("even", l, sl)]
                        wb = self.wi.get(jj)
                        for tb in range(NTB):
                            zb_, zb_t = zst[nz % 2], zst_t[nz % 2]
                            nz += 1
                            for ti in range(4):
                                i = tb * 4 + ti
                                pz = self.psum()
                                for c in range(NC_):
                                    K.op(PE, lambda c=c, pz=pz, i=i: nc.tensor.matmul(
                                        pz.ap[:, 0:256], hT_t[:, c, i * 128:(i + 1) * 128], wb.ap[:, c, :],
                                        start=(c == 0), stop=(c == NC_ - 1)), w=[pz], r=[wb, hT[c][tb]])
                                K.op(ACT, lambda pz=pz, ti=ti, zb_t=zb_t: nc.scalar.copy(out=zb_t[:, ti, :],
                                                                                         in_=pz.ap[:, 0:256]),
                                     w=[zb_], r=[pz])
                            col0 = (sl - 4) * 256
                            dst = self.d_zB[1 + tb * 512: 1 + (tb + 1) * 512, col0:col0 + 256].rearrange(
                                "(ti p) n -> p ti n", p=128)
                            K.dma(SP, dst, zb_t[:], w=[self.zB], r=[zb_])
                        self.wi.release(jj)
                    K.barrier()
                with ExitStack() as esm:
                    ya_t = K.sb(esm, [128, 4, S], BF16, "yaT")
                    ya = [[Buf(ya_t[:, g, tb * 512:(tb + 1) * 512]) for tb in range(NTB)] for g in range(4)]
                    for tb in range(NTB):
                        for g in range(4):
                            pm = self.psum()
                            K.op(PE, lambda g=g, pm=pm: nc.tensor.matmul(pm.ap, self.ones_f.ap[0:1, :], bs4_t[0:1, g, :],
                                                                         start=True, stop=False),
                                 w=[pm], r=[self.ones_f, bs4])
                            for nb in range(4):
                                i = tb * 4 + nb
                                K.op(PE, lambda i=i, g=g, nb=nb, pm=pm: nc.tensor.matmul(
                                    pm.ap[:, nb * 128:(nb + 1) * 128], vtm_t[:, i, g * 128:(g + 1) * 128], wsT_t[:, g, :],
                                    start=False, stop=(nb == 3)), w=[pm], r=[vtm[i], wsT])
                            K.op(DVE, lambda g=g, tb=tb, pm=pm: nc.vector.tensor_tensor(
                                out=ya[g][tb].ap, in0=pm.ap, in1=uT[g][tb].ap, op=ALU.mult),
                                w=[ya[g][tb]], r=[pm, uT[g][tb]])
                    self.psum4 = self.psum
                    outproj(ya, [0, 1, 2, 3])
                    for ch in range(4):
                        self.wo.release(self.jobs_wo[("even", l, ch)])
                    K.barrier()

            if cfg.get("only") == "a":
                for ch in range(4, 8):
                    self.wo.get(self.jobs_wo[("even", l, ch)])
                    self.wo.release(self.jobs_wo[("even", l, ch)])
            else:
                self.rwkv(l, es, bc_tile, outproj)
            K.barrier()

    def rwkv(self, l, es, bc_tile, outproj):
        K, nc = self.K, self.nc
        e = l // 2
        PE, ACT, DVE, POOL, SP = K.PE, K.ACT, K.DVE, K.POOL, K.SP
        ps = self.ps
        rr = [0]
        NROT = 7

        pinned = []

        def psum4():
            for k in range(NROT):
                bb = ps[(rr[0] + k) % NROT]
                if any(bb is p_ for p_ in pinned):
                    continue
                if bb.lw is None or len(bb.rd) > 0:
                    rr[0] += k + 1
                    return bb
            raise AssertionError("no free rotating PSUM bank")
        self.psum4 = psum4
        pY = ps[7]
        rows = self.d_even_rows[e]
        with ExitStack() as esr:
            def T(shape, dt, name):
                t = K.sb(esr, shape, dt, name)
                return t, Buf(t[:], name)

            def T2(shape, dt, name):
                return [T(shape, dt, name + "_0"), T(shape, dt, name + "_1")]
            mu_t, mu = bc_tile(esr, OFF_MU, 1792, "mu")
            kkb_t, kkb = bc_tile(esr, OFF_KK, 512, "kkb")
            kab_t, kab = bc_tile(esr, OFF_KA, 512, "kab")
            rkb_t, rkb = bc_tile(esr, OFF_RK, 512, "rkb")
            lgb_t, lgb = bc_tile(esr, OFF_LG, 512, "lgb")
            lbb_t, lbb = bc_tile(esr, OFF_LB, 512, "lbb")
            dup_t, dup = T([65, 512], F32, "dup")
            K.dma(SP, dup_t[0:64, :], self.d_lora_d[e], w=[dup])
            K.dma(SP, dup_t[64:65, :], rows[0:1, OFF_W0:OFF_W0 + 512], w=[dup])
            iup_t, iup = T([65, 512], BF16, "iup")
            K.dma(POOL, iup_t[0:64, :], self.d_lora_i[e][64:128, :], w=[iup])
            K.dma(POOL, iup_t[64:65, :], rows[0:1, OFF_A0:OFF_A0 + 512], w=[iup])
            gup_t, gup = T([128, 512], BF16, "gup")
            K.dma(POOL, gup_t[:], self.d_lora_g[e], w=[gup])
            Ui_t, Ui = T([128, 128], F32, "Uincl")
            Us_t, Us = T([128, 128], F32, "Ustrict")
            Ls_t, Ls = T([128, 128], F32, "Lstrict")
            mk2_t, mk2 = T([128, 256], BF16, "mask2")
            for (t_, b_, cmp_, st_, cm_) in ((Ui_t, Ui, ALU.is_ge, 1, -1), (Us_t, Us, ALU.is_gt, 1, -1),
                                             (Ls_t, Ls, ALU.is_gt, -1, 1)):
                K.op(POOL, lambda t_=t_: nc.gpsimd.memset(t_[:], 1.0), w=[b_])
                K.op(POOL, lambda t_=t_, cmp_=cmp_, st_=st_, cm_=cm_: nc.gpsimd.affine_select(
                    out=t_[:], in_=t_[:], pattern=[[st_, 128]], compare_op=cmp_, fill=0.0, base=0,
                    channel_multiplier=cm_), w=[b_], r=[b_])
            K.op(POOL, lambda: nc.gpsimd.tensor_copy(mk2_t[:, 0:128], Us_t[:]), w=[mk2], r=[Us])
            K.op(POOL, lambda: nc.gpsimd.tensor_copy(mk2_t[:, 128:256], Ui_t[:]), w=[mk2], r=[Ui])
            mk2b_t, mk2b = mk2_t, mk2
            Hf_t, Hf = T([64, 512], F32, "Hf")
            Hb_t, Hb = T([64, 512], BF16, "Hb")
            K.op(DVE, lambda: nc.vector.memset(Hf_t[:], 0.0), w=[Hf])
            K.op(DVE, lambda: nc.vector.memset(Hb_t[:], 0.0), w=[Hb])
            GTa_t, GTa = T([64, 512], F32, "GTa")
            Fa_t, Fa = T([64, 512], F32, "Fa")
            ybT_t = K.sb(esr, [128, 4, 512], BF16, "ybT")
            ybT = [Buf(ybT_t[:, q, :]) for q in range(4)]
            scr_t = K.sb(esr, [128, 2048], F32, "scr")
            zs_t, zs = scr_t[:, 0:1792], Buf(scr_t[:, 0:1792], "zs")
            tA_t, tA = scr_t[:, 0:512], Buf(scr_t[:, 0:512], "tA")
            tB_t, tB = scr_t[:, 512:1024], Buf(scr_t[:, 512:1024], "tB")
            Ea_t, Ea = scr_t[:, 1024:1536], Buf(scr_t[:, 1024:1536], "Ea")
            Eb_t, Eb = scr_t[:, 1536:2048], Buf(scr_t[:, 1536:2048], "Eb")
            ZS = [zs, tA, tB, Ea, Eb]
            lwT_t, lwT = T([65, 128], F32, "lwT")
            K.op(DVE, lambda: nc.vector.memset(lwT_t[64:65, :], 1.0), w=[lwT])
            laT_t, laT = T([65, 128], BF16, "laT")
            K.op(DVE, lambda: nc.vector.memset(laT_t[64:65, :], 1.0), w=[laT])
            lgT_t, lgT = T([128, 128], BF16, "lgT")
            sg_t, sg = T([128, 512], F32, "sg")
            as_t, asg = T([128, 512], F32, "asig")
            kk_t, kk = T([128, 512], F32, "kk")
            km_t, km = T([128, 512], F32, "kmod")
            bq_t, bq = T([128, 512], F32, "bq")
            bt_t, bt_b = T([128, 512], BF16, "bt_b")
            kt_t, kt_b = T([128, 512], BF16, "kt_b")
            yb_t, yb_b = T([128, 512], BF16, "yb_b")
            T3 = lambda shape, dt, name: [T(shape, dt, name + "_%d" % k_) for k_ in range(3)]
            zt1 = T([128, 1792], F32, "zt")
            gg3 = T3([128, 512], BF16, "gg")
            st3 = T3([128, 64], F32, "rst")
            pA_t, pAb = T([128, 512], F32, "postA")
            pB_t, pBb = T([128, 512], F32, "postB")
            at2 = T2([128, 512], BF16, "at_b")
            rt2 = T2([128, 512], BF16, "rt_b")
            bh2 = T2([128, 512], BF16, "bh_b")
            kh2 = T2([128, 512], BF16, "kh_b")
            v3b = T3([128, 512], BF16, "v_b")
            pC2 = T2([64, 8], F32, "pC")
            GT2 = [[T([128, 4, 128], BF16, "GT%d_%d" % (p_, i_)) for i_ in range(4)] for p_ in range(2)]
            HS = []
            for i_ in range(NHS):
                d = {}
                for nm, shp in (("M1", [128, 256]), ("M2", [128, 256]), ("XT0", [128, 128]), ("XXa", [128, 256]),
                                ("XXb", [128, 256]), ("Pa", [128, 128]), ("X8", [128, 128]), ("S2", [128, 128]),
                                ("AT", [128, 128]), ("Bm", [128, 128]), ("XT8", [128, 128]), ("W1", [128, 64]),
                                ("AU", [128, 128]), ("RbT", [64, 128])):
                    t_ = K.sb(esr, shp, BF16, "%s_%d" % (nm, i_))
                    d[nm] = (t_, Buf(t_[:], nm))
                HS.append(d)

            v3 = lambda ap: ap.rearrange("p (h f) -> p h f", f=64)
            bc3 = lambda ap: ap.unsqueeze(2).broadcast_to([128, 8, 64])
            L = NEUMANN_L

            def pre_gen(i):
                par = i % 2
                zt_t, zt = zt1
                gg_t, gg = gg3[i % 3]
                st_t, st = st3[i % 3]
                at_t, at_b = at2[par]
                rt_t, rt_b = rt2[par]
                bh_t, bh_b = bh2[par]
                kh_t, kh_b = kh2[par]
                v_t, v_b = v3b[i % 3]
                pC_t, pCs = pC2[par]
                K.dma(SP, zt_t[:], self.d_zB[1 + i * 128: 1 + (i + 1) * 128, :], w=[zt], r=[self.zB])
                K.dma(SP, zs_t, self.d_zB[i * 128:(i + 1) * 128, :], w=ZS, r=[self.zB])
                yield
                K.op(DVE, lambda: nc.vector.tensor_tensor(out=zs_t, in0=zs_t, in1=zt_t[:], op=ALU.subtract),
                     w=ZS, r=[zs, zt])
                yield
                K.op(POOL, lambda: nc.gpsimd.tensor_tensor(out=zs_t, in0=zs_t, in1=mu_t[:], op=ALU.mult),
                     w=ZS, r=[zs, mu])
                yield
                K.op(DVE, lambda: nc.vector.tensor_tensor(out=zt_t[:], in0=zt_t[:], in1=zs_t, op=ALU.add),
                     w=[zt] + ZS, r=[zs, zt])
                r_ap, k_ap, vv_ap = zt_t[:, 0:512], zt_t[:, 512:1024], zt_t[:, 1024:1536]
                yield
                pl = psum4()
                K.op(PE, lambda: nc.tensor.transpose(pl.ap[0:64, 0:128], zt_t[:, 1536:1600], self.ident_f.ap),
                     w=[pl], r=[zt, self.ident_f])
                K.op(PE, lambda: nc.tensor.transpose(pl.ap[0:64, 128:256], zt_t[:, 1600:1664], self.ident_f.ap),
                     w=[pl], r=[zt, self.ident_f])
                K.op(PE, lambda: nc.tensor.transpose(pl.ap[:, 256:384], zt_t[:, 1664:1792], self.ident_f.ap),
                     w=[pl], r=[zt, self.ident_f])
                yield
                K.op(ACT, lambda: nc.scalar.activation(out=lwT_t[0:64, :], in_=pl.ap[0:64, 0:128], func=AF.Tanh),
                     w=[lwT], r=[pl])
                K.op(ACT, lambda: nc.scalar.activation(out=lgT_t[:], in_=pl.ap[:, 256:384], func=AF.Sigmoid),
                     w=[lgT], r=[pl])
                K.op(ACT, lambda: nc.scalar.copy(out=laT_t[0:64, :], in_=pl.ap[0:64, 128:256]), w=[laT], r=[pl])
                yield
                pw = psum4()
                K.op(PE, lambda: nc.tensor.matmul(pw.ap, lwT_t[0:65, :], dup_t[0:65, :], start=True, stop=True),
                     w=[pw], r=[lwT, dup])
                yield
                K.op(ACT, lambda: nc.scalar.activation(out=sg_t[:], in_=pw.ap, func=AF.Sigmoid), w=[sg], r=[pw])
                pa = psum4()
                K.op(PE, lambda: nc.tensor.matmul(pa.ap, laT_t[0:65, :], iup_t[0:65, :], start=True, stop=True),
                     w=[pa], r=[laT, iup])
                yield
                K.op(ACT, lambda: nc.scalar.activation(out=as_t[:], in_=pa.ap, func=AF.Sigmoid), w=[asg], r=[pa])
                K.op(DVE, lambda: nc.vector.tensor_tensor(out=tA_t, in0=k_ap, in1=kkb_t[:], op=ALU.mult),
                     w=[tA], r=[zt, kkb])
                yield
                pg = psum4()
                K.op(PE, lambda: nc.tensor.matmul(pg.ap, lgT_t[:], gup_t[:], start=True, stop=True),
                     w=[pg], r=[lgT, gup])
                pcs = psum4()
                pinned.append(pcs)
                K.op(PE, lambda: nc.tensor.matmul(pcs.ap, Ui_t[:], sg_t[:], start=True, stop=True), w=[pcs], r=[Ui, sg])
                K.op(DVE, lambda: nc.vector.scalar_tensor_tensor(out=km_t[:], in0=as_t[:], scalar=-1.0, in1=kab_t[:],
                                                                 op0=ALU.add, op1=ALU.mult), w=[km], r=[asg, kab])
                K.op(POOL, lambda: nc.gpsimd.tensor_tensor(out=tB_t, in0=tA_t, in1=tA_t, op=ALU.mult),
                     w=[tB], r=[tA])
                yield
                K.op(ACT, lambda: nc.scalar.copy(out=gg_t[:], in_=pg.ap), w=[gg], r=[pg])
                K.op(ACT, lambda: nc.scalar.activation(out=Ea_t, in_=pcs.ap, func=AF.Exp, scale=-C0), w=[Ea], r=[pcs])
                K.op(DVE, lambda: nc.vector.scalar_tensor_tensor(out=km_t[:], in0=km_t[:], scalar=1.0, in1=k_ap,
                                                                 op0=ALU.add, op1=ALU.mult), w=[km], r=[km, zt])
                K.op(DVE, lambda: nc.vector.tensor_reduce(out=st_t[:, 0:8], in_=v3(tB_t), axis=AX.X, op=ALU.add),
                     w=[st], r=[tB])
                yield
                pcx = psum4()
                K.op(PE, lambda: nc.tensor.matmul(pcx.ap, Us_t[:], sg_t[:], start=True, stop=True), w=[pcx], r=[Us, sg])
                K.op(DVE, lambda: nc.vector.tensor_tensor(out=rt_t[:], in0=r_ap, in1=Ea_t, op=ALU.mult),
                     w=[rt_b], r=[zt, Ea])
                K.op(ACT, lambda: nc.scalar.activation(out=st_t[:, 8:16], in_=st_t[:, 0:8], func=AF.Sqrt),
                     w=[st], r=[st])
                K.op(POOL, lambda: nc.gpsimd.tensor_tensor(out=tB_t, in0=r_ap, in1=km_t[:], op=ALU.mult),
                     w=[tB], r=[zt, km])
                yield
                K.op(ACT, lambda: nc.scalar.activation(out=Ea_t, in_=pcs.ap, func=AF.Exp, scale=C0), w=[Ea], r=[pcs])
                pinned.remove(pcs)
                K.op(ACT, lambda: nc.scalar.activation(out=Eb_t, in_=pcx.ap, func=AF.Exp, scale=-C0), w=[Eb], r=[pcx])
                K.op(DVE, lambda: nc.vector.tensor_scalar(out=st_t[:, 8:16], in0=st_t[:, 8:16], scalar1=1e-12,
                                                          scalar2=None, op0=ALU.max), w=[st], r=[st])
                K.op(DVE, lambda: nc.vector.reciprocal(out=st_t[:, 16:24], in_=st_t[:, 8:16]), w=[st], r=[st])
                K.op(DVE, lambda: nc.vector.tensor_tensor(out=v3(kk_t[:]), in0=v3(tA_t), in1=bc3(st_t[:, 16:24]),
                                                          op=ALU.mult), w=[kk], r=[tA, st])
                K.op(POOL, lambda: nc.gpsimd.tensor_tensor(out=tB_t, in0=tB_t, in1=rkb_t[:], op=ALU.mult),
                     w=[tB], r=[tB, rkb])
                yield
                prq = psum4()
                K.op(PE, lambda: nc.tensor.matmul(prq.ap, Ls_t[:], sg_t[:], start=True, stop=True), w=[prq], r=[Ls, sg])
                K.op(DVE, lambda: nc.vector.tensor_tensor(out=kt_t[:], in0=km_t[:], in1=Ea_t, op=ALU.mult),
                     w=[kt_b], r=[km, Ea])
                K.op(DVE, lambda: nc.vector.scalar_tensor_tensor(out=at_t[:], in0=kk_t[:], scalar=-1.0, in1=Eb_t,
                                                                 op0=ALU.mult, op1=ALU.mult), w=[at_b], r=[kk, Eb])
                K.op(POOL, lambda: nc.gpsimd.tensor_tensor(out=bq_t[:], in0=kk_t[:], in1=as_t[:], op=ALU.mult),
                     w=[bq], r=[kk, asg])
                yield
                K.op(DVE, lambda: nc.vector.tensor_reduce(out=st_t[:, 24:32], in_=v3(tB_t), axis=AX.X, op=ALU.add),
                     w=[st], r=[tB])
                K.op(DVE, lambda: nc.vector.tensor_tensor(out=bt_t[:], in0=bq_t[:], in1=Ea_t, op=ALU.mult),
                     w=[bt_b], r=[bq, Ea])
                K.op(ACT, lambda: nc.scalar.activation(out=Eb_t, in_=prq.ap, func=AF.Exp, scale=-C0), w=[Eb], r=[prq])
                K.op(POOL, lambda: nc.gpsimd.tensor_copy(v_t[:], vv_ap), w=[v_b], r=[zt])
                yield
                K.op(DVE, lambda: nc.vector.tensor_tensor(out=bh_t[:], in0=bq_t[:], in1=Eb_t, op=ALU.mult),
                     w=[bh_b], r=[bq, Eb])
                K.op(DVE, lambda: nc.vector.tensor_tensor(out=kh_t[:], in0=km_t[:], in1=Eb_t, op=ALU.mult),
                     w=[kh_b], r=[km, Eb])
                ptot = psum4()
                for h in range(8):
                    K.op(PE, lambda h=h: nc.tensor.matmul(ptot.ap[0:64, h:h + 1], sg_t[:, h * 64:(h + 1) * 64],
                                                          self.ones_f.ap[:, 0:1], start=True, stop=True),
                         w=[ptot], r=[sg, self.ones_f])
                yield
                K.op(ACT, lambda: nc.scalar.activation(out=pC_t[:], in_=ptot.ap[0:64, 0:8], func=AF.Exp, scale=-C0),
                     w=[pCs], r=[ptot])
                for pr in range(4):
                    yield
                    ptb = psum4()
                    pt16 = ptb.ap.bitcast(BF16)
                    for kind, (xt_, xb_) in enumerate(((at_t, at_b), (rt_t, rt_b), (bt_t, bt_b), (kt_t, kt_b))):
                        K.op(PE, lambda kind=kind, xt_=xt_: nc.tensor.transpose(
                            pt16[:, kind * 128:(kind + 1) * 128], xt_[:, pr * 128:(pr + 1) * 128], self.ident_b.ap),
                            w=[ptb], r=[xb_, self.ident_b])
                    yield
                    gT_t, gT = GT2[par][pr]
                    K.op(ACT, lambda: nc.scalar.copy(out=gT_t[:].rearrange("p k t -> p (k t)"), in_=pt16[:, 0:512]),
                         w=[gT], r=[ptb])

            def head_gen(h, i):
                par = i % 2
                at_t, at_b = at2[par]
                rt_t, rt_b = rt2[par]
                bh_t, bh_b = bh2[par]
                kh_t, kh_b = kh2[par]
                v_t, v_b = v3b[i % 3]
                pC_t, pCs = pC2[par]
                pr, hb = h // 2, (h % 2) * 64
                hs = HS[h % NHS]
                hc = slice(h * 64, (h + 1) * 64)
                gt, gtb = GT2[par][pr]
                ar = gt[hb:hb + 64, 0:2, :].rearrange("p k t -> p (k t)")
                M1_t, M1 = hs["M1"]
                M2_t, M2 = hs["M2"]
                XT0_t, XT0 = hs["XT0"]
                p12 = psum4()
                K.op(PE, lambda: nc.tensor.matmul(p12.ap[:, 0:256], gt[hb:hb + 64, 2, :], ar, start=True, stop=True),
                     w=[p12], r=[gtb])
                K.op(PE, lambda: nc.tensor.matmul(p12.ap[:, 256:512], gt[hb:hb + 64, 3, :], ar, start=True, stop=True),
                     w=[p12], r=[gtb])
                yield
                K.op(DVE, lambda: nc.vector.tensor_tensor(out=M1_t[:], in0=p12.ap[:, 0:256], in1=mk2_t[:],
                                                          op=ALU.mult), w=[M1], r=[p12, mk2])
                K.op(ACT, lambda: nc.scalar.copy(out=M2_t[:], in_=p12.ap[:, 256:512]), w=[M2], r=[p12])
                K.op(POOL, lambda: nc.gpsimd.tensor_tensor(out=M2_t[:], in0=M2_t[:], in1=mk2b_t[:], op=ALU.mult),
                     w=[M2], r=[M2, mk2b])
                p3 = psum4()
                K.op(PE, lambda: nc.tensor.matmul(p3.ap[:, 0:128], gt[hb:hb + 64, 0, :], gt[hb:hb + 64, 2, :],
                                                  start=True, stop=True), w=[p3], r=[gtb])
                yield
                K.op(DVE, lambda: nc.vector.tensor_tensor(out=XT0_t[:], in0=p3.ap[:, 0:128], in1=Ls_t[:],
                                                          op=ALU.mult), w=[XT0], r=[p3, Ls])
                X_ap = M1_t[:, 0:128]
                W1_t, W1 = hs["W1"]
                AU_t, AU = hs["AU"]
                RbT_t, RbT = hs["RbT"]
                XXa_t, XXa = hs["XXa"]
                XXb_t, XXb = hs["XXb"]
                X8_t, X8 = hs["X8"]
                S2_t, S2 = hs["S2"]
                AT_t, ATb = hs["AT"]
                Bm_t, Bmb = hs["Bm"]
                XT8_t, XT8 = hs["XT8"]
                Pc_t, Pc = hs["Pa"]
                yield
                px = psum4()
                K.op(PE, lambda: nc.tensor.matmul(px.ap[:, 0:128], XT0_t[:], X_ap, start=True, stop=True),
                     w=[px], r=[M1, XT0])
                K.op(PE, lambda: nc.tensor.matmul(px.ap[:, 128:256], X_ap, XT0_t[:], start=True, stop=True),
                     w=[px], r=[M1, XT0])
                yield
                K.op(ACT, lambda: nc.scalar.copy(out=XXa_t[:], in_=px.ap[:, 0:256]), w=[XXa], r=[px])
                K.op(DVE, lambda: nc.vector.tensor_tensor(out=XT0_t[:], in0=XT0_t[:], in1=self.ident_b.ap,
                                                          op=ALU.add), w=[XT0], r=[XT0, self.ident_b])
                yield
                p2 = psum4()
                K.op(PE, lambda: nc.tensor.matmul(p2.ap[:, 0:128], XXa_t[:, 128:256], XXa_t[:, 0:128], start=True,
                                                  stop=True), w=[p2], r=[XXa])
                K.op(PE, lambda: nc.tensor.matmul(p2.ap[:, 128:256], XXa_t[:, 0:128], XXa_t[:, 128:256], start=True,
                                                  stop=True), w=[p2], r=[XXa])
                K.op(PE, lambda: nc.tensor.matmul(p2.ap[:, 256:384], X_ap, XXa_t[:, 128:256], start=True, stop=True),
                     w=[p2], r=[M1, XXa])
                K.op(DVE, lambda: nc.vector.tensor_tensor(out=XT0_t[:], in0=XT0_t[:], in1=XXa_t[:, 128:256],
                                                          op=ALU.add), w=[XT0], r=[XT0, XXa])
                yield
                K.op(ACT, lambda: nc.scalar.copy(out=XXb_t[:], in_=p2.ap[:, 0:256]), w=[XXb], r=[p2])
                K.op(DVE, lambda: nc.vector.tensor_tensor(out=AT_t[:], in0=p2.ap[:, 256:384], in1=XT0_t[:], op=ALU.add),
                     w=[ATb], r=[p2, XT0])
                K.op(POOL, lambda: nc.gpsimd.tensor_tensor(out=S2_t[:], in0=XXb_t[:, 0:128], in1=self.ident_b.ap,
                                                           op=ALU.add), w=[S2], r=[XXb, self.ident_b])
                yield
                p3b = psum4()
                K.op(PE, lambda: nc.tensor.matmul(p3b.ap[:, 0:128], XXb_t[:, 128:256], XXb_t[:, 0:128], start=True,
                                                  stop=True), w=[p3b], r=[XXb])
                K.op(PE, lambda: nc.tensor.matmul(p3b.ap[:, 128:192], M2_t[:, 0:128], v_t[:, hc], start=True, stop=True),
                     w=[p3b], r=[M2, v_b])
                K.op(PE, lambda: nc.tensor.matmul(p3b.ap[:, 192:320], XXb_t[:, 0:128], XXb_t[:, 128:256], start=True,
                                                  stop=True), w=[p3b], r=[XXb])
                yield
                K.op(ACT, lambda: nc.scalar.copy(out=X8_t[:], in_=p3b.ap[:, 0:128]), w=[X8], r=[p3b])
                K.op(ACT, lambda: nc.scalar.copy(out=W1_t[:], in_=p3b.ap[:, 128:192]), w=[W1], r=[p3b])
                K.op(ACT, lambda: nc.scalar.copy(out=XT8_t[:], in_=p3b.ap[:, 192:320]), w=[XT8], r=[p3b])
                yield
                p4 = psum4()
                K.op(PE, lambda: nc.tensor.matmul(p4.ap[:, 0:128], XXb_t[:, 128:256], X8_t[:], start=True, stop=True),
                     w=[p4], r=[XXb, X8])
                K.op(DVE, lambda: nc.vector.tensor_tensor(out=S2_t[:], in0=S2_t[:], in1=X8_t[:], op=ALU.add),
                     w=[S2], r=[S2, X8])
                K.op(PE, lambda: nc.tensor.matmul(p4.ap[:, 128:256], X8_t[:], XT8_t[:], start=True, stop=True),
                     w=[p4], r=[X8, XT8])
                yield
                K.op(DVE, lambda: nc.vector.tensor_tensor(out=Bm_t[:], in0=p4.ap[:, 0:128], in1=S2_t[:], op=ALU.add),
                     w=[Bmb], r=[p4, S2])
                K.op(ACT, lambda: nc.scalar.copy(out=XT8_t[:], in_=p4.ap[:, 128:256]), w=[XT8], r=[p4])
                yield
                p5 = psum4()
                K.op(PE, lambda: nc.tensor.matmul(p5.ap[:, 0:128], AT_t[:], Bm_t[:], start=True, stop=True),
                     w=[p5], r=[ATb, Bmb])
                yield
                K.op(ACT, lambda: nc.scalar.copy(out=Pc_t[:], in_=p5.ap[:, 0:128]), w=[Pc], r=[p5])
                yield
                p6 = psum4()
                K.op(PE, lambda: nc.tensor.matmul(p6.ap[:, 0:128], XT8_t[:], Pc_t[:], start=True, stop=True),
                     w=[p6], r=[XT8, Pc])
                yield
                K.op(DVE, lambda: nc.vector.tensor_tensor(out=Pc_t[:], in0=p6.ap[:, 0:128], in1=Pc_t[:], op=ALU.add),
                     w=[Pc], r=[p6, Pc])
                yield
                pau = psum4()
                K.op(PE, lambda: nc.tensor.matmul(pau.ap[:, 0:64], Pc_t[:], at_t[:, hc], start=True, stop=True),
                     w=[pau], r=[Pc, at_b])
                K.op(PE, lambda: nc.tensor.matmul(pau.ap[:, 64:128], Pc_t[:], W1_t[:], start=True, stop=True),
                     w=[pau], r=[Pc, W1])
                yield
                K.op(ACT, lambda: nc.scalar.copy(out=AU_t[:], in_=pau.ap[:, 0:128]), w=[AU], r=[pau])
                yield
                if i == self.cfg.get("dbg_tile", 0) and h == self.cfg.get("dbg_head", 0):
                    self.dbg("M1", M1_t[:], M1)
                    self.dbg("P", Pc_t[:], Pc)
                    self.dbg("AU", AU_t[:], AU)
                pgf = psum4()
                K.op(PE, lambda: nc.tensor.matmul(pgf.ap[0:64, 0:64], AU_t[:, 0:64], bh_t[:, hc], start=True, stop=True),
                     w=[pgf], r=[AU, bh_b])
                K.op(PE, lambda: nc.tensor.matmul(pgf.ap[0:64, 64:128], bh_t[:, hc], AU_t[:, 64:128], start=True,
                                                  stop=False), w=[pgf], r=[AU, bh_b])
                K.op(PE, lambda: nc.tensor.matmul(pgf.ap[0:64, 64:128], kh_t[:, hc], v_t[:, hc], start=False, stop=True),
                     w=[pgf], r=[kh_b, v_b])
                prb = pgf
                K.op(PE, lambda: nc.tensor.matmul(prb.ap[0:64, 128:256], AU_t[:, 0:64], M1_t[:, 128:256], start=True,
                                                  stop=False), w=[prb], r=[AU, M1])
                K.op(PE, lambda: nc.tensor.matmul(prb.ap[0:64, 128:256], rt_t[:, hc], self.ident_b.ap, start=False,
                                                  stop=True), w=[prb], r=[rt_b, self.ident_b])
                yield
                K.op(DVE, lambda: nc.vector.scalar_tensor_tensor(
                    out=GTa_t[:, hc], in0=self.ident_f.ap[0:64, 0:64], scalar=pC_t[:, h:h + 1], in1=pgf.ap[0:64, 0:64],
                    op0=ALU.mult, op1=ALU.add), w=[GTa], r=[self.ident_f, pCs, pgf])
                K.op(ACT, lambda: nc.scalar.copy(out=Fa_t[:, hc], in_=pgf.ap[0:64, 64:128]), w=[Fa], r=[pgf])
                K.op(ACT, lambda: nc.scalar.copy(out=RbT_t[:], in_=prb.ap[0:64, 128:256]), w=[RbT], r=[prb])
                yield
                K.op(PE, lambda: nc.tensor.matmul(pY.ap[:, hc], M1_t[:, 128:256], AU_t[:, 64:128], start=True,
                                                  stop=False), w=[pY], r=[M1, AU])
                K.op(PE, lambda: nc.tensor.matmul(pY.ap[:, hc], M2_t[:, 128:256], v_t[:, hc], start=False,
                                                  stop=False), w=[pY], r=[M2, v_b])
                K.op(PE, lambda: nc.tensor.matmul(pY.ap[:, hc], RbT_t[0:64, :], Hb_t[0:64, hc], start=False,
                                                  stop=True), w=[pY], r=[RbT, Hb])

            def recurrence(i):
                pH = psum4()
                for h in range(8):
                    hc = slice(h * 64, (h + 1) * 64)
                    K.op(PE, lambda hc=hc: nc.tensor.matmul(pH.ap[0:64, hc], GTa_t[:, hc], Hf_t[:, hc], start=True,
                                                            stop=True), w=[pH], r=[GTa, Hf])
                K.op(DVE, lambda: nc.vector.tensor_tensor(out=Hf_t[:], in0=pH.ap[0:64, :], in1=Fa_t[:], op=ALU.add),
                     w=[Hf], r=[pH, Fa])
                K.op(ACT, lambda: nc.scalar.copy(out=Hb_t[:], in_=Hf_t[:]), w=[Hb], r=[Hf])

            def post_gen(i):
                tb, ti = i // 4, i % 4
                gg_t, gg = gg3[i % 3]
                st_t, st = st3[i % 3]
                v_t, v_b = v3b[i % 3]
                K.op(ACT, lambda: nc.scalar.activation(out=pB_t[:], in_=pY.ap, func=AF.Square), w=[pBb], r=[pY])
                K.op(DVE, lambda: nc.vector.tensor_reduce(out=st_t[:, 32:40], in_=v3(pY.ap), axis=AX.X, op=ALU.add),
                     w=[st], r=[pY])
                K.op(ACT, lambda: nc.scalar.copy(out=pA_t[:], in_=pY.ap), w=[pAb], r=[pY])
                yield
                K.op(DVE, lambda: nc.vector.tensor_reduce(out=st_t[:, 40:48], in_=v3(pB_t[:]), axis=AX.X, op=ALU.add),
                     w=[st], r=[pBb])
                K.op(DVE, lambda: nc.vector.tensor_scalar(out=st_t[:, 32:40], in0=st_t[:, 32:40], scalar1=1.0 / 64,
                                                          scalar2=None, op0=ALU.mult), w=[st], r=[st])
                K.op(DVE, lambda: nc.vector.tensor_tensor(out=st_t[:, 48:56], in0=st_t[:, 32:40], in1=st_t[:, 32:40],
                                                          op=ALU.mult), w=[st], r=[st])
                K.op(DVE, lambda: nc.vector.scalar_tensor_tensor(out=st_t[:, 40:48], in0=st_t[:, 40:48], scalar=1.0 / 64,
                                                                 in1=st_t[:, 48:56], op0=ALU.mult, op1=ALU.subtract),
                     w=[st], r=[st])
                yield
                K.op(ACT, lambda: nc.scalar.activation(out=st_t[:, 40:48], in_=st_t[:, 40:48], func=AF.Sqrt, scale=1.0,
                                                       bias=self.eps_cols[LNX_EPS]), w=[st], r=[st, self.epsb])
                yield
                K.op(DVE, lambda: nc.vector.reciprocal(out=st_t[:, 56:64], in_=st_t[:, 40:48]), w=[st], r=[st])
                K.op(DVE, lambda: nc.vector.tensor_tensor(out=v3(pA_t[:]), in0=v3(pA_t[:]), in1=bc3(st_t[:, 32:40]),
                                                          op=ALU.subtract), w=[pAb], r=[pAb, st])
                K.op(DVE, lambda: nc.vector.tensor_tensor(out=v3(pA_t[:]), in0=v3(pA_t[:]), in1=bc3(st_t[:, 56:64]),
                                                          op=ALU.mult), w=[pAb], r=[pAb, st])
                K.op(DVE, lambda: nc.vector.tensor_tensor(out=v3(pB_t[:]), in0=v3(v_t[:]), in1=bc3(st_t[:, 24:32]),
                                                          op=ALU.mult), w=[pBb], r=[v_b, st])
                yield
                K.op(POOL, lambda: nc.gpsimd.tensor_tensor(out=pA_t[:], in0=pA_t[:], in1=lgb_t[:], op=ALU.mult),
                     w=[pAb], r=[pAb, lgb])
                yield
                K.op(POOL, lambda: nc.gpsimd.tensor_tensor(out=pB_t[:], in0=pB_t[:], in1=lbb_t[:], op=ALU.add),
                     w=[pBb], r=[pBb, lbb])
                yield
                K.op(DVE, lambda: nc.vector.tensor_tensor(out=pA_t[:], in0=pA_t[:], in1=pB_t[:], op=ALU.add),
                     w=[pAb], r=[pAb, pBb])
                K.op(DVE, lambda: nc.vector.tensor_tensor(out=yb_t[:], in0=pA_t[:], in1=gg_t[:], op=ALU.mult),
                     w=[yb_b], r=[pAb, gg])
                if i == self.cfg.get("dbg_tile", 0):
                    self.dbg("yb", yb_t[:], yb_b)
                yield
                pyt = psum4()
                py16 = pyt.ap.bitcast(BF16)
                for q in range(4):
                    K.op(PE, lambda q=q: nc.tensor.transpose(py16[:, q * 128:(q + 1) * 128], yb_t[:, q * 128:(q + 1) * 128],
                                                             self.ident_b.ap), w=[pyt], r=[yb_b, self.ident_b])
                yield
                K.op(ACT, lambda: nc.scalar.copy(out=ybT_t[:, :, ti * 128:(ti + 1) * 128],
                                                 in_=py16[:, 0:512].rearrange("p (q t) -> p q t", q=4)),
                     w=ybT, r=[pyt])
                if ti == 3:
                    wos = [self.wo.get(self.jobs_wo[("even", l, ch)]) for ch in (4, 5, 6, 7)]
                    for d in range(NC_):
                        yield
                        po = psum4()
                        for q in range(4):
                            K.op(PE, lambda q=q: nc.tensor.matmul(po.ap, wos[q].ap[:, d * 128:(d + 1) * 128], ybT[q].ap,
                                                                  start=(q == 0), stop=(q == 3)),
                                 w=[po], r=[wos[q], ybT[q]])
                        yield
                        xb = self.x[d][tb]
                        K.op(DVE, lambda: nc.vector.tensor_tensor(out=xb.ap, in0=po.ap, in1=xb.ap, op=ALU.add),
                             w=[xb], r=[po, xb])

            def step(g):
                try:
                    next(g)
                    return True
                except StopIteration:
                    return False

            def run(mains, bgs):
                mains = list(mains)
                while mains:
                    for g_ in list(mains):
                        if not step(g_):
                            mains.remove(g_)
                    for g_ in list(bgs):
                        if not step(g_):
                            bgs.remove(g_)

            def drain(bgs):
                while bgs:
                    for g_ in list(bgs):
                        if not step(g_):
                            bgs.remove(g_)

            drain([pre_gen(0)])
            for i in range(16):
                bgs = []
                if i + 1 < 16:
                    bgs.append(pre_gen(i + 1))
                if i >= 1:
                    bgs.append(post_gen(i - 1))
                for g0 in range(0, 8, NHS):
                    run([head_gen(h, i) for h in range(g0, g0 + NHS)], bgs)
                drain(bgs)
                recurrence(i)
            drain([post_gen(15)])
            for ch in range(4, 8):
                self.wo.release(self.jobs_wo[("even", l, ch)])

    def mixer_odd(self, l):
        K, nc = self.K, self.nc
        o = l // 2
        PE, ACT, DVE, POOL, SP = K.PE, K.ACT, K.DVE, K.POOL, K.SP
        ps = self.ps

        def outproj(mixb, chs):
            wos = [self.wo.get(self.jobs_wo[("odd", l, ch)]) for ch in chs]
            for d in range(NC_):
                for tb in range(NTB):
                    po = self.psum()
                    for i, ch in enumerate(chs):
                        K.op(PE, lambda i=i: nc.tensor.matmul(po.ap, wos[i].ap[:, d * 128:(d + 1) * 128], mixb[i][tb].ap,
                                                              start=(i == 0), stop=(i == len(chs) - 1)),
                             w=[po], r=[wos[i], mixb[i][tb]])
                    xb = self.x[d][tb]
                    K.op(DVE, lambda: nc.vector.tensor_tensor(out=xb.ap, in0=po.ap, in1=xb.ap, op=ALU.add),
                         w=[xb], r=[po, xb])
            for ch in chs:
                self.wo.release(self.jobs_wo[("odd", l, ch)])

        with ExitStack() as es:
            small_t = K.sb(es, [128, ODD_NS], F32, "osmall")
            small = Buf(small_t[:], "osmall")
            K.dma(SP, small_t[:], self.d_odd_small[o], w=[small])
            qn_col = lambda c: small_t[:, c:c + 1]
            kvn_col = lambda c: small_t[:, 2:3]
            cw = lambda q, j: small_t[:, 3 + q * 31 + j: 3 + q * 31 + j + 1]
            cb = lambda q: small_t[:, 127 + q:128 + q]
            lng = lambda q: small_t[:, 131 + q:132 + q]
            lnb = lambda q: small_t[:, 135 + q:136 + q]
            cqn_t = K.sb(es, [128, 2, S], BF16, "cqn")
            cqn = [[Buf(cqn_t[:, c, tb * 512:(tb + 1) * 512]) for tb in range(NTB)] for c in range(2)]
            ckvn_t = K.sb(es, [128, S], BF16, "ckvn")
            ckvn = [[Buf(ckvn_t[:, tb * 512:(tb + 1) * 512]) for tb in range(NTB)]]
            kr_t = K.sb(es, [128, S], BF16, "krope")
            kr = [Buf(kr_t[:, tb * 512:(tb + 1) * 512]) for tb in range(NTB)]
            rope_t = K.sb(es, [128, 2, S], BF16, "rope")
            rope = Buf(rope_t[:], "rope")
            K.dma(POOL, rope_t[:], self.d_rope[:, :, :], w=[rope])
            wq_t = K.sb(es, [128, 2, 768], BF16, "wq")
            wq = Buf(wq_t[:], "wq")
            wqr_t = K.sb(es, [128, 2, 768], BF16, "wqr")
            wqr = Buf(wqr_t[:], "wqr")
            wkv_t = K.sb(es, [128, 1024], BF16, "wkv")
            wkv = Buf(wkv_t[:], "wkv")
            K.dma(POOL, wq_t[:], self.d_wq[o], w=[wq])
            K.dma(POOL, wqr_t[:], self.d_wqr[o], w=[wqr])
            K.dma(POOL, wkv_t[:], self.d_wkv[o], w=[wkv])

            with ExitStack() as esx:
                hglu_t = K.sb(esx, [128, 4, 30 + S], BF16, "hglu")
                hglu = [Buf(hglu_t[:, q, :]) for q in range(4)]
                for q in range(4):
                    K.op(DVE, lambda q=q: nc.vector.memset(hglu_t[:, q, 0:30], 0.0), w=[hglu[q]])
                with ExitStack() as esa:
                    hT_t = K.sb(esa, [128, NC_, S], BF16, "hT")
                    hT = [[Buf(hT_t[:, c, tb * 512:(tb + 1) * 512]) for tb in range(NTB)] for c in range(NC_)]
                    self.rmsnorm_fm(esa, self.x, self.gain_col(DEPTH + l), hT, NC_, D, RMS_EPS)
                    cq32_t = K.sb(esa, [128, 2, 512], F32, "cq32")
                    cq32 = [[Buf(cq32_t[:, c, :])] for c in range(2)]
                    ckv32_t = K.sb(esa, [128, 512], F32, "ckv32")
                    ckv32 = [[Buf(ckv32_t[:])]]
                    tmpa = Buf(K.sb(esa, [128, 512], F32, "tmpa")[:])
                    tmpb = Buf(K.sb(esa, [128, 512], F32, "tmpb")[:])

                    def proj(wb, lo, M, tb, pb):
                        for c in range(NC_):
                            K.op(PE, lambda c=c: nc.tensor.matmul(pb.ap[0:M, :], wb.ap[:, c, lo:lo + M], hT[c][tb].ap,
                                                                  start=(c == 0), stop=(c == NC_ - 1)),
                                 w=[pb], r=[wb, hT[c][tb]])

                    j0 = self.jobs_wi[("odd", l, 0)]
                    w0 = self.wi.get(j0)
                    for tb in range(NTB):
                        for c in range(2):
                            pb = self.psum()
                            proj(w0, c * 128, 128, tb, pb)
                            K.op(ACT, lambda c=c, pb=pb: nc.scalar.copy(out=cq32[c][0].ap, in_=pb.ap),
                                 w=[cq32[c][0]], r=[pb])
                        self.rmsnorm_fm(esa, [[cq32[0][0]], [cq32[1][0]]], qn_col, [[cqn[0][tb]], [cqn[1][tb]]], 2, 256,
                                        RMS_EPS, tbs=[0], extra_r=[small])
                    self.wi.release(j0)
                    j1 = self.jobs_wi[("odd", l, 1)]
                    w1 = self.wi.get(j1)
                    j2 = self.jobs_wi[("odd", l, 2)]
                    w2 = self.wi.get(j2)
                    for tb in range(NTB):
                        pb = self.psum()
                        proj(w1, 0, 128, tb, pb)
                        K.op(ACT, lambda pb=pb: nc.scalar.copy(out=ckv32[0][0].ap, in_=pb.ap), w=[ckv32[0][0]], r=[pb])
                        self.rmsnorm_fm(esa, [[ckv32[0][0]]], kvn_col, [[ckvn[0][tb]]], 1, 128, RMS_EPS, tbs=[0],
                                        extra_r=[small])
                        p1 = self.psum()
                        proj(w1, 128, 96, tb, p1)
                        p2 = self.psum()
                        proj(w2, 0, 96, tb, p2)
                        sl = slice(tb * 512, (tb + 1) * 512)
                        K.op(DVE, lambda p1=p1, sl=sl: nc.vector.tensor_tensor(out=tmpa.ap[64:96, :], in0=p1.ap[64:96, :],
                                                                               in1=rope_t[64:96, 0, sl], op=ALU.mult),
                             w=[tmpa], r=[p1, rope])
                        K.op(DVE, lambda p2=p2, sl=sl: nc.vector.tensor_tensor(out=tmpb.ap[64:96, :], in0=p2.ap[64:96, :],
                                                                               in1=rope_t[64:96, 1, sl], op=ALU.mult),
                             w=[tmpb], r=[p2, rope])
                        K.op(DVE, lambda tb=tb: nc.vector.tensor_tensor(out=kr[tb].ap[64:96, :], in0=tmpa.ap[64:96, :],
                                                                        in1=tmpb.ap[64:96, :], op=ALU.add),
                             w=[kr[tb]], r=[tmpa, tmpb])
                    self.wi.release(j1)
                    self.wi.release(j2)
                    for q in range(4):
                        jq = self.jobs_wi[("odd", l, 3 + q)]
                        wq_ = self.wi.get(jq)
                        for tb in range(NTB):
                            pa = self.psum()
                            proj(wq_, 0, 128, tb, pa)
                            pbb = self.psum()
                            proj(wq_, 128, 128, tb, pbb)
                            K.op(ACT, lambda pbb=pbb: nc.scalar.activation(out=tmpa.ap, in_=pbb.ap, func=AF.Sigmoid),
                                 w=[tmpa], r=[pbb])
                            K.op(DVE, lambda pa=pa, q=q, tb=tb: nc.vector.tensor_tensor(
                                out=hglu_t[:, q, 30 + tb * 512: 30 + (tb + 1) * 512], in0=pa.ap, in1=tmpa.ap,
                                op=ALU.mult), w=[hglu[q]], r=[pa, tmpa])
                        self.wi.release(jq)
                    K.barrier()

                with ExitStack() as esb:
                    yd_t = K.sb(esb, [128, 4, S], BF16, "ydT")
                    yd = [[Buf(yd_t[:, q, tb * 512:(tb + 1) * 512]) for tb in range(NTB)] for q in range(4)]
                    diag_t = [K.sb(esb, [128, 31, 128], BF16, "diag%d" % i) for i in range(2)]
                    diag = [Buf(t[:]) for t in diag_t]
                    c32_t = K.sb(esb, [128, 4, 512], F32, "c32")
                    c32 = [Buf(c32_t[:, q, :]) for q in range(4)]
                    sq32 = [Buf(K.sb(esb, [128, 512], F32, "sq32_%d" % i)[:]) for i in range(2)]
                    mean = Buf(K.sb(esb, [128, 512], F32, "mean")[:])
                    rstd = Buf(K.sb(esb, [128, 512], F32, "rstd")[:])
                    msq = Buf(K.sb(esb, [128, 512], F32, "msq")[:])
                    nd = 0
                    for tb in range(NTB):
                        p_s1 = ps[4]
                        p_s2 = ps[5]
                        for q in range(4):
                            dg, dg_t = diag[nd % 2], diag_t[nd % 2]
                            nd += 1
                            for j in range(31):
                                K.op(DVE, lambda j=j, q=q, dg_t=dg_t: nc.vector.tensor_scalar(
                                    out=dg_t[:, j, :], in0=self.ident_f.ap, scalar1=cw(q, j), scalar2=None,
                                    op0=ALU.mult), w=[dg], r=[self.ident_f, small])
                            pc = ps[q]
                            for j in range(31):
                                K.op(PE, lambda q=q, j=j, pc=pc, dg_t=dg_t: nc.tensor.matmul(
                                    pc.ap, dg_t[:, j, :], hglu_t[:, q, tb * 512 + j: tb * 512 + j + 512],
                                    start=(j == 0), stop=(j == 30)), w=[pc], r=[dg, hglu[q]])
                            K.op(ACT, lambda q=q, pc=pc: nc.scalar.activation(out=c32[q].ap, in_=pc.ap, func=AF.Identity,
                                                                              bias=cb(q), scale=1.0),
                                 w=[c32[q]], r=[pc, small])
                            sqb = sq32[q % 2]
                            K.op(ACT, lambda q=q, sqb=sqb: nc.scalar.activation(out=sqb.ap, in_=c32[q].ap, func=AF.Square),
                                 w=[sqb], r=[c32[q]])
                            K.op(PE, lambda q=q: nc.tensor.matmul(p_s1.ap, self.ones_f.ap, c32[q].ap, start=(q == 0),
                                                                  stop=(q == 3)), w=[p_s1], r=[self.ones_f, c32[q]])
                            K.op(PE, lambda q=q, sqb=sqb: nc.tensor.matmul(p_s2.ap, self.ones_f.ap, sqb.ap, start=(q == 0),
                                                                           stop=(q == 3)), w=[p_s2], r=[self.ones_f, sqb])
                        K.op(ACT, lambda: nc.scalar.mul(out=mean.ap, in_=p_s1.ap, mul=1.0 / 512), w=[mean], r=[p_s1])
                        K.op(DVE, lambda: nc.vector.tensor_tensor(out=msq.ap, in0=mean.ap, in1=mean.ap, op=ALU.mult),
                             w=[msq], r=[mean])
                        K.op(DVE, lambda: nc.vector.scalar_tensor_tensor(out=rstd.ap, in0=p_s2.ap, scalar=1.0 / 512,
                                                                         in1=msq.ap, op0=ALU.mult, op1=ALU.subtract),
                             w=[rstd], r=[p_s2, msq])
                        K.op(ACT, lambda: nc.scalar.activation(out=rstd.ap, in_=rstd.ap, func=AF.Sqrt, scale=1.0,
                                                               bias=self.eps_cols[LN_EPS]), w=[rstd], r=[rstd, self.epsb])
                        K.op(DVE, lambda: nc.vector.reciprocal(out=rstd.ap, in_=rstd.ap), w=[rstd], r=[rstd])
                        for q in range(4):
                            K.op(DVE, lambda q=q: nc.vector.tensor_tensor(out=c32[q].ap, in0=c32[q].ap, in1=mean.ap,
                                                                          op=ALU.subtract), w=[c32[q]], r=[c32[q], mean])
                            K.op(DVE, lambda q=q: nc.vector.tensor_tensor(out=c32[q].ap, in0=c32[q].ap, in1=rstd.ap,
                                                                          op=ALU.mult), w=[c32[q]], r=[c32[q], rstd])
                            K.op(ACT, lambda q=q, tb=tb: nc.scalar.activation(out=yd[q][tb].ap, in_=c32[q].ap,
                                                                              func=AF.Silu, bias=lnb(q), scale=lng(q)),
                                 w=[yd[q][tb]], r=[c32[q], small])
                    outproj(yd, [4, 5, 6, 7])
                    K.barrier()

            with ExitStack() as esc:
                yc_t = K.sb(esc, [128, 4, S], BF16, "ycT")
                yc = [[Buf(yc_t[:, c, tb * 512:(tb + 1) * 512]) for tb in range(NTB)] for c in range(4)]
                vall_t = K.sb(esc, [128, 16, 512], BF16, "vall")
                vall = [Buf(vall_t[:, i, :]) for i in range(16)]
                for i in range(16):
                    pv = self.psum()
                    tbi = i // 4
                    K.op(PE, lambda i=i, pv=pv: nc.tensor.matmul(
                        pv.ap, ckvn_t[:, i * 128:(i + 1) * 128], wkv_t[:, 512:1024],
                        start=True, stop=True), w=[pv], r=[ckvn[0][tbi], wkv])
                    K.op(ACT, lambda i=i, pv=pv: nc.scalar.copy(out=vall[i].ap, in_=pv.ap), w=[vall[i]], r=[pv])
                qT_t = [K.sb(esc, [128, S], BF16, "qT%d" % i) for i in range(2)]
                kT_t = [K.sb(esc, [128, S], BF16, "kT%d" % i) for i in range(2)]
                qT = [Buf(t[:]) for t in qT_t]
                kT = [Buf(t[:]) for t in kT_t]
                oc_t = [K.sb(esc, [128, S], BF16, "oc%d" % i) for i in range(2)]
                oc = [Buf(t[:]) for t in oc_t]
                pT_t = [K.sb(esc, [128, 512], BF16, "pT%d" % i) for i in range(3)]
                pT = [Buf(t[:]) for t in pT_t]
                rden = Buf(K.sb(esc, [128, 512], F32, "rden")[:])
                tq1 = Buf(K.sb(esc, [128, 512], F32, "tq1")[:])
                tq2 = Buf(K.sb(esc, [128, 512], F32, "tq2")[:])
                npt = 0
                for h in range(8):
                    qh, kh, och = qT[h % 2], kT[h % 2], oc[h % 2]
                    qh_t, kh_t, och_t = qT_t[h % 2], kT_t[h % 2], oc_t[h % 2]
                    for tb in range(NTB):
                        sl = slice(tb * 512, (tb + 1) * 512)
                        pq = ps[0 + (tb % 2)]
                        pr = ps[2 + (tb % 2)]
                        pk = ps[4 + (tb % 2)]
                        for c in range(2):
                            K.op(PE, lambda c=c, pq=pq: nc.tensor.matmul(pq.ap[0:96, :], wq_t[:, c, h * 96:(h + 1) * 96],
                                                                         cqn[c][tb].ap, start=(c == 0), stop=(c == 1)),
                                 w=[pq], r=[wq, cqn[c][tb]])
                        for c in range(2):
                            K.op(PE, lambda c=c, pr=pr: nc.tensor.matmul(pr.ap[0:96, :], wqr_t[:, c, h * 96:(h + 1) * 96],
                                                                         cqn[c][tb].ap, start=(c == 0), stop=(c == 1)),
                                 w=[pr], r=[wqr, cqn[c][tb]])
                        K.op(PE, lambda pk=pk: nc.tensor.matmul(pk.ap[0:64, :], wkv_t[:, h * 64:h * 64 + 64],
                                                                ckvn[0][tb].ap, start=True, stop=True),
                             w=[pk], r=[wkv, ckvn[0][tb]])
                        K.op(ACT, lambda pq=pq, sl=sl: nc.scalar.copy(out=qh_t[0:64, sl], in_=pq.ap[0:64, :]),
                             w=[qh], r=[pq])
                        K.op(DVE, lambda pq=pq, sl=sl: nc.vector.tensor_tensor(out=tq1.ap[64:96, :], in0=pq.ap[64:96, :],
                                                                               in1=rope_t[64:96, 0, sl], op=ALU.mult),
                             w=[tq1], r=[pq, rope])
                        K.op(DVE, lambda pr=pr, sl=sl: nc.vector.tensor_tensor(out=tq2.ap[64:96, :], in0=pr.ap[64:96, :],
                                                                               in1=rope_t[64:96, 1, sl], op=ALU.mult),
                             w=[tq2], r=[pr, rope])
                        K.op(DVE, lambda sl=sl: nc.vector.tensor_tensor(out=qh_t[64:96, sl], in0=tq1.ap[64:96, :],
                                                                        in1=tq2.ap[64:96, :], op=ALU.add),
                             w=[qh], r=[tq1, tq2])
                        K.op(ACT, lambda pk=pk, sl=sl: nc.scalar.copy(out=kh_t[0:64, sl], in_=pk.ap[0:64, :]),
                             w=[kh], r=[pk])
                        K.op(DVE, lambda sl=sl, tb=tb: nc.vector.tensor_copy(kh_t[64:96, sl], kr_t[64:96, sl]),
                             w=[kh], r=[kr[tb]])
                    for qb in range(NTB):
                        pO = ps[6]
                        pD = ps[7]
                        nkt = 4 * qb + 4
                        def emit_S(kt, qb=qb, nkt=nkt):
                            nonlocal npt
                            m = kt - 4 * qb
                            q0 = max(m, 0) * 128
                            pS = ps[kt % 2]
                            pt = pT[npt % 3]
                            pt_t = pT_t[npt % 3]
                            npt += 1
                            K.op(PE, lambda: nc.tensor.matmul(
                                pS.ap[:, q0:512], kh_t[0:96, kt * 128:(kt + 1) * 128],
                                qh_t[0:96, qb * 512 + q0:(qb + 1) * 512], start=True, stop=True),
                                w=[pS], r=[kh, qh])
                            K.op(ACT, lambda: nc.scalar.activation(
                                out=pt_t[:, q0:512], in_=pS.ap[:, q0:512], func=AF.Exp, scale=ATTN_SCALE),
                                w=[pt], r=[pS])
                            if m >= 0:
                                K.op(POOL, lambda: nc.gpsimd.memset(pt_t[64:128, q0:q0 + 64], 0.0),
                                     w=[pt], r=[pt])
                            return (pt, pt_t, q0)

                        def emit_PV(kt, st_, nkt=nkt):
                            pt, pt_t, q0 = st_
                            K.op(PE, lambda: nc.tensor.matmul(
                                pO.ap[0:64, q0:512], vall_t[:, kt, h * 64:(h + 1) * 64], pt_t[:, q0:512],
                                start=(kt == 0), stop=(kt == nkt - 1)), w=[pO], r=[vall[kt], pt])
                            K.op(PE, lambda: nc.tensor.matmul(
                                pD.ap[0:64, q0:512], self.ones_bf.ap[:, 0:64], pt_t[:, q0:512],
                                start=(kt == 0), stop=(kt == nkt - 1)), w=[pD], r=[self.ones_bf, pt])

                        nxt_st = emit_S(0)
                        for kt in range(nkt):
                            cur_st = nxt_st
                            if kt + 1 < nkt:
                                nxt_st = emit_S(kt + 1)
                            emit_PV(kt, cur_st)
                        K.op(DVE, lambda: nc.vector.reciprocal(out=rden.ap[0:64, :], in_=pD.ap[0:64, :]),
                             w=[rden], r=[pD])
                        K.op(DVE, lambda qb=qb: nc.vector.tensor_tensor(out=och_t[0:64, qb * 512:(qb + 1) * 512],
                                                                        in0=pO.ap[0:64, :], in1=rden.ap[0:64, :],
                                                                        op=ALU.mult), w=[och], r=[pO, rden])
                    pb0 = (h % 2) * 64
                    K.dma(SP, yc_t[pb0:pb0 + 64, h // 2, :], och_t[0:64, :], w=[yc[h // 2][tb] for tb in range(NTB)],
                          r=[och])
                outproj(yc, [0, 1, 2, 3])
                K.barrier()

    def final(self, do_norm):
        K, nc = self.K, self.nc
        with ExitStack() as es:
            if do_norm:
                o_t = K.sb(es, [128, NC_, S], F32, "oT")
                o = [[Buf(o_t[:, c, tb * 512:(tb + 1) * 512]) for tb in range(NTB)] for c in range(NC_)]
                self.rmsnorm_fm(es, self.x, self.gain_col(12), o, NC_, D, RMS_EPS)
            else:
                o = self.x
            toks = []
            for c in range(NC_):
                for tb in range(NTB):
                    toks.append(K.dma(K.SP, self.d_out[:, c, tb * 512:(tb + 1) * 512], o[c][tb].ap, r=[o[c][tb]]))
            for t in toks + list(self.dbg_toks.values()):
                K.wait_tok(K.SP, t)
            K.barrier()


def build_program(cfg):
    p = Prog(cfg)
    return p.build()


def prep_shared(inp):
    f32 = np.float32
    sh = {}
    gains = np.concatenate([inp["norm_ffn1"], inp["norm_mix"], inp["norm_ffn2"], inp["final_norm"][None]], axis=0)
    sh["gains"] = np.ascontiguousarray(gains.reshape(13, NC_, 128).transpose(2, 0, 1).reshape(128, 13 * NC_)).astype(f32)
    fin = np.stack([inp["ffn1_in"], inp["ffn2_in"]], axis=1).reshape(2 * DEPTH, D, 2 * DFF)
    fin = fin.reshape(2 * DEPTH, NC_, 128, 2, NHC, 128).transpose(0, 4, 2, 1, 3, 5)
    sh["ffn_in"] = np.ascontiguousarray(fin).reshape(2 * DEPTH, NHC, 128, NC_, 256)
    fout = np.stack([inp["ffn1_out"], inp["ffn2_out"]], axis=1).reshape(2 * DEPTH, NHC, 128, D)
    sh["ffn_out"] = np.ascontiguousarray(fout)
    wi = inp["odd_w_in"]
    Z = lambda n: np.zeros((2, D, n), f32)
    cq, ckv, krc = wi[:, :, 0:256], wi[:, :, 256:384], wi[:, :, 384:416]
    za, zb = wi[:, :, 416:928], wi[:, :, 928:1440]
    kr_rot = np.concatenate([krc[:, :, 16:32], krc[:, :, 0:16]], axis=2)
    slabs = [cq,
             np.concatenate([ckv, Z(64), krc, Z(32)], axis=2),
             np.concatenate([Z(64), kr_rot, Z(160)], axis=2)]
    for q in range(4):
        slabs.append(np.concatenate([za[:, :, q * 128:(q + 1) * 128], zb[:, :, q * 128:(q + 1) * 128]], axis=2))
    oin = np.stack(slabs, axis=1)
    oin = oin.reshape(2, 7, NC_, 128, 256).transpose(0, 1, 3, 2, 4)
    sh["odd_in"] = np.ascontiguousarray(oin).astype(f32)
    sh["odd_out"] = np.ascontiguousarray(inp["odd_w_out"].reshape(2, 8, 128, D)).astype(f32)
    wq = inp["wq_up"]
    sh["wq"] = np.ascontiguousarray(wq.reshape(2, 2, 128, 768).transpose(0, 2, 1, 3)).astype(f32)
    wq4 = wq.reshape(2, 256, 8, 96)
    wqr = np.concatenate([np.zeros((2, 256, 8, 64), f32), wq4[..., 80:96], wq4[..., 64:80]], axis=-1).reshape(2, 256, 768)
    sh["wqr"] = np.ascontiguousarray(wqr.reshape(2, 2, 128, 768).transpose(0, 2, 1, 3)).astype(f32)
    wkv4 = inp["wkv_up"].reshape(2, 128, 8, 128)
    sh["wkv"] = np.ascontiguousarray(np.concatenate([wkv4[..., 0:64].reshape(2, 128, 512),
                                                     wkv4[..., 64:128].reshape(2, 128, 512)], axis=-1)).astype(f32)
    col = lambda v, n: v.reshape(2, n, 128).transpose(0, 2, 1)
    cwp = inp["conv_w"].reshape(2, 31, 4, 128).transpose(0, 3, 2, 1).reshape(2, 128, 124)
    sh["odd_small"] = np.ascontiguousarray(np.concatenate(
        [col(inp["q_norm"], 2), col(inp["kv_norm"], 1), cwp, col(inp["conv_b"], 4), col(inp["conv_ln_g"], 4),
         col(inp["conv_ln_b"], 4)], axis=2)).astype(f32)
    sh["rope"] = rope_table()
    ew = inp["even_w_in"]
    ein = ew.reshape(2, NC_, 128, 11, 256).transpose(0, 3, 2, 1, 4)
    sh["even_in"] = np.ascontiguousarray(ein).astype(f32)
    sh["even_out"] = np.ascontiguousarray(inp["even_w_out"].reshape(2, 8, 128, D)).astype(f32)
    sh["gsu_wsT"] = np.ascontiguousarray(inp["gsu_ws"].transpose(0, 3, 1, 2)).astype(f32)
    sh["gsu_bs4"] = np.ascontiguousarray(np.tile(inp["gsu_bs"], (1, 1, 4)).reshape(2, 1, 4, 512)).astype(f32)
    rows = np.concatenate([inp["shift_mu"], inp["k_k"], inp["k_a"], inp["r_k"].reshape(2, 512), inp["lnx_g"],
                           inp["lnx_b"], inp["gsu_ln_g"], inp["gsu_ln_b"], inp["decay_w0"], inp["iclr_a0"]], axis=1)
    sh["even_rows"] = np.ascontiguousarray(rows.reshape(2, 1, EV_NR)).astype(f32)
    sh["lora_d"] = np.ascontiguousarray(inp["decay_up"]).astype(f32)
    sh["lora_i"] = np.ascontiguousarray(np.concatenate([np.zeros((2, 64, 512), f32), inp["iclr_up"]], axis=1)).astype(f32)
    sh["lora_g"] = np.ascontiguousarray(inp["gate_up"]).astype(f32)
    return sh


def rope_table():
    f32 = np.float32
    inv_freq = (f32(10000.0) ** (-(np.arange(0, 32, 2, dtype=f32) / f32(32)))).astype(f32)
    ang = (np.arange(S, dtype=f32)[:, None] * inv_freq[None, :]).astype(f32)
    cos = np.cos(ang.astype(np.float64)).astype(f32).T
    sin = np.sin(ang.astype(np.float64)).astype(f32).T
    t = np.zeros((128, 2, S), f32)
    t[64:80, 0] = cos
    t[80:96, 0] = cos
    t[64:80, 1] = -sin
    t[80:96, 1] = sin
    return t


def prep_x(x):
    return [np.ascontiguousarray(x[b].T.reshape(NC_, 128, S).transpose(1, 0, 2)) for b in range(x.shape[0])]


def unprep_out(o):
    return np.ascontiguousarray(o.transpose(2, 1, 0).reshape(S, D))


FULL_CFG = {"layers": DEPTH}


def kernel(**inputs):
    inp = {k: np.asarray(v) for k, v in inputs.items()}
    sh = prep_shared(inp)
    xs = prep_x(inp["x"].astype(np.float32))
    nc = build_program(FULL_CFG)
    in_maps = [dict(sh, xT=xs[b]) for b in range(len(xs))]
    res = run_bass_kernel_spmd(nc, in_maps, core_ids=list(range(len(xs))))
    out = np.stack([unprep_out(np.asarray(r["outT"])) for r in res.results], axis=0)
    return out.astype(np.float32)
```

```python
import numpy as np
from contextlib import ExitStack
import concourse.bass as bass
import concourse.mybir as mybir
from concourse.bass_utils import run_bass_kernel_spmd

F32 = mybir.dt.float32
BF16 = mybir.dt.bfloat16
AF = mybir.ActivationFunctionType
ALU = mybir.AluOpType
AX = mybir.AxisListType

S = 2048
D = 1024
NC_ = 8
NTB = 4
DFF = 2816
NHC = 22
DEPTH = 4
GROUPS = [(0, 4), (4, 8), (8, 12), (12, 16), (16, 19), (19, 22)]
RMS_EPS = 1e-6
LN_EPS = 1e-5
ODD_NS = 2 + 1 + 4 * 31 + 4 + 4 + 4
ATTN_SCALE = 96.0 ** -0.5
LNX_EPS = 64e-5
EV_NR = 1792 + 9 * 512
OFF_MU, OFF_KK, OFF_KA, OFF_RK, OFF_LG, OFF_LB, OFF_GLG, OFF_GLB, OFF_W0, OFF_A0 = (
    0, 1792, 2304, 2816, 3328, 3840, 4352, 4864, 5376, 5888)
C0 = float(np.exp(-0.5))
NEUMANN_L = 4
NHS = 4


class Eng:
    def __init__(self, name, eng, sem, skip_self):
        self.name = name
        self.eng = eng
        self.sem = sem
        self.count = 0
        self.seen = {}
        self.skip_self = skip_self


class Buf:
    __slots__ = ("ap", "lw", "rd", "sem", "dcnt", "name", "persist", "psum")

    def __init__(self, ap, name="", persist=False, psum=False):
        self.persist = persist
        self.psum = psum
        self.ap = ap
        self.lw = None
        self.rd = []
        self.sem = None
        self.dcnt = 0
        self.name = name


class KB:
    def __init__(self, nc, es):
        self.nc = nc
        self.es = es
        self.nsem = 0
        self.PE = Eng("pe", nc.tensor, self.mksem("s_pe"), True)
        self.ACT = Eng("act", nc.scalar, self.mksem("s_act"), False)
        self.DVE = Eng("dve", nc.vector, self.mksem("s_dve"), False)
        self.POOL = Eng("pool", nc.gpsimd, self.mksem("s_pool"), False)
        self.SP = Eng("sp", nc.sync, self.mksem("s_sp"), False)
        self.compute = [self.PE, self.ACT, self.DVE, self.POOL]
        self.all = self.compute + [self.SP]
        self.nuniq = 0
        self.sem_pool = []
        self.sem_q = {}
        self.nwait = 0
        self.phase_stack = []

    def phase(self):
        kb = self

        class _Ph:
            def __enter__(self_):
                kb.phase_stack.append([])

            def __exit__(self_, *a):
                for b in kb.phase_stack.pop():
                    kb.sem_pool.append((b.sem, b.dcnt, kb.sem_q[id(b.sem)]))
                    b.sem = None
                return False
        return _Ph()

    def rotate(self, E):
        E.sem = self.mksem("s_%s_r%d" % (E.name, self.nsem))
        E.count = 0

    def mksem(self, name):
        self.nsem += 1
        return self.es.enter_context(self.nc.semaphore(name))

    def sb(self, es, shape, dt, name=None):
        self.nuniq += 1
        return es.enter_context(self.nc.sbuf_tensor("sb_%s_%d" % (name or "t", self.nuniq), shape, dt))

    def _need(self, E, tok, lst):
        sem, val, src = tok
        if src is E and E.skip_self:
            return
        k = id(sem)
        if E.seen.get(k, 0) >= val:
            return
        for i, (s2, v2) in enumerate(lst):
            if s2 is sem:
                if val > v2:
                    lst[i] = (sem, val)
                return
        lst.append((sem, val))

    def _emit_waits(self, E, lst, inst_fn):
        for sem, val in lst[:-1]:
            E.eng.wait_ge(sem, val)
            E.seen[id(sem)] = val
            self.nwait += 1
        inst = inst_fn()
        if lst:
            sem, val = lst[-1]
            inst._wait_ge(sem, val)
            E.seen[id(sem)] = val
        return inst

    def _wait(self, E, tok):
        lst = []
        self._need(E, tok, lst)
        for sem, val in lst:
            E.eng.wait_ge(sem, val)
            E.seen[id(sem)] = val
            self.nwait += 1

    def op(self, E, fn, w=(), r=()):
        lst = []
        for b in r:
            if b.lw is not None:
                self._need(E, b.lw, lst)
            if b.psum:
                for t in b.rd:
                    if t[2] is not E:
                        self._need(E, t, lst)
        for b in w:
            if b.lw is not None:
                self._need(E, b.lw, lst)
            for t in b.rd:
                if t[2] is not E:
                    self._need(E, t, lst)
        inst = self._emit_waits(E, lst, fn)
        E.count += 1
        inst.then_inc(E.sem, 1)
        tok = (E.sem, E.count, E)
        for b in r:
            b.rd.append(tok)
        for b in w:
            b.lw = tok
            b.rd = []
        return inst

    def dma(self, Q, out_ap, in_ap, w=(), r=()):
        lst = []
        for b in r:
            if b.lw is not None:
                self._need(Q, b.lw, lst)
        for b in w:
            if b.lw is not None:
                self._need(Q, b.lw, lst)
            for t in b.rd:
                self._need(Q, t, lst)
        owner = w[0] if len(w) else r[0]
        if owner.sem is None:
            pool = [e_ for e_ in self.sem_pool if e_[2] == Q.name]
            if pool and self.phase_stack and not owner.persist:
                ent = pool[-1]
                self.sem_pool.remove(ent)
                owner.sem, owner.dcnt = ent[0], ent[1]
            else:
                owner.sem = self.mksem("s_b%d" % self.nsem)
            self.sem_q[id(owner.sem)] = Q.name
            if self.phase_stack and not owner.persist:
                self.phase_stack[-1].append(owner)
        assert self.sem_q[id(owner.sem)] == Q.name, "semaphore shared between DMA queue types"
        inst = self._emit_waits(Q, lst, lambda: Q.eng.dma_start(out=out_ap, in_=in_ap))
        owner.dcnt += 1
        inst.then_inc(owner.sem, 16)
        tok = (owner.sem, 16 * owner.dcnt, None)
        for b in r:
            b.rd.append(tok)
        for b in w:
            b.lw = tok
            b.rd = []
        return tok

    def barrier(self):
        for E in self.all:
            for P in self.compute:
                if P is E:
                    continue
                if P.count > 0:
                    self._wait(E, (P.sem, P.count, None))

    def wait_tok(self, E, tok):
        self._wait(E, tok)


class WStream:
    def __init__(self, K, Q, slots):
        self.K = K
        self.Q = Q
        self.slots = slots
        self.jobs = []
        self.issued = 0
        self.released = 0

    def add(self, dram_ap, view):
        self.jobs.append((dram_ap, view))
        return len(self.jobs) - 1

    def ensure(self, j):
        j = min(j, len(self.jobs) - 1)
        n = len(self.slots)
        while self.issued <= j:
            i = self.issued
            assert i - n < self.released, "weight ring slot still in use"
            slot = self.slots[i % n]
            dram_ap, view = self.jobs[i]
            self.K.dma(self.Q, view(slot.ap), dram_ap, w=[slot])
            self.issued += 1

    def get(self, j):
        self.ensure(j)
        return self.slots[j % len(self.slots)]

    def release(self, j):
        self.released = max(self.released, j + 1)
        self.ensure(self.released + len(self.slots) - 1)


class Prog:
    def __init__(self, cfg):
        self.cfg = cfg

    def build(self):
        cfg = self.cfg
        nc = bass.Bass("TRN2", target_bir_lowering=False)
        self.nc = nc
        dram = {}

        def din(name, shape, dt=F32):
            dram[name] = nc.dram_tensor(name, list(shape), dt, kind="ExternalInput").ap()
            return dram[name]

        self.d_x = din("xT", [128, NC_, S])
        self.d_gains = din("gains", [128, 13 * NC_])
        self.d_ffn_in = din("ffn_in", [2 * DEPTH, NHC, 128, NC_, 256])
        self.d_ffn_out = din("ffn_out", [2 * DEPTH, NHC, 128, D])
        self.d_odd_in = din("odd_in", [2, 7, 128, NC_, 256])
        self.d_odd_out = din("odd_out", [2, 8, 128, D])
        self.d_wq = din("wq", [2, 128, 2, 768])
        self.d_wqr = din("wqr", [2, 128, 2, 768])
        self.d_wkv = din("wkv", [2, 128, 1024])
        self.d_odd_small = din("odd_small", [2, 128, ODD_NS])
        self.d_rope = din("rope", [128, 2, S])
        self.d_even_in = din("even_in", [2, 11, 128, NC_, 256])
        self.d_even_out = din("even_out", [2, 8, 128, D])
        self.d_gsu_ws = din("gsu_wsT", [2, 128, 4, 128])
        self.d_gsu_bs = din("gsu_bs4", [2, 1, 4, 512])
        self.d_even_rows = din("even_rows", [2, 1, EV_NR])
        self.d_lora_d = din("lora_d", [2, 64, 512])
        self.d_lora_i = din("lora_i", [2, 128, 512])
        self.d_lora_g = din("lora_g", [2, 128, 512])
        self.d_zB = nc.dram_tensor("zB_scratch", [S + 1, 1792], F32).ap()
        self.zB = Buf(self.d_zB, "zB", persist=True)
        self.d_out = nc.dram_tensor("outT", [128, NC_, S], F32, kind="ExternalOutput").ap()

        self.dbg_toks = {}
        with ExitStack() as es:
            K = KB(nc, es)
            self.K = K
            xT = K.sb(es, [128, NC_, S], F32, "xT")
            self.xT = xT
            self.x = [[Buf(xT[:, c, tb * 512:(tb + 1) * 512], "x%d_%d" % (c, tb)) for tb in range(NTB)]
                      for c in range(NC_)]
            gains = K.sb(es, [128, 13 * NC_], F32, "gains")
            self.gains = Buf(gains[:], "gains")
            ones_bf = K.sb(es, [128, 128], BF16, "ones_bf")
            self.ones_bf = Buf(ones_bf[:], "ones")
            ones_f = K.sb(es, [128, 128], F32, "ones_f")
            self.ones_f = Buf(ones_f[:], "ones_f")
            K.op(K.DVE, lambda: nc.vector.memset(ones_f[:], 1.0), w=[self.ones_f])
            ident_f = K.sb(es, [128, 128], F32, "ident_f")
            self.ident_f = Buf(ident_f[:], "ident_f")
            K.op(K.POOL, lambda: nc.gpsimd.memset(ident_f[:], 1.0), w=[self.ident_f])
            K.op(K.POOL, lambda: nc.gpsimd.affine_select(out=ident_f[:], in_=ident_f[:], pattern=[[-1, 128]],
                                                         compare_op=ALU.is_equal, fill=0.0, base=0,
                                                         channel_multiplier=1), w=[self.ident_f], r=[self.ident_f])
            ident_b = K.sb(es, [128, 128], BF16, "ident_b")
            self.ident_b = Buf(ident_b[:], "ident_b")
            K.op(K.DVE, lambda: nc.vector.tensor_copy(ident_b[:], ident_f[:]), w=[self.ident_b], r=[self.ident_f])
            epst = K.sb(es, [128, 4], F32, "epst")
            self.epsb = Buf(epst[:], "eps")
            self.eps_cols = {}
            for i, e in enumerate([1e-6, 1e-5, 64e-5, 0.0]):
                K.op(K.DVE, lambda i=i, e=e: nc.vector.memset(epst[:, i:i + 1], e), w=[self.epsb])
                self.eps_cols[e] = epst[:, i:i + 1]
            self.nrm_sq = [Buf(K.sb(es, [128, 512], BF16)[:], "sq") for _ in range(2)]
            self.nrm_rs = [Buf(K.sb(es, [128, 512], F32)[:], "rstd") for _ in range(2)]
            wi_t = [K.sb(es, [128, NC_, 256], BF16, "wi%d" % i) for i in range(3)]
            wo_t = [K.sb(es, [128, D], BF16, "wo%d" % i) for i in range(8)]
            self.wi = WStream(K, K.POOL, [Buf(t[:], "wi", persist=True) for t in wi_t])
            self.wo = WStream(K, K.POOL, [Buf(t[:], "wo", persist=True) for t in wo_t])
            self.ps = [Buf(es.enter_context(nc.psum_tensor("ps%d" % i, [128, 512], F32))[:], "ps%d" % i, psum=True)
                       for i in range(8)]
            self.ps_i = 0

            self.plan_jobs()

            K.op(K.DVE, lambda: nc.vector.memset(ones_bf[:], 1.0), w=[self.ones_bf])
            K.dma(K.SP, gains[:], self.d_gains[:, :], w=[self.gains])
            for c in range(NC_):
                for tb in range(NTB):
                    K.dma(K.SP, self.x[c][tb].ap, self.d_x[:, c, tb * 512:(tb + 1) * 512], w=[self.x[c][tb]])
            self.wi.ensure(2)
            self.wo.ensure(3)

            for l in range(cfg["layers"]):
                if l in cfg.get("skip_layers", []):
                    continue
                if l > 0:
                    K.barrier()
                    K.rotate(K.PE)
                if cfg.get("ffn1", True):
                    with K.phase():
                        self.ffn(l, 0)
                if cfg.get("mixer", True):
                    with K.phase():
                        self.mixer(l)
                if cfg.get("ffn2", True):
                    with K.phase():
                        self.ffn(l, 1)
            with K.phase():
                self.final(cfg.get("final_norm", True))
        return nc

    def dbg(self, name, ap, buf):
        if not self.cfg.get("dbg"):
            return
        if name in self.dbg_toks:
            return
        if self.cfg.get("dbg_names") is not None and name not in self.cfg["dbg_names"]:
            return
        d = self.nc.dram_tensor("dbg_" + name, list(ap.shape), ap.dtype, kind="ExternalOutput").ap()
        self.dbg_toks[name] = self.K.dma(self.K.SP, d, ap, r=[buf])

    def psum(self):
        b = self.ps[self.ps_i % 8]
        self.ps_i += 1
        return b

    def plan_jobs(self):
        cfg = self.cfg
        self.jobs_wi = {}
        self.jobs_wo = {}
        full = lambda ap: ap
        for l in range(cfg["layers"]):
            if l in cfg.get("skip_layers", []):
                continue
            for which in range(2):
                if which == 1 and cfg.get("mixer", True):
                    if l % 2 == 0:
                        e = l // 2
                        for sl in range(11):
                            self.jobs_wi[("even", l, sl)] = self.wi.add(self.d_even_in[e, sl], full)
                        for ch in range(8):
                            self.jobs_wo[("even", l, ch)] = self.wo.add(self.d_even_out[e, ch], full)
                    if l % 2 == 1:
                        o = l // 2
                        for sl in range(7):
                            self.jobs_wi[("odd", l, sl)] = self.wi.add(self.d_odd_in[o, sl], full)
                        for ch in [4, 5, 6, 7, 0, 1, 2, 3]:
                            self.jobs_wo[("odd", l, ch)] = self.wo.add(self.d_odd_out[o, ch], full)
                if not cfg.get("ffn%d" % (which + 1), True):
                    continue
                f = l * 2 + which
                for (a, b) in GROUPS:
                    for hc in range(a, b):
                        self.jobs_wi[(f, hc)] = self.wi.add(self.d_ffn_in[f, hc], full)
                    for hc in range(a, b):
                        self.jobs_wo[(f, hc)] = self.wo.add(self.d_ffn_out[f, hc], full)

    def rmsnorm_fm(self, es, src, gain_col, dst, nchunks, nfeat, eps, tbs=range(NTB), extra_r=()):
        K, nc = self.K, self.nc
        sq, rs = self.nrm_sq, self.nrm_rs
        for tb in tbs:
            pb = self.psum()
            for c in range(nchunks):
                s = sq[c % 2]
                K.op(K.ACT, lambda s=s, c=c: nc.scalar.activation(out=s.ap, in_=src[c][tb].ap, func=AF.Square),
                     w=[s], r=[src[c][tb]])
                K.op(K.PE, lambda s=s, c=c: nc.tensor.matmul(pb.ap, self.ones_bf.ap, s.ap, start=(c == 0),
                                                             stop=(c == nchunks - 1)),
                     w=[pb], r=[s, self.ones_bf])
            r = rs[tb % 2]
            K.op(K.ACT, lambda: nc.scalar.activation(out=r.ap, in_=pb.ap, func=AF.Sqrt, scale=1.0 / nfeat,
                                                     bias=self.eps_ap(eps)),
                 w=[r], r=[pb, self.epsb])
            K.op(K.DVE, lambda: nc.vector.reciprocal(out=r.ap, in_=r.ap), w=[r], r=[r])
            for c in range(nchunks):
                K.op(K.DVE, lambda c=c: nc.vector.scalar_tensor_tensor(
                    out=dst[c][tb].ap, in0=src[c][tb].ap, scalar=gain_col(c), in1=r.ap,
                    op0=ALU.mult, op1=ALU.mult), w=[dst[c][tb]], r=[src[c][tb], r, self.gains] + list(extra_r))

    def eps_ap(self, eps):
        return self.eps_cols[eps]

    def gain_col(self, idx):
        return lambda c: self.gains.ap[:, idx * NC_ + c: idx * NC_ + c + 1]

    def ffn(self, l, which):
        K, nc = self.K, self.nc
        f = l * 2 + which
        gidx = (0 if which == 0 else 2) * DEPTH + l
        with ExitStack() as es:
            hT_t = K.sb(es, [128, NC_, S], BF16, "hT")
            hT = [[Buf(hT_t[:, c, tb * 512:(tb + 1) * 512]) for tb in range(NTB)] for c in range(NC_)]
            hid_t = K.sb(es, [128, 4, S], BF16, "hid")
            hid = [[Buf(hid_t[:, j, tb * 512:(tb + 1) * 512]) for tb in range(NTB)] for j in range(4)]
            sg = [Buf(K.sb(es, [128, 512], F32)[:], "sg") for _ in range(2)]
            self.rmsnorm_fm(es, self.x, self.gain_col(gidx), hT, NC_, D, RMS_EPS)
            n = 0
            for (a, b) in GROUPS:
                for j, hc in enumerate(range(a, b)):
                    ji = self.jobs_wi[(f, hc)]
                    wi = self.wi.get(ji)
                    for tb in range(NTB):
                        pg = self.psum()
                        pu = self.psum()
                        for c in range(NC_):
                            K.op(K.PE, lambda c=c: nc.tensor.matmul(pg.ap, wi.ap[:, c, 0:128], hT[c][tb].ap,
                                                                    start=(c == 0), stop=(c == NC_ - 1)),
                                 w=[pg], r=[wi, hT[c][tb]])
                        for c in range(NC_):
                            K.op(K.PE, lambda c=c: nc.tensor.matmul(pu.ap, wi.ap[:, c, 128:256], hT[c][tb].ap,
                                                                    start=(c == 0), stop=(c == NC_ - 1)),
                                 w=[pu], r=[wi, hT[c][tb]])
                        s = sg[n % 2]
                        n += 1
                        K.op(K.ACT, lambda: nc.scalar.activation(out=s.ap, in_=pg.ap, func=AF.Silu), w=[s], r=[pg])
                        K.op(K.DVE, lambda: nc.vector.tensor_tensor(out=hid[j][tb].ap, in0=s.ap, in1=pu.ap,
                                                                    op=ALU.mult), w=[hid[j][tb]], r=[s, pu])
                    self.wi.release(ji)
                wos = [self.wo.get(self.jobs_wo[(f, hc)]) for hc in range(a, b)]
                ng = b - a
                for d in range(NC_):
                    for tb in range(NTB):
                        po = self.psum()
                        for j in range(ng):
                            K.op(K.PE, lambda j=j: nc.tensor.matmul(po.ap, wos[j].ap[:, d * 128:(d + 1) * 128],
                                                                    hid[j][tb].ap, start=(j == 0), stop=(j == ng - 1)),
                                 w=[po], r=[wos[j], hid[j][tb]])
                        xb = self.x[d][tb]
                        K.op(K.DVE, lambda: nc.vector.scalar_tensor_tensor(
                            out=xb.ap, in0=po.ap, scalar=0.5, in1=xb.ap, op0=ALU.mult, op1=ALU.add),
                            w=[xb], r=[po, xb])
                for hc in range(a, b):
                    self.wo.release(self.jobs_wo[(f, hc)])
            K.barrier()

    def mixer(self, l):
        if l % 2 == 1:
            self.mixer_odd(l)
        else:
            self.mixer_even(l)

    def mixer_even(self, l):
        K, nc = self.K, self.nc
        e = l // 2
        PE, ACT, DVE, POOL, SP = K.PE, K.ACT, K.DVE, K.POOL, K.SP
        ps = self.ps
        cfg = self.cfg

        def outproj(mixb, chs, tbs=range(NTB)):
            wos = [self.wo.get(self.jobs_wo[("even", l, ch)]) for ch in chs]
            for d in range(NC_):
                for tb in tbs:
                    po = self.psum4()
                    for i, ch in enumerate(chs):
                        K.op(PE, lambda i=i: nc.tensor.matmul(po.ap, wos[i].ap[:, d * 128:(d + 1) * 128], mixb[i][tb].ap,
                                                              start=(i == 0), stop=(i == len(chs) - 1)),
                             w=[po], r=[wos[i], mixb[i][tb]])
                    xb = self.x[d][tb]
                    K.op(DVE, lambda: nc.vector.tensor_tensor(out=xb.ap, in0=po.ap, in1=xb.ap, op=ALU.add),
                         w=[xb], r=[po, xb])

        with ExitStack() as es:
            rows = self.d_even_rows[e]

            def bc_tile(esx, off, n, name):
                t = K.sb(esx, [128, n], F32, name)
                bf = Buf(t[:], name)
                K.dma(SP, t[:], rows[0:1, off:off + n].partition_broadcast(128), w=[bf])
                return t, bf

            with ExitStack() as esg:
                uT_t = K.sb(esg, [128, 4, S], BF16, "uT")
                uT = [[Buf(uT_t[:, c, tb * 512:(tb + 1) * 512]) for tb in range(NTB)] for c in range(4)]
                vtm_t = K.sb(esg, [128, 16, 512], BF16, "vtm")
                vtm = [Buf(vtm_t[:, i, :]) for i in range(16)]
                wsT_t = K.sb(esg, [128, 4, 128], BF16, "wsT")
                wsT = Buf(wsT_t[:], "wsT")
                K.dma(POOL, wsT_t[:], self.d_gsu_ws[e], w=[wsT])
                K.op(POOL, lambda: nc.gpsimd.memset(wsT_t[64:128, :, 0:64], 0.0), w=[wsT], r=[wsT])
                bs4_t = K.sb(esg, [1, 4, 512], F32, "bs4")
                bs4 = Buf(bs4_t[:], "bs4")
                K.dma(SP, bs4_t[:], self.d_gsu_bs[e], w=[bs4])
                with ExitStack() as esa:
                    hT_t = K.sb(esa, [128, NC_, S], BF16, "hT")
                    hT = [[Buf(hT_t[:, c, tb * 512:(tb + 1) * 512]) for tb in range(NTB)] for c in range(NC_)]
                    self.rmsnorm_fm(esa, self.x, self.gain_col(DEPTH + l), hT, NC_, D, RMS_EPS)
                    glg_t, glg = bc_tile(esa, OFF_GLG, 512, "glg")
                    glb_t, glb = bc_tile(esa, OFF_GLB, 512, "glb")
                    g32 = [Buf(K.sb(esa, [128, 512], F32, "g32_%d" % i)[:]) for i in range(2)]
                    gsq = Buf(K.sb(esa, [128, 512], F32, "gsq")[:])
                    st_t = K.sb(esa, [128, 8], F32, "vstat")
                    st = Buf(st_t[:], "vstat")
                    zst_t = [K.sb(esa, [128, 4, 256], F32, "zstage%d" % i) for i in range(2)]
                    zst = [Buf(t[:]) for t in zst_t]
                    zrow_t = K.sb(esa, [1, 1792], F32, "zrow")
                    zrow = Buf(zrow_t[:], "zrow")
                    K.op(DVE, lambda: nc.vector.memset(zrow_t[:], 0.0), w=[zrow])
                    K.dma(SP, self.d_zB[0:1, :], zrow_t[:], w=[self.zB], r=[zrow])
                    for sl in range(2):
                        jj = self.jobs_wi[("even", l, sl)]
                        wb = self.wi.get(jj)
                        for tb in range(NTB):
                            for half in range(2):
                                pb = self.psum()
                                for c in range(NC_):
                                    K.op(PE, lambda c=c, pb=pb: nc.tensor.matmul(
                                        pb.ap, wb.ap[:, c, half * 128:(half + 1) * 128], hT[c][tb].ap,
                                        start=(c == 0), stop=(c == NC_ - 1)), w=[pb], r=[wb, hT[c][tb]])
                                ub = uT[sl * 2 + half][tb]
                                K.op(ACT, lambda pb=pb, ub=ub: nc.scalar.activation(out=ub.ap, in_=pb.ap,
                                                                                    func=AF.Gelu_apprx_tanh),
                                     w=[ub], r=[pb])
                        self.wi.release(jj)
                    j2 = self.jobs_wi[("even", l, 2)]
                    w2 = self.wi.get(j2)
                    j3 = self.jobs_wi[("even", l, 3)]
                    w3 = self.wi.get(j3)
                    for i in range(16):
                        tb, off = i // 4, (i % 4) * 128
                        pv = self.psum()
                        for hi, wb in enumerate((w2, w3)):
                            for c in range(NC_):
                                K.op(PE, lambda c=c, wb=wb, hi=hi: nc.tensor.matmul(
                                    pv.ap[:, hi * 256:(hi + 1) * 256], hT_t[:, c, i * 128:(i + 1) * 128], wb.ap[:, c, :],
                                    start=(c == 0), stop=(c == NC_ - 1)), w=[pv], r=[wb, hT[c][tb]])
                        gb = g32[i % 2]
                        K.op(ACT, lambda gb=gb, pv=pv: nc.scalar.activation(out=gb.ap, in_=pv.ap, func=AF.Gelu_apprx_tanh),
                             w=[gb], r=[pv])
                        K.op(DVE, lambda gb=gb: nc.vector.tensor_reduce(out=st_t[:, 0:1], in_=gb.ap, axis=AX.X, op=ALU.add),
                             w=[st], r=[gb])
                        K.op(ACT, lambda gb=gb: nc.scalar.activation(out=gsq.ap, in_=gb.ap, func=AF.Square),
                             w=[gsq], r=[gb])
                        K.op(DVE, lambda: nc.vector.tensor_reduce(out=st_t[:, 1:2], in_=gsq.ap, axis=AX.X, op=ALU.add),
                             w=[st], r=[gsq])
                        K.op(DVE, lambda: nc.vector.tensor_scalar(out=st_t[:, 2:3], in0=st_t[:, 0:1], scalar1=1.0 / 512,
                                                                  scalar2=None, op0=ALU.mult), w=[st], r=[st])
                        K.op(DVE, lambda: nc.vector.tensor_tensor(out=st_t[:, 3:4], in0=st_t[:, 2:3], in1=st_t[:, 2:3],
                                                                  op=ALU.mult), w=[st], r=[st])
                        K.op(DVE, lambda: nc.vector.scalar_tensor_tensor(out=st_t[:, 4:5], in0=st_t[:, 1:2],
                                                                         scalar=1.0 / 512, in1=st_t[:, 3:4],
                                                                         op0=ALU.mult, op1=ALU.subtract), w=[st], r=[st])
                        K.op(ACT, lambda: nc.scalar.activation(out=st_t[:, 5:6], in_=st_t[:, 4:5], func=AF.Sqrt, scale=1.0,
                                                               bias=self.eps_cols[LN_EPS]), w=[st], r=[st, self.epsb])
                        K.op(DVE, lambda: nc.vector.reciprocal(out=st_t[:, 6:7], in_=st_t[:, 5:6]), w=[st], r=[st])
                        K.op(DVE, lambda gb=gb: nc.vector.tensor_scalar(out=gb.ap, in0=gb.ap, scalar1=st_t[:, 2:3],
                                                                        scalar2=st_t[:, 6:7], op0=ALU.subtract,
                                                                        op1=ALU.mult), w=[gb], r=[gb, st])
                        K.op(DVE, lambda gb=gb: nc.vector.tensor_tensor(out=gb.ap, in0=gb.ap, in1=glg_t[:], op=ALU.mult),
                             w=[gb], r=[gb, glg])
                        K.op(DVE, lambda gb=gb, i=i: nc.vector.tensor_tensor(out=vtm[i].ap, in0=gb.ap, in1=glb_t[:],
                                                                             op=ALU.add), w=[vtm[i]], r=[gb, glb])
                    self.wi.release(j2)
                    self.wi.release(j3)
                    nz = 0
                    for sl in range(4, 11):
                        if cfg.get("only") == "a":
                            jj = self.jobs_wi[("even", l, sl)]
                            self.wi.get(jj)
                            self.wi.release(jj)
                            continue
                        jj = self.jobs_wi[("even", l, sl)]
                        wb = self.wi.get(jj)
                        for tb in range(NTB):
                            zb_, zb_t = zst[nz % 2], zst_t[nz % 2]
                            nz += 1
                            for ti in range(4):
                                i = tb * 4 + ti
                                pz = self.psum()
                                for c in range(NC_):
                                    K.op(PE, lambda c=c, pz=pz, i=i: nc.tensor.matmul(
                                        pz.ap[:, 0:256], hT_t[:, c, i * 128:(i + 1) * 128], wb.ap[:, c, :],
                                        start=(c == 0), stop=(c == NC_ - 1)), w=[pz], r=[wb, hT[c][tb]])
                                K.op(ACT, lambda pz=pz, ti=ti, zb_t=zb_t: nc.scalar.copy(out=zb_t[:, ti, :],
                                                                                         in_=pz.ap[:, 0:256]),
                                     w=[zb_], r=[pz])
                            col0 = (sl - 4) * 256
                            dst = self.d_zB[1 + tb * 512: 1 + (tb + 1) * 512, col0:col0 + 256].rearrange(
                                "(ti p) n -> p ti n", p=128)
                            K.dma(SP, dst, zb_t[:], w=[self.zB], r=[zb_])
                        self.wi.release(jj)
                    K.barrier()
                with ExitStack() as esm:
                    ya_t = K.sb(esm, [128, 4, S], BF16, "yaT")
                    ya = [[Buf(ya_t[:, g, tb * 512:(tb + 1) * 512]) for tb in range(NTB)] for g in range(4)]
                    for tb in range(NTB):
                        for g in range(4):
                            pm = self.psum()
                            K.op(PE, lambda g=g, pm=pm: nc.tensor.matmul(pm.ap, self.ones_f.ap[0:1, :], bs4_t[0:1, g, :],
                                                                         start=True, stop=False),
                                 w=[pm], r=[self.ones_f, bs4])
                            for nb in range(4):
                                i = tb * 4 + nb
                                K.op(PE, lambda i=i, g=g, nb=nb, pm=pm: nc.tensor.matmul(
                                    pm.ap[:, nb * 128:(nb + 1) * 128], vtm_t[:, i, g * 128:(g + 1) * 128], wsT_t[:, g, :],
                                    start=False, stop=(nb == 3)), w=[pm], r=[vtm[i], wsT])
                            K.op(DVE, lambda g=g, tb=tb, pm=pm: nc.vector.tensor_tensor(
                                out=ya[g][tb].ap, in0=pm.ap, in1=uT[g][tb].ap, op=ALU.mult),
                                w=[ya[g][tb]], r=[pm, uT[g][tb]])
                    self.psum4 = self.psum
                    outproj(ya, [0, 1, 2, 3])
                    for ch in range(4):
                        self.wo.release(self.jobs_wo[("even", l, ch)])
                    K.barrier()

            if cfg.get("only") == "a":
                for ch in range(4, 8):
                    self.wo.get(self.jobs_wo[("even", l, ch)])
                    self.wo.release(self.jobs_wo[("even", l, ch)])
            else:
                self.rwkv(l, es, bc_tile, outproj)
            K.barrier()

    def rwkv(self, l, es, bc_tile, outproj):
        K, nc = self.K, self.nc
        e = l // 2
        PE, ACT, DVE, POOL, SP = K.PE, K.ACT, K.DVE, K.POOL, K.SP
        ps = self.ps
        rr = [0]
        NROT = 7

        pinned = []

        def psum4():
            for k in range(NROT):
                bb = ps[(rr[0] + k) % NROT]
                if any(bb is p_ for p_ in pinned):
                    continue
                if bb.lw is None or len(bb.rd) > 0:
                    rr[0] += k + 1
                    return bb
            raise AssertionError("no free rotating PSUM bank")
        self.psum4 = psum4
        pY = ps[7]
        rows = self.d_even_rows[e]
        with ExitStack() as esr:
            def T(shape, dt, name):
                t = K.sb(esr, shape, dt, name)
                return t, Buf(t[:], name)

            def T2(shape, dt, name):
                return [T(shape, dt, name + "_0"), T(shape, dt, name + "_1")]
            mu_t, mu = bc_tile(esr, OFF_MU, 1792, "mu")
            kkb_t, kkb = bc_tile(esr, OFF_KK, 512, "kkb")
            kab_t, kab = bc_tile(esr, OFF_KA, 512, "kab")
            rkb_t, rkb = bc_tile(esr, OFF_RK, 512, "rkb")
            lgb_t, lgb = bc_tile(esr, OFF_LG, 512, "lgb")
            lbb_t, lbb = bc_tile(esr, OFF_LB, 512, "lbb")
            dup_t, dup = T([65, 512], F32, "dup")
            K.dma(SP, dup_t[0:64, :], self.d_lora_d[e], w=[dup])
            K.dma(SP, dup_t[64:65, :], rows[0:1, OFF_W0:OFF_W0 + 512], w=[dup])
            iup_t, iup = T([65, 512], BF16, "iup")
            K.dma(POOL, iup_t[0:64, :], self.d_lora_i[e][64:128, :], w=[iup])
            K.dma(POOL, iup_t[64:65, :], rows[0:1, OFF_A0:OFF_A0 + 512], w=[iup])
            gup_t, gup = T([128, 512], BF16, "gup")
            K.dma(POOL, gup_t[:], self.d_lora_g[e], w=[gup])
            Ui_t, Ui = T([128, 128], F32, "Uincl")
            Us_t, Us = T([128, 128], F32, "Ustrict")
            Ls_t, Ls = T([128, 128], F32, "Lstrict")
            mk2_t, mk2 = T([128, 256], BF16, "mask2")
            for (t_, b_, cmp_, st_, cm_) in ((Ui_t, Ui, ALU.is_ge, 1, -1), (Us_t, Us, ALU.is_gt, 1, -1),
                                             (Ls_t, Ls, ALU.is_gt, -1, 1)):
                K.op(POOL, lambda t_=t_: nc.gpsimd.memset(t_[:], 1.0), w=[b_])
                K.op(POOL, lambda t_=t_, cmp_=cmp_, st_=st_, cm_=cm_: nc.gpsimd.affine_select(
                    out=t_[:], in_=t_[:], pattern=[[st_, 128]], compare_op=cmp_, fill=0.0, base=0,
                    channel_multiplier=cm_), w=[b_], r=[b_])
            K.op(POOL, lambda: nc.gpsimd.tensor_copy(mk2_t[:, 0:128], Us_t[:]), w=[mk2], r=[Us])
            K.op(POOL, lambda: nc.gpsimd.tensor_copy(mk2_t[:, 128:256], Ui_t[:]), w=[mk2], r=[Ui])
            mk2b_t, mk2b = mk2_t, mk2
            Hf_t, Hf = T([64, 512], F32, "Hf")
            Hb_t, Hb = T([64, 512], BF16, "Hb")
            K.op(DVE, lambda: nc.vector.memset(Hf_t[:], 0.0), w=[Hf])
            K.op(DVE, lambda: nc.vector.memset(Hb_t[:], 0.0), w=[Hb])
            GTa_t, GTa = T([64, 512], F32, "GTa")
            Fa_t, Fa = T([64, 512], F32, "Fa")
            ybT_t = K.sb(esr, [128, 4, 512], BF16, "ybT")
            ybT = [Buf(ybT_t[:, q, :]) for q in range(4)]
            scr_t = K.sb(esr, [128, 2048], F32, "scr")
            zs_t, zs = scr_t[:, 0:1792], Buf(scr_t[:, 0:1792], "zs")
            tA_t, tA = scr_t[:, 0:512], Buf(scr_t[:, 0:512], "tA")
            tB_t, tB = scr_t[:, 512:1024], Buf(scr_t[:, 512:1024], "tB")
            Ea_t, Ea = scr_t[:, 1024:1536], Buf(scr_t[:, 1024:1536], "Ea")
            Eb_t, Eb = scr_t[:, 1536:2048], Buf(scr_t[:, 1536:2048], "Eb")
            ZS = [zs, tA, tB, Ea, Eb]
            lwT_t, lwT = T([65, 128], F32, "lwT")
            K.op(DVE, lambda: nc.vector.memset(lwT_t[64:65, :], 1.0), w=[lwT])
            laT_t, laT = T([65, 128], BF16, "laT")
            K.op(DVE, lambda: nc.vector.memset(laT_t[64:65, :], 1.0), w=[laT])
            lgT_t, lgT = T([128, 128], BF16, "lgT")
            sg_t, sg = T([128, 512], F32, "sg")
            as_t, asg = T([128, 512], F32, "asig")
            kk_t, kk = T([128, 512], F32, "kk")
            km_t, km = T([128, 512], F32, "kmod")
            bq_t, bq = T([128, 512], F32, "bq")
            bt_t, bt_b = T([128, 512], BF16, "bt_b")
            kt_t, kt_b = T([128, 512], BF16, "kt_b")
            yb_t, yb_b = T([128, 512], BF16, "yb_b")
            T3 = lambda shape, dt, name: [T(shape, dt, name + "_%d" % k_) for k_ in range(3)]
            zt1 = T([128, 1792], F32, "zt")
            gg3 = T3([128, 512], BF16, "gg")
            st3 = T3([128, 64], F32, "rst")
            pA_t, pAb = T([128, 512], F32, "postA")
            pB_t, pBb = T([128, 512], F32, "postB")
            at2 = T2([128, 512], BF16, "at_b")
            rt2 = T2([128, 512], BF16, "rt_b")
            bh2 = T2([128, 512], BF16, "bh_b")
            kh2 = T2([128, 512], BF16, "kh_b")
            v3b = T3([128, 512], BF16, "v_b")
            pC2 = T2([64, 8], F32, "pC")
            GT2 = [[T([128, 4, 128], BF16, "GT%d_%d" % (p_, i_)) for i_ in range(4)] for p_ in range(2)]
            HS = []
            for i_ in range(NHS):
                d = {}
                for nm, shp in (("M1", [128, 256]), ("M2", [128, 256]), ("XT0", [128, 128]), ("XXa", [128, 256]),
                                ("XXb", [128, 256]), ("Pa", [128, 128]), ("X8", [128, 128]), ("S2", [128, 128]),
                                ("AT", [128, 128]), ("Bm", [128, 128]), ("XT8", [128, 128]), ("W1", [128, 64]),
                                ("AU", [128, 128]), ("RbT", [64, 128])):
                    t_ = K.sb(esr, shp, BF16, "%s_%d" % (nm, i_))
                    d[nm] = (t_, Buf(t_[:], nm))
                HS.append(d)

            v3 = lambda ap: ap.rearrange("p (h f) -> p h f", f=64)
            bc3 = lambda ap: ap.unsqueeze(2).broadcast_to([128, 8, 64])
            L = NEUMANN_L

            def pre_gen(i):
                par = i % 2
                zt_t, zt = zt1
                gg_t, gg = gg3[i % 3]
                st_t, st = st3[i % 3]
                at_t, at_b = at2[par]
                rt_t, rt_b = rt2[par]
                bh_t, bh_b = bh2[par]
                kh_t, kh_b = kh2[par]
                v_t, v_b = v3b[i % 3]
                pC_t, pCs = pC2[par]
                K.dma(SP, zt_t[:], self.d_zB[1 + i * 128: 1 + (i + 1) * 128, :], w=[zt], r=[self.zB])
                K.dma(SP, zs_t, self.d_zB[i * 128:(i + 1) * 128, :], w=ZS, r=[self.zB])
                yield
                K.op(DVE, lambda: nc.vector.tensor_tensor(out=zs_t, in0=zs_t, in1=zt_t[:], op=ALU.subtract),
                     w=ZS, r=[zs, zt])
                yield
                K.op(POOL, lambda: nc.gpsimd.tensor_tensor(out=zs_t, in0=zs_t, in1=mu_t[:], op=ALU.mult),
                     w=ZS, r=[zs, mu])
                yield
                K.op(DVE, lambda: nc.vector.tensor_tensor(out=zt_t[:], in0=zt_t[:], in1=zs_t, op=ALU.add),
                     w=[zt] + ZS, r=[zs, zt])
                r_ap, k_ap, vv_ap = zt_t[:, 0:512], zt_t[:, 512:1024], zt_t[:, 1024:1536]
                yield
                pl = psum4()
                K.op(PE, lambda: nc.tensor.transpose(pl.ap[0:64, 0:128], zt_t[:, 1536:1600], self.ident_f.ap),
                     w=[pl], r=[zt, self.ident_f])
                K.op(PE, lambda: nc.tensor.transpose(pl.ap[0:64, 128:256], zt_t[:, 1600:1664], self.ident_f.ap),
                     w=[pl], r=[zt, self.ident_f])
                K.op(PE, lambda: nc.tensor.transpose(pl.ap[:, 256:384], zt_t[:, 1664:1792], self.ident_f.ap),
                     w=[pl], r=[zt, self.ident_f])
                yield
                K.op(ACT, lambda: nc.scalar.activation(out=lwT_t[0:64, :], in_=pl.ap[0:64, 0:128], func=AF.Tanh),
                     w=[lwT], r=[pl])
                K.op(ACT, lambda: nc.scalar.activation(out=lgT_t[:], in_=pl.ap[:, 256:384], func=AF.Sigmoid),
                     w=[lgT], r=[pl])
                K.op(ACT, lambda: nc.scalar.copy(out=laT_t[0:64, :], in_=pl.ap[0:64, 128:256]), w=[laT], r=[pl])
                yield
                pw = psum4()
                K.op(PE, lambda: nc.tensor.matmul(pw.ap, lwT_t[0:65, :], dup_t[0:65, :], start=True, stop=True),
                     w=[pw], r=[lwT, dup])
                yield
                K.op(ACT, lambda: nc.scalar.activation(out=sg_t[:], in_=pw.ap, func=AF.Sigmoid), w=[sg], r=[pw])
                pa = psum4()
                K.op(PE, lambda: nc.tensor.matmul(pa.ap, laT_t[0:65, :], iup_t[0:65, :], start=True, stop=True),
                     w=[pa], r=[laT, iup])
                yield
                K.op(ACT, lambda: nc.scalar.activation(out=as_t[:], in_=pa.ap, func=AF.Sigmoid), w=[asg], r=[pa])
                K.op(DVE, lambda: nc.vector.tensor_tensor(out=tA_t, in0=k_ap, in1=kkb_t[:], op=ALU.mult),
                     w=[tA], r=[zt, kkb])
                yield
                pg = psum4()
                K.op(PE, lambda: nc.tensor.matmul(pg.ap, lgT_t[:], gup_t[:], start=True, stop=True),
                     w=[pg], r=[lgT, gup])
                pcs = psum4()
                pinned.append(pcs)
                K.op(PE, lambda: nc.tensor.matmul(pcs.ap, Ui_t[:], sg_t[:], start=True, stop=True), w=[pcs], r=[Ui, sg])
                K.op(DVE, lambda: nc.vector.scalar_tensor_tensor(out=km_t[:], in0=as_t[:], scalar=-1.0, in1=kab_t[:],
                                                                 op0=ALU.add, op1=ALU.mult), w=[km], r=[asg, kab])
                K.op(POOL, lambda: nc.gpsimd.tensor_tensor(out=tB_t, in0=tA_t, in1=tA_t, op=ALU.mult),
                     w=[tB], r=[tA])
                yield
                K.op(ACT, lambda: nc.scalar.copy(out=gg_t[:], in_=pg.ap), w=[gg], r=[pg])
                K.op(ACT, lambda: nc.scalar.activation(out=Ea_t, in_=pcs.ap, func=AF.Exp, scale=-C0), w=[Ea], r=[pcs])
                K.op(DVE, lambda: nc.vector.scalar_tensor_tensor(out=km_t[:], in0=km_t[:], scalar=1.0, in1=k_ap,
                                                                 op0=ALU.add, op1=ALU.mult), w=[km], r=[km, zt])
                K.op(DVE, lambda: nc.vector.tensor_reduce(out=st_t[:, 0:8], in_=v3(tB_t), axis=AX.X, op=ALU.add),
                     w=[st], r=[tB])
                yield
                pcx = psum4()
                K.op(PE, lambda: nc.tensor.matmul(pcx.ap, Us_t[:], sg_t[:], start=True, stop=True), w=[pcx], r=[Us, sg])
                K.op(DVE, lambda: nc.vector.tensor_tensor(out=rt_t[:], in0=r_ap, in1=Ea_t, op=ALU.mult),
                     w=[rt_b], r=[zt, Ea])
                K.op(ACT, lambda: nc.scalar.activation(out=st_t[:, 8:16], in_=st_t[:, 0:8], func=AF.Sqrt),
                     w=[st], r=[st])
                K.op(POOL, lambda: nc.gpsimd.tensor_tensor(out=tB_t, in0=r_ap, in1=km_t[:], op=ALU.mult),
                     w=[tB], r=[zt, km])
                yield
                K.op(ACT, lambda: nc.scalar.activation(out=Ea_t, in_=pcs.ap, func=AF.Exp, scale=C0), w=[Ea], r=[pcs])
                pinned.remove(pcs)
                K.op(ACT, lambda: nc.scalar.activation(out=Eb_t, in_=pcx.ap, func=AF.Exp, scale=-C0), w=[Eb], r=[pcx])
                K.op(DVE, lambda: nc.vector.tensor_scalar(out=st_t[:, 8:16], in0=st_t[:, 8:16], scalar1=1e-12,
                                                          scalar2=None, op0=ALU.max), w=[st], r=[st])
                K.op(DVE, lambda: nc.vector.reciprocal(out=st_t[:, 16:24], in_=st_t[:, 8:16]), w=[st], r=[st])
                K.op(DVE, lambda: nc.vector.tensor_tensor(out=v3(kk_t[:]), in0=v3(tA_t), in1=bc3(st_t[:, 16:24]),
                                                          op=ALU.mult), w=[kk], r=[tA, st])
                K.op(POOL, lambda: nc.gpsimd.tensor_tensor(out=tB_t, in0=tB_t, in1=rkb_t[:], op=ALU.mult),
                     w=[tB], r=[tB, rkb])
                yield
                prq = psum4()
                K.op(PE, lambda: nc.tensor.matmul(prq.ap, Ls_t[:], sg_t[:], start=True, stop=True), w=[prq], r=[Ls, sg])
                K.op(DVE, lambda: nc.vector.tensor_tensor(out=kt_t[:], in0=km_t[:], in1=Ea_t, op=ALU.mult),
                     w=[kt_b], r=[km, Ea])
                K.op(DVE, lambda: nc.vector.scalar_tensor_tensor(out=at_t[:], in0=kk_t[:], scalar=-1.0, in1=Eb_t,
                                                                 op0=ALU.mult, op1=ALU.mult), w=[at_b], r=[kk, Eb])
                K.op(POOL, lambda: nc.gpsimd.tensor_tensor(out=bq_t[:], in0=kk_t[:], in1=as_t[:], op=ALU.mult),
                     w=[bq], r=[kk, asg])
                yield
                K.op(DVE, lambda: nc.vector.tensor_reduce(out=st_t[:, 24:32], in_=v3(tB_t), axis=AX.X, op=ALU.add),
                     w=[st], r=[tB])
                K.op(DVE, lambda: nc.vector.tensor_tensor(out=bt_t[:], in0=bq_t[:], in1=Ea_t, op=ALU.mult),
                     w=[bt_b], r=[bq, Ea])
                K.op(ACT, lambda: nc.scalar.activation(out=Eb_t, in_=prq.ap, func=AF.Exp, scale=-C0), w=[Eb], r=[prq])
                K.op(POOL, lambda: nc.gpsimd.tensor_copy(v_t[:], vv_ap), w=[v_b], r=[zt])
                yield
                K.op(DVE, lambda: nc.vector.tensor_tensor(out=bh_t[:], in0=bq_t[:], in1=Eb_t, op=ALU.mult),
                     w=[bh_b], r=[bq, Eb])
                K.op(DVE, lambda: nc.vector.tensor_tensor(out=kh_t[:], in0=km_t[:], in1=Eb_t, op=ALU.mult),
                     w=[kh_b], r=[km, Eb])
                ptot = psum4()
                for h in range(8):
                    K.op(PE, lambda h=h: nc.tensor.matmul(ptot.ap[0:64, h:h + 1], sg_t[:, h * 64:(h + 1) * 64],
                                                          self.ones_f.ap[:, 0:1], start=True, stop=True),
                         w=[ptot], r=[sg, self.ones_f])
                yield
                K.op(ACT, lambda: nc.scalar.activation(out=pC_t[:], in_=ptot.ap[0:64, 0:8], func=AF.Exp, scale=-C0),
                     w=[pCs], r=[ptot])
                for pr in range(4):
                    yield
                    ptb = psum4()
                    pt16 = ptb.ap.bitcast(BF16)
                    for kind, (xt_, xb_) in enumerate(((at_t, at_b), (rt_t, rt_b), (bt_t, bt_b), (kt_t, kt_b))):
                        K.op(PE, lambda kind=kind, xt_=xt_: nc.tensor.transpose(
                            pt16[:, kind * 128:(kind + 1) * 128], xt_[:, pr * 128:(pr + 1) * 128], self.ident_b.ap),
                            w=[ptb], r=[xb_, self.ident_b])
                    yield
                    gT_t, gT = GT2[par][pr]
                    K.op(ACT, lambda: nc.scalar.copy(out=gT_t[:].rearrange("p k t -> p (k t)"), in_=pt16[:, 0:512]),
                         w=[gT], r=[ptb])

            def head_gen(h, i):
                par = i % 2
                at_t, at_b = at2[par]
                rt_t, rt_b = rt2[par]
                bh_t, bh_b = bh2[par]
                kh_t, kh_b = kh2[par]
                v_t, v_b = v3b[i % 3]
                pC_t, pCs = pC2[par]
                pr, hb = h // 2, (h % 2) * 64
                hs = HS[h % NHS]
                hc = slice(h * 64, (h + 1) * 64)
                gt, gtb = GT2[par][pr]
                ar = gt[hb:hb + 64, 0:2, :].rearrange("p k t -> p (k t)")
                M1_t, M1 = hs["M1"]
                M2_t, M2 = hs["M2"]
                XT0_t, XT0 = hs["XT0"]
                p12 = psum4()
                K.op(PE, lambda: nc.tensor.matmul(p12.ap[:, 0:256], gt[hb:hb + 64, 2, :], ar, start=True, stop=True),
                     w=[p12], r=[gtb])
                K.op(PE, lambda: nc.tensor.matmul(p12.ap[:, 256:512], gt[hb:hb + 64, 3, :], ar, start=True, stop=True),
                     w=[p12], r=[gtb])
                yield
                K.op(DVE, lambda: nc.vector.tensor_tensor(out=M1_t[:], in0=p12.ap[:, 0:256], in1=mk2_t[:],
                                                          op=ALU.mult), w=[M1], r=[p12, mk2])
                K.op(ACT, lambda: nc.scalar.copy(out=M2_t[:], in_=p12.ap[:, 256:512]), w=[M2], r=[p12])
                K.op(POOL, lambda: nc.gpsimd.tensor_tensor(out=M2_t[:], in0=M2_t[:], in1=mk2b_t[:], op=ALU.mult),
                     w=[M2], r=[M2, mk2b])
                p3 = psum4()
                K.op(PE, lambda: nc.tensor.matmul(p3.ap[:, 0:128], gt[hb:hb + 64, 0, :], gt[hb:hb + 64, 2, :],
                                                  start=True, stop=True), w=[p3], r=[gtb])
                yield
                K.op(DVE, lambda: nc.vector.tensor_tensor(out=XT0_t[:], in0=p3.ap[:, 0:128], in1=Ls_t[:],
                                                          op=ALU.mult), w=[XT0], r=[p3, Ls])
                X_ap = M1_t[:, 0:128]
                W1_t, W1 = hs["W1"]
                AU_t, AU = hs["AU"]
                RbT_t, RbT = hs["RbT"]
                XXa_t, XXa = hs["XXa"]
                XXb_t, XXb = hs["XXb"]
                X8_t, X8 = hs["X8"]
                S2_t, S2 = hs["S2"]
                AT_t, ATb = hs["AT"]
                Bm_t, Bmb = hs["Bm"]
                XT8_t, XT8 = hs["XT8"]
                Pc_t, Pc = hs["Pa"]
                yield
                px = psum4()
                K.op(PE, lambda: nc.tensor.matmul(px.ap[:, 0:128], XT0_t[:], X_ap, start=True, stop=True),
                     w=[px], r=[M1, XT0])
                K.op(PE, lambda: nc.tensor.matmul(px.ap[:, 128:256], X_ap, XT0_t[:], start=True, stop=True),
                     w=[px], r=[M1, XT0])
                yield
                K.op(ACT, lambda: nc.scalar.copy(out=XXa_t[:], in_=px.ap[:, 0:256]), w=[XXa], r=[px])
                K.op(DVE, lambda: nc.vector.tensor_tensor(out=XT0_t[:], in0=XT0_t[:], in1=self.ident_b.ap,
                                                          op=ALU.add), w=[XT0], r=[XT0, self.ident_b])
                yield
                p2 = psum4()
                K.op(PE, lambda: nc.tensor.matmul(p2.ap[:, 0:128], XXa_t[:, 128:256], XXa_t[:, 0:128], start=True,
                                                  stop=True), w=[p2], r=[XXa])
                K.op(PE, lambda: nc.tensor.matmul(p2.ap[:, 128:256], XXa_t[:, 0:128], XXa_t[:, 128:256], start=True,
                                                  stop=True), w=[p2], r=[XXa])
                K.op(PE, lambda: nc.tensor.matmul(p2.ap[:, 256:384], X_ap, XXa_t[:, 128:256], start=True, stop=True),
                     w=[p2], r=[M1, XXa])
                K.op(DVE, lambda: nc.vector.tensor_tensor(out=XT0_t[:], in0=XT0_t[:], in1=XXa_t[:, 128:256],
                                                          op=ALU.add), w=[XT0], r=[XT0, XXa])
                yield
                K.op(ACT, lambda: nc.scalar.copy(out=XXb_t[:], in_=p2.ap[:, 0:256]), w=[XXb], r=[p2])
                K.op(DVE, lambda: nc.vector.tensor_tensor(out=AT_t[:], in0=p2.ap[:, 256:384], in1=XT0_t[:], op=ALU.add),
                     w=[ATb], r=[p2, XT0])
                K.op(POOL, lambda: nc.gpsimd.tensor_tensor(out=S2_t[:], in0=XXb_t[:, 0:128], in1=self.ident_b.ap,
                                                           op=ALU.add), w=[S2], r=[XXb, self.ident_b])
                yield
                p3b = psum4()
                K.op(PE, lambda: nc.tensor.matmul(p3b.ap[:, 0:128], XXb_t[:, 128:256], XXb_t[:, 0:128], start=True,
                                                  stop=True), w=[p3b], r=[XXb])
                K.op(PE, lambda: nc.tensor.matmul(p3b.ap[:, 128:192], M2_t[:, 0:128], v_t[:, hc], start=True, stop=True),
                     w=[p3b], r=[M2, v_b])
                K.op(PE, lambda: nc.tensor.matmul(p3b.ap[:, 192:320], XXb_t[:, 0:128], XXb_t[:, 128:256], start=True,
                                                  stop=True), w=[p3b], r=[XXb])
                yield
                K.op(ACT, lambda: nc.scalar.copy(out=X8_t[:], in_=p3b.ap[:, 0:128]), w=[X8], r=[p3b])
                K.op(ACT, lambda: nc.scalar.copy(out=W1_t[:], in_=p3b.ap[:, 128:192]), w=[W1], r=[p3b])
                K.op(ACT, lambda: nc.scalar.copy(out=XT8_t[:], in_=p3b.ap[:, 192:320]), w=[XT8], r=[p3b])
                yield
                p4 = psum4()
                K.op(PE, lambda: nc.tensor.matmul(p4.ap[:, 0:128], XXb_t[:, 128:256], X8_t[:], start=True, stop=True),
                     w=[p4], r=[XXb, X8])
                K.op(DVE, lambda: nc.vector.tensor_tensor(out=S2_t[:], in0=S2_t[:], in1=X8_t[:], op=ALU.add),
                     w=[S2], r=[S2, X8])
                K.op(PE, lambda: nc.tensor.matmul(p4.ap[:, 128:256], X8_t[:], XT8_t[:], start=True, stop=True),
                     w=[p4], r=[X8, XT8])
                yield
                K.op(DVE, lambda: nc.vector.tensor_tensor(out=Bm_t[:], in0=p4.ap[:, 0:128], in1=S2_t[:], op=ALU.add),
                     w=[Bmb], r=[p4, S2])
                K.op(ACT, lambda: nc.scalar.copy(out=XT8_t[:], in_=p4.ap[:, 128:256]), w=[XT8], r=[p4])
                yield
                p5 = psum4()
                K.op(PE, lambda: nc.tensor.matmul(p5.ap[:, 0:128], AT_t[:], Bm_t[:], start=True, stop=True),
                     w=[p5], r=[ATb, Bmb])
                yield
                K.op(ACT, lambda: nc.scalar.copy(out=Pc_t[:], in_=p5.ap[:, 0:128]), w=[Pc], r=[p5])
                yield
                p6 = psum4()
                K.op(PE, lambda: nc.tensor.matmul(p6.ap[:, 0:128], XT8_t[:], Pc_t[:], start=True, stop=True),
                     w=[p6], r=[XT8, Pc])
                yield
                K.op(DVE, lambda: nc.vector.tensor_tensor(out=Pc_t[:], in0=p6.ap[:, 0:128], in1=Pc_t[:], op=ALU.add),
                     w=[Pc], r=[p6, Pc])
                yield
                pau = psum4()
                K.op(PE, lambda: nc.tensor.matmul(pau.ap[:, 0:64], Pc_t[:], at_t[:, hc], start=True, stop=True),
                     w=[pau], r=[Pc, at_b])
                K.op(PE, lambda: nc.tensor.matmul(pau.ap[:, 64:128], Pc_t[:], W1_t[:], start=True, stop=True),
                     w=[pau], r=[Pc, W1])
                yield
                K.op(ACT, lambda: nc.scalar.copy(out=AU_t[:], in_=pau.ap[:, 0:128]), w=[AU], r=[pau])
                yield
                if i == self.cfg.get("dbg_tile", 0) and h == self.cfg.get("dbg_head", 0):
                    self.dbg("M1", M1_t[:], M1)
                    self.dbg("P", Pc_t[:], Pc)
                    self.dbg("AU", AU_t[:], AU)
                pgf = psum4()
                K.op(PE, lambda: nc.tensor.matmul(pgf.ap[0:64, 0:64], AU_t[:, 0:64], bh_t[:, hc], start=True, stop=True),
                     w=[pgf], r=[AU, bh_b])
                K.op(PE, lambda: nc.tensor.matmul(pgf.ap[0:64, 64:128], bh_t[:, hc], AU_t[:, 64:128], start=True,
                                                  stop=False), w=[pgf], r=[AU, bh_b])
                K.op(PE, lambda: nc.tensor.matmul(pgf.ap[0:64, 64:128], kh_t[:, hc], v_t[:, hc], start=False, stop=True),
                     w=[pgf], r=[kh_b, v_b])
                prb = pgf
                K.op(PE, lambda: nc.tensor.matmul(prb.ap[0:64, 128:256], AU_t[:, 0:64], M1_t[:, 128:256], start=True,
                                                  stop=False), w=[prb], r=[AU, M1])
                K.op(PE, lambda: nc.tensor.matmul(prb.ap[0:64, 128:256], rt_t[:, hc], self.ident_b.ap, start=False,
                                                  stop=True), w=[prb], r=[rt_b, self.ident_b])
                yield
                K.op(DVE, lambda: nc.vector.scalar_tensor_tensor(
                    out=GTa_t[:, hc], in0=self.ident_f.ap[0:64, 0:64], scalar=pC_t[:, h:h + 1], in1=pgf.ap[0:64, 0:64],
                    op0=ALU.mult, op1=ALU.add), w=[GTa], r=[self.ident_f, pCs, pgf])
                K.op(ACT, lambda: nc.scalar.copy(out=Fa_t[:, hc], in_=pgf.ap[0:64, 64:128]), w=[Fa], r=[pgf])
                K.op(ACT, lambda: nc.scalar.copy(out=RbT_t[:], in_=prb.ap[0:64, 128:256]), w=[RbT], r=[prb])
                yield
                K.op(PE, lambda: nc.tensor.matmul(pY.ap[:, hc], M1_t[:, 128:256], AU_t[:, 64:128], start=True,
                                                  stop=False), w=[pY], r=[M1, AU])
                K.op(PE, lambda: nc.tensor.matmul(pY.ap[:, hc], M2_t[:, 128:256], v_t[:, hc], start=False,
                                                  stop=False), w=[pY], r=[M2, v_b])
                K.op(PE, lambda: nc.tensor.matmul(pY.ap[:, hc], RbT_t[0:64, :], Hb_t[0:64, hc], start=False,
                                                  stop=True), w=[pY], r=[RbT, Hb])

            def recurrence(i):
                pH = psum4()
                for h in range(8):
                    hc = slice(h * 64, (h + 1) * 64)
                    K.op(PE, lambda hc=hc: nc.tensor.matmul(pH.ap[0:64, hc], GTa_t[:, hc], Hf_t[:, hc], start=True,
                                                            stop=True), w=[pH], r=[GTa, Hf])
                K.op(DVE, lambda: nc.vector.tensor_tensor(out=Hf_t[:], in0=pH.ap[0:64, :], in1=Fa_t[:], op=ALU.add),
                     w=[Hf], r=[pH, Fa])
                K.op(ACT, lambda: nc.scalar.copy(out=Hb_t[:], in_=Hf_t[:]), w=[Hb], r=[Hf])

            def post_gen(i):
                tb, ti = i // 4, i % 4
                gg_t, gg = gg3[i % 3]
                st_t, st = st3[i % 3]
                v_t, v_b = v3b[i % 3]
                K.op(ACT, lambda: nc.scalar.activation(out=pB_t[:], in_=pY.ap, func=AF.Square), w=[pBb], r=[pY])
                K.op(DVE, lambda: nc.vector.tensor_reduce(out=st_t[:, 32:40], in_=v3(pY.ap), axis=AX.X, op=ALU.add),
                     w=[st], r=[pY])
                K.op(ACT, lambda: nc.scalar.copy(out=pA_t[:], in_=pY.ap), w=[pAb], r=[pY])
                yield
                K.op(DVE, lambda: nc.vector.tensor_reduce(out=st_t[:, 40:48], in_=v3(pB_t[:]), axis=AX.X, op=ALU.add),
                     w=[st], r=[pBb])
                K.op(DVE, lambda: nc.vector.tensor_scalar(out=st_t[:, 32:40], in0=st_t[:, 32:40], scalar1=1.0 / 64,
                                                          scalar2=None, op0=ALU.mult), w=[st], r=[st])
                K.op(DVE, lambda: nc.vector.tensor_tensor(out=st_t[:, 48:56], in0=st_t[:, 32:40], in1=st_t[:, 32:40],
                                                          op=ALU.mult), w=[st], r=[st])
                K.op(DVE, lambda: nc.vector.scalar_tensor_tensor(out=st_t[:, 40:48], in0=st_t[:, 40:48], scalar=1.0 / 64,
                                                                 in1=st_t[:, 48:56], op0=ALU.mult, op1=ALU.subtract),
                     w=[st], r=[st])
                yield
                K.op(ACT, lambda: nc.scalar.activation(out=st_t[:, 40:48], in_=st_t[:, 40:48], func=AF.Sqrt, scale=1.0,
                                                       bias=self.eps_cols[LNX_EPS]), w=[st], r=[st, self.epsb])
                yield
                K.op(DVE, lambda: nc.vector.reciprocal(out=st_t[:, 56:64], in_=st_t[:, 40:48]), w=[st], r=[st])
                K.op(DVE, lambda: nc.vector.tensor_tensor(out=v3(pA_t[:]), in0=v3(pA_t[:]), in1=bc3(st_t[:, 32:40]),
                                                          op=ALU.subtract), w=[pAb], r=[pAb, st])
                K.op(DVE, lambda: nc.vector.tensor_tensor(out=v3(pA_t[:]), in0=v3(pA_t[:]), in1=bc3(st_t[:, 56:64]),
                                                          op=ALU.mult), w=[pAb], r=[pAb, st])
                K.op(DVE, lambda: nc.vector.tensor_tensor(out=v3(pB_t[:]), in0=v3(v_t[:]), in1=bc3(st_t[:, 24:32]),
                                                          op=ALU.mult), w=[pBb], r=[v_b, st])
                yield
                K.op(POOL, lambda: nc.gpsimd.tensor_tensor(out=pA_t[:], in0=pA_t[:], in1=lgb_t[:], op=ALU.mult),
                     w=[pAb], r=[pAb, lgb])
                yield
                K.op(POOL, lambda: nc.gpsimd.tensor_tensor(out=pB_t[:], in0=pB_t[:], in1=lbb_t[:], op=ALU.add),
                     w=[pBb], r=[pBb, lbb])
                yield
                K.op(DVE, lambda: nc.vector.tensor_tensor(out=pA_t[:], in0=pA_t[:], in1=pB_t[:], op=ALU.add),
                     w=[pAb], r=[pAb, pBb])
                K.op(DVE, lambda: nc.vector.tensor_tensor(out=yb_t[:], in0=pA_t[:], in1=gg_t[:], op=ALU.mult),
                     w=[yb_b], r=[pAb, gg])
                if i == self.cfg.get("dbg_tile", 0):
                    self.dbg("yb", yb_t[:], yb_b)
                yield
                pyt = psum4()
                py16 = pyt.ap.bitcast(BF16)
                for q in range(4):
                    K.op(PE, lambda q=q: nc.tensor.transpose(py16[:, q * 128:(q + 1) * 128], yb_t[:, q * 128:(q + 1) * 128],
                                                             self.ident_b.ap), w=[pyt], r=[yb_b, self.ident_b])
                yield
                K.op(ACT, lambda: nc.scalar.copy(out=ybT_t[:, :, ti * 128:(ti + 1) * 128],
                                                 in_=py16[:, 0:512].rearrange("p (q t) -> p q t", q=4)),
                     w=ybT, r=[pyt])
                if ti == 3:
                    wos = [self.wo.get(self.jobs_wo[("even", l, ch)]) for ch in (4, 5, 6, 7)]
                    for d in range(NC_):
                        yield
                        po = psum4()
                        for q in range(4):
                            K.op(PE, lambda q=q: nc.tensor.matmul(po.ap, wos[q].ap[:, d * 128:(d + 1) * 128], ybT[q].ap,
                                                                  start=(q == 0), stop=(q == 3)),
                                 w=[po], r=[wos[q], ybT[q]])
                        yield
                        xb = self.x[d][tb]
                        K.op(DVE, lambda: nc.vector.tensor_tensor(out=xb.ap, in0=po.ap, in1=xb.ap, op=ALU.add),
                             w=[xb], r=[po, xb])

            def step(g):
                try:
                    next(g)
                    return True
                except StopIteration:
                    return False

            def run(mains, bgs):
                mains = list(mains)
                while mains:
                    for g_ in list(mains):
                        if not step(g_):
                            mains.remove(g_)
                    for g_ in list(bgs):
                        if not step(g_):
                            bgs.remove(g_)

            def drain(bgs):
                while bgs:
                    for g_ in list(bgs):
                        if not step(g_):
                            bgs.remove(g_)

            drain([pre_gen(0)])
            for i in range(16):
                bgs = []
                if i + 1 < 16:
                    bgs.append(pre_gen(i + 1))
                if i >= 1:
                    bgs.append(post_gen(i - 1))
                for g0 in range(0, 8, NHS):
                    run([head_gen(h, i) for h in range(g0, g0 + NHS)], bgs)
                drain(bgs)
                recurrence(i)
            drain([post_gen(15)])
            for ch in range(4, 8):
                self.wo.release(self.jobs_wo[("even", l, ch)])

    def mixer_odd(self, l):
        K, nc = self.K, self.nc
        o = l // 2
        PE, ACT, DVE, POOL, SP = K.PE, K.ACT, K.DVE, K.POOL, K.SP
        ps = self.ps

        def outproj(mixb, chs):
            wos = [self.wo.get(self.jobs_wo[("odd", l, ch)]) for ch in chs]
            for d in range(NC_):
                for tb in range(NTB):
                    po = self.psum()
                    for i, ch in enumerate(chs):
                        K.op(PE, lambda i=i: nc.tensor.matmul(po.ap, wos[i].ap[:, d * 128:(d + 1) * 128], mixb[i][tb].ap,
                                                              start=(i == 0), stop=(i == len(chs) - 1)),
                             w=[po], r=[wos[i], mixb[i][tb]])
                    xb = self.x[d][tb]
                    K.op(DVE, lambda: nc.vector.tensor_tensor(out=xb.ap, in0=po.ap, in1=xb.ap, op=ALU.add),
                         w=[xb], r=[po, xb])
            for ch in chs:
                self.wo.release(self.jobs_wo[("odd", l, ch)])

        with ExitStack() as es:
            small_t = K.sb(es, [128, ODD_NS], F32, "osmall")
            small = Buf(small_t[:], "osmall")
            K.dma(SP, small_t[:], self.d_odd_small[o], w=[small])
            qn_col = lambda c: small_t[:, c:c + 1]
            kvn_col = lambda c: small_t[:, 2:3]
            cw = lambda q, j: small_t[:, 3 + q * 31 + j: 3 + q * 31 + j + 1]
            cb = lambda q: small_t[:, 127 + q:128 + q]
            lng = lambda q: small_t[:, 131 + q:132 + q]
            lnb = lambda q: small_t[:, 135 + q:136 + q]
            cqn_t = K.sb(es, [128, 2, S], BF16, "cqn")
            cqn = [[Buf(cqn_t[:, c, tb * 512:(tb + 1) * 512]) for tb in range(NTB)] for c in range(2)]
            ckvn_t = K.sb(es, [128, S], BF16, "ckvn")
            ckvn = [[Buf(ckvn_t[:, tb * 512:(tb + 1) * 512]) for tb in range(NTB)]]
            kr_t = K.sb(es, [128, S], BF16, "krope")
            kr = [Buf(kr_t[:, tb * 512:(tb + 1) * 512]) for tb in range(NTB)]
            rope_t = K.sb(es, [128, 2, S], BF16, "rope")
            rope = Buf(rope_t[:], "rope")
            K.dma(POOL, rope_t[:], self.d_rope[:, :, :], w=[rope])
            wq_t = K.sb(es, [128, 2, 768], BF16, "wq")
            wq = Buf(wq_t[:], "wq")
            wqr_t = K.sb(es, [128, 2, 768], BF16, "wqr")
            wqr = Buf(wqr_t[:], "wqr")
            wkv_t = K.sb(es, [128, 1024], BF16, "wkv")
            wkv = Buf(wkv_t[:], "wkv")
            K.dma(POOL, wq_t[:], self.d_wq[o], w=[wq])
            K.dma(POOL, wqr_t[:], self.d_wqr[o], w=[wqr])
            K.dma(POOL, wkv_t[:], self.d_wkv[o], w=[wkv])

            with ExitStack() as esx:
                hglu_t = K.sb(esx, [128, 4, 30 + S], BF16, "hglu")
                hglu = [Buf(hglu_t[:, q, :]) for q in range(4)]
                for q in range(4):
                    K.op(DVE, lambda q=q: nc.vector.memset(hglu_t[:, q, 0:30], 0.0), w=[hglu[q]])
                with ExitStack() as esa:
                    hT_t = K.sb(esa, [128, NC_, S], BF16, "hT")
                    hT = [[Buf(hT_t[:, c, tb * 512:(tb + 1) * 512]) for tb in range(NTB)] for c in range(NC_)]
                    self.rmsnorm_fm(esa, self.x, self.gain_col(DEPTH + l), hT, NC_, D, RMS_EPS)
                    cq32_t = K.sb(esa, [128, 2, 512], F32, "cq32")
                    cq32 = [[Buf(cq32_t[:, c, :])] for c in range(2)]
                    ckv32_t = K.sb(esa, [128, 512], F32, "ckv32")
                    ckv32 = [[Buf(ckv32_t[:])]]
                    tmpa = Buf(K.sb(esa, [128, 512], F32, "tmpa")[:])
                    tmpb = Buf(K.sb(esa, [128, 512], F32, "tmpb")[:])

                    def proj(wb, lo, M, tb, pb):
                        for c in range(NC_):
                            K.op(PE, lambda c=c: nc.tensor.matmul(pb.ap[0:M, :], wb.ap[:, c, lo:lo + M], hT[c][tb].ap,
                                                                  start=(c == 0), stop=(c == NC_ - 1)),
                                 w=[pb], r=[wb, hT[c][tb]])

                    j0 = self.jobs_wi[("odd", l, 0)]
                    w0 = self.wi.get(j0)
                    for tb in range(NTB):
                        for c in range(2):
                            pb = self.psum()
                            proj(w0, c * 128, 128, tb, pb)
                            K.op(ACT, lambda c=c, pb=pb: nc.scalar.copy(out=cq32[c][0].ap, in_=pb.ap),
                                 w=[cq32[c][0]], r=[pb])
                        self.rmsnorm_fm(esa, [[cq32[0][0]], [cq32[1][0]]], qn_col, [[cqn[0][tb]], [cqn[1][tb]]], 2, 256,
                                        RMS_EPS, tbs=[0], extra_r=[small])
                    self.wi.release(j0)
                    j1 = self.jobs_wi[("odd", l, 1)]
                    w1 = self.wi.get(j1)
                    j2 = self.jobs_wi[("odd", l, 2)]
                    w2 = self.wi.get(j2)
                    for tb in range(NTB):
                        pb = self.psum()
                        proj(w1, 0, 128, tb, pb)
                        K.op(ACT, lambda pb=pb: nc.scalar.copy(out=ckv32[0][0].ap, in_=pb.ap), w=[ckv32[0][0]], r=[pb])
                        self.rmsnorm_fm(esa, [[ckv32[0][0]]], kvn_col, [[ckvn[0][tb]]], 1, 128, RMS_EPS, tbs=[0],
                                        extra_r=[small])
                        p1 = self.psum()
                        proj(w1, 128, 96, tb, p1)
                        p2 = self.psum()
                        proj(w2, 0, 96, tb, p2)
                        sl = slice(tb * 512, (tb + 1) * 512)
                        K.op(DVE, lambda p1=p1, sl=sl: nc.vector.tensor_tensor(out=tmpa.ap[64:96, :], in0=p1.ap[64:96, :],
                                                                               in1=rope_t[64:96, 0, sl], op=ALU.mult),
                             w=[tmpa], r=[p1, rope])
                        K.op(DVE, lambda p2=p2, sl=sl: nc.vector.tensor_tensor(out=tmpb.ap[64:96, :], in0=p2.ap[64:96, :],
                                                                               in1=rope_t[64:96, 1, sl], op=ALU.mult),
                             w=[tmpb], r=[p2, rope])
                        K.op(DVE, lambda tb=tb: nc.vector.tensor_tensor(out=kr[tb].ap[64:96, :], in0=tmpa.ap[64:96, :],
                                                                        in1=tmpb.ap[64:96, :], op=ALU.add),
                             w=[kr[tb]], r=[tmpa, tmpb])
                    self.wi.release(j1)
                    self.wi.release(j2)
                    for q in range(4):
                        jq = self.jobs_wi[("odd", l, 3 + q)]
                        wq_ = self.wi.get(jq)
                        for tb in range(NTB):
                            pa = self.psum()
                            proj(wq_, 0, 128, tb, pa)
                            pbb = self.psum()
                            proj(wq_, 128, 128, tb, pbb)
                            K.op(ACT, lambda pbb=pbb: nc.scalar.activation(out=tmpa.ap, in_=pbb.ap, func=AF.Sigmoid),
                                 w=[tmpa], r=[pbb])
                            K.op(DVE, lambda pa=pa, q=q, tb=tb: nc.vector.tensor_tensor(
                                out=hglu_t[:, q, 30 + tb * 512: 30 + (tb + 1) * 512], in0=pa.ap, in1=tmpa.ap,
                                op=ALU.mult), w=[hglu[q]], r=[pa, tmpa])
                        self.wi.release(jq)
                    K.barrier()

                with ExitStack() as esb:
                    yd_t = K.sb(esb, [128, 4, S], BF16, "ydT")
                    yd = [[Buf(yd_t[:, q, tb * 512:(tb + 1) * 512]) for tb in range(NTB)] for q in range(4)]
                    diag_t = [K.sb(esb, [128, 31, 128], BF16, "diag%d" % i) for i in range(2)]
                    diag = [Buf(t[:]) for t in diag_t]
                    c32_t = K.sb(esb, [128, 4, 512], F32, "c32")
                    c32 = [Buf(c32_t[:, q, :]) for q in range(4)]
                    sq32 = [Buf(K.sb(esb, [128, 512], F32, "sq32_%d" % i)[:]) for i in range(2)]
                    mean = Buf(K.sb(esb, [128, 512], F32, "mean")[:])
                    rstd = Buf(K.sb(esb, [128, 512], F32, "rstd")[:])
                    msq = Buf(K.sb(esb, [128, 512], F32, "msq")[:])
                    nd = 0
                    for tb in range(NTB):
                        p_s1 = ps[4]
                        p_s2 = ps[5]
                        for q in range(4):
                            dg, dg_t = diag[nd % 2], diag_t[nd % 2]
                            nd += 1
                            for j in range(31):
                                K.op(DVE, lambda j=j, q=q, dg_t=dg_t: nc.vector.tensor_scalar(
                                    out=dg_t[:, j, :], in0=self.ident_f.ap, scalar1=cw(q, j), scalar2=None,
                                    op0=ALU.mult), w=[dg], r=[self.ident_f, small])
                            pc = ps[q]
                            for j in range(31):
                                K.op(PE, lambda q=q, j=j, pc=pc, dg_t=dg_t: nc.tensor.matmul(
                                    pc.ap, dg_t[:, j, :], hglu_t[:, q, tb * 512 + j: tb * 512 + j + 512],
                                    start=(j == 0), stop=(j == 30)), w=[pc], r=[dg, hglu[q]])
                            K.op(ACT, lambda q=q, pc=pc: nc.scalar.activation(out=c32[q].ap, in_=pc.ap, func=AF.Identity,
                                                                              bias=cb(q), scale=1.0),
                                 w=[c32[q]], r=[pc, small])
                            sqb = sq32[q % 2]
                            K.op(ACT, lambda q=q, sqb=sqb: nc.scalar.activation(out=sqb.ap, in_=c32[q].ap, func=AF.Square),
                                 w=[sqb], r=[c32[q]])
                            K.op(PE, lambda q=q: nc.tensor.matmul(p_s1.ap, self.ones_f.ap, c32[q].ap, start=(q == 0),
                                                                  stop=(q == 3)), w=[p_s1], r=[self.ones_f, c32[q]])
                            K.op(PE, lambda q=q, sqb=sqb: nc.tensor.matmul(p_s2.ap, self.ones_f.ap, sqb.ap, start=(q == 0),
                                                                           stop=(q == 3)), w=[p_s2], r=[self.ones_f, sqb])
                        K.op(ACT, lambda: nc.scalar.mul(out=mean.ap, in_=p_s1.ap, mul=1.0 / 512), w=[mean], r=[p_s1])
                        K.op(DVE, lambda: nc.vector.tensor_tensor(out=msq.ap, in0=mean.ap, in1=mean.ap, op=ALU.mult),
                             w=[msq], r=[mean])
                        K.op(DVE, lambda: nc.vector.scalar_tensor_tensor(out=rstd.ap, in0=p_s2.ap, scalar=1.0 / 512,
                                                                         in1=msq.ap, op0=ALU.mult, op1=ALU.subtract),
                             w=[rstd], r=[p_s2, msq])
                        K.op(ACT, lambda: nc.scalar.activation(out=rstd.ap, in_=rstd.ap, func=AF.Sqrt, scale=1.0,
                                                               bias=self.eps_cols[LN_EPS]), w=[rstd], r=[rstd, self.epsb])
                        K.op(DVE, lambda: nc.vector.reciprocal(out=rstd.ap, in_=rstd.ap), w=[rstd], r=[rstd])
                        for q in range(4):
                            K.op(DVE, lambda q=q: nc.vector.tensor_tensor(out=c32[q].ap, in0=c32[q].ap, in1=mean.ap,
                                                                          op=ALU.subtract), w=[c32[q]], r=[c32[q], mean])
                            K.op(DVE, lambda q=q: nc.vector.tensor_tensor(out=c32[q].ap, in0=c32[q].ap, in1=rstd.ap,
                                                                          op=ALU.mult), w=[c32[q]], r=[c32[q], rstd])
                            K.op(ACT, lambda q=q, tb=tb: nc.scalar.activation(out=yd[q][tb].ap, in_=c32[q].ap,
                                                                              func=AF.Silu, bias=lnb(q), scale=lng(q)),
                                 w=[yd[q][tb]], r=[c32[q], small])
                    outproj(yd, [4, 5, 6, 7])
                    K.barrier()

            with ExitStack() as esc:
                yc_t = K.sb(esc, [128, 4, S], BF16, "ycT")
                yc = [[Buf(yc_t[:, c, tb * 512:(tb + 1) * 512]) for tb in range(NTB)] for c in range(4)]
                vall_t = K.sb(esc, [128, 16, 512], BF16, "vall")
                vall = [Buf(vall_t[:, i, :]) for i in range(16)]
                for i in range(16):
                    pv = self.psum()
                    tbi = i // 4
                    K.op(PE, lambda i=i, pv=pv: nc.tensor.matmul(
                        pv.ap, ckvn_t[:, i * 128:(i + 1) * 128], wkv_t[:, 512:1024],
                        start=True, stop=True), w=[pv], r=[ckvn[0][tbi], wkv])
                    K.op(ACT, lambda i=i, pv=pv: nc.scalar.copy(out=vall[i].ap, in_=pv.ap), w=[vall[i]], r=[pv])
                qT_t = [K.sb(esc, [128, S], BF16, "qT%d" % i) for i in range(2)]
                kT_t = [K.sb(esc, [128, S], BF16, "kT%d" % i) for i in range(2)]
                qT = [Buf(t[:]) for t in qT_t]
                kT = [Buf(t[:]) for t in kT_t]
                oc_t = [K.sb(esc, [128, S], BF16, "oc%d" % i) for i in range(2)]
                oc = [Buf(t[:]) for t in oc_t]
                pT_t = [K.sb(esc, [128, 512], BF16, "pT%d" % i) for i in range(3)]
                pT = [Buf(t[:]) for t in pT_t]
                rden = Buf(K.sb(esc, [128, 512], F32, "rden")[:])
                tq1 = Buf(K.sb(esc, [128, 512], F32, "tq1")[:])
                tq2 = Buf(K.sb(esc, [128, 512], F32, "tq2")[:])
                npt = 0
                for h in range(8):
                    qh, kh, och = qT[h % 2], kT[h % 2], oc[h % 2]
                    qh_t, kh_t, och_t = qT_t[h % 2], kT_t[h % 2], oc_t[h % 2]
                    for tb in range(NTB):
                        sl = slice(tb * 512, (tb + 1) * 512)
                        pq = ps[0 + (tb % 2)]
                        pr = ps[2 + (tb % 2)]
                        pk = ps[4 + (tb % 2)]
                        for c in range(2):
                            K.op(PE, lambda c=c, pq=pq: nc.tensor.matmul(pq.ap[0:96, :], wq_t[:, c, h * 96:(h + 1) * 96],
                                                                         cqn[c][tb].ap, start=(c == 0), stop=(c == 1)),
                                 w=[pq], r=[wq, cqn[c][tb]])
                        for c in range(2):
                            K.op(PE, lambda c=c, pr=pr: nc.tensor.matmul(pr.ap[0:96, :], wqr_t[:, c, h * 96:(h + 1) * 96],
                                                                         cqn[c][tb].ap, start=(c == 0), stop=(c == 1)),
                                 w=[pr], r=[wqr, cqn[c][tb]])
                        K.op(PE, lambda pk=pk: nc.tensor.matmul(pk.ap[0:64, :], wkv_t[:, h * 64:h * 64 + 64],
                                                                ckvn[0][tb].ap, start=True, stop=True),
                             w=[pk], r=[wkv, ckvn[0][tb]])
                        K.op(ACT, lambda pq=pq, sl=sl: nc.scalar.copy(out=qh_t[0:64, sl], in_=pq.ap[0:64, :]),
                             w=[qh], r=[pq])
                        K.op(DVE, lambda pq=pq, sl=sl: nc.vector.tensor_tensor(out=tq1.ap[64:96, :], in0=pq.ap[64:96, :],
                                                                               in1=rope_t[64:96, 0, sl], op=ALU.mult),
                             w=[tq1], r=[pq, rope])
                        K.op(DVE, lambda pr=pr, sl=sl: nc.vector.tensor_tensor(out=tq2.ap[64:96, :], in0=pr.ap[64:96, :],
                                                                               in1=rope_t[64:96, 1, sl], op=ALU.mult),
                             w=[tq2], r=[pr, rope])
                        K.op(DVE, lambda sl=sl: nc.vector.tensor_tensor(out=qh_t[64:96, sl], in0=tq1.ap[64:96, :],
                                                                        in1=tq2.ap[64:96, :], op=ALU.add),
                             w=[qh], r=[tq1, tq2])
                        K.op(ACT, lambda pk=pk, sl=sl: nc.scalar.copy(out=kh_t[0:64, sl], in_=pk.ap[0:64, :]),
                             w=[kh], r=[pk])
                        K.op(DVE, lambda sl=sl, tb=tb: nc.vector.tensor_copy(kh_t[64:96, sl], kr_t[64:96, sl]),
                             w=[kh], r=[kr[tb]])
                    for qb in range(NTB):
                        pO = ps[6]
                        pD = ps[7]
                        nkt = 4 * qb + 4
                        def emit_S(kt, qb=qb, nkt=nkt):
                            nonlocal npt
                            m = kt - 4 * qb
                            q0 = max(m, 0) * 128
                            pS = ps[kt % 2]
                            pt = pT[npt % 3]
                            pt_t = pT_t[npt % 3]
                            npt += 1
                            K.op(PE, lambda: nc.tensor.matmul(
                                pS.ap[:, q0:512], kh_t[0:96, kt * 128:(kt + 1) * 128],
                                qh_t[0:96, qb * 512 + q0:(qb + 1) * 512], start=True, stop=True),
                                w=[pS], r=[kh, qh])
                            K.op(ACT, lambda: nc.scalar.activation(
                                out=pt_t[:, q0:512], in_=pS.ap[:, q0:512], func=AF.Exp, scale=ATTN_SCALE),
                                w=[pt], r=[pS])
                            if m >= 0:
                                K.op(POOL, lambda: nc.gpsimd.memset(pt_t[64:128, q0:q0 + 64], 0.0),
                                     w=[pt], r=[pt])
                            return (pt, pt_t, q0)

                        def emit_PV(kt, st_, nkt=nkt):
                            pt, pt_t, q0 = st_
                            K.op(PE, lambda: nc.tensor.matmul(
                                pO.ap[0:64, q0:512], vall_t[:, kt, h * 64:(h + 1) * 64], pt_t[:, q0:512],
                                start=(kt == 0), stop=(kt == nkt - 1)), w=[pO], r=[vall[kt], pt])
                            K.op(PE, lambda: nc.tensor.matmul(
                                pD.ap[0:64, q0:512], self.ones_bf.ap[:, 0:64], pt_t[:, q0:512],
                                start=(kt == 0), stop=(kt == nkt - 1)), w=[pD], r=[self.ones_bf, pt])

                        nxt_st = emit_S(0)
                        for kt in range(nkt):
                            cur_st = nxt_st
                            if kt + 1 < nkt:
                                nxt_st = emit_S(kt + 1)
                            emit_PV(kt, cur_st)
                        K.op(DVE, lambda: nc.vector.reciprocal(out=rden.ap[0:64, :], in_=pD.ap[0:64, :]),
                             w=[rden], r=[pD])
                        K.op(DVE, lambda qb=qb: nc.vector.tensor_tensor(out=och_t[0:64, qb * 512:(qb + 1) * 512],
                                                                        in0=pO.ap[0:64, :], in1=rden.ap[0:64, :],
                                                                        op=ALU.mult), w=[och], r=[pO, rden])
                    pb0 = (h % 2) * 64
                    K.dma(SP, yc_t[pb0:pb0 + 64, h // 2, :], och_t[0:64, :], w=[yc[h // 2][tb] for tb in range(NTB)],
                          r=[och])
                outproj(yc, [0, 1, 2, 3])
                K.barrier()

    def final(self, do_norm):
        K, nc = self.K, self.nc
        with ExitStack() as es:
            if do_norm:
                o_t = K.sb(es, [128, NC_, S], F32, "oT")
                o = [[Buf(o_t[:, c, tb * 512:(tb + 1) * 512]) for tb in range(NTB)] for c in range(NC_)]
                self.rmsnorm_fm(es, self.x, self.gain_col(12), o, NC_, D, RMS_EPS)
            else:
                o = self.x
            toks = []
            for c in range(NC_):
                for tb in range(NTB):
                    toks.append(K.dma(K.SP, self.d_out[:, c, tb * 512:(tb + 1) * 512], o[c][tb].ap, r=[o[c][tb]]))
            for t in toks + list(self.dbg_toks.values()):
                K.wait_tok(K.SP, t)
            K.barrier()


def build_program(cfg):
    p = Prog(cfg)
    return p.build()


def prep_shared(inp):
    f32 = np.float32
    sh = {}
    gains = np.concatenate([inp["norm_ffn1"], inp["norm_mix"], inp["norm_ffn2"], inp["final_norm"][None]], axis=0)
    sh["gains"] = np.ascontiguousarray(gains.reshape(13, NC_, 128).transpose(2, 0, 1).reshape(128, 13 * NC_)).astype(f32)
    fin = np.stack([inp["ffn1_in"], inp["ffn2_in"]], axis=1).reshape(2 * DEPTH, D, 2 * DFF)
    fin = fin.reshape(2 * DEPTH, NC_, 128, 2, NHC, 128).transpose(0, 4, 2, 1, 3, 5)
    sh["ffn_in"] = np.ascontiguousarray(fin).reshape(2 * DEPTH, NHC, 128, NC_, 256)
    fout = np.stack([inp["ffn1_out"], inp["ffn2_out"]], axis=1).reshape(2 * DEPTH, NHC, 128, D)
    sh["ffn_out"] = np.ascontiguousarray(fout)
    wi = inp["odd_w_in"]
    Z = lambda n: np.zeros((2, D, n), f32)
    cq, ckv, krc = wi[:, :, 0:256], wi[:, :, 256:384], wi[:, :, 384:416]
    za, zb = wi[:, :, 416:928], wi[:, :, 928:1440]
    kr_rot = np.concatenate([krc[:, :, 16:32], krc[:, :, 0:16]], axis=2)
    slabs = [cq,
             np.concatenate([ckv, Z(64), krc, Z(32)], axis=2),
             np.concatenate([Z(64), kr_rot, Z(160)], axis=2)]
    for q in range(4):
        slabs.append(np.concatenate([za[:, :, q * 128:(q + 1) * 128], zb[:, :, q * 128:(q + 1) * 128]], axis=2))
    oin = np.stack(slabs, axis=1)
    oin = oin.reshape(2, 7, NC_, 128, 256).transpose(0, 1, 3, 2, 4)
    sh["odd_in"] = np.ascontiguousarray(oin).astype(f32)
    sh["odd_out"] = np.ascontiguousarray(inp["odd_w_out"].reshape(2, 8, 128, D)).astype(f32)
    wq = inp["wq_up"]
    sh["wq"] = np.ascontiguousarray(wq.reshape(2, 2, 128, 768).transpose(0, 2, 1, 3)).astype(f32)
    wq4 = wq.reshape(2, 256, 8, 96)
    wqr = np.concatenate([np.zeros((2, 256, 8, 64), f32), wq4[..., 80:96], wq4[..., 64:80]], axis=-1).reshape(2, 256, 768)
    sh["wqr"] = np.ascontiguousarray(wqr.reshape(2, 2, 128, 768).transpose(0, 2, 1, 3)).astype(f32)
    wkv4 = inp["wkv_up"].reshape(2, 128, 8, 128)
    sh["wkv"] = np.ascontiguousarray(np.concatenate([wkv4[..., 0:64].reshape(2, 128, 512),
                                                     wkv4[..., 64:128].reshape(2, 128, 512)], axis=-1)).astype(f32)
    col = lambda v, n: v.reshape(2, n, 128).transpose(0, 2, 1)
    cwp = inp["conv_w"].reshape(2, 31, 4, 128).transpose(0, 3, 2, 1).reshape(2, 128, 124)
    sh["odd_small"] = np.ascontiguousarray(np.concatenate(
        [col(inp["q_norm"], 2), col(inp["kv_norm"], 1), cwp, col(inp["conv_b"], 4), col(inp["conv_ln_g"], 4),
         col(inp["conv_ln_b"], 4)], axis=2)).astype(f32)
    sh["rope"] = rope_table()
    ew = inp["even_w_in"]
    ein = ew.reshape(2, NC_, 128, 11, 256).transpose(0, 3, 2, 1, 4)
    sh["even_in"] = np.ascontiguousarray(ein).astype(f32)
    sh["even_out"] = np.ascontiguousarray(inp["even_w_out"].reshape(2, 8, 128, D)).astype(f32)
    sh["gsu_wsT"] = np.ascontiguousarray(inp["gsu_ws"].transpose(0, 3, 1, 2)).astype(f32)
    sh["gsu_bs4"] = np.ascontiguousarray(np.tile(inp["gsu_bs"], (1, 1, 4)).reshape(2, 1, 4, 512)).astype(f32)
    rows = np.concatenate([inp["shift_mu"], inp["k_k"], inp["k_a"], inp["r_k"].reshape(2, 512), inp["lnx_g"],
                           inp["lnx_b"], inp["gsu_ln_g"], inp["gsu_ln_b"], inp["decay_w0"], inp["iclr_a0"]], axis=1)
    sh["even_rows"] = np.ascontiguousarray(rows.reshape(2, 1, EV_NR)).astype(f32)
    sh["lora_d"] = np.ascontiguousarray(inp["decay_up"]).astype(f32)
    sh["lora_i"] = np.ascontiguousarray(np.concatenate([np.zeros((2, 64, 512), f32), inp["iclr_up"]], axis=1)).astype(f32)
    sh["lora_g"] = np.ascontiguousarray(inp["gate_up"]).astype(f32)
    return sh


def rope_table():
    f32 = np.float32
    inv_freq = (f32(10000.0) ** (-(np.arange(0, 32, 2, dtype=f32) / f32(32)))).astype(f32)
    ang = (np.arange(S, dtype=f32)[:, None] * inv_freq[None, :]).astype(f32)
    cos = np.cos(ang.astype(np.float64)).astype(f32).T
    sin = np.sin(ang.astype(np.float64)).astype(f32).T
    t = np.zeros((128, 2, S), f32)
    t[64:80, 0] = cos
    t[80:96, 0] = cos
    t[64:80, 1] = -sin
    t[80:96, 1] = sin
    return t


def prep_x(x):
    return [np.ascontiguousarray(x[b].T.reshape(NC_, 128, S).transpose(1, 0, 2)) for b in range(x.shape[0])]


def unprep_out(o):
    return np.ascontiguousarray(o.transpose(2, 1, 0).reshape(S, D))


FULL_CFG = {"layers": DEPTH}


def kernel(**inputs):
    inp = {k: np.asarray(v) for k, v in inputs.items()}
    sh = prep_shared(inp)
    xs = prep_x(inp["x"].astype(np.float32))
    nc = build_program(FULL_CFG)
    in_maps = [dict(sh, xT=xs[b]) for b in range(len(xs))]
    res = run_bass_kernel_spmd(nc, in_maps, core_ids=list(range(len(xs))))
    out = np.stack([unprep_out(np.asarray(r["outT"])) for r in res.results], axis=0)
    return out.astype(np.float32)
```
